# Optimizing a Trainium2 kernel written in Bass

```python
import jax
import jax.numpy as jnp
from jax import lax
import numpy as np

D_MODEL = 2048
BATCH = 2
SEQ = 4096
DEPTH = 4

GRID_W = 64
CTX_LEN = 256
N_MIXERS = 3
N_SUB = 3
D_FF = 5632
RMS_EPS = 1e-6
MOD_INIT = 0.5

HG_HEAD_DIM = 128
HG_HEADS = D_MODEL // HG_HEAD_DIM
HG_QF = HG_HEADS * HG_HEAD_DIM
HG_IV = HG_HEADS * HG_HEAD_DIM
HG_CHUNK = 16

MLA_HEADS = 16
MLA_Q_RANK = 512
MLA_KV_RANK = 512
MLA_NOPE = 128
MLA_ROPE = 64
MLA_V = 128
MLA_SCALE = (MLA_NOPE + MLA_ROPE) ** -0.5
ATTN_BLOCK = 128
ROPE_THETA = 10000.0
ROPE_FREQS = MLA_ROPE // 4

FOURIER_GROUPS = 8

kernel_name = 'hybrid_hgrn2_mla_fnet_macaron_dit'


def n_layers_of_kind(kind):
    return len(range(kind, DEPTH, N_MIXERS))


def rmsnorm(x, g):
    xf = x.astype(jnp.float32)
    y = xf * lax.rsqrt(jnp.mean(xf * xf, axis=-1, keepdims=True) + RMS_EPS)
    return (y * g.astype(jnp.float32)).astype(x.dtype)


def adanorm(z, g, shift, scale):
    return rmsnorm(z, g) * (1 + scale) + shift


def swiglu(h, w_gu, w_down):
    gate, up = jnp.split(h @ w_gu, 2, axis=-1)
    return (jax.nn.silu(gate) * up) @ w_down


def ffn_sublayer(z, g, mm, s, w_gu, w_down):
    return z + 0.5 * mm[:, :, s, 2] * swiglu(adanorm(z, g, mm[:, :, s, 0], mm[:, :, s, 1]), w_gu, w_down)


def axial_rope_tables(n_tokens):
    rows = n_tokens // GRID_W
    r = jnp.broadcast_to(jnp.arange(rows, dtype=jnp.float32)[:, None], (rows, GRID_W)).reshape(-1)
    col = jnp.broadcast_to(jnp.arange(GRID_W, dtype=jnp.float32)[None, :], (rows, GRID_W)).reshape(-1)
    inv_freq = ROPE_THETA ** (-jnp.arange(ROPE_FREQS, dtype=jnp.float32) / ROPE_FREQS)
    ang = jnp.stack([r, col], axis=-1)[..., None] * inv_freq
    return jnp.cos(ang), jnp.sin(ang)


def axial_rope(x, cos, sin):
    xs = x.reshape(x.shape[:-1] + (2, 2, ROPE_FREQS))
    xa, xb = xs[..., 0, :], xs[..., 1, :]
    cos, sin = cos.astype(x.dtype), sin.astype(x.dtype)
    return jnp.stack([xa * cos - xb * sin, xb * cos + xa * sin], axis=-2).reshape(x.shape)


def hgrn2_chunk_scan(q, k, v, log_f, s0):
    B, T, H, _ = q.shape
    V = v.shape[-1]
    n = T // HG_CHUNK

    def chunks(a):
        return a.reshape(B, n, HG_CHUNK, H, a.shape[-1]).transpose(1, 0, 3, 2, 4)

    q, k, v, log_f = chunks(q), chunks(k), chunks(v), chunks(log_f)
    b = jnp.cumsum(log_f, axis=-2)
    b_end = b[..., -1:, :]
    q_dec = q * jnp.exp(b)
    k_to_end = k * jnp.exp(b_end - b)
    lower_tri = jnp.tril(jnp.ones((HG_CHUNK, HG_CHUNK), dtype=bool))
    a = jnp.einsum('nbhck,nbhsk->nbhcs', q_dec, k * jnp.exp(-b))
    o_intra = jnp.einsum('nbhcs,nbhsv->nbhcv', jnp.where(lower_tri, a, 0.0), v)
    decay = jnp.exp(b_end[..., 0, :])

    def step(s, inp):
        qd, kd, vn, dn = inp
        o_inter = jnp.einsum('bhck,bhkv->bhcv', qd, s)
        s = s * dn[..., None] + jnp.einsum('bhck,bhcv->bhkv', kd, vn)
        return s, o_inter

    s_final, o_inter = lax.scan(step, s0, (q_dec, k_to_end, v, decay))
    o = (o_intra + o_inter).transpose(1, 0, 3, 2, 4).reshape(B, T, H, V)
    return o, s_final


def hgrn2_mixer(h_lat, h_ctx, w_in, lb_fwd, lb_bwd, g_norm, w_out, with_ctx_out):
    splits = [HG_QF, 2 * HG_QF, 3 * HG_QF, 3 * HG_QF + HG_IV]

    def branch_inputs(h):
        B, T, _ = h.shape
        q, zf, zb, i, gate = jnp.split(h @ w_in, splits, axis=-1)
        heads = lambda a: a.astype(jnp.float32).reshape(B, T, HG_HEADS, HG_HEAD_DIM)
        q, i = heads(jax.nn.silu(q)), heads(i)
        dirs = []
        for z, lb in ((zf, lb_fwd), (zb, lb_bwd)):
            z = heads(z)
            lb = lb.reshape(HG_HEADS, HG_HEAD_DIM)
            log_f = jnp.logaddexp(jnp.log(lb), jnp.log1p(-lb) + jax.nn.log_sigmoid(z))
            k = (1.0 - lb) * jax.nn.sigmoid(-z)
            dirs.append((log_f, k))
        return q, i, gate, dirs

    def readout(o, gate, dtype):
        B, T = o.shape[:2]
        o = rmsnorm(o, g_norm.reshape(HG_HEADS, HG_HEAD_DIM)).reshape(B, T, HG_IV).astype(dtype)
        return (o * jax.nn.silu(gate)) @ w_out

    flip = lambda a: a[:, ::-1]
    qc, ic, gc, (fc, bc) = branch_inputs(h_ctx)
    ql, il, gl, (fl, bl) = branch_inputs(h_lat)
    s_zero = jnp.zeros((h_lat.shape[0], HG_HEADS, HG_HEAD_DIM, HG_HEAD_DIM), jnp.float32)
    o_cf, s_f = hgrn2_chunk_scan(qc, fc[1], ic, fc[0], s_zero)
    o_cb, s_b = hgrn2_chunk_scan(flip(qc), flip(bc[1]), flip(ic), flip(bc[0]), s_zero)
    o_lf, _ = hgrn2_chunk_scan(ql, fl[1], il, fl[0], s_f)
    o_lb, _ = hgrn2_chunk_scan(flip(ql), flip(bl[1]), flip(il), flip(bl[0]), s_b)
    y_lat = readout(o_lf + flip(o_lb), gl, h_lat.dtype)
    y_ctx = readout(o_cf + flip(o_cb), gc, h_ctx.dtype) if with_ctx_out else None
    return y_lat, y_ctx


def mla_mixer(h_lat, h_ctx, w_dqkv, q_norm, kv_norm, w_uq, w_ukv, w_o, cos, sin, with_ctx_out):
    def project(h, rotate):
        B, T, _ = h.shape
        cq, ckv, k_rope = jnp.split(h @ w_dqkv, [MLA_Q_RANK, MLA_Q_RANK + MLA_KV_RANK], axis=-1)
        q = (rmsnorm(cq, q_norm) @ w_uq).reshape(B, T, MLA_HEADS, MLA_NOPE + MLA_ROPE)
        kv = (rmsnorm(ckv, kv_norm) @ w_ukv).reshape(B, T, MLA_HEADS, MLA_NOPE + MLA_V)
        q_nope, q_rope = q[..., :MLA_NOPE], q[..., MLA_NOPE:]
        k_nope, v = kv[..., :MLA_NOPE], kv[..., MLA_NOPE:]
        if rotate:
            q_rope = axial_rope(q_rope, cos[:, None], sin[:, None])
            k_rope = axial_rope(k_rope, cos, sin)
        return q_nope, q_rope, k_nope, k_rope, v

    def attend(q_nope, q_rope, k_nope, k_rope, v):
        s = (jnp.einsum('bqhd,bkhd->bhqk', q_nope, k_nope, preferred_element_type=jnp.float32)
             + jnp.einsum('bqhr,bkr->bhqk', q_rope, k_rope, preferred_element_type=jnp.float32)) * MLA_SCALE
        p = jax.nn.softmax(s, axis=-1).astype(v.dtype)
        return jnp.einsum('bhqk,bkhd->bqhd', p, v)

    qn_c, qr_c, kn_c, kr_c, v_c = project(h_ctx, False)
    qn_l, qr_l, kn_l, kr_l, v_l = project(h_lat, True)
    kn = jnp.concatenate([kn_c, kn_l], axis=1)
    kr = jnp.concatenate([kr_c, kr_l], axis=1)
    vv = jnp.concatenate([v_c, v_l], axis=1)
    B, T = h_lat.shape[:2]
    nb = T // ATTN_BLOCK
    to_blocks = lambda a: jnp.moveaxis(a.reshape((B, nb, ATTN_BLOCK) + a.shape[2:]), 1, 0)
    o_blk = lax.map(lambda qb: attend(qb[0], qb[1], kn, kr, vv), (to_blocks(qn_l), to_blocks(qr_l)))
    y_lat = jnp.moveaxis(o_blk, 0, 1).reshape(B, T, MLA_HEADS * MLA_V) @ w_o
    y_ctx = None
    if with_ctx_out:
        y_ctx = attend(qn_c, qr_c, kn_c, kr_c, v_c).reshape(B, h_ctx.shape[1], MLA_HEADS * MLA_V) @ w_o
    return y_lat, y_ctx


def fourier_mixer(h, w_out):
    B, T, D = h.shape
    hg = h.astype(jnp.float32).reshape(B, T, FOURIER_GROUPS, D // FOURIER_GROUPS)
    y = jnp.fft.fft2(hg, axes=(1, 3), norm='ortho').real
    return y.reshape(B, T, D).astype(h.dtype) @ w_out


def setup_inputs(seed: int = 0) -> dict:
    key = jax.random.key(seed)
    keys = iter(jax.random.split(key, 40))

    def normal(shape, scale=1.0):
        return scale * jax.random.normal(next(keys), shape, jnp.float32)

    def dense(shape, scale=1.0):
        return normal(shape, scale * shape[-2] ** -0.5)

    def gain(shape):
        return 1.0 + normal(shape, 0.05)

    n_a, n_b, n_c = (n_layers_of_kind(k) for k in range(N_MIXERS))
    D = D_MODEL
    return {
        'x': normal((BATCH, SEQ, D)),
        'c': normal((BATCH, D)),
        'ctx': normal((BATCH, CTX_LEN, D)),
        'c_ctx': normal((D,)),
        'mod_w': dense((DEPTH, D, N_SUB * 3 * D), MOD_INIT),
        'mod_b': normal((DEPTH, N_SUB * 3 * D), 0.02),
        'norm_g': gain((DEPTH, N_SUB, D)),
        'ffn1_w_gu': dense((DEPTH, D, 2 * D_FF)),
        'ffn1_w_down': dense((DEPTH, D_FF, D)),
        'ffn2_w_gu': dense((DEPTH, D, 2 * D_FF)),
        'ffn2_w_down': dense((DEPTH, D_FF, D)),
        'hgrn_w_in': dense((n_a, D, 3 * HG_QF + 2 * HG_IV)),
        'hgrn_lb_logits': normal((2, n_a, HG_QF), 0.5),
        'hgrn_g_norm': gain((n_a, HG_IV)),
        'hgrn_w_out': dense((n_a, HG_IV, D)),
        'mla_w_dqkv': dense((n_b, D, MLA_Q_RANK + MLA_KV_RANK + MLA_ROPE)),
        'mla_q_norm': gain((n_b, MLA_Q_RANK)),
        'mla_kv_norm': gain((n_b, MLA_KV_RANK)),
        'mla_w_uq': dense((n_b, MLA_Q_RANK, MLA_HEADS * (MLA_NOPE + MLA_ROPE))),
        'mla_w_ukv': dense((n_b, MLA_KV_RANK, MLA_HEADS * (MLA_NOPE + MLA_V))),
        'mla_w_o': dense((n_b, MLA_HEADS * MLA_V, D)),
        'fnet_w_out': dense((n_c, D, D)),
        'final_g': gain((D,)),
    }


def reference(x, c, ctx, c_ctx, mod_w, mod_b, norm_g, ffn1_w_gu, ffn1_w_down, ffn2_w_gu, ffn2_w_down,
              hgrn_w_in, hgrn_lb_logits, hgrn_g_norm, hgrn_w_out,
              mla_w_dqkv, mla_q_norm, mla_kv_norm, mla_w_uq, mla_w_ukv, mla_w_o,
              fnet_w_out, final_g):
    B, T, D = x.shape
    cos, sin = axial_rope_tables(T)
    lb = jnp.cumsum(jax.nn.softmax(hgrn_lb_logits.astype(jnp.float32), axis=1), axis=1)
    lb = lb - lb[:, :1]
    h, hc = x, ctx
    for i in range(DEPTH):
        kind, j = i % N_MIXERS, i // N_MIXERS
        last = i == DEPTH - 1
        ctx_in = not (last and kind == 2)
        m = (jax.nn.silu(c) @ mod_w[i] + mod_b[i]).reshape(B, 1, N_SUB, 3, D)
        mc = (jax.nn.silu(c_ctx) @ mod_w[i] + mod_b[i]).reshape(1, 1, N_SUB, 3, D)
        h = ffn_sublayer(h, norm_g[i, 0], m, 0, ffn1_w_gu[i], ffn1_w_down[i])
        if ctx_in:
            hc = ffn_sublayer(hc, norm_g[i, 0], mc, 0, ffn1_w_gu[i], ffn1_w_down[i])
        n = adanorm(h, norm_g[i, 1], m[:, :, 1, 0], m[:, :, 1, 1])
        nc = adanorm(hc, norm_g[i, 1], mc[:, :, 1, 0], mc[:, :, 1, 1]) if ctx_in else None
        if kind == 0:
            y, yc = hgrn2_mixer(n, nc, hgrn_w_in[j], lb[0, j], lb[1, j], hgrn_g_norm[j], hgrn_w_out[j], not last)
        elif kind == 1:
            y, yc = mla_mixer(n, nc, mla_w_dqkv[j], mla_q_norm[j], mla_kv_norm[j], mla_w_uq[j], mla_w_ukv[j],
                              mla_w_o[j], cos, sin, not last)
        else:
            y = fourier_mixer(n, fnet_w_out[j])
            yc = None if last else fourier_mixer(nc, fnet_w_out[j])
        h = h + m[:, :, 1, 2] * y
        if not last:
            hc = hc + mc[:, :, 1, 2] * yc
        h = ffn_sublayer(h, norm_g[i, 2], m, 2, ffn2_w_gu[i], ffn2_w_down[i])
        if not last:
            hc = ffn_sublayer(hc, norm_g[i, 2], mc, 2, ffn2_w_gu[i], ffn2_w_down[i])
    return rmsnorm(h, final_g)
```

```python
import contextlib
import types
import numpy as np
import concourse.bass as bass
import concourse.mybir as mybir
from concourse.bass_utils import run_bass_kernel_spmd

F32 = mybir.dt.float32
BF16 = mybir.dt.bfloat16
AF = mybir.ActivationFunctionType
ALU = mybir.AluOpType

ENGS = ("pe", "act", "dve", "pool", "sp")
NDMASEM = 6

D = 2048
DC = 16
DFF = 5632
FC = 44
EPS = 1e-6
NCTX = 256
NLAT = 4096
L = NCTX + NLAT
TB = 1024
BLOCKS = [(0, NCTX, 1)] + [(NCTX + TB * i, TB, 0) for i in range(NLAT // TB)]
NMODC = 144
ARENA = (DC + FC) * TB + 3 * 8192 + 2 * 4096


def freeze(fn):
    if fn.__closure__ is None:
        return fn
    cells = []
    for c in fn.__closure__:
        try:
            cells.append(types.CellType(c.cell_contents))
        except ValueError:
            cells.append(c)
    return types.FunctionType(fn.__code__, fn.__globals__, fn.__name__, fn.__defaults__, tuple(cells))


class Prog:
    def __init__(self, nc):
        self.nc = nc
        self.ops = []
        self.last_w = {}
        self.readers = {}
        self.last_eng = {}
        self.dmas_since = []

    def add(self, eng, fn, reads=(), writes=(), dma=False, extra_deps=()):
        idx = len(self.ops)
        deps = set(extra_deps)
        for k in reads:
            w = self.last_w.get(k)
            if w is not None:
                deps.add(w)
        for k in writes:
            w = self.last_w.get(k)
            if w is not None:
                deps.add(w)
            for r in self.readers.get(k, ()):
                deps.add(r)
        for k in reads:
            lst = self.readers.setdefault(k, [])
            if not dma:
                lst[:] = [r for r in lst if self.ops[r]["dma"] or self.ops[r]["eng"] != eng]
            lst.append(idx)
        for k in writes:
            self.last_w[k] = idx
            self.readers[k] = []
        deps.discard(idx)
        self.ops.append(dict(eng=eng, fn=freeze(fn), deps=deps, dma=dma, sig=False))
        self.last_eng[eng] = idx
        if dma:
            self.dmas_since.append(idx)
        return idx

    def barrier(self):
        deps = set(self.last_eng.values()) | set(self.dmas_since)
        self.dmas_since = []
        for e in ENGS:
            self.add(e, lambda eng: eng.nop(), extra_deps=deps)
        self.last_w = {}
        self.readers = {}

    def emit(self):
        nc = self.nc
        ops = self.ops
        for i, o in enumerate(ops):
            nd = set()
            for d in o["deps"]:
                p = ops[d]
                if p["eng"] == o["eng"] and o["eng"] == "pe" and not p["dma"]:
                    continue
                nd.add(d)
                p["sig"] = True
            o["deps"] = nd
        with contextlib.ExitStack() as st:
            esem = {e: st.enter_context(nc.semaphore("s_" + e)) for e in ENGS}
            dsem = {e: [st.enter_context(nc.semaphore("d_%s%d" % (e, j))) for j in range(NDMASEM)]
                    for e in ("sp", "act", "pool")}
            ecount = {e: 0 for e in ENGS}
            dcount = {e: 0 for e in dsem}
            dtarget = {e: [0] * NDMASEM for e in dsem}
            per_eng = {e: [] for e in ENGS}
            for i, o in enumerate(ops):
                e = o["eng"]
                if o["dma"]:
                    j = dcount[e] % NDMASEM
                    dcount[e] += 1
                    o["prev_target"] = dtarget[e][j]
                    dtarget[e][j] += 16
                    o["sem"] = ("d", e, j)
                    o["target"] = dtarget[e][j]
                elif o["sig"]:
                    ecount[e] += 1
                    o["sem"] = ("e", e, 0)
                    o["target"] = ecount[e]
                per_eng[e].append(i)
            self.stats = dict(ecount=dict(ecount), dcount=dict(dcount), nops={e: len(per_eng[e]) for e in ENGS})

            def semof(s):
                return esem[s[1]] if s[0] == "e" else dsem[s[1]][s[2]]

            block = st.enter_context(nc.Block())

            def body(e, eng):
                seen = {}
                for i in per_eng[e]:
                    o = ops[i]
                    waits = {}
                    for d in o["deps"]:
                        p = ops[d]
                        s = p["sem"]
                        waits[s] = max(waits.get(s, 0), p["target"])
                    if o["dma"] and o["prev_target"] > 0:
                        s = o["sem"]
                        waits[s] = max(waits.get(s, 0), o["prev_target"])
                    for s, v in waits.items():
                        if seen.get(s, 0) >= v:
                            continue
                        seen[s] = v
                        eng.wait_ge(semof(s), v)
                    ins = o["fn"](eng)
                    if o["dma"]:
                        ins.then_inc(semof(o["sem"]), 16)
                    elif o["sig"]:
                        ins.then_inc(semof(o["sem"]), 1)
                if e in dsem:
                    for j in range(NDMASEM):
                        if dtarget[e][j] > 0 and seen.get(("d", e, j), 0) < dtarget[e][j]:
                            eng.wait_ge(dsem[e][j], dtarget[e][j])

            if per_eng["sp"]:
                block.sync(lambda eng: body("sp", eng))
            if per_eng["act"]:
                block.scalar(lambda eng: body("act", eng))
            if per_eng["dve"]:
                block.vector(lambda eng: body("dve", eng))
            if per_eng["pool"]:
                block.gpsimd(lambda eng: body("pool", eng))
            if per_eng["pe"]:
                block.tensor(lambda eng: body("pe", eng))


class Ring:
    def __init__(self, name, aps):
        self.name, self.aps, self.i = name, aps, 0

    def next(self):
        j = self.i % len(self.aps)
        self.i += 1
        return self.aps[j], (self.name, j)


def tgroups(n, step):
    return [(t, min(step, n - t)) for t in range(0, n, step)]


def xkeys(pref, dc, t0, n):
    return [(pref, dc, t) for t in range((t0 // 256) * 256, t0 + n, 256)]


class G:
    pass


def cast_op(P, g, i, dst, src, reads, writes):
    if i % 2 == 0:
        P.add("act", lambda e: e.copy(dst, src), reads=reads, writes=writes)
    else:
        P.add("dve", lambda e: e.tensor_copy(dst, src), reads=reads, writes=writes)


def emit_rstd(P, g, ssp, ssk, n, dim):
    rs, rsk = g.rstd.next()
    P.add("dve", lambda e: e.tensor_scalar(rs[:, 0:n], ssp[:, 0:n], 1.0 / dim, EPS, ALU.mult, ALU.add),
          reads=[ssk], writes=[rsk])
    P.add("act", lambda e: e.sqrt(rs[:, 0:n], rs[:, 0:n]), reads=[rsk], writes=[rsk])
    P.add("dve", lambda e: e.reciprocal(rs[:, 0:n], rs[:, 0:n]), reads=[rsk], writes=[rsk])
    return rs, rsk


def emit_adanorm(P, g, src, c0, nb, gcol, shcol, colkeys, hkey="h"):
    xTv = src.rearrange("(c p) t -> p c t", p=128)
    for (t0, n) in tgroups(nb, 256):
        xs, xk = g.stg.next()
        P.add("sp", lambda e, xs=xs, t0=t0, n=n: e.dma_start(
            out=xs[:, 0:DC * n].rearrange("p (c t) -> p c t", t=n), in_=xTv[:, :, c0 + t0:c0 + t0 + n]),
            reads=[(hkey, c0 + t0)], writes=[xk], dma=True)
        ssp, ssk = g.psr.next()
        for dc in range(DC):
            sq, sqk = g.sq.next()
            P.add("act", lambda e, sq=sq, xs=xs, dc=dc, n=n: e.activation(
                out=sq[:, 0:n], in_=xs[:, dc * n:(dc + 1) * n], func=AF.Square), reads=[xk], writes=[sqk])
            P.add("pe", lambda e, sq=sq, ssp=ssp, dc=dc, n=n: e.matmul(
                ssp[:, 0:n], lhsT=g.ones[:, :], rhs=sq[:, 0:n], start=(dc == 0), stop=(dc == DC - 1)),
                reads=[sqk, "ones"], writes=[ssk])
        rs, rsk = emit_rstd(P, g, ssp, ssk, n, D)
        for dc in range(DC):
            tm, tmk = g.tmp.next()
            P.add("dve", lambda e, tm=tm, xs=xs, rs=rs, dc=dc, n=n: e.scalar_tensor_tensor(
                out=tm[:, 0:n], in0=xs[:, dc * n:(dc + 1) * n], scalar=gcol(dc), in1=rs[:, 0:n],
                op0=ALU.mult, op1=ALU.mult), reads=[xk, rsk] + colkeys, writes=[tmk])
            P.add("act", lambda e, tm=tm, dc=dc, n=n, t0=t0: e.activation(
                out=g.xn[:, dc * nb + t0: dc * nb + t0 + n], in_=tm[:, 0:n], func=AF.Identity,
                bias=shcol(dc), scale=1.0), reads=[tmk] + colkeys, writes=[("xn", dc, t0)])


def emit_linear(P, g, xn, KC, nb, wsrc, NOC, epi, xpref="xn", step=512):
    for oc in range(NOC):
        s, sk = g.stg.next()
        P.add("sp", lambda e, s=s, oc=oc: e.dma_start(out=s[:, 0:KC * 128], in_=wsrc[oc]), writes=[sk], dma=True)
        w, wk = g.wbr.next()
        cast_op(P, g, oc, w[:, 0:KC * 128], s[:, 0:KC * 128], [sk], [wk])
        for (t0, n) in tgroups(nb, step):
            pp, ppk = g.psr.next()
            for kc in range(KC):
                P.add("pe", lambda e, pp=pp, w=w, kc=kc, t0=t0, n=n: e.matmul(
                    pp[:, 0:n], lhsT=w[:, kc * 128:(kc + 1) * 128],
                    rhs=xn[:, kc * nb + t0: kc * nb + t0 + n], start=(kc == 0), stop=(kc == KC - 1)),
                    reads=[wk] + xkeys(xpref, kc, t0, n), writes=[ppk])
            epi(oc, t0, n, pp, ppk)


def epi_store(P, g, dst, c0, okey, scale=None):
    def epi(oc, t0, n, pp, ppk):
        o, ok = g.ost.next()
        if (oc + t0 // 512) % 2 == 0:
            P.add("act", lambda e: e.copy(o[:, 0:n], pp[:, 0:n]), reads=[ppk], writes=[ok])
        else:
            P.add("dve", lambda e: e.tensor_copy(o[:, 0:n], pp[:, 0:n]), reads=[ppk], writes=[ok])
        P.add("sp", lambda e: e.dma_start(out=dst[oc * 128:(oc + 1) * 128, c0 + t0:c0 + t0 + n], in_=o[:, 0:n]),
              reads=[ok], writes=[(okey, oc, c0 + t0)], dma=True)
    return epi


def epi_residual(P, g, h, c0, gtcol, colkeys):
    def epi(oc, t0, n, pp, ppk):
        o, ok = g.ost.next()
        P.add("sp", lambda e: e.dma_start(out=o[:, 0:n], in_=h[oc * 128:(oc + 1) * 128, c0 + t0:c0 + t0 + n]),
              reads=[("hc", oc, c0 + t0)], writes=[ok], dma=True)
        P.add("dve", lambda e: e.scalar_tensor_tensor(
            out=o[:, 0:n], in0=pp[:, 0:n], scalar=gtcol(oc), in1=o[:, 0:n], op0=ALU.mult, op1=ALU.add),
            reads=[ppk, ok] + colkeys, writes=[ok])
        P.add("sp", lambda e: e.dma_start(out=h[oc * 128:(oc + 1) * 128, c0 + t0:c0 + t0 + n], in_=o[:, 0:n]),
              reads=[ok], writes=[("hc", oc, c0 + t0)] + [("h", c0 + t) for t in range((t0 // 256) * 256, t0 + n, 256)],
              dma=True)
    return epi


def modcol(g, l, kind, s, t, dc):
    j = ((l * NMODC + s * 48 + t * 16 + dc) * 2) + kind
    return g.modc[:, j:j + 1]


def emit_cols(P, g, l, s, kind, gsrc, half_gate):
    sc = g.modc[:, :].rearrange("p (l c r) -> p l c r", l=4, r=2)[:, l, s * 48 + 16: s * 48 + 32, kind]
    gt = g.modc[:, :].rearrange("p (l c r) -> p l c r", l=4, r=2)[:, l, s * 48 + 32: s * 48 + 48, kind]
    P.add("dve", lambda e: e.scalar_tensor_tensor(out=g.dcol[:, 0:16], in0=sc, scalar=1.0, in1=gsrc,
                                                   op0=ALU.add, op1=ALU.mult), reads=["modc", "gcols"], writes=["dcol"])
    P.add("dve", lambda e: e.tensor_scalar_mul(g.dcol[:, 16:32], gt, 0.5 if half_gate else 1.0),
          reads=["modc", "dcol"], writes=["dcol"])


def stage_mod(P, g):
    P.add("sp", lambda e: e.dma_start(out=g.cs[:, :], in_=g.d_cT), writes=["cs"], dma=True)
    P.add("act", lambda e: e.activation(out=g.cs[:, :], in_=g.cs[:, :], func=AF.Silu), reads=["cs"], writes=["cs"])
    g.modb = carve(g, 0, 4 * NMODC, F32)
    P.add("sp", lambda e: e.dma_start(out=g.modb[:, :], in_=g.d_modb), writes=["modb"], dma=True)
    for l in range(4):
        wv = g.d_modw[l].rearrange("(c p) n -> p c n", p=128)
        for nb_ in range(36):
            for hf in range(2):
                s, sk = g.stg.next()
                P.add("sp", lambda e, s=s, nb_=nb_, hf=hf, wv=wv: e.dma_start(
                    out=s[:, :].rearrange("p (c n) -> p c n", n=512), in_=wv[:, hf * 8:(hf + 1) * 8, nb_ * 512:(nb_ + 1) * 512]),
                    writes=[sk], dma=True)
                for j in range(4):
                    pp = g.ps[(nb_ * 4 + j) % 8]
                    ppk = ("ps", (nb_ * 4 + j) % 8)
                    for k8 in range(8):
                        kc = hf * 8 + k8
                        P.add("pe", lambda e, s=s, pp=pp, j=j, k8=k8, kc=kc: e.matmul(
                            pp[:, 0:2], lhsT=s[:, k8 * 512 + j * 128: k8 * 512 + (j + 1) * 128],
                            rhs=g.cs[:, kc * 2:(kc + 1) * 2], start=(kc == 0), stop=(kc == 15)),
                            reads=[sk, "cs"], writes=[ppk])
                    if hf == 1:
                        ch = l * NMODC + nb_ * 4 + j
                        P.add("dve", lambda e, pp=pp, ch=ch: e.tensor_tensor(
                            g.modc[:, ch * 2:ch * 2 + 2], pp[:, 0:2],
                            g.modb[:, ch:ch + 1].to_broadcast([128, 2]), ALU.add),
                            reads=[ppk, "modb"], writes=["modc"])


def stage_ffn(P, g, l, which, skip_ctx=False):
    s = 0 if which == 0 else 2
    wgu, wdn = (g.d_wgu1[l], g.d_wdn1[l]) if which == 0 else (g.d_wgu2[l], g.d_wdn2[l])
    gsrc = g.gcols[:, (l * 3 + s) * 16:(l * 3 + s + 1) * 16]
    for (c0, nb, kind) in BLOCKS:
        if kind == 1 and skip_ctx:
            continue
        ffn_block(P, g, l, s, wgu, wdn, gsrc, c0, nb, kind)
    P.barrier()


def ffn_block(P, g, l, s, wgu, wdn, gsrc, c0, nb, kind):
    if True:
        emit_cols(P, g, l, s, kind, gsrc, True)
        gcol = lambda dc: g.dcol[:, dc:dc + 1]
        hgcol = lambda dc: g.dcol[:, 16 + dc:17 + dc]
        shcol = lambda dc, kind=kind: modcol(g, l, kind, s, 0, dc)
        emit_adanorm(P, g, g.d_h, c0, nb, gcol, shcol, ["dcol", "modc"])
        TG = tgroups(nb, 512)
        for fc in range(FC):
            st_, sk = g.stg.next()
            P.add("sp", lambda e, st_=st_, fc=fc: e.dma_start(out=st_[:, :], in_=wgu[fc]), writes=[sk], dma=True)
            w, wk = g.wbr.next()
            P.add("act", lambda e, w=w, st_=st_: e.copy(w[:, 0:2048], st_[:, 0:2048]), reads=[sk], writes=[(wk, 0)])
            P.add("dve", lambda e, w=w, st_=st_: e.tensor_copy(w[:, 2048:4096], st_[:, 2048:4096]), reads=[sk], writes=[(wk, 1)])
            for (t0, n) in TG:
                pg, pgk = g.psr.next()
                pu, puk = g.psr.next()
                for hh, (pp, ppk) in enumerate(((pg, pgk), (pu, puk))):
                    for dc in range(DC):
                        P.add("pe", lambda e, pp=pp, w=w, hh=hh, dc=dc, t0=t0, n=n: e.matmul(
                            pp[:, 0:n], lhsT=w[:, (hh * DC + dc) * 128:(hh * DC + dc + 1) * 128],
                            rhs=g.xn[:, dc * nb + t0: dc * nb + t0 + n], start=(dc == 0), stop=(dc == DC - 1)),
                            reads=[(wk, hh)] + xkeys("xn", dc, t0, n), writes=[ppk])
                g_, gk = g.sg.next()
                P.add("act", lambda e, g_=g_, pg=pg, n=n: e.activation(out=g_[:, 0:n], in_=pg[:, 0:n], func=AF.Silu),
                      reads=[pgk], writes=[gk])
                P.add("dve", lambda e, g_=g_, pu=pu, n=n, fc=fc, t0=t0: e.tensor_tensor(
                    g.act[:, fc * nb + t0: fc * nb + t0 + n], pu[:, 0:n], g_[:, 0:n], ALU.mult),
                    reads=[puk, gk], writes=[("act", fc, t0)])
        for dc in range(DC):
            ws, wks = [], []
            for hf in range(2):
                st_, sk = g.stg.next()
                P.add("sp", lambda e, st_=st_, dc=dc, hf=hf: e.dma_start(
                    out=st_[:, 0:2816], in_=wdn[dc][:, hf * 2816:(hf + 1) * 2816]), writes=[sk], dma=True)
                w, wk = g.wbr.next()
                cast_op(P, g, hf, w[:, 0:2816], st_[:, 0:2816], [sk], [wk])
                ws.append(w)
                wks.append(wk)
            for (t0, n) in TG:
                xr, xrk = g.ost.next()
                P.add("sp", lambda e, xr=xr, dc=dc, t0=t0, n=n: e.dma_start(
                    out=xr[:, 0:n], in_=g.d_h[dc * 128:(dc + 1) * 128, c0 + t0:c0 + t0 + n]),
                    reads=[("hc", dc, c0 + t0)], writes=[xrk], dma=True)
                pp, ppk = g.psr.next()
                for fc in range(FC):
                    w, wk = ws[fc // 22], wks[fc // 22]
                    P.add("pe", lambda e, pp=pp, w=w, fc=fc, t0=t0, n=n: e.matmul(
                        pp[:, 0:n], lhsT=w[:, (fc % 22) * 128:(fc % 22 + 1) * 128],
                        rhs=g.act[:, fc * nb + t0: fc * nb + t0 + n], start=(fc == 0), stop=(fc == FC - 1)),
                        reads=[wk, ("act", fc, t0)], writes=[ppk])
                P.add("dve", lambda e, xr=xr, pp=pp, n=n, dc=dc: e.scalar_tensor_tensor(
                    out=xr[:, 0:n], in0=pp[:, 0:n], scalar=hgcol(dc), in1=xr[:, 0:n],
                    op0=ALU.mult, op1=ALU.add), reads=[ppk, xrk, "dcol"], writes=[xrk])
                P.add("sp", lambda e, xr=xr, dc=dc, t0=t0, n=n: e.dma_start(
                    out=g.d_h[dc * 128:(dc + 1) * 128, c0 + t0:c0 + t0 + n], in_=xr[:, 0:n]), reads=[xrk],
                    writes=[("hc", dc, c0 + t0)] + [("h", c0 + t) for t in range((t0 // 256) * 256, t0 + n, 256)], dma=True)


def carve(g, off_bytes, ncols, dt):
    assert off_bytes % 4 == 0
    e0 = off_bytes // 2
    if dt == BF16:
        assert e0 + ncols <= ARENA
        return g.arena[:, e0:e0 + ncols]
    assert e0 + 2 * ncols <= ARENA
    return g.arena[:, e0:e0 + 2 * ncols].bitcast(F32)


def lin_views(g):
    g.xn = g.arena[:, 0:DC * TB]
    o_ = DC * TB * 2
    stage = [carve(g, o_ + i * 16384, 4096, F32) for i in range(6)]
    o_ += 6 * 16384
    wbf = [carve(g, o_ + i * 8192, 4096, BF16) for i in range(3)]
    g.stg, g.wbr = Ring("stage", stage[0:3]), Ring("wbf", wbf)
    g.stg2 = Ring("stage2", stage[3:6])
    g.psr = Ring("ps", g.ps)


def ffn_views(g):
    g.xn = g.arena[:, 0:DC * TB]
    g.act = g.arena[:, DC * TB:(DC + FC) * TB]
    o_ = (DC + FC) * TB
    stage = [g.arena[:, o_ + i * 8192: o_ + (i + 1) * 8192].bitcast(F32) for i in range(3)]
    o_ += 3 * 8192
    wbf = [g.arena[:, o_ + i * 4096: o_ + (i + 1) * 4096] for i in range(2)]
    g.stg, g.wbr = Ring("stage", stage), Ring("wbf", wbf)
    g.psr = Ring("ps", g.ps)


def stage_lbcols(P, g):
    P.add("sp", lambda e: e.dma_start(out=g.lbl[:, :], in_=g.d_lbl), writes=["lbl"], dma=True)
    P.add("pool", lambda e: e.memset(g.lbc[:, :], 0.0), writes=["lbc"])
    for d_ in range(2):
        b0 = ((1 * 2 + d_) * 3) * 16
        l0 = g.lbl[:, (d_ * 2 + 0) * 16:(d_ * 2 + 1) * 16]
        l1 = g.lbl[:, (d_ * 2 + 1) * 16:(d_ * 2 + 2) * 16]
        P.add("dve", lambda e, b0=b0, l0=l0, l1=l1: e.tensor_tensor(g.lbc[:, b0:b0 + 16], l1, l0, ALU.subtract),
              reads=["lbl", "lbc"], writes=["lbc"])
        P.add("act", lambda e, b0=b0: e.activation(out=g.lbc[:, b0:b0 + 16], in_=g.lbc[:, b0:b0 + 16], func=AF.Sigmoid),
              reads=["lbc"], writes=["lbc"])
    for j in range(2):
        for d_ in range(2):
            b0 = ((j * 2 + d_) * 3) * 16
            P.add("dve", lambda e, b0=b0: e.tensor_scalar(g.lbc[:, b0 + 16:b0 + 32], g.lbc[:, b0:b0 + 16], -1.0, 1.0, ALU.mult, ALU.add),
                  reads=["lbc"], writes=["lbc"])
            P.add("dve", lambda e, b0=b0: e.tensor_scalar(g.lbc[:, b0 + 32:b0 + 48], g.lbc[:, b0:b0 + 16], 1.0, -1.0, ALU.mult, ALU.add),
                  reads=["lbc"], writes=["lbc"])
    P.add("sp", lambda e: e.dma_start(out=g.cst[:, :], in_=g.d_cst), writes=["cst"], dma=True)
    P.add("act", lambda e: e.copy(g.identb[:, :], g.cst[:, 0:128]), reads=["cst"], writes=["identb"])


def stage_hgrn_inproj(P, g, l, j):
    lin_views(g)
    gsrc = g.gcols[:, (l * 3 + 1) * 16:(l * 3 + 2) * 16]
    for (c0, nb, kind) in BLOCKS:
        emit_cols(P, g, l, 1, kind, gsrc, False)
        gcol = lambda dc: g.dcol[:, dc:dc + 1]
        shcol = lambda dc, kind=kind: modcol(g, l, kind, 1, 0, dc)
        emit_adanorm(P, g, g.d_h, c0, nb, gcol, shcol, ["dcol", "modc"])
        emit_linear(P, g, g.xn, DC, nb, g.d_hgw_in[j], 80, epi_store(P, g, g.d_pj, c0, "pj"))
    P.barrier()


def stage_hgrn_scan(P, g, j, nheads=16):
    NTI = L // 128
    NCH = L // 32
    SZ = L * 4
    Q = carve(g, 0 * SZ, L, F32)
    Z = carve(g, 1 * SZ, L, F32)
    A = carve(g, 2 * SZ, L, F32)
    Bc = carve(g, 3 * SZ, L, F32)
    E = carve(g, 4 * SZ, L, F32)
    I = carve(g, 5 * SZ, L, F32)
    o_ = 6 * SZ
    qd = carve(g, o_, L, BF16); o_ += L * 2
    ki = carve(g, o_, L, BF16); o_ += L * 2
    vtok = carve(g, o_, L, BF16); o_ += L * 2
    ibf = carve(g, o_, L, BF16); o_ += L * 2
    dec = carve(g, o_, NCH, F32); o_ += NCH * 4
    S32 = [carve(g, o_ + i * 512, 128, F32) for i in range(4)]; o_ += 4 * 512
    T32 = [carve(g, o_ + i * 512, 128, F32) for i in range(4)]; o_ += 4 * 512
    Sbf = [carve(g, o_ + i * 256, 128, BF16) for i in range(4)]; o_ += 4 * 256
    ATm = [carve(g, o_ + i * 256, 128, BF16) for i in range(3)]; o_ += 3 * 256
    KT = [carve(g, o_ + i * 1024, 512, BF16) for i in range(3)]; o_ += 3 * 1024
    OS = [carve(g, o_ + i * 512, 128, F32) for i in range(4)]; o_ += 4 * 512
    assert o_ <= ARENA * 2
    s32r, t32r, sbfr, atmr, ktr, osr = Ring("S32", S32), Ring("T32", T32), Ring("Sbf", Sbf), Ring("ATm", ATm), Ring("KT", KT), Ring("OS", OS)
    par, ptr, por, pur = Ring("ps", g.ps[0:2]), Ring("ps2", g.ps[2:3]), Ring("ps3", g.ps[3:5]), Ring("ps5", g.ps[5:7])
    pv = g.ps[7]
    identb = g.identb
    maskf = [g.cst[:, 128:256], g.cst[:, 256:384]]
    m01 = g.cst[:, 384:512]
    HALF = L // 2
    for h in range(nheads):
        hs = slice(h * 128, (h + 1) * 128)
        for hf in range(2):
            cs_ = slice(hf * HALF, (hf + 1) * HALF)
            P.add("sp", lambda e, cs_=cs_, h=h: e.dma_start(out=Q[:, cs_], in_=g.d_pj[h * 128:(h + 1) * 128, cs_]),
                  reads=["pj_all"], writes=[("Q", hf)], dma=True)
            P.add("sp", lambda e, cs_=cs_, h=h: e.dma_start(out=I[:, cs_], in_=g.d_pj[6144 + h * 128:6144 + (h + 1) * 128, cs_]),
                  reads=["pj_all"], writes=[("I", hf)], dma=True)
            P.add("act", lambda e, cs_=cs_: e.activation(out=Q[:, cs_], in_=Q[:, cs_], func=AF.Silu), reads=[("Q", hf)], writes=[("Q", hf)])
            P.add("pool", lambda e, cs_=cs_: e.tensor_copy(ibf[:, cs_], I[:, cs_]), reads=[("I", hf)], writes=[("ibf", hf)])
        pvb = pv[:, :].bitcast(BF16)
        for t4 in range(0, NTI, 4):
            nt_ = min(4, NTI - t4)
            for q in range(nt_):
                ti = t4 + q
                P.add("pe", lambda e, q=q, ti=ti: e.transpose(pvb[:, q * 128:(q + 1) * 128], ibf[:, ti * 128:(ti + 1) * 128], identb[:, :]),
                      reads=[("ibf", 0), ("ibf", 1), "identb"], writes=["pv"])
            P.add("act", lambda e, t4=t4, nt_=nt_: e.copy(vtok[:, t4 * 128:(t4 + nt_) * 128], pvb[:, 0:nt_ * 128]),
                  reads=["pv"], writes=["vtok"])
        for d_ in range(2):
            lb0 = ((j * 2 + d_) * 3) * 16
            lbcol = g.lbc[:, lb0 + h: lb0 + h + 1]
            omlcol = g.lbc[:, lb0 + 16 + h: lb0 + 16 + h + 1]
            nomlcol = g.lbc[:, lb0 + 32 + h: lb0 + 32 + h + 1]
            zrow = 2048 * (1 + d_) + h * 128
            P.add("sp", lambda e, zrow=zrow: e.dma_start(out=Z[:, :], in_=g.d_pj[zrow:zrow + 128, :]), reads=["pj_all"], writes=["Z"], dma=True)
            P.add("act", lambda e: e.activation(out=Z[:, :], in_=Z[:, :], func=AF.Sigmoid), reads=["Z"], writes=["Z"])
            P.add("dve", lambda e, omlcol=omlcol, lbcol=lbcol: e.tensor_scalar(A[:, :], Z[:, :], omlcol, lbcol, ALU.mult, ALU.add),
                  reads=["Z", "lbc"], writes=["A"])
            P.add("act", lambda e: e.activation(out=A[:, :], in_=A[:, :], func=AF.Ln), reads=["A"], writes=["A"])
            P.add("dve", lambda e, omlcol=omlcol, nomlcol=nomlcol: e.tensor_scalar(Z[:, :], Z[:, :], nomlcol, omlcol, ALU.mult, ALU.add),
                  reads=["Z", "lbc"], writes=["Z"])
            for (t0, n) in tgroups(L, 128):
                P.add("dve", lambda e, t0=t0, n=n: e.tensor_tensor_scan(Bc[:, t0:t0 + n], m01[:, 0:n], A[:, t0:t0 + n], 0.0, ALU.mult, ALU.add),
                      reads=["A", "cst"], writes=["Bc"])
            P.add("act", lambda e: e.activation(out=dec[:, :], in_=Bc[:, :].rearrange("p (n c) -> p n c", c=32)[:, :, 31], func=AF.Exp),
                  reads=["Bc"], writes=["dec"])
            if d_ == 1:
                P.add("dve", lambda e: e.tensor_tensor(Bc[:, :], Bc[:, :], A[:, :], ALU.subtract), reads=["Bc", "A", "dec"], writes=["Bc"])
            sq_, sk_ = (1.0, -1.0) if d_ == 0 else (-1.0, 1.0)
            P.add("act", lambda e, sq_=sq_: e.activation(out=E[:, :], in_=Bc[:, :], func=AF.Exp, scale=sq_), reads=["Bc"], writes=["E"])
            P.add("dve", lambda e: e.tensor_tensor(qd[:, :], Q[:, :], E[:, :], ALU.mult), reads=["E", ("Q", 0), ("Q", 1)], writes=["qd"])
            P.add("act", lambda e, sk_=sk_: e.activation(out=E[:, :], in_=Bc[:, :], func=AF.Exp, scale=sk_), reads=["Bc", "qd"], writes=["E"])
            P.add("dve", lambda e: e.tensor_tensor(ki[:, :], Z[:, :], E[:, :], ALU.mult), reads=["E", "Z"], writes=["ki"])
            s_cur, s_cur_k = s32r.next()
            P.add("pool", lambda e, s_cur=s_cur: e.memset(s_cur[:, :], 0.0), writes=[s_cur_k])
            sb_cur, sb_cur_k = sbfr.next()
            P.add("pool", lambda e, sb_cur=sb_cur: e.memset(sb_cur[:, :], 0.0), writes=[sb_cur_k])
            if d_ == 0:
                order = list(range(NTI))
            else:
                order = [1, 0] + list(range(NTI - 1, 1, -1))
            dst = g.d_of if d_ == 0 else g.d_ob
            fr = {}

            def front(ti):
                ts_ = slice(ti * 128, (ti + 1) * 128)
                pa, pak = par.next()
                P.add("pe", lambda e: e.matmul(pa[:, 0:128], lhsT=ki[:, ts_], rhs=qd[:, ts_], start=True, stop=True),
                      reads=["ki", "qd"], writes=[pak])
                am, amk = atmr.next()
                P.add("dve", lambda e: e.tensor_tensor(am[:, :], pa[:, 0:128], maskf[d_], ALU.mult), reads=[pak, "cst"], writes=[amk])
                pt, ptk = ptr.next()
                ptb = pt[:, :].bitcast(BF16)
                P.add("pe", lambda e: e.transpose(ptb[:, 0:128], ki[:, ts_], identb[:, :]), reads=["ki", "identb"], writes=[ptk])
                kt, ktk = ktr.next()
                for c in range(4):
                    P.add("act", lambda e, c=c: e.activation(out=kt[:, c * 128:(c + 1) * 128], in_=ptb[:, 0:128], func=AF.Identity,
                                                            scale=g.cst[:, 512 + c:513 + c]), reads=[ptk, "cst"], writes=[(ktk, c)])
                po, pok = por.next()
                P.add("pe", lambda e: e.matmul(po[:, 0:128], lhsT=vtok[:, ts_], rhs=am[:, :], start=True, stop=False),
                      reads=["vtok", amk], writes=[pok])
                pu, puk = pur.next()
                for c in range(4):
                    P.add("pe", lambda e, c=c: e.matmul(pu[:, c * 128:(c + 1) * 128], lhsT=kt[:, c * 128:(c + 1) * 128], rhs=vtok[:, ts_],
                                                        start=True, stop=True), reads=[(ktk, c), "vtok"], writes=[(puk, c)])
                fr[ti] = (po, pok, pu, puk)

            def chain(ti, s_cur, s_cur_k, sb_cur, sb_cur_k):
                po, pok, pu, puk = fr.pop(ti)
                corder = range(4) if d_ == 0 else range(3, -1, -1)
                for ci, c in enumerate(corder):
                    ch = ti * 4 + c
                    cs_ = slice(ti * 128 + c * 32, ti * 128 + c * 32 + 32)
                    last = (ci == 3)
                    if d_ == 0:
                        P.add("pe", lambda e, c=c, sb_cur=sb_cur, cs_=cs_, last=last: e.matmul(
                            po[:, c * 32:(c + 1) * 32], lhsT=sb_cur[:, :], rhs=qd[:, cs_], start=False, stop=last),
                            reads=[sb_cur_k, "qd"], writes=[pok])
                        tt, ttk = t32r.next()
                        P.add("dve", lambda e, tt=tt, s_cur=s_cur, c=c: e.tensor_tensor(tt[:, :], pu[:, c * 128:(c + 1) * 128], s_cur[:, :], ALU.add),
                              reads=[(puk, c), s_cur_k], writes=[ttk])
                        s_new, s_new_k = s32r.next()
                        P.add("dve", lambda e, tt=tt, s_new=s_new, ch=ch: e.tensor_scalar_mul(s_new[:, :], tt[:, :], dec[:, ch:ch + 1]),
                              reads=[ttk, "dec"], writes=[s_new_k])
                        sb_new, sb_new_k = sbfr.next()
                        P.add("act", lambda e, sb_new=sb_new, s_new=s_new: e.copy(sb_new[:, :], s_new[:, :]), reads=[s_new_k], writes=[sb_new_k])
                    else:
                        tt, ttk = t32r.next()
                        P.add("dve", lambda e, tt=tt, s_cur=s_cur, ch=ch: e.tensor_scalar_mul(tt[:, :], s_cur[:, :], dec[:, ch:ch + 1]),
                              reads=[s_cur_k, "dec"], writes=[ttk])
                        sb_new, sb_new_k = sbfr.next()
                        P.add("act", lambda e, sb_new=sb_new, tt=tt: e.copy(sb_new[:, :], tt[:, :]), reads=[ttk], writes=[sb_new_k])
                        P.add("pe", lambda e, c=c, sb_new=sb_new, cs_=cs_, last=last: e.matmul(
                            po[:, c * 32:(c + 1) * 32], lhsT=sb_new[:, :], rhs=qd[:, cs_], start=False, stop=last),
                            reads=[sb_new_k, "qd"], writes=[pok])
                        s_new, s_new_k = s32r.next()
                        P.add("dve", lambda e, tt=tt, s_new=s_new, c=c: e.tensor_tensor(s_new[:, :], pu[:, c * 128:(c + 1) * 128], tt[:, :], ALU.add),
                              reads=[(puk, c), ttk], writes=[s_new_k])
                    s_cur, s_cur_k, sb_cur, sb_cur_k = s_new, s_new_k, sb_new, sb_new_k
                os_, osk = osr.next()
                P.add("act", lambda e: e.copy(os_[:, :], po[:, 0:128]), reads=[pok], writes=[osk])
                P.add("sp", lambda e: e.dma_start(out=dst[h * 128:(h + 1) * 128, ti * 128:(ti + 1) * 128], in_=os_[:, :]),
                      reads=[osk], writes=[("o", d_, h, ti)], dma=True)
                return s_cur, s_cur_k, sb_cur, sb_cur_k

            front(order[0])
            for i_, ti in enumerate(order):
                if i_ + 1 < len(order):
                    front(order[i_ + 1])
                s_cur, s_cur_k, sb_cur, sb_cur_k = chain(ti, s_cur, s_cur_k, sb_cur, sb_cur_k)
    P.barrier()


def stage_hgrn_readout(P, g, l, j, skip_ctx):
    lin_views(g)
    gtcol = None
    for (c0, nb, kind) in BLOCKS:
        if kind == 1 and skip_ctx:
            continue
        hgrn_readout_block(P, g, l, j, c0, nb, kind)
    P.barrier()


def hgrn_readout_block(P, g, l, j, c0, nb, kind):
    gtcol = lambda oc: modcol(g, l, kind, 1, 2, oc)
    ofv = g.d_of.rearrange("(c p) t -> p c t", p=128)
    obv = g.d_ob.rearrange("(c p) t -> p c t", p=128)
    gtv = g.d_pj[8192:10240, :].rearrange("(c p) t -> p c t", p=128)
    for (t0, n) in tgroups(nb, 256):
        s1, k1 = g.stg.next()
        s2, k2 = g.stg.next()
        s3, k3 = g.stg.next()
        for (sx, kx, src) in ((s1, k1, ofv), (s2, k2, obv), (s3, k3, gtv)):
            P.add("sp", lambda e, sx=sx, src=src: e.dma_start(
                out=sx[:, 0:DC * n].rearrange("p (c t) -> p c t", t=n), in_=src[:, :, c0 + t0:c0 + t0 + n]), writes=[kx], dma=True)
        P.add("pool", lambda e: e.tensor_tensor(s1[:, 0:DC * n], s1[:, 0:DC * n], s2[:, 0:DC * n], ALU.add), reads=[k1, k2], writes=[k1])
        P.add("act", lambda e: e.activation(out=s3[:, 0:DC * n], in_=s3[:, 0:DC * n], func=AF.Silu), reads=[k3], writes=[k3])
        for dc in range(DC):
            sq_, sqk = g.sq.next()
            ssp, ssk = g.psr.next()
            P.add("act", lambda e, sq_=sq_, dc=dc: e.activation(out=sq_[:, 0:n], in_=s1[:, dc * n:(dc + 1) * n], func=AF.Square),
                  reads=[k1], writes=[sqk])
            P.add("pe", lambda e, sq_=sq_, ssp=ssp: e.matmul(ssp[:, 0:n], lhsT=g.ones[:, :], rhs=sq_[:, 0:n], start=True, stop=True),
                  reads=[sqk, "ones"], writes=[ssk])
            rs, rsk = emit_rstd(P, g, ssp, ssk, n, 128)
            tm, tmk = g.tmp.next()
            P.add("dve", lambda e, tm=tm, rs=rs, dc=dc: e.scalar_tensor_tensor(
                out=tm[:, 0:n], in0=s1[:, dc * n:(dc + 1) * n], scalar=g.hgn[:, j * 16 + dc: j * 16 + dc + 1], in1=rs[:, 0:n],
                op0=ALU.mult, op1=ALU.mult), reads=[k1, rsk, "hgn"], writes=[tmk])
            P.add("pool", lambda e, tm=tm, dc=dc: e.tensor_tensor(
                g.xn[:, dc * nb + t0: dc * nb + t0 + n], tm[:, 0:n], s3[:, dc * n:(dc + 1) * n], ALU.mult),
                reads=[tmk, k3], writes=[("xn", dc, t0)])
    emit_linear(P, g, g.xn, DC, nb, g.d_hgw_out[j], DC, epi_residual(P, g, g.d_h, c0, gtcol, ["modc"]))


MLA_SCALE = (128 + 64) ** -0.5


def stage_mla_proj(P, g, l):
    gsrc = g.gcols[:, (l * 3 + 1) * 16:(l * 3 + 2) * 16]
    for (c0, nb, kind) in BLOCKS:
        mla_proj_block(P, g, l, gsrc, c0, nb, kind)
    P.barrier()


def mla_proj_block(P, g, l, gsrc, c0, nb, kind):
    g.xn = g.arena[:, 0:DC * TB]
    cbuf = carve(g, 32768, 8 * TB, F32)
    cn = carve(g, 65536, 8 * TB, BF16)
    rope = carve(g, 81920, 2 * TB, F32)
    vbf = [carve(g, 90112 + i * 2048, TB, BF16) for i in range(2)]
    obf = [carve(g, 94208 + i * 1024, 512, BF16) for i in range(4)]
    stage = [carve(g, 98304 + i * 16384, 4096, F32) for i in range(3)]
    wbf = [carve(g, 147456 + i * 8192, 4096, BF16) for i in range(3)]
    rt = [carve(g, 172032 + i * 2048, 512, F32) for i in range(4)]
    g.stg, g.wbr = Ring("stage", stage), Ring("wbf", wbf)
    g.psr = Ring("ps", g.ps[0:7])
    vbr, obr, rtr = Ring("vbf", vbf), Ring("obf", obf), Ring("rt", rt)
    emit_cols(P, g, l, 1, kind, gsrc, False)
    gcol = lambda dc: g.dcol[:, dc:dc + 1]
    shcol = lambda dc: modcol(g, l, kind, 1, 0, dc)
    emit_adanorm(P, g, g.d_h, c0, nb, gcol, shcol, ["dcol", "modc"])
    P.add("sp", lambda e: e.dma_start(out=rope[:, 0:2 * nb].rearrange("p (a t) -> p a t", a=2),
                                      in_=g.d_rope.rearrange("p (a t) -> p a t", a=2)[:, :, c0:c0 + nb]), writes=["rope"], dma=True)
    pvb = g.ps[7][:, :].bitcast(BF16)

    def rope_epi(dst, first):
        st_ = {}

        def epi(oc, t0, n, pp, ppk):
            if first(oc):
                r1, r1k = rtr.next()
                P.add("dve", lambda e: e.tensor_tensor(r1[:, 0:n], pp[:, 0:n], rope[:, t0:t0 + n], ALU.mult), reads=[ppk, "rope"], writes=[r1k])
                st_[t0] = (r1, r1k)
            else:
                r1, r1k = st_.pop(t0)
                r2, r2k = rtr.next()
                P.add("dve", lambda e: e.tensor_tensor(r2[:, 0:n], pp[:, 0:n], rope[:, nb + t0:nb + t0 + n], ALU.mult), reads=[ppk, "rope"], writes=[r2k])
                o, ok = obr.next()
                P.add("pool", lambda e: e.tensor_tensor(o[:, 0:n], r1[:, 0:n], r2[:, 0:n], ALU.add), reads=[r1k, r2k], writes=[ok])
                P.add("sp", lambda e: e.dma_start(out=dst(oc)[:, c0 + t0:c0 + t0 + n], in_=o[:, 0:n]), reads=[ok], writes=[("mla_o", oc, c0 + t0)], dma=True)
        return epi

    def store_bf(dst):
        def epi(oc, t0, n, pp, ppk):
            o, ok = obr.next()
            if (oc + t0 // 512) % 2 == 0:
                P.add("act", lambda e: e.copy(o[:, 0:n], pp[:, 0:n]), reads=[ppk], writes=[ok])
            else:
                P.add("dve", lambda e: e.tensor_copy(o[:, 0:n], pp[:, 0:n]), reads=[ppk], writes=[ok])
            P.add("sp", lambda e: e.dma_start(out=dst(oc)[:, c0 + t0:c0 + t0 + n], in_=o[:, 0:n]), reads=[ok], writes=[("mla_o2", oc, c0 + t0)], dma=True)
        return epi

    krope = rope_epi(lambda oc: g.d_kr, lambda oc: oc == 8)

    def epi1(oc, t0, n, pp, ppk):
        if oc < 8:
            if oc % 2 == 0:
                P.add("act", lambda e: e.copy(cbuf[:, oc * nb + t0: oc * nb + t0 + n], pp[:, 0:n]), reads=[ppk], writes=[("cb", oc, t0)])
            else:
                P.add("dve", lambda e: e.tensor_copy(cbuf[:, oc * nb + t0: oc * nb + t0 + n], pp[:, 0:n]), reads=[ppk], writes=[("cb", oc, t0)])
        else:
            krope(oc, t0, n, pp, ppk)
    emit_linear(P, g, g.xn, DC, nb, g.d_wdqkv, 10, epi1)
    for grp in range(2):
        for (t0, n) in tgroups(nb, 256):
            ssp, ssk = g.psr.next()
            for q in range(4):
                oc = grp * 4 + q
                sq, sqk = g.sq.next()
                P.add("act", lambda e, sq=sq, oc=oc: e.activation(out=sq[:, 0:n], in_=cbuf[:, oc * nb + t0: oc * nb + t0 + n], func=AF.Square),
                      reads=[("cb", oc, (t0 // 512) * 512)], writes=[sqk])
                P.add("pe", lambda e, sq=sq, q=q: e.matmul(ssp[:, 0:n], lhsT=g.ones[:, :], rhs=sq[:, 0:n], start=(q == 0), stop=(q == 3)),
                      reads=[sqk, "ones"], writes=[ssk])
            rs, rsk = emit_rstd(P, g, ssp, ssk, n, 512)
            for q in range(4):
                oc = grp * 4 + q
                P.add("dve", lambda e, rs=rs, oc=oc: e.scalar_tensor_tensor(
                    out=cn[:, oc * nb + t0: oc * nb + t0 + n], in0=cbuf[:, oc * nb + t0: oc * nb + t0 + n],
                    scalar=g.mlan[:, oc:oc + 1], in1=rs[:, 0:n], op0=ALU.mult, op1=ALU.mult),
                    reads=[("cb", oc, (t0 // 512) * 512), rsk, "mlan"], writes=[("cn" if oc < 4 else "cn4", oc % 4, t0)])
    qrope = rope_epi(lambda oc: g.d_qr[oc // 3], lambda oc: oc % 3 == 1)
    qn_store = store_bf(lambda oc: g.d_qn[oc // 3])

    def epi3(oc, t0, n, pp, ppk):
        if oc % 3 == 0:
            qn_store(oc, t0, n, pp, ppk)
        else:
            qrope(oc, t0, n, pp, ppk)
    emit_linear(P, g, cn, 4, nb, g.d_wuq, 48, epi3, xpref="cn")
    kn_store = store_bf(lambda oc: g.d_kn[oc // 2])

    def epi4(oc, t0, n, pp, ppk):
        if oc % 2 == 0:
            kn_store(oc, t0, n, pp, ppk)
        else:
            hh = oc // 2
            v, vk = vbr.next()
            P.add("act", lambda e: e.copy(v[:, 0:n], pp[:, 0:n]), reads=[ppk], writes=[vk])
            for q in range(n // 128):
                P.add("pe", lambda e, q=q: e.transpose(pvb[:, q * 128:(q + 1) * 128], v[:, q * 128:(q + 1) * 128], g.identb[:, :]),
                      reads=[vk, "identb"], writes=["pv"])
            o, ok = obr.next()
            P.add("dve", lambda e: e.tensor_copy(o[:, 0:n], pvb[:, 0:n]), reads=["pv"], writes=[ok])
            tb0 = (c0 + t0) // 128
            P.add("sp", lambda e: e.dma_start(
                out=g.d_vt[hh].rearrange("(n p) v -> p n v", p=128)[:, tb0:tb0 + n // 128, :],
                in_=o[:, 0:n].rearrange("p (n v) -> p n v", v=128)), reads=[ok], writes=[("mla_v", oc, c0 + t0)], dma=True)
    emit_linear(P, g, cn[:, 4 * nb:8 * nb], 4, nb, g.d_wukv, 32, epi4, xpref="cn4")


def stage_mla_attn(P, g, last):
    NTI = L // 128
    Kr = carve(g, 0, L, BF16)
    o_ = L * 2
    Kn = [carve(g, o_ + i * L * 2, L, BF16) for i in range(2)]; o_ += 2 * L * 2
    Vt = [carve(g, o_ + i * L * 2, L, BF16) for i in range(2)]; o_ += 2 * L * 2
    Qn = [carve(g, o_ + i * L * 2, L, BF16) for i in range(2)]; o_ += 2 * L * 2
    Qr = [carve(g, o_ + i * L * 2, L, BF16) for i in range(2)]; o_ += 2 * L * 2
    pT = [carve(g, o_ + i * 1024, 512, BF16) for i in range(4)]; o_ += 4 * 1024
    rl = [carve(g, o_ + i * 2048, 512, F32) for i in range(2)]; o_ += 2 * 2048
    oo = [carve(g, o_ + i * 2048, 512, F32) for i in range(3)]; o_ += 3 * 2048
    onesb = carve(g, o_, 128, BF16); o_ += 256
    assert o_ <= ARENA * 2
    knr, vtr, qnr, qrr, ptr_, rlr, oor = Ring("Kn", Kn), Ring("Vt", Vt), Ring("Qn", Qn), Ring("Qr", Qr), Ring("pT", pT), Ring("rl", rl), Ring("oo", oo)
    psr_s, psr_o, psr_l = Ring("ps", g.ps[0:3]), Ring("ps3", g.ps[3:5]), Ring("ps5", g.ps[5:7])
    P.add("pool", lambda e: e.memset(onesb[:, :], 1.0), writes=["onesb"])
    P.add("sp", lambda e: e.dma_start(out=Kr[:, :], in_=g.d_kr), writes=["Kr"], dma=True)
    qblocks = [(NCTX + 512 * i, 512, 0, NTI) for i in range(NLAT // 512)]
    if not last:
        qblocks.append((0, NCTX, 0, NCTX // 128))
    for h in range(16):
        kn, knk = knr.next()
        vt, vtk = vtr.next()
        qn, qnk = qnr.next()
        qr, qrk = qrr.next()
        P.add("sp", lambda e, kn=kn, h=h: e.dma_start(out=kn[:, :], in_=g.d_kn[h]), writes=[knk], dma=True)
        P.add("sp", lambda e, vt=vt, h=h: e.dma_start(out=vt[:, :].rearrange("p (n v) -> p n v", v=128),
                                                     in_=g.d_vt[h].rearrange("(n p) v -> p n v", p=128)), writes=[vtk], dma=True)
        P.add("sp", lambda e, qn=qn, h=h: e.dma_start(out=qn[:, :], in_=g.d_qn[h]), writes=[qnk], dma=True)
        P.add("sp", lambda e, qr=qr, h=h: e.dma_start(out=qr[:, :], in_=g.d_qr[h]), writes=[qrk], dma=True)
        for (q0, nq, k0, k1) in qblocks:
            po, pok = psr_o.next()
            pl, plk = psr_l.next()
            for kt in range(k0, k1):
                ks = slice(kt * 128, (kt + 1) * 128)
                ps_, psk = psr_s.next()
                P.add("pe", lambda e, ps_=ps_, kn=kn, qn=qn, ks=ks, q0=q0, nq=nq: e.matmul(
                    ps_[:, 0:nq], lhsT=kn[:, ks], rhs=qn[:, q0:q0 + nq], start=True, stop=False), reads=[knk, qnk], writes=[psk])
                P.add("pe", lambda e, ps_=ps_, qr=qr, ks=ks, q0=q0, nq=nq: e.matmul(
                    ps_[:, 0:nq], lhsT=Kr[:, ks], rhs=qr[:, q0:q0 + nq], start=False, stop=True), reads=["Kr", qrk], writes=[psk])
                p_, pk = ptr_.next()
                P.add("act", lambda e, p_=p_, ps_=ps_, nq=nq: e.activation(out=p_[:, 0:nq], in_=ps_[:, 0:nq], func=AF.Exp, scale=MLA_SCALE),
                      reads=[psk], writes=[pk])
                P.add("pe", lambda e, p_=p_, po=po, vt=vt, ks=ks, nq=nq, kt=kt, k0=k0, k1=k1: e.matmul(
                    po[:, 0:nq], lhsT=vt[:, ks], rhs=p_[:, 0:nq], start=(kt == k0), stop=(kt == k1 - 1)), reads=[pk, vtk], writes=[pok])
                P.add("pe", lambda e, p_=p_, pl=pl, nq=nq, kt=kt, k0=k0, k1=k1: e.matmul(
                    pl[:, 0:nq], lhsT=onesb[:, :], rhs=p_[:, 0:nq], start=(kt == k0), stop=(kt == k1 - 1)), reads=[pk, "onesb"], writes=[plk])
            r_, rk = rlr.next()
            P.add("dve", lambda e, r_=r_, pl=pl, nq=nq: e.reciprocal(r_[:, 0:nq], pl[:, 0:nq]), reads=[plk], writes=[rk])
            o, ok = oor.next()
            P.add("dve", lambda e, o=o, po=po, r_=r_, nq=nq: e.tensor_tensor(o[:, 0:nq], po[:, 0:nq], r_[:, 0:nq], ALU.mult),
                  reads=[pok, rk], writes=[ok])
            P.add("sp", lambda e, o=o, h=h, q0=q0, nq=nq: e.dma_start(out=g.d_of[h * 128:(h + 1) * 128, q0:q0 + nq], in_=o[:, 0:nq]),
                  reads=[ok], writes=[("ao", h, q0)], dma=True)
    P.barrier()


def stage_outproj(P, g, l, src, wsrc, skip_ctx):
    lin_views(g)
    for (c0, nb, kind) in BLOCKS:
        if kind == 1 and skip_ctx:
            continue
        outproj_block(P, g, l, src, wsrc, c0, nb, kind)
    P.barrier()


def outproj_block(P, g, l, src, wsrc, c0, nb, kind):
    gtcol = lambda oc: modcol(g, l, kind, 1, 2, oc)
    sv = src.rearrange("(c p) t -> p c t", p=128)
    for (t0, n) in tgroups(nb, 256):
        s_, k_ = g.stg.next()
        P.add("sp", lambda e, s_=s_, t0=t0, n=n: e.dma_start(
            out=s_[:, 0:DC * n].rearrange("p (c t) -> p c t", t=n), in_=sv[:, :, c0 + t0:c0 + t0 + n]), writes=[k_], dma=True)
        for dc in range(DC):
            cast_op(P, g, dc, g.xn[:, dc * nb + t0: dc * nb + t0 + n], s_[:, dc * n:(dc + 1) * n], [k_], [("xn", dc, t0)])
    emit_linear(P, g, g.xn, DC, nb, wsrc, DC, epi_residual(P, g, g.d_h, c0, gtcol, ["modc"]))


def stage_fnet_a(P, g, l):
    gsrc = g.gcols[:, (l * 3 + 1) * 16:(l * 3 + 2) * 16]
    dst32 = carve(g, 32768, 1024, F32)
    dftc = carve(g, 32768 + 4096, 1024, BF16)
    P.add("sp", lambda e: e.dma_start(out=dst32[:, :], in_=g.d_dft256), writes=["dft32"], dma=True)
    P.add("act", lambda e: e.copy(dftc[:, :], dst32[:, :]), reads=["dft32"], writes=["dftc"])
    for (c0, nb, kind) in BLOCKS:
        fnet_a_block(P, g, l, gsrc, dftc, c0, nb, kind)
    P.barrier()


def fnet_a_block(P, g, l, gsrc, dftc, c0, nb, kind):
    g.xn = g.arena[:, 0:DC * TB]
    stage = [carve(g, 40960 + i * 16384, 4096, F32) for i in range(3)]
    xo = [carve(g, 90112 + i * 1024, 512, BF16) for i in range(4)]
    g.stg = Ring("stage", stage)
    g.psr = Ring("ps", g.ps)
    xor_ = Ring("xo", xo)
    emit_cols(P, g, l, 1, kind, gsrc, False)
    gcol = lambda dc: g.dcol[:, dc:dc + 1]
    shcol = lambda dc: modcol(g, l, kind, 1, 0, dc)
    emit_adanorm(P, g, g.d_h, c0, nb, gcol, shcol, ["dcol", "modc"])
    for tt in range(nb // 128):
        for gq in range(8):
            pp, ppk = g.psr.next()
            for cs_ in range(2):
                for kk in range(2):
                    dc = gq * 2 + kk
                    P.add("pe", lambda e: e.matmul(
                        pp[:, cs_ * 256:(cs_ + 1) * 256], lhsT=g.xn[:, dc * nb + tt * 128: dc * nb + (tt + 1) * 128],
                        rhs=dftc[:, kk * 512 + cs_ * 256: kk * 512 + (cs_ + 1) * 256], start=(kk == 0), stop=(kk == 1)),
                        reads=xkeys("xn", dc, tt * 128, 128) + ["dftc"], writes=[ppk])
            o, ok = xor_.next()
            if gq % 2 == 0:
                P.add("act", lambda e: e.copy(o[:, :], pp[:, :]), reads=[ppk], writes=[ok])
            else:
                P.add("dve", lambda e: e.tensor_copy(o[:, :], pp[:, :]), reads=[ppk], writes=[ok])
            r0 = c0 + tt * 128
            P.add("sp", lambda e: e.dma_start(out=g.d_xc[r0:r0 + 128, gq * 512:(gq + 1) * 512], in_=o[:, :]),
                  reads=[ok], writes=[("xc", r0, gq)], dma=True)


def stage_fnet_b(P, g):
    TC = NLAT // 128
    tabs = [carve(g, i * 32768, TC * 512, BF16) for i in range(2)]
    xt = [[carve(g, 65536 + (i * 2 + a) * 8192, TC * 128, BF16) for a in range(2)] for i in range(2)]
    oo = [carve(g, 98304 + i * 2048, 512, F32) for i in range(3)]
    ctab = carve(g, 104448, 2 * 2 * 256, BF16)
    xtr, oor = Ring("xt", xt), Ring("oo", oo)
    g.psr = Ring("ps", g.ps)
    xcl = g.d_xc[NCTX:L, :].rearrange("(n p) c -> p n c", p=128)
    xcc = g.d_xc[0:NCTX, :].rearrange("(n p) c -> p n c", p=128)
    for tb in range(NLAT // 512):
        for a in range(2):
            P.add("sp", lambda e: e.dma_start(
                out=tabs[a][:, :].rearrange("p (n t) -> p n t", t=512),
                in_=g.d_dftT[a].rearrange("(n p) t -> p n t", p=128)[:, :, tb * 512:(tb + 1) * 512]), writes=[("tab", a)], dma=True)
        for cc in range(16):
            x2, xk = xtr.next()
            for a in range(2):
                col = (cc // 2) * 512 + a * 256 + (cc % 2) * 128
                P.add("sp", lambda e: e.dma_start(out=x2[a][:, :].rearrange("p (n c) -> p n c", c=128), in_=xcl[:, :, col:col + 128]),
                      writes=[(xk, a)], dma=True)
            pp, ppk = g.psr.next()
            for a in range(2):
                for tc in range(TC):
                    P.add("pe", lambda e: e.matmul(pp[:, :], lhsT=x2[a][:, tc * 128:(tc + 1) * 128], rhs=tabs[a][:, tc * 512:(tc + 1) * 512],
                                                   start=(a == 0 and tc == 0), stop=(a == 1 and tc == TC - 1)),
                          reads=[(xk, a), ("tab", a)], writes=[ppk])
            o, ok = oor.next()
            if cc % 2 == 0:
                P.add("act", lambda e: e.copy(o[:, :], pp[:, :]), reads=[ppk], writes=[ok])
            else:
                P.add("dve", lambda e: e.tensor_copy(o[:, :], pp[:, :]), reads=[ppk], writes=[ok])
            P.add("sp", lambda e: e.dma_start(out=g.d_of[cc * 128:(cc + 1) * 128, NCTX + tb * 512: NCTX + (tb + 1) * 512], in_=o[:, :]),
                  reads=[ok], writes=[("fo", cc, tb)], dma=True)
    P.add("sp", lambda e: e.dma_start(out=ctab[:, :].rearrange("p (a n t) -> p a n t", a=2, t=256),
                                      in_=g.d_dftC.rearrange("a (n p) t -> p a n t", p=128)), writes=["ctab"], dma=True)
    for cc in range(16):
        x2, xk = xtr.next()
        for a in range(2):
            col = (cc // 2) * 512 + a * 256 + (cc % 2) * 128
            P.add("sp", lambda e: e.dma_start(out=x2[a][:, 0:256].rearrange("p (n c) -> p n c", c=128), in_=xcc[:, :, col:col + 128]),
                  writes=[(xk, a)], dma=True)
        pp, ppk = g.psr.next()
        for a in range(2):
            for tc in range(2):
                P.add("pe", lambda e: e.matmul(pp[:, 0:256], lhsT=x2[a][:, tc * 128:(tc + 1) * 128],
                                               rhs=ctab[:, (a * 2 + tc) * 256:(a * 2 + tc + 1) * 256],
                                               start=(a == 0 and tc == 0), stop=(a == 1 and tc == 1)),
                      reads=[(xk, a), "ctab"], writes=[ppk])
        o, ok = oor.next()
        P.add("act", lambda e: e.copy(o[:, 0:256], pp[:, 0:256]), reads=[ppk], writes=[ok])
        P.add("sp", lambda e: e.dma_start(out=g.d_of[cc * 128:(cc + 1) * 128, 0:NCTX], in_=o[:, 0:256]),
              reads=[ok], writes=[("fo", cc, -1)], dma=True)
    P.barrier()


def stage_final(P, g):
    for (c0, nb, kind) in BLOCKS:
        if kind == 1:
            continue
        final_block(P, g, c0, nb)


def final_block(P, g, c0, nb):
    if True:
        xTv = g.d_h.rearrange("(c p) t -> p c t", p=128)
        for (t0, n) in tgroups(nb, 256):
            xs, xk = g.stg.next()
            P.add("sp", lambda e, xs=xs, t0=t0, n=n: e.dma_start(
                out=xs[:, 0:DC * n].rearrange("p (c t) -> p c t", t=n), in_=xTv[:, :, c0 + t0:c0 + t0 + n]),
                reads=[("h", c0 + t0)], writes=[xk], dma=True)
            ssp, ssk = g.psr.next()
            for dc in range(DC):
                sq, sqk = g.sq.next()
                P.add("act", lambda e, sq=sq, xs=xs, dc=dc, n=n: e.activation(
                    out=sq[:, 0:n], in_=xs[:, dc * n:(dc + 1) * n], func=AF.Square), reads=[xk], writes=[sqk])
                P.add("pe", lambda e, sq=sq, ssp=ssp, dc=dc, n=n: e.matmul(
                    ssp[:, 0:n], lhsT=g.ones[:, :], rhs=sq[:, 0:n], start=(dc == 0), stop=(dc == DC - 1)),
                    reads=[sqk, "ones"], writes=[ssk])
            rs, rsk = emit_rstd(P, g, ssp, ssk, n, D)
            for dc in range(DC):
                P.add("dve", lambda e, xs=xs, rs=rs, dc=dc, n=n: e.scalar_tensor_tensor(
                    out=xs[:, dc * n:(dc + 1) * n], in0=xs[:, dc * n:(dc + 1) * n], scalar=g.fgcol[:, dc:dc + 1],
                    in1=rs[:, 0:n], op0=ALU.mult, op1=ALU.mult), reads=[xk, rsk, "gcols"], writes=[xk])
            ov = g.d_out.rearrange("(c p) t -> p c t", p=128)
            P.add("sp", lambda e, xs=xs, t0=t0, n=n, ov=ov: e.dma_start(
                out=ov[:, :, c0 - NCTX + t0: c0 - NCTX + t0 + n], in_=xs[:, 0:DC * n].rearrange("p (c t) -> p c t", t=n)),
                reads=[xk], dma=True)


def build(plan):
    nc = bass.Bass("TRN2", target_bir_lowering=False)
    g = G()
    kinds = set(p if isinstance(p, str) else p[0] for p in plan)
    fam = {"wgu1": "ffn", "wdn1": "ffn", "wgu2": "ffn", "wdn2": "ffn", "hgw_in": "hgrn", "hgw_out": "hgrn",
           "wdqkv": "mla", "wuq": "mla", "wukv": "mla", "wo": "mla", "rope": "mla", "fw": "fnet", "dft256": "fnet",
           "dftT": "fnet", "dftC": "fnet", "mod_w": "mod"}

    def di(name, shape, dt=F32):
        f = fam.get(name)
        if f is not None and not any(k.startswith(f) for k in kinds):
            return None
        return nc.dram_tensor(name, shape, dt, kind="ExternalInput").ap()
    g.d_x = di("x", [D, L])
    g.d_cT = di("cT", [128, 32])
    g.d_modw = di("mod_w", [4, D, 18432])
    g.d_modb = di("mod_b", [128, 4 * NMODC])
    g.d_gcols = di("gcols", [128, 13 * 16])
    g.d_wgu1 = di("wgu1", [4, FC, 128, 4096])
    g.d_wdn1 = di("wdn1", [4, DC, 128, DFF])
    g.d_wgu2 = di("wgu2", [4, FC, 128, 4096])
    g.d_wdn2 = di("wdn2", [4, DC, 128, DFF])
    g.d_hgw_in = di("hgw_in", [2, 80, 128, D])
    g.d_hgw_out = di("hgw_out", [2, DC, 128, D])
    g.d_hgn = di("hgn", [128, 32])
    g.d_lbl = di("lbl", [128, 64])
    g.d_cst = di("cst", [128, 516])
    g.d_wdqkv = di("wdqkv", [10, 128, D])
    g.d_wuq = di("wuq", [48, 128, 512])
    g.d_wukv = di("wukv", [32, 128, 512])
    g.d_wo = di("wo", [DC, 128, D])
    g.d_mlan = di("mlan", [128, 8])
    g.d_rope = di("rope", [128, 2 * L])
    g.d_fw = di("fw", [DC, 128, D])
    g.d_dft256 = di("dft256", [128, 1024])
    g.d_dftT = di("dftT", [2, NLAT, NLAT], BF16)
    g.d_dftC = di("dftC", [2, NCTX, NCTX], BF16)
    g.d_xc = nc.dram_tensor("xc_scr", [L, 4096], BF16).ap()
    g.d_kr = nc.dram_tensor("kr_scr", [128, L], BF16).ap()
    g.d_qn = nc.dram_tensor("qn_scr", [16, 128, L], BF16).ap()
    g.d_qr = nc.dram_tensor("qr_scr", [16, 128, L], BF16).ap()
    g.d_kn = nc.dram_tensor("kn_scr", [16, 128, L], BF16).ap()
    g.d_vt = nc.dram_tensor("vt_scr", [16, L, 128], BF16).ap()
    g.d_out = nc.dram_tensor("out", [D, NLAT], F32, kind="ExternalOutput").ap()
    g.d_h = nc.dram_tensor("h_scr", [D, L], F32).ap()
    g.d_pj = nc.dram_tensor("pj_scr", [10240, L], F32).ap()
    g.d_of = nc.dram_tensor("of_scr", [D, L], F32).ap()
    g.d_ob = nc.dram_tensor("ob_scr", [D, L], F32).ap()
    with contextlib.ExitStack() as st:
        sb = lambda name, shape, dt: st.enter_context(nc.sbuf_tensor(name, shape, dt))
        g.arena = sb("arena", [128, ARENA], BF16)
        g.xn = g.arena[:, 0:DC * TB]
        g.act = g.arena[:, DC * TB:(DC + FC) * TB]
        o_ = (DC + FC) * TB
        stage = [g.arena[:, o_ + i * 8192: o_ + (i + 1) * 8192].bitcast(F32) for i in range(3)]
        o_ += 3 * 8192
        wbf = [g.arena[:, o_ + i * 4096: o_ + (i + 1) * 4096] for i in range(2)]
        g.modc = sb("modc", [128, 4 * NMODC * 2], F32)
        g.gcols = sb("gcols_sb", [128, 13 * 16], F32)
        g.fgcol = g.gcols[:, 12 * 16:13 * 16]
        g.dcol = sb("dcol", [128, 32], F32)
        g.hgn = sb("hgn_sb", [128, 32], F32)
        g.mlan = sb("mlan_sb", [128, 8], F32)
        g.lbl = sb("lbl_sb", [128, 64], F32)
        g.lbc = sb("lbc", [128, 192], F32)
        g.cst = sb("cst_sb", [128, 516], F32)
        g.identb = sb("identb", [128, 128], BF16)
        g.cs = sb("cs", [128, 32], F32)
        g.ones = sb("ones", [128, 128], F32)
        sq = [sb("sq%d" % i, [128, 256], F32) for i in range(2)]
        rstd = [sb("rstd%d" % i, [128, 256], F32) for i in range(2)]
        tmp = [sb("tmp%d" % i, [128, 256], F32) for i in range(2)]
        sg = [sb("sg%d" % i, [128, 512], F32) for i in range(2)]
        ost = [sb("ost%d" % i, [128, 512], F32) for i in range(2)]
        g.ps = [st.enter_context(nc.psum_tensor("ps%d" % i, [128, 512], F32)) for i in range(8)]
        g.psr = Ring("ps", g.ps)
        g.stg, g.wbr = Ring("stage", stage), Ring("wbf", wbf)
        g.sq, g.rstd, g.tmp, g.sg, g.ost = Ring("sq", sq), Ring("rstd", rstd), Ring("tmp", tmp), Ring("sg", sg), Ring("ost", ost)
        P = Prog(nc)
        P.add("pool", lambda e: e.memset(g.ones[:, :], 1.0), writes=["ones"])
        P.add("sp", lambda e: e.dma_start(out=g.gcols[:, :], in_=g.d_gcols), writes=["gcols"], dma=True)
        P.add("sp", lambda e: e.dma_start(out=g.hgn[:, :], in_=g.d_hgn), writes=["hgn"], dma=True)
        P.add("sp", lambda e: e.dma_start(out=g.mlan[:, :], in_=g.d_mlan), writes=["mlan"], dma=True)
        stage_lbcols(P, g)
        for i in range(DC):
            P.add("sp", lambda e, i=i: e.dma_start(out=g.d_h[i * 128:(i + 1) * 128, :], in_=g.d_x[i * 128:(i + 1) * 128, :]),
                  writes=[("hinit", i)], dma=True)
        P.barrier()
        for stg_ in plan:
            if stg_ == "mod":
                stage_mod(P, g)
                P.barrier()
            elif stg_[0] == "ffn":
                ffn_views(g)
                stage_ffn(P, g, stg_[1], stg_[2], skip_ctx=(len(stg_) > 3 and stg_[3]))
            elif stg_[0] == "fnet":
                l_ = stg_[1]
                stage_fnet_a(P, g, l_)
                stage_fnet_b(P, g)
                stage_outproj(P, g, l_, g.d_of, g.d_fw, False)
            elif stg_[0] == "mla":
                l_, last_ = stg_[1], stg_[2]
                stage_mla_proj(P, g, l_)
                stage_mla_attn(P, g, last_)
                stage_outproj(P, g, l_, g.d_of, g.d_wo, last_)
            elif stg_[0] == "hgrn_in":
                stage_hgrn_inproj(P, g, stg_[1], stg_[2])
            elif stg_[0] == "hgrn_scan":
                stage_hgrn_scan(P, g, stg_[1], stg_[2])
            elif stg_[0] == "hgrn_out":
                stage_hgrn_readout(P, g, stg_[1], stg_[2], stg_[3])
            elif stg_[0] == "dumpcols":
                P.barrier()
                for i in range(DC):
                    P.add("sp", lambda e, i=i, a=stg_[1], n=stg_[2], o=stg_[3]: e.dma_start(
                        out=g.d_out[i * 128:(i + 1) * 128, o:o + n], in_=g.d_h[i * 128:(i + 1) * 128, a:a + n]), dma=True)
                P.barrier()
                g.dumped = True
            elif stg_[0] == "dump":
                src_ = getattr(g, stg_[1])
                P.barrier()
                for i in range(stg_[3] // 128):
                    P.add("sp", lambda e, i=i, src_=src_, r0=stg_[2], o0=stg_[4]: e.dma_start(
                        out=g.d_out[o0 + i * 128: o0 + (i + 1) * 128, :], in_=src_[r0 + i * 128: r0 + (i + 1) * 128, NCTX:L]), dma=True)
                g.dumped = True
            elif stg_[0] == "hgrn":
                l_, j_, last_ = stg_[1], stg_[2], stg_[3]
                stage_hgrn_inproj(P, g, l_, j_)
                stage_hgrn_scan(P, g, j_)
                stage_hgrn_readout(P, g, l_, j_, last_)
            elif stg_ == "final":
                stage_final(P, g)
            elif stg_ == "dbg_modc":
                P.barrier()
                P.add("sp", lambda e: e.dma_start(out=g.d_out[0:128, 0:4 * NMODC * 2], in_=g.modc[:, :]), dma=True)
                P.emit()
                g.stats = P.stats
                return nc, g
        if "final" not in plan and not getattr(g, "dumped", False):
            P.barrier()
            for i in range(DC):
                P.add("sp", lambda e, i=i: e.dma_start(out=g.d_out[i * 128:(i + 1) * 128, :], in_=g.d_h[i * 128:(i + 1) * 128, NCTX:L]),
                      dma=True)
        P.emit()
        g.stats = P.stats
    return nc, g


def tile_w(W):
    K, N = W.shape
    return np.ascontiguousarray(W.reshape(K // 128, 128, N // 128, 128).transpose(2, 1, 0, 3))


def col16(v):
    return np.ascontiguousarray(v.reshape(-1, 128).T)


def prep_ffn_w(w_gu, w_dn):
    tg = tile_w(w_gu)
    wgu = np.ascontiguousarray(np.stack([tg[:FC], tg[FC:]], axis=2).reshape(FC, 128, 4096))
    wdn = np.ascontiguousarray(tile_w(w_dn).reshape(DC, 128, DFF))
    return wgu, wdn


def prep_inputs(inp, nlayers=4):
    shared = {}
    shared["mod_w"] = np.ascontiguousarray(inp["mod_w"])
    shared["mod_b"] = np.ascontiguousarray(np.concatenate([col16(inp["mod_b"][l]) for l in range(4)], axis=1))
    shared["gcols"] = np.ascontiguousarray(np.concatenate(
        [col16(inp["norm_g"][l, s]) for l in range(4) for s in range(3)] + [col16(inp["final_g"])], axis=1))
    for nm, gu, dn in (("1", "ffn1_w_gu", "ffn1_w_down"), ("2", "ffn2_w_gu", "ffn2_w_down")):
        a, b = zip(*[prep_ffn_w(inp[gu][l], inp[dn][l]) for l in range(4)])
        shared["wgu" + nm] = np.stack(a)
        shared["wdn" + nm] = np.stack(b)
    shared["hgw_in"] = np.stack([tile_w(inp["hgrn_w_in"][j]).reshape(80, 128, D) for j in range(2)])
    shared["hgw_out"] = np.stack([tile_w(inp["hgrn_w_out"][j]).reshape(DC, 128, D) for j in range(2)])
    shared["hgn"] = np.ascontiguousarray(np.concatenate([col16(inp["hgrn_g_norm"][j]) for j in range(2)], axis=1))
    shared["lbl"] = np.ascontiguousarray(np.concatenate(
        [col16(inp["hgrn_lb_logits"][d_, j]) for d_ in range(2) for j in range(2)], axis=1))
    shared["cst"] = make_consts()
    pidx = np.array([a * 32 + (1 - hf) * 16 + f for a in range(2) for hf in range(2) for f in range(16)])
    zpad = lambda w: np.concatenate([w, np.zeros((w.shape[0], 64), np.float32)], axis=1)
    wd = inp["mla_w_dqkv"][0]
    wd_ext = np.concatenate([wd[:, :1024], zpad(wd[:, 1024:1088]), zpad(wd[:, 1024:1088][:, pidx])], axis=1)
    shared["wdqkv"] = tile_w(wd_ext).reshape(10, 128, D)
    wq = inp["mla_w_uq"][0].reshape(512, 16, 192)
    wq_ext = np.concatenate([np.concatenate([wq[:, hh, :128], zpad(wq[:, hh, 128:]), zpad(wq[:, hh, 128:][:, pidx])], axis=1)
                             for hh in range(16)], axis=1)
    shared["wuq"] = tile_w(wq_ext).reshape(48, 128, 512)
    shared["wukv"] = tile_w(inp["mla_w_ukv"][0]).reshape(32, 128, 512)
    shared["wo"] = tile_w(inp["mla_w_o"][0]).reshape(DC, 128, D)
    shared["mlan"] = np.ascontiguousarray(np.concatenate([col16(inp["mla_q_norm"][0]), col16(inp["mla_kv_norm"][0])], axis=1))
    shared["rope"] = make_rope()
    shared["fw"] = tile_w(inp["fnet_w_out"][0]).reshape(DC, 128, D)
    shared.update(make_dft())
    maps = []
    for b in range(2):
        m = dict(shared)
        m["x"] = np.ascontiguousarray(np.concatenate([inp["ctx"][b], inp["x"][b]], axis=0).T)
        cvec = np.stack([inp["c"][b], inp["c_ctx"]])
        m["cT"] = np.ascontiguousarray(cvec.reshape(2, 16, 128).transpose(2, 1, 0).reshape(128, 32))
        maps.append(m)
    return maps


def make_consts():
    c = np.zeros((128, 516), np.float32)
    idx = np.arange(128)
    c[:, 0:128] = np.eye(128, dtype=np.float32)
    same = (idx[:, None] // 32) == (idx[None, :] // 32)
    c[:, 128:256] = (same & (idx[:, None] <= idx[None, :])).astype(np.float32)
    c[:, 256:384] = (same & (idx[:, None] >= idx[None, :])).astype(np.float32)
    c[:, 384:512] = (np.arange(128) % 32 != 0).astype(np.float32)[None, :]
    for k in range(4):
        c[:, 512 + k] = (idx // 32 == k).astype(np.float32)
    return c


def make_rope():
    t = np.arange(NLAT)
    pos = np.stack([(t // 64).astype(np.float32), (t % 64).astype(np.float32)], axis=-1)
    inv_freq = (np.float32(10000.0) ** (-np.arange(16, dtype=np.float32) / np.float32(16))).astype(np.float32)
    ang = (pos[:, :, None] * inv_freq[None, None, :]).astype(np.float32)
    cos, sin = np.cos(ang).astype(np.float32), np.sin(ang).astype(np.float32)
    tab = np.zeros((128, 2, L), np.float32)
    tab[:, 0, :] = 1.0
    for a in range(2):
        for hf in range(2):
            rows = slice(a * 32 + hf * 16, a * 32 + hf * 16 + 16)
            tab[rows, 0, NCTX:] = cos[:, a, :].T
            tab[rows, 1, NCTX:] = (-sin[:, a, :].T) if hf == 0 else sin[:, a, :].T
    return np.ascontiguousarray(tab.reshape(128, 2 * L))


def make_dft():
    import ml_dtypes
    c = np.arange(256)
    ang = 2.0 * np.pi * ((c[:, None] * c[None, :]) % 256) / 256.0
    t256 = np.zeros((128, 2, 2, 256), np.float32)
    for kk in range(2):
        t256[:, kk, 0, :] = np.cos(ang[kk * 128:(kk + 1) * 128])
        t256[:, kk, 1, :] = np.sin(ang[kk * 128:(kk + 1) * 128])
    t = np.arange(NLAT)
    angT = 2.0 * np.pi * ((t[:, None] * t[None, :]) % NLAT) / float(NLAT)
    dftT = np.stack([np.cos(angT) / 1024.0, -np.sin(angT) / 1024.0]).astype(np.float32).astype(ml_dtypes.bfloat16)
    tc_ = np.arange(NCTX)
    angC = 2.0 * np.pi * ((tc_[:, None] * tc_[None, :]) % NCTX) / float(NCTX)
    dftC = np.stack([np.cos(angC) / 256.0, -np.sin(angC) / 256.0]).astype(np.float32).astype(ml_dtypes.bfloat16)
    return {"dft256": np.ascontiguousarray(t256.reshape(128, 1024)), "dftT": np.ascontiguousarray(dftT), "dftC": np.ascontiguousarray(dftC)}


PLAN = ['mod',
        ('ffn', 0, 0), ('hgrn', 0, 0, False), ('ffn', 0, 1),
        ('ffn', 1, 0), ('mla', 1, False), ('ffn', 1, 1),
        ('ffn', 2, 0), ('fnet', 2), ('ffn', 2, 1),
        ('ffn', 3, 0), ('hgrn', 3, 1, True), ('ffn', 3, 1, True),
        'final']


def kernel(**inputs):
    inp = {k: np.asarray(v) for k, v in inputs.items()}
    maps = prep_inputs(inp)
    nc, g = build(PLAN)
    need = [a.memorylocations[0].name for a in nc.allocations
            if getattr(a, "kind", None) == "ExternalInput"]
    in_maps = [{k: m[k] for k in need if k in m} for m in maps]
    res = run_bass_kernel_spmd(nc, in_maps, core_ids=[0, 1])
    out = np.stack([np.ascontiguousarray(res.results[b]["out"].T) for b in range(2)])
    return out.astype(np.float32)
```

```python
import contextlib
import types
import numpy as np
import concourse.bass as bass
import concourse.mybir as mybir
from concourse.bass_utils import run_bass_kernel_spmd

F32 = mybir.dt.float32
BF16 = mybir.dt.bfloat16
AF = mybir.ActivationFunctionType
ALU = mybir.AluOpType

ENGS = ("pe", "act", "dve", "pool", "sp")
ST = "pool"
NDMASEM = 6

D = 2048
DC = 16
DFF = 5632
FC = 44
EPS = 1e-6
NCTX = 256
NLAT = 4096
L = NCTX + NLAT
TB = 1024
BLOCKS = [(0, NCTX, 1)] + [(NCTX + TB * i, TB, 0) for i in range(NLAT // TB)]
NMODC = 144
ARENA = (DC + FC) * TB + 3 * 8192 + 2 * 4096


def freeze(fn):
    if fn.__closure__ is None:
        return fn
    cells = []
    for c in fn.__closure__:
        try:
            cells.append(types.CellType(c.cell_contents))
        except ValueError:
            cells.append(c)
    return types.FunctionType(fn.__code__, fn.__globals__, fn.__name__, fn.__defaults__, tuple(cells))


class Prog:
    def __init__(self, nc):
        self.nc = nc
        self.ops = []
        self.last_w = {}
        self.readers = {}
        self.last_eng = {}
        self.dmas_since = []

    def add(self, eng, fn, reads=(), writes=(), dma=False, extra_deps=()):
        idx = len(self.ops)
        deps = set(extra_deps)
        for k in reads:
            w = self.last_w.get(k)
            if w is not None:
                deps.add(w)
        for k in writes:
            w = self.last_w.get(k)
            if w is not None:
                deps.add(w)
            for r in self.readers.get(k, ()):
                deps.add(r)
        for k in reads:
            lst = self.readers.setdefault(k, [])
            if not dma:
                lst[:] = [r for r in lst if self.ops[r]["dma"] or self.ops[r]["eng"] != eng]
            lst.append(idx)
        for k in writes:
            self.last_w[k] = idx
            self.readers[k] = []
        deps.discard(idx)
        self.ops.append(dict(eng=eng, fn=freeze(fn), deps=deps, dma=dma, sig=False))
        self.last_eng[eng] = idx
        if dma:
            self.dmas_since.append(idx)
        return idx

    def barrier(self):
        deps = set(self.last_eng.values()) | set(self.dmas_since)
        self.dmas_since = []
        for e in ENGS:
            self.add(e, lambda eng: eng.nop(), extra_deps=deps)
        self.last_w = {}
        self.readers = {}

    def emit(self):
        nc = self.nc
        ops = self.ops
        for i, o in enumerate(ops):
            nd = set()
            for d in o["deps"]:
                p = ops[d]
                if p["eng"] == o["eng"] and o["eng"] == "pe" and not p["dma"]:
                    continue
                nd.add(d)
                p["sig"] = True
            o["deps"] = nd
        with contextlib.ExitStack() as st:
            esem = {e: st.enter_context(nc.semaphore("s_" + e)) for e in ENGS}
            dsem = {e: [st.enter_context(nc.semaphore("d_%s%d" % (e, j))) for j in range(NDMASEM)]
                    for e in ("sp", "act", "pool")}
            ecount = {e: 0 for e in ENGS}
            dcount = {e: 0 for e in dsem}
            dtarget = {e: [0] * NDMASEM for e in dsem}
            per_eng = {e: [] for e in ENGS}
            for i, o in enumerate(ops):
                e = o["eng"]
                if o["dma"]:
                    j = dcount[e] % NDMASEM
                    dcount[e] += 1
                    o["prev_target"] = dtarget[e][j]
                    dtarget[e][j] += 16
                    o["sem"] = ("d", e, j)
                    o["target"] = dtarget[e][j]
                elif o["sig"]:
                    ecount[e] += 1
                    o["sem"] = ("e", e, 0)
                    o["target"] = ecount[e]
                per_eng[e].append(i)
            self.stats = dict(ecount=dict(ecount), dcount=dict(dcount), nops={e: len(per_eng[e]) for e in ENGS})

            def semof(s):
                return esem[s[1]] if s[0] == "e" else dsem[s[1]][s[2]]

            block = st.enter_context(nc.Block())

            def body(e, eng):
                seen = {}
                for i in per_eng[e]:
                    o = ops[i]
                    waits = {}
                    for d in o["deps"]:
                        p = ops[d]
                        s = p["sem"]
                        waits[s] = max(waits.get(s, 0), p["target"])
                    if o["dma"] and o["prev_target"] > 0:
                        s = o["sem"]
                        waits[s] = max(waits.get(s, 0), o["prev_target"])
                    for s, v in waits.items():
                        if seen.get(s, 0) >= v:
                            continue
                        seen[s] = v
                        eng.wait_ge(semof(s), v)
                    ins = o["fn"](eng)
                    if o["dma"]:
                        ins.then_inc(semof(o["sem"]), 16)
                    elif o["sig"]:
                        ins.then_inc(semof(o["sem"]), 1)
                if e in dsem:
                    for j in range(NDMASEM):
                        if dtarget[e][j] > 0 and seen.get(("d", e, j), 0) < dtarget[e][j]:
                            eng.wait_ge(dsem[e][j], dtarget[e][j])

            if per_eng["sp"]:
                block.sync(lambda eng: body("sp", eng))
            if per_eng["act"]:
                block.scalar(lambda eng: body("act", eng))
            if per_eng["dve"]:
                block.vector(lambda eng: body("dve", eng))
            if per_eng["pool"]:
                block.gpsimd(lambda eng: body("pool", eng))
            if per_eng["pe"]:
                block.tensor(lambda eng: body("pe", eng))


class Ring:
    def __init__(self, name, aps):
        self.name, self.aps, self.i = name, aps, 0

    def next(self):
        j = self.i % len(self.aps)
        self.i += 1
        return self.aps[j], (self.name, j)


def tgroups(n, step):
    return [(t, min(step, n - t)) for t in range(0, n, step)]


def xkeys(pref, dc, t0, n):
    return [(pref, dc, t) for t in range((t0 // 256) * 256, t0 + n, 256)]


class G:
    pass


def cast_op(P, g, i, dst, src, reads, writes):
    if i % 2 == 0:
        P.add("act", lambda e: e.copy(dst, src), reads=reads, writes=writes)
    else:
        P.add("dve", lambda e: e.tensor_copy(dst, src), reads=reads, writes=writes)


def emit_rstd(P, g, ssp, ssk, n, dim):
    rs, rsk = g.rstd.next()
    P.add("dve", lambda e: e.tensor_scalar(rs[:, 0:n], ssp[:, 0:n], 1.0 / dim, EPS, ALU.mult, ALU.add),
          reads=[ssk], writes=[rsk])
    P.add("act", lambda e: e.sqrt(rs[:, 0:n], rs[:, 0:n]), reads=[rsk], writes=[rsk])
    P.add("dve", lambda e: e.reciprocal(rs[:, 0:n], rs[:, 0:n]), reads=[rsk], writes=[rsk])
    return rs, rsk


def emit_adanorm(P, g, src, c0, nb, gcol, shcol, colkeys, hkey="h"):
    xTv = src.rearrange("(c p) t -> p c t", p=128)
    for (t0, n) in tgroups(nb, 256):
        xs, xk = g.stg.next()
        P.add("sp", lambda e, xs=xs, t0=t0, n=n: e.dma_start(
            out=xs[:, 0:DC * n].rearrange("p (c t) -> p c t", t=n), in_=xTv[:, :, c0 + t0:c0 + t0 + n]),
            reads=[(hkey, c0 + t0)], writes=[xk], dma=True)
        ssp, ssk = g.psr.next()
        for dc in range(DC):
            sq, sqk = g.sq.next()
            P.add("act", lambda e, sq=sq, xs=xs, dc=dc, n=n: e.activation(
                out=sq[:, 0:n], in_=xs[:, dc * n:(dc + 1) * n], func=AF.Square), reads=[xk], writes=[sqk])
            P.add("pe", lambda e, sq=sq, ssp=ssp, dc=dc, n=n: e.matmul(
                ssp[:, 0:n], lhsT=g.ones[:, :], rhs=sq[:, 0:n], start=(dc == 0), stop=(dc == DC - 1)),
                reads=[sqk, "ones"], writes=[ssk])
        rs, rsk = emit_rstd(P, g, ssp, ssk, n, D)
        for dc in range(DC):
            tm, tmk = g.tmp.next()
            P.add("dve", lambda e, tm=tm, xs=xs, rs=rs, dc=dc, n=n: e.scalar_tensor_tensor(
                out=tm[:, 0:n], in0=xs[:, dc * n:(dc + 1) * n], scalar=gcol(dc), in1=rs[:, 0:n],
                op0=ALU.mult, op1=ALU.mult), reads=[xk, rsk] + colkeys, writes=[tmk])
            P.add("act", lambda e, tm=tm, dc=dc, n=n, t0=t0: e.activation(
                out=g.xn[:, dc * nb + t0: dc * nb + t0 + n], in_=tm[:, 0:n], func=AF.Identity,
                bias=shcol(dc), scale=1.0), reads=[tmk] + colkeys, writes=[("xn", dc, t0)])


def emit_linear(P, g, xn, KC, nb, wsrc, NOC, epi, xpref="xn", step=512):
    for oc in range(NOC):
        s, sk = g.stg.next()
        P.add("sp", lambda e, s=s, oc=oc: e.dma_start(out=s[:, 0:KC * 128], in_=wsrc[oc]), writes=[sk], dma=True)
        w, wk = g.wbr.next()
        cast_op(P, g, oc, w[:, 0:KC * 128], s[:, 0:KC * 128], [sk], [wk])
        for (t0, n) in tgroups(nb, step):
            pp, ppk = g.psr.next()
            for kc in range(KC):
                P.add("pe", lambda e, pp=pp, w=w, kc=kc, t0=t0, n=n: e.matmul(
                    pp[:, 0:n], lhsT=w[:, kc * 128:(kc + 1) * 128],
                    rhs=xn[:, kc * nb + t0: kc * nb + t0 + n], start=(kc == 0), stop=(kc == KC - 1)),
                    reads=[wk] + xkeys(xpref, kc, t0, n), writes=[ppk])
            epi(oc, t0, n, pp, ppk)


def epi_store(P, g, dst, c0, okey, scale=None):
    def epi(oc, t0, n, pp, ppk):
        o, ok = g.ost.next()
        if (oc + t0 // 512) % 2 == 0:
            P.add("act", lambda e: e.copy(o[:, 0:n], pp[:, 0:n]), reads=[ppk], writes=[ok])
        else:
            P.add("dve", lambda e: e.tensor_copy(o[:, 0:n], pp[:, 0:n]), reads=[ppk], writes=[ok])
        P.add(ST, lambda e: e.dma_start(out=dst[oc * 128:(oc + 1) * 128, c0 + t0:c0 + t0 + n], in_=o[:, 0:n]),
              reads=[ok], writes=[(okey, oc, c0 + t0)], dma=True)
    return epi


def epi_residual(P, g, h, c0, gtcol, colkeys):
    def epi(oc, t0, n, pp, ppk):
        o, ok = g.ost.next()
        P.add("sp", lambda e: e.dma_start(out=o[:, 0:n], in_=h[oc * 128:(oc + 1) * 128, c0 + t0:c0 + t0 + n]),
              reads=[("hc", oc, c0 + t0)], writes=[ok], dma=True)
        P.add("dve", lambda e: e.scalar_tensor_tensor(
            out=o[:, 0:n], in0=pp[:, 0:n], scalar=gtcol(oc), in1=o[:, 0:n], op0=ALU.mult, op1=ALU.add),
            reads=[ppk, ok] + colkeys, writes=[ok])
        P.add(ST, lambda e: e.dma_start(out=h[oc * 128:(oc + 1) * 128, c0 + t0:c0 + t0 + n], in_=o[:, 0:n]),
              reads=[ok], writes=[("hc", oc, c0 + t0)] + [("h", c0 + t) for t in range((t0 // 256) * 256, t0 + n, 256)],
              dma=True)
    return epi


def modcol(g, l, kind, s, t, dc):
    j = ((l * NMODC + s * 48 + t * 16 + dc) * 2) + kind
    return g.modc[:, j:j + 1]


def emit_cols(P, g, l, s, kind, gsrc, half_gate):
    sc = g.modc[:, :].rearrange("p (l c r) -> p l c r", l=4, r=2)[:, l, s * 48 + 16: s * 48 + 32, kind]
    gt = g.modc[:, :].rearrange("p (l c r) -> p l c r", l=4, r=2)[:, l, s * 48 + 32: s * 48 + 48, kind]
    P.add("dve", lambda e: e.scalar_tensor_tensor(out=g.dcol[:, 0:16], in0=sc, scalar=1.0, in1=gsrc,
                                                   op0=ALU.add, op1=ALU.mult), reads=["modc", "gcols"], writes=["dcol"])
    P.add("dve", lambda e: e.tensor_scalar_mul(g.dcol[:, 16:32], gt, 0.5 if half_gate else 1.0),
          reads=["modc", "dcol"], writes=["dcol"])


def stage_mod(P, g):
    P.add("sp", lambda e: e.dma_start(out=g.cs[:, :], in_=g.d_cT), writes=["cs"], dma=True)
    P.add("act", lambda e: e.activation(out=g.cs[:, :], in_=g.cs[:, :], func=AF.Silu), reads=["cs"], writes=["cs"])
    g.modb = carve(g, 0, 4 * NMODC, F32)
    P.add("sp", lambda e: e.dma_start(out=g.modb[:, :], in_=g.d_modb), writes=["modb"], dma=True)
    for l in range(4):
        wv = g.d_modw[l].rearrange("(c p) n -> p c n", p=128)
        for nb_ in range(36):
            for hf in range(2):
                s, sk = g.stg.next()
                P.add("sp", lambda e, s=s, nb_=nb_, hf=hf, wv=wv: e.dma_start(
                    out=s[:, :].rearrange("p (c n) -> p c n", n=512), in_=wv[:, hf * 8:(hf + 1) * 8, nb_ * 512:(nb_ + 1) * 512]),
                    writes=[sk], dma=True)
                for j in range(4):
                    pp = g.ps[(nb_ * 4 + j) % 8]
                    ppk = ("ps", (nb_ * 4 + j) % 8)
                    for k8 in range(8):
                        kc = hf * 8 + k8
                        P.add("pe", lambda e, s=s, pp=pp, j=j, k8=k8, kc=kc: e.matmul(
                            pp[:, 0:2], lhsT=s[:, k8 * 512 + j * 128: k8 * 512 + (j + 1) * 128],
                            rhs=g.cs[:, kc * 2:(kc + 1) * 2], start=(kc == 0), stop=(kc == 15)),
                            reads=[sk, "cs"], writes=[ppk])
                    if hf == 1:
                        ch = l * NMODC + nb_ * 4 + j
                        P.add("dve", lambda e, pp=pp, ch=ch: e.tensor_tensor(
                            g.modc[:, ch * 2:ch * 2 + 2], pp[:, 0:2],
                            g.modb[:, ch:ch + 1].to_broadcast([128, 2]), ALU.add),
                            reads=[ppk, "modb"], writes=["modc"])


def stage_ffn(P, g, l, which, skip_ctx=False):
    s = 0 if which == 0 else 2
    wgu, wdn = (g.d_wgu1[l], g.d_wdn1[l]) if which == 0 else (g.d_wgu2[l], g.d_wdn2[l])
    gsrc = g.gcols[:, (l * 3 + s) * 16:(l * 3 + s + 1) * 16]
    for (c0, nb, kind) in BLOCKS:
        if kind == 1 and skip_ctx:
            continue
        ffn_block(P, g, l, s, wgu, wdn, gsrc, c0, nb, kind)
    P.barrier()


def ffn_block(P, g, l, s, wgu, wdn, gsrc, c0, nb, kind):
    if True:
        emit_cols(P, g, l, s, kind, gsrc, True)
        gcol = lambda dc: g.dcol[:, dc:dc + 1]
        hgcol = lambda dc: g.dcol[:, 16 + dc:17 + dc]
        shcol = lambda dc, kind=kind: modcol(g, l, kind, s, 0, dc)
        emit_adanorm(P, g, g.d_h, c0, nb, gcol, shcol, ["dcol", "modc"])
        TG = tgroups(nb, 512)
        for fc in range(FC):
            st_, sk = g.stg.next()
            P.add("sp", lambda e, st_=st_, fc=fc: e.dma_start(out=st_[:, :], in_=wgu[fc]), writes=[sk], dma=True)
            w, wk = g.wbr.next()
            P.add("act", lambda e, w=w, st_=st_: e.copy(w[:, 0:2048], st_[:, 0:2048]), reads=[sk], writes=[(wk, 0)])
            P.add("dve", lambda e, w=w, st_=st_: e.tensor_copy(w[:, 2048:4096], st_[:, 2048:4096]), reads=[sk], writes=[(wk, 1)])
            for (t0, n) in TG:
                pg, pgk = g.psr.next()
                pu, puk = g.psr.next()
                for hh, (pp, ppk) in enumerate(((pg, pgk), (pu, puk))):
                    for dc in range(DC):
                        P.add("pe", lambda e, pp=pp, w=w, hh=hh, dc=dc, t0=t0, n=n: e.matmul(
                            pp[:, 0:n], lhsT=w[:, (hh * DC + dc) * 128:(hh * DC + dc + 1) * 128],
                            rhs=g.xn[:, dc * nb + t0: dc * nb + t0 + n], start=(dc == 0), stop=(dc == DC - 1)),
                            reads=[(wk, hh)] + xkeys("xn", dc, t0, n), writes=[ppk])
                g_, gk = g.sg.next()
                P.add("act", lambda e, g_=g_, pg=pg, n=n: e.activation(out=g_[:, 0:n], in_=pg[:, 0:n], func=AF.Silu),
                      reads=[pgk], writes=[gk])
                P.add("dve", lambda e, g_=g_, pu=pu, n=n, fc=fc, t0=t0: e.tensor_tensor(
                    g.act[:, fc * nb + t0: fc * nb + t0 + n], pu[:, 0:n], g_[:, 0:n], ALU.mult),
                    reads=[puk, gk], writes=[("act", fc, t0)])
        for dc in range(DC):
            ws, wks = [], []
            for hf in range(2):
                st_, sk = g.stg.next()
                P.add("sp", lambda e, st_=st_, dc=dc, hf=hf: e.dma_start(
                    out=st_[:, 0:2816], in_=wdn[dc][:, hf * 2816:(hf + 1) * 2816]), writes=[sk], dma=True)
                w, wk = g.wbr.next()
                cast_op(P, g, hf, w[:, 0:2816], st_[:, 0:2816], [sk], [wk])
                ws.append(w)
                wks.append(wk)
            for (t0, n) in TG:
                xr, xrk = g.ost.next()
                P.add("sp", lambda e, xr=xr, dc=dc, t0=t0, n=n: e.dma_start(
                    out=xr[:, 0:n], in_=g.d_h[dc * 128:(dc + 1) * 128, c0 + t0:c0 + t0 + n]),
                    reads=[("hc", dc, c0 + t0)], writes=[xrk], dma=True)
                pp, ppk = g.psr.next()
                for fc in range(FC):
                    w, wk = ws[fc // 22], wks[fc // 22]
                    P.add("pe", lambda e, pp=pp, w=w, fc=fc, t0=t0, n=n: e.matmul(
                        pp[:, 0:n], lhsT=w[:, (fc % 22) * 128:(fc % 22 + 1) * 128],
                        rhs=g.act[:, fc * nb + t0: fc * nb + t0 + n], start=(fc == 0), stop=(fc == FC - 1)),
                        reads=[wk, ("act", fc, t0)], writes=[ppk])
                P.add("dve", lambda e, xr=xr, pp=pp, n=n, dc=dc: e.scalar_tensor_tensor(
                    out=xr[:, 0:n], in0=pp[:, 0:n], scalar=hgcol(dc), in1=xr[:, 0:n],
                    op0=ALU.mult, op1=ALU.add), reads=[ppk, xrk, "dcol"], writes=[xrk])
                P.add(ST, lambda e, xr=xr, dc=dc, t0=t0, n=n: e.dma_start(
                    out=g.d_h[dc * 128:(dc + 1) * 128, c0 + t0:c0 + t0 + n], in_=xr[:, 0:n]), reads=[xrk],
                    writes=[("hc", dc, c0 + t0)] + [("h", c0 + t) for t in range((t0 // 256) * 256, t0 + n, 256)], dma=True)


def carve(g, off_bytes, ncols, dt):
    assert off_bytes % 4 == 0
    e0 = off_bytes // 2
    if dt == BF16:
        assert e0 + ncols <= ARENA
        return g.arena[:, e0:e0 + ncols]
    assert e0 + 2 * ncols <= ARENA
    return g.arena[:, e0:e0 + 2 * ncols].bitcast(F32)


def lin_views(g):
    g.xn = g.arena[:, 0:DC * TB]
    o_ = DC * TB * 2
    stage = [carve(g, o_ + i * 16384, 4096, F32) for i in range(6)]
    o_ += 6 * 16384
    wbf = [carve(g, o_ + i * 8192, 4096, BF16) for i in range(3)]
    g.stg, g.wbr = Ring("stage", stage[0:3]), Ring("wbf", wbf)
    g.stg2 = Ring("stage2", stage[3:6])
    g.psr = Ring("ps", g.ps)


def ffn_views(g):
    g.xn = g.arena[:, 0:DC * TB]
    g.act = g.arena[:, DC * TB:(DC + FC) * TB]
    o_ = (DC + FC) * TB
    stage = [g.arena[:, o_ + i * 8192: o_ + (i + 1) * 8192].bitcast(F32) for i in range(3)]
    o_ += 3 * 8192
    wbf = [g.arena[:, o_ + i * 4096: o_ + (i + 1) * 4096] for i in range(2)]
    g.stg, g.wbr = Ring("stage", stage), Ring("wbf", wbf)
    g.psr = Ring("ps", g.ps)


def stage_lbcols(P, g):
    P.add("sp", lambda e: e.dma_start(out=g.lbl[:, :], in_=g.d_lbl), writes=["lbl"], dma=True)
    P.add("pool", lambda e: e.memset(g.lbc[:, :], 0.0), writes=["lbc"])
    for d_ in range(2):
        b0 = ((1 * 2 + d_) * 3) * 16
        l0 = g.lbl[:, (d_ * 2 + 0) * 16:(d_ * 2 + 1) * 16]
        l1 = g.lbl[:, (d_ * 2 + 1) * 16:(d_ * 2 + 2) * 16]
        P.add("dve", lambda e, b0=b0, l0=l0, l1=l1: e.tensor_tensor(g.lbc[:, b0:b0 + 16], l1, l0, ALU.subtract),
              reads=["lbl", "lbc"], writes=["lbc"])
        P.add("act", lambda e, b0=b0: e.activation(out=g.lbc[:, b0:b0 + 16], in_=g.lbc[:, b0:b0 + 16], func=AF.Sigmoid),
              reads=["lbc"], writes=["lbc"])
    for j in range(2):
        for d_ in range(2):
            b0 = ((j * 2 + d_) * 3) * 16
            P.add("dve", lambda e, b0=b0: e.tensor_scalar(g.lbc[:, b0 + 16:b0 + 32], g.lbc[:, b0:b0 + 16], -1.0, 1.0, ALU.mult, ALU.add),
                  reads=["lbc"], writes=["lbc"])
            P.add("dve", lambda e, b0=b0: e.tensor_scalar(g.lbc[:, b0 + 32:b0 + 48], g.lbc[:, b0:b0 + 16], 1.0, -1.0, ALU.mult, ALU.add),
                  reads=["lbc"], writes=["lbc"])
    P.add("sp", lambda e: e.dma_start(out=g.cst[:, :], in_=g.d_cst), writes=["cst"], dma=True)
    P.add("act", lambda e: e.copy(g.identb[:, :], g.cst[:, 0:128]), reads=["cst"], writes=["identb"])


def stage_hgrn_inproj(P, g, l, j):
    lin_views(g)
    gsrc = g.gcols[:, (l * 3 + 1) * 16:(l * 3 + 2) * 16]
    for (c0, nb, kind) in BLOCKS:
        emit_cols(P, g, l, 1, kind, gsrc, False)
        gcol = lambda dc: g.dcol[:, dc:dc + 1]
        shcol = lambda dc, kind=kind: modcol(g, l, kind, 1, 0, dc)
        emit_adanorm(P, g, g.d_h, c0, nb, gcol, shcol, ["dcol", "modc"])
        emit_linear(P, g, g.xn, DC, nb, g.d_hgw_in[j], 80, epi_store(P, g, g.d_pj, c0, "pj"))
    P.barrier()


def stage_hgrn_scan(P, g, j, nheads=16):
    NTI = L // 128
    NCH = L // 32
    SZ = L * 4
    Q = carve(g, 0 * SZ, L, F32)
    Z = carve(g, 1 * SZ, L, F32)
    A = carve(g, 2 * SZ, L, F32)
    Bc = carve(g, 3 * SZ, L, F32)
    E = carve(g, 4 * SZ, L, F32)
    I = carve(g, 5 * SZ, L, F32)
    o_ = 6 * SZ
    qd = carve(g, o_, L, BF16); o_ += L * 2
    ki = carve(g, o_, L, BF16); o_ += L * 2
    vtok = carve(g, o_, L, BF16); o_ += L * 2
    ibf = carve(g, o_, L, BF16); o_ += L * 2
    dec = carve(g, o_, NCH, F32); o_ += NCH * 4
    S32 = [carve(g, o_ + i * 512, 128, F32) for i in range(4)]; o_ += 4 * 512
    T32 = [carve(g, o_ + i * 512, 128, F32) for i in range(4)]; o_ += 4 * 512
    Sbf = [carve(g, o_ + i * 256, 128, BF16) for i in range(4)]; o_ += 4 * 256
    ATm = [carve(g, o_ + i * 256, 128, BF16) for i in range(3)]; o_ += 3 * 256
    KT = [carve(g, o_ + i * 1024, 512, BF16) for i in range(3)]; o_ += 3 * 1024
    OS = [carve(g, o_ + i * 512, 128, F32) for i in range(4)]; o_ += 4 * 512
    assert o_ <= ARENA * 2
    s32r, t32r, sbfr, atmr, ktr, osr = Ring("S32", S32), Ring("T32", T32), Ring("Sbf", Sbf), Ring("ATm", ATm), Ring("KT", KT), Ring("OS", OS)
    par, ptr, por, pur = Ring("ps", g.ps[0:2]), Ring("ps2", g.ps[2:3]), Ring("ps3", g.ps[3:5]), Ring("ps5", g.ps[5:7])
    pv = g.ps[7]
    identb = g.identb
    maskf = [g.cst[:, 128:256], g.cst[:, 256:384]]
    m01 = g.cst[:, 384:512]
    HALF = L // 2
    for h in range(nheads):
        hs = slice(h * 128, (h + 1) * 128)
        for hf in range(2):
            cs_ = slice(hf * HALF, (hf + 1) * HALF)
            P.add("sp", lambda e, cs_=cs_, h=h: e.dma_start(out=Q[:, cs_], in_=g.d_pj[h * 128:(h + 1) * 128, cs_]),
                  reads=["pj_all"], writes=[("Q", hf)], dma=True)
            P.add("sp", lambda e, cs_=cs_, h=h: e.dma_start(out=I[:, cs_], in_=g.d_pj[6144 + h * 128:6144 + (h + 1) * 128, cs_]),
                  reads=["pj_all"], writes=[("I", hf)], dma=True)
            P.add("act", lambda e, cs_=cs_: e.activation(out=Q[:, cs_], in_=Q[:, cs_], func=AF.Silu), reads=[("Q", hf)], writes=[("Q", hf)])
            P.add("pool", lambda e, cs_=cs_: e.tensor_copy(ibf[:, cs_], I[:, cs_]), reads=[("I", hf)], writes=[("ibf", hf)])
        pvb = pv[:, :].bitcast(BF16)
        for t4 in range(0, NTI, 4):
            nt_ = min(4, NTI - t4)
            for q in range(nt_):
                ti = t4 + q
                P.add("pe", lambda e, q=q, ti=ti: e.transpose(pvb[:, q * 128:(q + 1) * 128], ibf[:, ti * 128:(ti + 1) * 128], identb[:, :]),
                      reads=[("ibf", 0), ("ibf", 1), "identb"], writes=["pv"])
            P.add("act", lambda e, t4=t4, nt_=nt_: e.copy(vtok[:, t4 * 128:(t4 + nt_) * 128], pvb[:, 0:nt_ * 128]),
                  reads=["pv"], writes=["vtok"])
        for d_ in range(2):
            lb0 = ((j * 2 + d_) * 3) * 16
            lbcol = g.lbc[:, lb0 + h: lb0 + h + 1]
            omlcol = g.lbc[:, lb0 + 16 + h: lb0 + 16 + h + 1]
            nomlcol = g.lbc[:, lb0 + 32 + h: lb0 + 32 + h + 1]
            zrow = 2048 * (1 + d_) + h * 128
            P.add("sp", lambda e, zrow=zrow: e.dma_start(out=Z[:, :], in_=g.d_pj[zrow:zrow + 128, :]), reads=["pj_all"], writes=["Z"], dma=True)
            P.add("act", lambda e: e.activation(out=Z[:, :], in_=Z[:, :], func=AF.Sigmoid), reads=["Z"], writes=["Z"])
            P.add("dve", lambda e, omlcol=omlcol, lbcol=lbcol: e.tensor_scalar(A[:, :], Z[:, :], omlcol, lbcol, ALU.mult, ALU.add),
                  reads=["Z", "lbc"], writes=["A"])
            P.add("act", lambda e: e.activation(out=A[:, :], in_=A[:, :], func=AF.Ln), reads=["A"], writes=["A"])
            P.add("dve", lambda e, omlcol=omlcol, nomlcol=nomlcol: e.tensor_scalar(Z[:, :], Z[:, :], nomlcol, omlcol, ALU.mult, ALU.add),
                  reads=["Z", "lbc"], writes=["Z"])
            for (t0, n) in tgroups(L, 128):
                P.add("dve", lambda e, t0=t0, n=n: e.tensor_tensor_scan(Bc[:, t0:t0 + n], m01[:, 0:n], A[:, t0:t0 + n], 0.0, ALU.mult, ALU.add),
                      reads=["A", "cst"], writes=["Bc"])
            P.add("act", lambda e: e.activation(out=dec[:, :], in_=Bc[:, :].rearrange("p (n c) -> p n c", c=32)[:, :, 31], func=AF.Exp),
                  reads=["Bc"], writes=["dec"])
            if d_ == 1:
                P.add("dve", lambda e: e.tensor_tensor(Bc[:, :], Bc[:, :], A[:, :], ALU.subtract), reads=["Bc", "A", "dec"], writes=["Bc"])
            sq_, sk_ = (1.0, -1.0) if d_ == 0 else (-1.0, 1.0)
            P.add("act", lambda e, sq_=sq_: e.activation(out=E[:, :], in_=Bc[:, :], func=AF.Exp, scale=sq_), reads=["Bc"], writes=["E"])
            P.add("dve", lambda e: e.tensor_tensor(qd[:, :], Q[:, :], E[:, :], ALU.mult), reads=["E", ("Q", 0), ("Q", 1)], writes=["qd"])
            P.add("act", lambda e, sk_=sk_: e.activation(out=E[:, :], in_=Bc[:, :], func=AF.Exp, scale=sk_), reads=["Bc", "qd"], writes=["E"])
            P.add("dve", lambda e: e.tensor_tensor(ki[:, :], Z[:, :], E[:, :], ALU.mult), reads=["E", "Z"], writes=["ki"])
            s_cur, s_cur_k = s32r.next()
            P.add("pool", lambda e, s_cur=s_cur: e.memset(s_cur[:, :], 0.0), writes=[s_cur_k])
            sb_cur, sb_cur_k = sbfr.next()
            P.add("pool", lambda e, sb_cur=sb_cur: e.memset(sb_cur[:, :], 0.0), writes=[sb_cur_k])
            if d_ == 0:
                order = list(range(NTI))
            else:
                order = [1, 0] + list(range(NTI - 1, 1, -1))
            dst = g.d_of if d_ == 0 else g.d_ob
            fr = {}

            def front(ti):
                ts_ = slice(ti * 128, (ti + 1) * 128)
                pa, pak = par.next()
                P.add("pe", lambda e: e.matmul(pa[:, 0:128], lhsT=ki[:, ts_], rhs=qd[:, ts_], start=True, stop=True),
                      reads=["ki", "qd"], writes=[pak])
                am, amk = atmr.next()
                P.add("dve", lambda e: e.tensor_tensor(am[:, :], pa[:, 0:128], maskf[d_], ALU.mult), reads=[pak, "cst"], writes=[amk])
                pt, ptk = ptr.next()
                ptb = pt[:, :].bitcast(BF16)
                P.add("pe", lambda e: e.transpose(ptb[:, 0:128], ki[:, ts_], identb[:, :]), reads=["ki", "identb"], writes=[ptk])
                kt, ktk = ktr.next()
                for c in range(4):
                    P.add("act", lambda e, c=c: e.activation(out=kt[:, c * 128:(c + 1) * 128], in_=ptb[:, 0:128], func=AF.Identity,
                                                            scale=g.cst[:, 512 + c:513 + c]), reads=[ptk, "cst"], writes=[(ktk, c)])
                po, pok = por.next()
                P.add("pe", lambda e: e.matmul(po[:, 0:128], lhsT=vtok[:, ts_], rhs=am[:, :], start=True, stop=False),
                      reads=["vtok", amk], writes=[pok])
                pu, puk = pur.next()
                for c in range(4):
                    P.add("pe", lambda e, c=c: e.matmul(pu[:, c * 128:(c + 1) * 128], lhsT=kt[:, c * 128:(c + 1) * 128], rhs=vtok[:, ts_],
                                                        start=True, stop=True), reads=[(ktk, c), "vtok"], writes=[(puk, c)])
                fr[ti] = (po, pok, pu, puk)

            def chain(ti, s_cur, s_cur_k, sb_cur, sb_cur_k):
                po, pok, pu, puk = fr.pop(ti)
                corder = range(4) if d_ == 0 else range(3, -1, -1)
                for ci, c in enumerate(corder):
                    ch = ti * 4 + c
                    cs_ = slice(ti * 128 + c * 32, ti * 128 + c * 32 + 32)
                    last = (ci == 3)
                    if d_ == 0:
                        P.add("pe", lambda e, c=c, sb_cur=sb_cur, cs_=cs_, last=last: e.matmul(
                            po[:, c * 32:(c + 1) * 32], lhsT=sb_cur[:, :], rhs=qd[:, cs_], start=False, stop=last),
                            reads=[sb_cur_k, "qd"], writes=[pok])
                        tt, ttk = t32r.next()
                        P.add("dve", lambda e, tt=tt, s_cur=s_cur, c=c: e.tensor_tensor(tt[:, :], pu[:, c * 128:(c + 1) * 128], s_cur[:, :], ALU.add),
                              reads=[(puk, c), s_cur_k], writes=[ttk])
                        s_new, s_new_k = s32r.next()
                        P.add("dve", lambda e, tt=tt, s_new=s_new, ch=ch: e.tensor_scalar_mul(s_new[:, :], tt[:, :], dec[:, ch:ch + 1]),
                              reads=[ttk, "dec"], writes=[s_new_k])
                        sb_new, sb_new_k = sbfr.next()
                        P.add("act", lambda e, sb_new=sb_new, s_new=s_new: e.copy(sb_new[:, :], s_new[:, :]), reads=[s_new_k], writes=[sb_new_k])
                    else:
                        tt, ttk = t32r.next()
                        P.add("dve", lambda e, tt=tt, s_cur=s_cur, ch=ch: e.tensor_scalar_mul(tt[:, :], s_cur[:, :], dec[:, ch:ch + 1]),
                              reads=[s_cur_k, "dec"], writes=[ttk])
                        sb_new, sb_new_k = sbfr.next()
                        P.add("act", lambda e, sb_new=sb_new, tt=tt: e.copy(sb_new[:, :], tt[:, :]), reads=[ttk], writes=[sb_new_k])
                        P.add("pe", lambda e, c=c, sb_new=sb_new, cs_=cs_, last=last: e.matmul(
                            po[:, c * 32:(c + 1) * 32], lhsT=sb_new[:, :], rhs=qd[:, cs_], start=False, stop=last),
                            reads=[sb_new_k, "qd"], writes=[pok])
                        s_new, s_new_k = s32r.next()
                        P.add("dve", lambda e, tt=tt, s_new=s_new, c=c: e.tensor_tensor(s_new[:, :], pu[:, c * 128:(c + 1) * 128], tt[:, :], ALU.add),
                              reads=[(puk, c), ttk], writes=[s_new_k])
                    s_cur, s_cur_k, sb_cur, sb_cur_k = s_new, s_new_k, sb_new, sb_new_k
                os_, osk = osr.next()
                P.add("act", lambda e: e.copy(os_[:, :], po[:, 0:128]), reads=[pok], writes=[osk])
                P.add(ST, lambda e: e.dma_start(out=dst[h * 128:(h + 1) * 128, ti * 128:(ti + 1) * 128], in_=os_[:, :]),
                      reads=[osk], writes=[("o", d_, h, ti)], dma=True)
                return s_cur, s_cur_k, sb_cur, sb_cur_k

            front(order[0])
            for i_, ti in enumerate(order):
                if i_ + 1 < len(order):
                    front(order[i_ + 1])
                s_cur, s_cur_k, sb_cur, sb_cur_k = chain(ti, s_cur, s_cur_k, sb_cur, sb_cur_k)
    P.barrier()


def stage_hgrn_readout(P, g, l, j, skip_ctx):
    lin_views(g)
    gtcol = None
    for (c0, nb, kind) in BLOCKS:
        if kind == 1 and skip_ctx:
            continue
        hgrn_readout_block(P, g, l, j, c0, nb, kind)
    P.barrier()


def hgrn_readout_block(P, g, l, j, c0, nb, kind):
    gtcol = lambda oc: modcol(g, l, kind, 1, 2, oc)
    ofv = g.d_of.rearrange("(c p) t -> p c t", p=128)
    obv = g.d_ob.rearrange("(c p) t -> p c t", p=128)
    gtv = g.d_pj[8192:10240, :].rearrange("(c p) t -> p c t", p=128)
    for (t0, n) in tgroups(nb, 256):
        s1, k1 = g.stg.next()
        s2, k2 = g.stg.next()
        s3, k3 = g.stg.next()
        for (sx, kx, src) in ((s1, k1, ofv), (s2, k2, obv), (s3, k3, gtv)):
            P.add("sp", lambda e, sx=sx, src=src: e.dma_start(
                out=sx[:, 0:DC * n].rearrange("p (c t) -> p c t", t=n), in_=src[:, :, c0 + t0:c0 + t0 + n]), writes=[kx], dma=True)
        P.add("pool", lambda e: e.tensor_tensor(s1[:, 0:DC * n], s1[:, 0:DC * n], s2[:, 0:DC * n], ALU.add), reads=[k1, k2], writes=[k1])
        P.add("act", lambda e: e.activation(out=s3[:, 0:DC * n], in_=s3[:, 0:DC * n], func=AF.Silu), reads=[k3], writes=[k3])
        for dc in range(DC):
            sq_, sqk = g.sq.next()
            ssp, ssk = g.psr.next()
            P.add("act", lambda e, sq_=sq_, dc=dc: e.activation(out=sq_[:, 0:n], in_=s1[:, dc * n:(dc + 1) * n], func=AF.Square),
                  reads=[k1], writes=[sqk])
            P.add("pe", lambda e, sq_=sq_, ssp=ssp: e.matmul(ssp[:, 0:n], lhsT=g.ones[:, :], rhs=sq_[:, 0:n], start=True, stop=True),
                  reads=[sqk, "ones"], writes=[ssk])
            rs, rsk = emit_rstd(P, g, ssp, ssk, n, 128)
            tm, tmk = g.tmp.next()
            P.add("dve", lambda e, tm=tm, rs=rs, dc=dc: e.scalar_tensor_tensor(
                out=tm[:, 0:n], in0=s1[:, dc * n:(dc + 1) * n], scalar=g.hgn[:, j * 16 + dc: j * 16 + dc + 1], in1=rs[:, 0:n],
                op0=ALU.mult, op1=ALU.mult), reads=[k1, rsk, "hgn"], writes=[tmk])
            P.add("pool", lambda e, tm=tm, dc=dc: e.tensor_tensor(
                g.xn[:, dc * nb + t0: dc * nb + t0 + n], tm[:, 0:n], s3[:, dc * n:(dc + 1) * n], ALU.mult),
                reads=[tmk, k3], writes=[("xn", dc, t0)])
    emit_linear(P, g, g.xn, DC, nb, g.d_hgw_out[j], DC, epi_residual(P, g, g.d_h, c0, gtcol, ["modc"]))


MLA_SCALE = (128 + 64) ** -0.5


def stage_mla_proj(P, g, l):
    gsrc = g.gcols[:, (l * 3 + 1) * 16:(l * 3 + 2) * 16]
    for (c0, nb, kind) in BLOCKS:
        mla_proj_block(P, g, l, gsrc, c0, nb, kind)
    P.barrier()


def mla_proj_block(P, g, l, gsrc, c0, nb, kind):
    g.xn = g.arena[:, 0:DC * TB]
    cbuf = carve(g, 32768, 8 * TB, F32)
    cn = carve(g, 65536, 8 * TB, BF16)
    rope = carve(g, 81920, 2 * TB, F32)
    vbf = [carve(g, 90112 + i * 2048, TB, BF16) for i in range(2)]
    obf = [carve(g, 94208 + i * 1024, 512, BF16) for i in range(4)]
    stage = [carve(g, 98304 + i * 16384, 4096, F32) for i in range(3)]
    wbf = [carve(g, 147456 + i * 8192, 4096, BF16) for i in range(3)]
    rt = [carve(g, 172032 + i * 2048, 512, F32) for i in range(4)]
    g.stg, g.wbr = Ring("stage", stage), Ring("wbf", wbf)
    g.psr = Ring("ps", g.ps[0:7])
    vbr, obr, rtr = Ring("vbf", vbf), Ring("obf", obf), Ring("rt", rt)
    emit_cols(P, g, l, 1, kind, gsrc, False)
    gcol = lambda dc: g.dcol[:, dc:dc + 1]
    shcol = lambda dc: modcol(g, l, kind, 1, 0, dc)
    emit_adanorm(P, g, g.d_h, c0, nb, gcol, shcol, ["dcol", "modc"])
    P.add("sp", lambda e: e.dma_start(out=rope[:, 0:2 * nb].rearrange("p (a t) -> p a t", a=2),
                                      in_=g.d_rope.rearrange("p (a t) -> p a t", a=2)[:, :, c0:c0 + nb]), writes=["rope"], dma=True)
    pvb = g.ps[7][:, :].bitcast(BF16)

    def rope_epi(dst, first):
        st_ = {}

        def epi(oc, t0, n, pp, ppk):
            if first(oc):
                r1, r1k = rtr.next()
                P.add("dve", lambda e: e.tensor_tensor(r1[:, 0:n], pp[:, 0:n], rope[:, t0:t0 + n], ALU.mult), reads=[ppk, "rope"], writes=[r1k])
                st_[t0] = (r1, r1k)
            else:
                r1, r1k = st_.pop(t0)
                r2, r2k = rtr.next()
                P.add("dve", lambda e: e.tensor_tensor(r2[:, 0:n], pp[:, 0:n], rope[:, nb + t0:nb + t0 + n], ALU.mult), reads=[ppk, "rope"], writes=[r2k])
                o, ok = obr.next()
                P.add("pool", lambda e: e.tensor_tensor(o[:, 0:n], r1[:, 0:n], r2[:, 0:n], ALU.add), reads=[r1k, r2k], writes=[ok])
                P.add(ST, lambda e: e.dma_start(out=dst(oc)[:, c0 + t0:c0 + t0 + n], in_=o[:, 0:n]), reads=[ok], writes=[("mla_o", oc, c0 + t0)], dma=True)
        return epi

    def store_bf(dst):
        def epi(oc, t0, n, pp, ppk):
            o, ok = obr.next()
            if (oc + t0 // 512) % 2 == 0:
                P.add("act", lambda e: e.copy(o[:, 0:n], pp[:, 0:n]), reads=[ppk], writes=[ok])
            else:
                P.add("dve", lambda e: e.tensor_copy(o[:, 0:n], pp[:, 0:n]), reads=[ppk], writes=[ok])
            P.add(ST, lambda e: e.dma_start(out=dst(oc)[:, c0 + t0:c0 + t0 + n], in_=o[:, 0:n]), reads=[ok], writes=[("mla_o2", oc, c0 + t0)], dma=True)
        return epi

    krope = rope_epi(lambda oc: g.d_kr, lambda oc: oc == 8)

    def epi1(oc, t0, n, pp, ppk):
        if oc < 8:
            if oc % 2 == 0:
                P.add("act", lambda e: e.copy(cbuf[:, oc * nb + t0: oc * nb + t0 + n], pp[:, 0:n]), reads=[ppk], writes=[("cb", oc, t0)])
            else:
                P.add("dve", lambda e: e.tensor_copy(cbuf[:, oc * nb + t0: oc * nb + t0 + n], pp[:, 0:n]), reads=[ppk], writes=[("cb", oc, t0)])
        else:
            krope(oc, t0, n, pp, ppk)
    emit_linear(P, g, g.xn, DC, nb, g.d_wdqkv, 10, epi1)
    for grp in range(2):
        for (t0, n) in tgroups(nb, 256):
            ssp, ssk = g.psr.next()
            for q in range(4):
                oc = grp * 4 + q
                sq, sqk = g.sq.next()
                P.add("act", lambda e, sq=sq, oc=oc: e.activation(out=sq[:, 0:n], in_=cbuf[:, oc * nb + t0: oc * nb + t0 + n], func=AF.Square),
                      reads=[("cb", oc, (t0 // 512) * 512)], writes=[sqk])
                P.add("pe", lambda e, sq=sq, q=q: e.matmul(ssp[:, 0:n], lhsT=g.ones[:, :], rhs=sq[:, 0:n], start=(q == 0), stop=(q == 3)),
                      reads=[sqk, "ones"], writes=[ssk])
            rs, rsk = emit_rstd(P, g, ssp, ssk, n, 512)
            for q in range(4):
                oc = grp * 4 + q
                P.add("dve", lambda e, rs=rs, oc=oc: e.scalar_tensor_tensor(
                    out=cn[:, oc * nb + t0: oc * nb + t0 + n], in0=cbuf[:, oc * nb + t0: oc * nb + t0 + n],
                    scalar=g.mlan[:, oc:oc + 1], in1=rs[:, 0:n], op0=ALU.mult, op1=ALU.mult),
                    reads=[("cb", oc, (t0 // 512) * 512), rsk, "mlan"], writes=[("cn" if oc < 4 else "cn4", oc % 4, t0)])
    qrope = rope_epi(lambda oc: g.d_qr[oc // 3], lambda oc: oc % 3 == 1)
    qn_store = store_bf(lambda oc: g.d_qn[oc // 3])

    def epi3(oc, t0, n, pp, ppk):
        if oc % 3 == 0:
            qn_store(oc, t0, n, pp, ppk)
        else:
            qrope(oc, t0, n, pp, ppk)
    emit_linear(P, g, cn, 4, nb, g.d_wuq, 48, epi3, xpref="cn")
    kn_store = store_bf(lambda oc: g.d_kn[oc // 2])

    def epi4(oc, t0, n, pp, ppk):
        if oc % 2 == 0:
            kn_store(oc, t0, n, pp, ppk)
        else:
            hh = oc // 2
            v, vk = vbr.next()
            P.add("act", lambda e: e.copy(v[:, 0:n], pp[:, 0:n]), reads=[ppk], writes=[vk])
            for q in range(n // 128):
                P.add("pe", lambda e, q=q: e.transpose(pvb[:, q * 128:(q + 1) * 128], v[:, q * 128:(q + 1) * 128], g.identb[:, :]),
                      reads=[vk, "identb"], writes=["pv"])
            o, ok = obr.next()
            P.add("dve", lambda e: e.tensor_copy(o[:, 0:n], pvb[:, 0:n]), reads=["pv"], writes=[ok])
            tb0 = (c0 + t0) // 128
            P.add(ST, lambda e: e.dma_start(
                out=g.d_vt[hh].rearrange("(n p) v -> p n v", p=128)[:, tb0:tb0 + n // 128, :],
                in_=o[:, 0:n].rearrange("p (n v) -> p n v", v=128)), reads=[ok], writes=[("mla_v", oc, c0 + t0)], dma=True)
    emit_linear(P, g, cn[:, 4 * nb:8 * nb], 4, nb, g.d_wukv, 32, epi4, xpref="cn4")


def stage_mla_attn(P, g, last):
    NTI = L // 128
    Kr = carve(g, 0, L, BF16)
    o_ = L * 2
    Kn = [carve(g, o_ + i * L * 2, L, BF16) for i in range(2)]; o_ += 2 * L * 2
    Vt = [carve(g, o_ + i * L * 2, L, BF16) for i in range(2)]; o_ += 2 * L * 2
    Qn = [carve(g, o_ + i * L * 2, L, BF16) for i in range(2)]; o_ += 2 * L * 2
    Qr = [carve(g, o_ + i * L * 2, L, BF16) for i in range(2)]; o_ += 2 * L * 2
    pT = [carve(g, o_ + i * 1024, 512, BF16) for i in range(4)]; o_ += 4 * 1024
    rl = [carve(g, o_ + i * 2048, 512, F32) for i in range(2)]; o_ += 2 * 2048
    oo = [carve(g, o_ + i * 2048, 512, F32) for i in range(3)]; o_ += 3 * 2048
    onesb = carve(g, o_, 128, BF16); o_ += 256
    assert o_ <= ARENA * 2
    knr, vtr, qnr, qrr, ptr_, rlr, oor = Ring("Kn", Kn), Ring("Vt", Vt), Ring("Qn", Qn), Ring("Qr", Qr), Ring("pT", pT), Ring("rl", rl), Ring("oo", oo)
    psr_s, psr_o, psr_l = Ring("ps", g.ps[0:3]), Ring("ps3", g.ps[3:5]), Ring("ps5", g.ps[5:7])
    P.add("pool", lambda e: e.memset(onesb[:, :], 1.0), writes=["onesb"])
    P.add("sp", lambda e: e.dma_start(out=Kr[:, :], in_=g.d_kr), writes=["Kr"], dma=True)
    qblocks = [(NCTX + 512 * i, 512, 0, NTI) for i in range(NLAT // 512)]
    if not last:
        qblocks.append((0, NCTX, 0, NCTX // 128))
    for h in range(16):
        kn, knk = knr.next()
        vt, vtk = vtr.next()
        qn, qnk = qnr.next()
        qr, qrk = qrr.next()
        P.add("sp", lambda e, kn=kn, h=h: e.dma_start(out=kn[:, :], in_=g.d_kn[h]), writes=[knk], dma=True)
        P.add("sp", lambda e, vt=vt, h=h: e.dma_start(out=vt[:, :].rearrange("p (n v) -> p n v", v=128),
                                                     in_=g.d_vt[h].rearrange("(n p) v -> p n v", p=128)), writes=[vtk], dma=True)
        P.add("sp", lambda e, qn=qn, h=h: e.dma_start(out=qn[:, :], in_=g.d_qn[h]), writes=[qnk], dma=True)
        P.add("sp", lambda e, qr=qr, h=h: e.dma_start(out=qr[:, :], in_=g.d_qr[h]), writes=[qrk], dma=True)
        for (q0, nq, k0, k1) in qblocks:
            po, pok = psr_o.next()
            pl, plk = psr_l.next()
            def scores(kt):
                ks = slice(kt * 128, (kt + 1) * 128)
                ps_, psk = psr_s.next()
                P.add("pe", lambda e: e.matmul(
                    ps_[:, 0:nq], lhsT=kn[:, ks], rhs=qn[:, q0:q0 + nq], start=True, stop=False), reads=[knk, qnk], writes=[psk])
                P.add("pe", lambda e: e.matmul(
                    ps_[:, 0:nq], lhsT=Kr[:, ks], rhs=qr[:, q0:q0 + nq], start=False, stop=True), reads=["Kr", qrk], writes=[psk])
                return ps_, psk

            nxt = scores(k0)
            for kt in range(k0, k1):
                ks = slice(kt * 128, (kt + 1) * 128)
                ps_, psk = nxt
                if kt + 1 < k1:
                    nxt = scores(kt + 1)
                p_, pk = ptr_.next()
                P.add("act", lambda e, p_=p_, ps_=ps_, nq=nq: e.activation(out=p_[:, 0:nq], in_=ps_[:, 0:nq], func=AF.Exp, scale=MLA_SCALE),
                      reads=[psk], writes=[pk])
                P.add("pe", lambda e, p_=p_, po=po, vt=vt, ks=ks, nq=nq, kt=kt, k0=k0, k1=k1: e.matmul(
                    po[:, 0:nq], lhsT=vt[:, ks], rhs=p_[:, 0:nq], start=(kt == k0), stop=(kt == k1 - 1)), reads=[pk, vtk], writes=[pok])
                P.add("pe", lambda e, p_=p_, pl=pl, nq=nq, kt=kt, k0=k0, k1=k1: e.matmul(
                    pl[:, 0:nq], lhsT=onesb[:, :], rhs=p_[:, 0:nq], start=(kt == k0), stop=(kt == k1 - 1)), reads=[pk, "onesb"], writes=[plk])
            r_, rk = rlr.next()
            P.add("dve", lambda e, r_=r_, pl=pl, nq=nq: e.reciprocal(r_[:, 0:nq], pl[:, 0:nq]), reads=[plk], writes=[rk])
            o, ok = oor.next()
            P.add("dve", lambda e, o=o, po=po, r_=r_, nq=nq: e.tensor_tensor(o[:, 0:nq], po[:, 0:nq], r_[:, 0:nq], ALU.mult),
                  reads=[pok, rk], writes=[ok])
            P.add(ST, lambda e, o=o, h=h, q0=q0, nq=nq: e.dma_start(out=g.d_of[h * 128:(h + 1) * 128, q0:q0 + nq], in_=o[:, 0:nq]),
                  reads=[ok], writes=[("ao", h, q0)], dma=True)
    P.barrier()


def stage_outproj(P, g, l, src, wsrc, skip_ctx):
    lin_views(g)
    for (c0, nb, kind) in BLOCKS:
        if kind == 1 and skip_ctx:
            continue
        outproj_block(P, g, l, src, wsrc, c0, nb, kind)
    P.barrier()


def outproj_block(P, g, l, src, wsrc, c0, nb, kind):
    gtcol = lambda oc: modcol(g, l, kind, 1, 2, oc)
    sv = src.rearrange("(c p) t -> p c t", p=128)
    for (t0, n) in tgroups(nb, 256):
        s_, k_ = g.stg.next()
        P.add("sp", lambda e, s_=s_, t0=t0, n=n: e.dma_start(
            out=s_[:, 0:DC * n].rearrange("p (c t) -> p c t", t=n), in_=sv[:, :, c0 + t0:c0 + t0 + n]), writes=[k_], dma=True)
        for dc in range(DC):
            cast_op(P, g, dc, g.xn[:, dc * nb + t0: dc * nb + t0 + n], s_[:, dc * n:(dc + 1) * n], [k_], [("xn", dc, t0)])
    emit_linear(P, g, g.xn, DC, nb, wsrc, DC, epi_residual(P, g, g.d_h, c0, gtcol, ["modc"]))


def stage_fnet_a(P, g, l):
    gsrc = g.gcols[:, (l * 3 + 1) * 16:(l * 3 + 2) * 16]
    dst32 = carve(g, 32768, 1024, F32)
    dftc = carve(g, 32768 + 4096, 1024, BF16)
    P.add("sp", lambda e: e.dma_start(out=dst32[:, :], in_=g.d_dft256), writes=["dft32"], dma=True)
    P.add("act", lambda e: e.copy(dftc[:, :], dst32[:, :]), reads=["dft32"], writes=["dftc"])
    for (c0, nb, kind) in BLOCKS:
        fnet_a_block(P, g, l, gsrc, dftc, c0, nb, kind)
    P.barrier()


def fnet_a_block(P, g, l, gsrc, dftc, c0, nb, kind):
    g.xn = g.arena[:, 0:DC * TB]
    stage = [carve(g, 40960 + i * 16384, 4096, F32) for i in range(3)]
    xo = [carve(g, 90112 + i * 1024, 512, BF16) for i in range(4)]
    g.stg = Ring("stage", stage)
    g.psr = Ring("ps", g.ps)
    xor_ = Ring("xo", xo)
    emit_cols(P, g, l, 1, kind, gsrc, False)
    gcol = lambda dc: g.dcol[:, dc:dc + 1]
    shcol = lambda dc: modcol(g, l, kind, 1, 0, dc)
    emit_adanorm(P, g, g.d_h, c0, nb, gcol, shcol, ["dcol", "modc"])
    for tt in range(nb // 128):
        for gq in range(8):
            pp, ppk = g.psr.next()
            for cs_ in range(2):
                for kk in range(2):
                    dc = gq * 2 + kk
                    P.add("pe", lambda e: e.matmul(
                        pp[:, cs_ * 256:(cs_ + 1) * 256], lhsT=g.xn[:, dc * nb + tt * 128: dc * nb + (tt + 1) * 128],
                        rhs=dftc[:, kk * 512 + cs_ * 256: kk * 512 + (cs_ + 1) * 256], start=(kk == 0), stop=(kk == 1)),
                        reads=xkeys("xn", dc, tt * 128, 128) + ["dftc"], writes=[ppk])
            o, ok = xor_.next()
            if gq % 2 == 0:
                P.add("act", lambda e: e.copy(o[:, :], pp[:, :]), reads=[ppk], writes=[ok])
            else:
                P.add("dve", lambda e: e.tensor_copy(o[:, :], pp[:, :]), reads=[ppk], writes=[ok])
            r0 = c0 + tt * 128
            P.add(ST, lambda e: e.dma_start(out=g.d_xc[r0:r0 + 128, gq * 512:(gq + 1) * 512], in_=o[:, :]),
                  reads=[ok], writes=[("xc", r0, gq)], dma=True)


def stage_fnet_b(P, g):
    TC = NLAT // 128
    tabs = [carve(g, i * 32768, TC * 512, BF16) for i in range(2)]
    xt = [[carve(g, 65536 + (i * 2 + a) * 8192, TC * 128, BF16) for a in range(2)] for i in range(2)]
    oo = [carve(g, 98304 + i * 2048, 512, F32) for i in range(3)]
    ctab = carve(g, 104448, 2 * 2 * 256, BF16)
    xtr, oor = Ring("xt", xt), Ring("oo", oo)
    g.psr = Ring("ps", g.ps)
    xcl = g.d_xc[NCTX:L, :].rearrange("(n p) c -> p n c", p=128)
    xcc = g.d_xc[0:NCTX, :].rearrange("(n p) c -> p n c", p=128)
    for tb in range(NLAT // 512):
        for a in range(2):
            P.add("sp", lambda e: e.dma_start(
                out=tabs[a][:, :].rearrange("p (n t) -> p n t", t=512),
                in_=g.d_dftT[a].rearrange("(n p) t -> p n t", p=128)[:, :, tb * 512:(tb + 1) * 512]), writes=[("tab", a)], dma=True)
        for cc in range(16):
            x2, xk = xtr.next()
            for a in range(2):
                col = (cc // 2) * 512 + a * 256 + (cc % 2) * 128
                P.add("sp", lambda e: e.dma_start(out=x2[a][:, :].rearrange("p (n c) -> p n c", c=128), in_=xcl[:, :, col:col + 128]),
                      writes=[(xk, a)], dma=True)
            pp, ppk = g.psr.next()
            for a in range(2):
                for tc in range(TC):
                    P.add("pe", lambda e: e.matmul(pp[:, :], lhsT=x2[a][:, tc * 128:(tc + 1) * 128], rhs=tabs[a][:, tc * 512:(tc + 1) * 512],
                                                   start=(a == 0 and tc == 0), stop=(a == 1 and tc == TC - 1)),
                          reads=[(xk, a), ("tab", a)], writes=[ppk])
            o, ok = oor.next()
            if cc % 2 == 0:
                P.add("act", lambda e: e.copy(o[:, :], pp[:, :]), reads=[ppk], writes=[ok])
            else:
                P.add("dve", lambda e: e.tensor_copy(o[:, :], pp[:, :]), reads=[ppk], writes=[ok])
            P.add(ST, lambda e: e.dma_start(out=g.d_of[cc * 128:(cc + 1) * 128, NCTX + tb * 512: NCTX + (tb + 1) * 512], in_=o[:, :]),
                  reads=[ok], writes=[("fo", cc, tb)], dma=True)
    P.add("sp", lambda e: e.dma_start(out=ctab[:, :].rearrange("p (a n t) -> p a n t", a=2, t=256),
                                      in_=g.d_dftC.rearrange("a (n p) t -> p a n t", p=128)), writes=["ctab"], dma=True)
    for cc in range(16):
        x2, xk = xtr.next()
        for a in range(2):
            col = (cc // 2) * 512 + a * 256 + (cc % 2) * 128
            P.add("sp", lambda e: e.dma_start(out=x2[a][:, 0:256].rearrange("p (n c) -> p n c", c=128), in_=xcc[:, :, col:col + 128]),
                  writes=[(xk, a)], dma=True)
        pp, ppk = g.psr.next()
        for a in range(2):
            for tc in range(2):
                P.add("pe", lambda e: e.matmul(pp[:, 0:256], lhsT=x2[a][:, tc * 128:(tc + 1) * 128],
                                               rhs=ctab[:, (a * 2 + tc) * 256:(a * 2 + tc + 1) * 256],
                                               start=(a == 0 and tc == 0), stop=(a == 1 and tc == 1)),
                      reads=[(xk, a), "ctab"], writes=[ppk])
        o, ok = oor.next()
        P.add("act", lambda e: e.copy(o[:, 0:256], pp[:, 0:256]), reads=[ppk], writes=[ok])
        P.add(ST, lambda e: e.dma_start(out=g.d_of[cc * 128:(cc + 1) * 128, 0:NCTX], in_=o[:, 0:256]),
              reads=[ok], writes=[("fo", cc, -1)], dma=True)
    P.barrier()


def stage_final(P, g):
    for (c0, nb, kind) in BLOCKS:
        if kind == 1:
            continue
        final_block(P, g, c0, nb)


def final_block(P, g, c0, nb):
    if True:
        xTv = g.d_h.rearrange("(c p) t -> p c t", p=128)
        for (t0, n) in tgroups(nb, 256):
            xs, xk = g.stg.next()
            P.add("sp", lambda e, xs=xs, t0=t0, n=n: e.dma_start(
                out=xs[:, 0:DC * n].rearrange("p (c t) -> p c t", t=n), in_=xTv[:, :, c0 + t0:c0 + t0 + n]),
                reads=[("h", c0 + t0)], writes=[xk], dma=True)
            ssp, ssk = g.psr.next()
            for dc in range(DC):
                sq, sqk = g.sq.next()
                P.add("act", lambda e, sq=sq, xs=xs, dc=dc, n=n: e.activation(
                    out=sq[:, 0:n], in_=xs[:, dc * n:(dc + 1) * n], func=AF.Square), reads=[xk], writes=[sqk])
                P.add("pe", lambda e, sq=sq, ssp=ssp, dc=dc, n=n: e.matmul(
                    ssp[:, 0:n], lhsT=g.ones[:, :], rhs=sq[:, 0:n], start=(dc == 0), stop=(dc == DC - 1)),
                    reads=[sqk, "ones"], writes=[ssk])
            rs, rsk = emit_rstd(P, g, ssp, ssk, n, D)
            for dc in range(DC):
                P.add("dve", lambda e, xs=xs, rs=rs, dc=dc, n=n: e.scalar_tensor_tensor(
                    out=xs[:, dc * n:(dc + 1) * n], in0=xs[:, dc * n:(dc + 1) * n], scalar=g.fgcol[:, dc:dc + 1],
                    in1=rs[:, 0:n], op0=ALU.mult, op1=ALU.mult), reads=[xk, rsk, "gcols"], writes=[xk])
            ov = g.d_out.rearrange("(c p) t -> p c t", p=128)
            P.add(ST, lambda e, xs=xs, t0=t0, n=n, ov=ov: e.dma_start(
                out=ov[:, :, c0 - NCTX + t0: c0 - NCTX + t0 + n], in_=xs[:, 0:DC * n].rearrange("p (c t) -> p c t", t=n)),
                reads=[xk], dma=True)


def build(plan):
    nc = bass.Bass("TRN2", target_bir_lowering=False)
    g = G()
    kinds = set(p if isinstance(p, str) else p[0] for p in plan)
    fam = {"wgu1": "ffn", "wdn1": "ffn", "wgu2": "ffn", "wdn2": "ffn", "hgw_in": "hgrn", "hgw_out": "hgrn",
           "wdqkv": "mla", "wuq": "mla", "wukv": "mla", "wo": "mla", "rope": "mla", "fw": "fnet", "dft256": "fnet",
           "dftT": "fnet", "dftC": "fnet", "mod_w": "mod"}

    def di(name, shape, dt=F32):
        f = fam.get(name)
        if f is not None and not any(k.startswith(f) for k in kinds):
            return None
        return nc.dram_tensor(name, shape, dt, kind="ExternalInput").ap()
    g.d_x = di("x", [D, L])
    g.d_cT = di("cT", [128, 32])
    g.d_modw = di("mod_w", [4, D, 18432])
    g.d_modb = di("mod_b", [128, 4 * NMODC])
    g.d_gcols = di("gcols", [128, 13 * 16])
    g.d_wgu1 = di("wgu1", [4, FC, 128, 4096])
    g.d_wdn1 = di("wdn1", [4, DC, 128, DFF])
    g.d_wgu2 = di("wgu2", [4, FC, 128, 4096])
    g.d_wdn2 = di("wdn2", [4, DC, 128, DFF])
    g.d_hgw_in = di("hgw_in", [2, 80, 128, D])
    g.d_hgw_out = di("hgw_out", [2, DC, 128, D])
    g.d_hgn = di("hgn", [128, 32])
    g.d_lbl = di("lbl", [128, 64])
    g.d_cst = di("cst", [128, 516])
    g.d_wdqkv = di("wdqkv", [10, 128, D])
    g.d_wuq = di("wuq", [48, 128, 512])
    g.d_wukv = di("wukv", [32, 128, 512])
    g.d_wo = di("wo", [DC, 128, D])
    g.d_mlan = di("mlan", [128, 8])
    g.d_rope = di("rope", [128, 2 * L])
    g.d_fw = di("fw", [DC, 128, D])
    g.d_dft256 = di("dft256", [128, 1024])
    g.d_dftT = di("dftT", [2, NLAT, NLAT], BF16)
    g.d_dftC = di("dftC", [2, NCTX, NCTX], BF16)
    g.d_xc = nc.dram_tensor("xc_scr", [L, 4096], BF16).ap()
    g.d_kr = nc.dram_tensor("kr_scr", [128, L], BF16).ap()
    g.d_qn = nc.dram_tensor("qn_scr", [16, 128, L], BF16).ap()
    g.d_qr = nc.dram_tensor("qr_scr", [16, 128, L], BF16).ap()
    g.d_kn = nc.dram_tensor("kn_scr", [16, 128, L], BF16).ap()
    g.d_vt = nc.dram_tensor("vt_scr", [16, L, 128], BF16).ap()
    g.d_out = nc.dram_tensor("out", [D, NLAT], F32, kind="ExternalOutput").ap()
    g.d_h = nc.dram_tensor("h_scr", [D, L], F32).ap()
    g.d_pj = nc.dram_tensor("pj_scr", [10240, L], F32).ap()
    g.d_of = nc.dram_tensor("of_scr", [D, L], F32).ap()
    g.d_ob = nc.dram_tensor("ob_scr", [D, L], F32).ap()
    with contextlib.ExitStack() as st:
        sb = lambda name, shape, dt: st.enter_context(nc.sbuf_tensor(name, shape, dt))
        g.arena = sb("arena", [128, ARENA], BF16)
        g.xn = g.arena[:, 0:DC * TB]
        g.act = g.arena[:, DC * TB:(DC + FC) * TB]
        o_ = (DC + FC) * TB
        stage = [g.arena[:, o_ + i * 8192: o_ + (i + 1) * 8192].bitcast(F32) for i in range(3)]
        o_ += 3 * 8192
        wbf = [g.arena[:, o_ + i * 4096: o_ + (i + 1) * 4096] for i in range(2)]
        g.modc = sb("modc", [128, 4 * NMODC * 2], F32)
        g.gcols = sb("gcols_sb", [128, 13 * 16], F32)
        g.fgcol = g.gcols[:, 12 * 16:13 * 16]
        g.dcol = sb("dcol", [128, 32], F32)
        g.hgn = sb("hgn_sb", [128, 32], F32)
        g.mlan = sb("mlan_sb", [128, 8], F32)
        g.lbl = sb("lbl_sb", [128, 64], F32)
        g.lbc = sb("lbc", [128, 192], F32)
        g.cst = sb("cst_sb", [128, 516], F32)
        g.identb = sb("identb", [128, 128], BF16)
        g.cs = sb("cs", [128, 32], F32)
        g.ones = sb("ones", [128, 128], F32)
        sq = [sb("sq%d" % i, [128, 256], F32) for i in range(2)]
        rstd = [sb("rstd%d" % i, [128, 256], F32) for i in range(2)]
        tmp = [sb("tmp%d" % i, [128, 256], F32) for i in range(2)]
        sg = [sb("sg%d" % i, [128, 512], F32) for i in range(2)]
        ost = [sb("ost%d" % i, [128, 512], F32) for i in range(2)]
        g.ps = [st.enter_context(nc.psum_tensor("ps%d" % i, [128, 512], F32)) for i in range(8)]
        g.psr = Ring("ps", g.ps)
        g.stg, g.wbr = Ring("stage", stage), Ring("wbf", wbf)
        g.sq, g.rstd, g.tmp, g.sg, g.ost = Ring("sq", sq), Ring("rstd", rstd), Ring("tmp", tmp), Ring("sg", sg), Ring("ost", ost)
        P = Prog(nc)
        P.add("pool", lambda e: e.memset(g.ones[:, :], 1.0), writes=["ones"])
        P.add("sp", lambda e: e.dma_start(out=g.gcols[:, :], in_=g.d_gcols), writes=["gcols"], dma=True)
        P.add("sp", lambda e: e.dma_start(out=g.hgn[:, :], in_=g.d_hgn), writes=["hgn"], dma=True)
        P.add("sp", lambda e: e.dma_start(out=g.mlan[:, :], in_=g.d_mlan), writes=["mlan"], dma=True)
        stage_lbcols(P, g)
        for i in range(DC):
            P.add("sp", lambda e, i=i: e.dma_start(out=g.d_h[i * 128:(i + 1) * 128, :], in_=g.d_x[i * 128:(i + 1) * 128, :]),
                  writes=[("hinit", i)], dma=True)
        P.barrier()
        for stg_ in plan:
            if stg_ == "mod":
                stage_mod(P, g)
                P.barrier()
            elif stg_[0] == "ffn":
                ffn_views(g)
                stage_ffn(P, g, stg_[1], stg_[2], skip_ctx=(len(stg_) > 3 and stg_[3]))
            elif stg_[0] == "fnet":
                l_ = stg_[1]
                stage_fnet_a(P, g, l_)
                stage_fnet_b(P, g)
                stage_outproj(P, g, l_, g.d_of, g.d_fw, False)
            elif stg_[0] == "mla":
                l_, last_ = stg_[1], stg_[2]
                stage_mla_proj(P, g, l_)
                stage_mla_attn(P, g, last_)
                stage_outproj(P, g, l_, g.d_of, g.d_wo, last_)
            elif stg_[0] == "hgrn_in":
                stage_hgrn_inproj(P, g, stg_[1], stg_[2])
            elif stg_[0] == "hgrn_scan":
                stage_hgrn_scan(P, g, stg_[1], stg_[2])
            elif stg_[0] == "hgrn_out":
                stage_hgrn_readout(P, g, stg_[1], stg_[2], stg_[3])
            elif stg_[0] == "dumpcols":
                P.barrier()
                for i in range(DC):
                    P.add("sp", lambda e, i=i, a=stg_[1], n=stg_[2], o=stg_[3]: e.dma_start(
                        out=g.d_out[i * 128:(i + 1) * 128, o:o + n], in_=g.d_h[i * 128:(i + 1) * 128, a:a + n]), dma=True)
                P.barrier()
                g.dumped = True
            elif stg_[0] == "dump":
                src_ = getattr(g, stg_[1])
                P.barrier()
                for i in range(stg_[3] // 128):
                    P.add("sp", lambda e, i=i, src_=src_, r0=stg_[2], o0=stg_[4]: e.dma_start(
                        out=g.d_out[o0 + i * 128: o0 + (i + 1) * 128, :], in_=src_[r0 + i * 128: r0 + (i + 1) * 128, NCTX:L]), dma=True)
                g.dumped = True
            elif stg_[0] == "hgrn":
                l_, j_, last_ = stg_[1], stg_[2], stg_[3]
                stage_hgrn_inproj(P, g, l_, j_)
                stage_hgrn_scan(P, g, j_)
                stage_hgrn_readout(P, g, l_, j_, last_)
            elif stg_ == "final":
                stage_final(P, g)
            elif stg_ == "dbg_modc":
                P.barrier()
                P.add(ST, lambda e: e.dma_start(out=g.d_out[0:128, 0:4 * NMODC * 2], in_=g.modc[:, :]), dma=True)
                P.emit()
                g.stats = P.stats
                return nc, g
        if "final" not in plan and not getattr(g, "dumped", False):
            P.barrier()
            for i in range(DC):
                P.add("sp", lambda e, i=i: e.dma_start(out=g.d_out[i * 128:(i + 1) * 128, :], in_=g.d_h[i * 128:(i + 1) * 128, NCTX:L]),
                      dma=True)
        P.emit()
        g.stats = P.stats
    return nc, g


def tile_w(W):
    K, N = W.shape
    return np.ascontiguousarray(W.reshape(K // 128, 128, N // 128, 128).transpose(2, 1, 0, 3))


def col16(v):
    return np.ascontiguousarray(v.reshape(-1, 128).T)


def prep_ffn_w(w_gu, w_dn):
    tg = tile_w(w_gu)
    wgu = np.ascontiguousarray(np.stack([tg[:FC], tg[FC:]], axis=2).reshape(FC, 128, 4096))
    wdn = np.ascontiguousarray(tile_w(w_dn).reshape(DC, 128, DFF))
    return wgu, wdn


def prep_inputs(inp, nlayers=4):
    shared = {}
    shared["mod_w"] = np.ascontiguousarray(inp["mod_w"])
    shared["mod_b"] = np.ascontiguousarray(np.concatenate([col16(inp["mod_b"][l]) for l in range(4)], axis=1))
    shared["gcols"] = np.ascontiguousarray(np.concatenate(
        [col16(inp["norm_g"][l, s]) for l in range(4) for s in range(3)] + [col16(inp["final_g"])], axis=1))
    for nm, gu, dn in (("1", "ffn1_w_gu", "ffn1_w_down"), ("2", "ffn2_w_gu", "ffn2_w_down")):
        a, b = zip(*[prep_ffn_w(inp[gu][l], inp[dn][l]) for l in range(4)])
        shared["wgu" + nm] = np.stack(a)
        shared["wdn" + nm] = np.stack(b)
    shared["hgw_in"] = np.stack([tile_w(inp["hgrn_w_in"][j]).reshape(80, 128, D) for j in range(2)])
    shared["hgw_out"] = np.stack([tile_w(inp["hgrn_w_out"][j]).reshape(DC, 128, D) for j in range(2)])
    shared["hgn"] = np.ascontiguousarray(np.concatenate([col16(inp["hgrn_g_norm"][j]) for j in range(2)], axis=1))
    shared["lbl"] = np.ascontiguousarray(np.concatenate(
        [col16(inp["hgrn_lb_logits"][d_, j]) for d_ in range(2) for j in range(2)], axis=1))
    shared["cst"] = make_consts()
    pidx = np.array([a * 32 + (1 - hf) * 16 + f for a in range(2) for hf in range(2) for f in range(16)])
    zpad = lambda w: np.concatenate([w, np.zeros((w.shape[0], 64), np.float32)], axis=1)
    wd = inp["mla_w_dqkv"][0]
    wd_ext = np.concatenate([wd[:, :1024], zpad(wd[:, 1024:1088]), zpad(wd[:, 1024:1088][:, pidx])], axis=1)
    shared["wdqkv"] = tile_w(wd_ext).reshape(10, 128, D)
    wq = inp["mla_w_uq"][0].reshape(512, 16, 192)
    wq_ext = np.concatenate([np.concatenate([wq[:, hh, :128], zpad(wq[:, hh, 128:]), zpad(wq[:, hh, 128:][:, pidx])], axis=1)
                             for hh in range(16)], axis=1)
    shared["wuq"] = tile_w(wq_ext).reshape(48, 128, 512)
    shared["wukv"] = tile_w(inp["mla_w_ukv"][0]).reshape(32, 128, 512)
    shared["wo"] = tile_w(inp["mla_w_o"][0]).reshape(DC, 128, D)
    shared["mlan"] = np.ascontiguousarray(np.concatenate([col16(inp["mla_q_norm"][0]), col16(inp["mla_kv_norm"][0])], axis=1))
    shared["rope"] = make_rope()
    shared["fw"] = tile_w(inp["fnet_w_out"][0]).reshape(DC, 128, D)
    shared.update(make_dft())
    maps = []
    for b in range(2):
        m = dict(shared)
        m["x"] = np.ascontiguousarray(np.concatenate([inp["ctx"][b], inp["x"][b]], axis=0).T)
        cvec = np.stack([inp["c"][b], inp["c_ctx"]])
        m["cT"] = np.ascontiguousarray(cvec.reshape(2, 16, 128).transpose(2, 1, 0).reshape(128, 32))
        maps.append(m)
    return maps


def make_consts():
    c = np.zeros((128, 516), np.float32)
    idx = np.arange(128)
    c[:, 0:128] = np.eye(128, dtype=np.float32)
    same = (idx[:, None] // 32) == (idx[None, :] // 32)
    c[:, 128:256] = (same & (idx[:, None] <= idx[None, :])).astype(np.float32)
    c[:, 256:384] = (same & (idx[:, None] >= idx[None, :])).astype(np.float32)
    c[:, 384:512] = (np.arange(128) % 32 != 0).astype(np.float32)[None, :]
    for k in range(4):
        c[:, 512 + k] = (idx // 32 == k).astype(np.float32)
    return c


def make_rope():
    t = np.arange(NLAT)
    pos = np.stack([(t // 64).astype(np.float32), (t % 64).astype(np.float32)], axis=-1)
    inv_freq = (np.float32(10000.0) ** (-np.arange(16, dtype=np.float32) / np.float32(16))).astype(np.float32)
    ang = (pos[:, :, None] * inv_freq[None, None, :]).astype(np.float32)
    cos, sin = np.cos(ang).astype(np.float32), np.sin(ang).astype(np.float32)
    tab = np.zeros((128, 2, L), np.float32)
    tab[:, 0, :] = 1.0
    for a in range(2):
        for hf in range(2):
            rows = slice(a * 32 + hf * 16, a * 32 + hf * 16 + 16)
            tab[rows, 0, NCTX:] = cos[:, a, :].T
            tab[rows, 1, NCTX:] = (-sin[:, a, :].T) if hf == 0 else sin[:, a, :].T
    return np.ascontiguousarray(tab.reshape(128, 2 * L))


def make_dft():
    import ml_dtypes
    c = np.arange(256)
    ang = 2.0 * np.pi * ((c[:, None] * c[None, :]) % 256) / 256.0
    t256 = np.zeros((128, 2, 2, 256), np.float32)
    for kk in range(2):
        t256[:, kk, 0, :] = np.cos(ang[kk * 128:(kk + 1) * 128])
        t256[:, kk, 1, :] = np.sin(ang[kk * 128:(kk + 1) * 128])
    t = np.arange(NLAT)
    angT = 2.0 * np.pi * ((t[:, None] * t[None, :]) % NLAT) / float(NLAT)
    dftT = np.stack([np.cos(angT) / 1024.0, -np.sin(angT) / 1024.0]).astype(np.float32).astype(ml_dtypes.bfloat16)
    tc_ = np.arange(NCTX)
    angC = 2.0 * np.pi * ((tc_[:, None] * tc_[None, :]) % NCTX) / float(NCTX)
    dftC = np.stack([np.cos(angC) / 256.0, -np.sin(angC) / 256.0]).astype(np.float32).astype(ml_dtypes.bfloat16)
    return {"dft256": np.ascontiguousarray(t256.reshape(128, 1024)), "dftT": np.ascontiguousarray(dftT), "dftC": np.ascontiguousarray(dftC)}


PLAN = ['mod',
        ('ffn', 0, 0), ('hgrn', 0, 0, False), ('ffn', 0, 1),
        ('ffn', 1, 0), ('mla', 1, False), ('ffn', 1, 1),
        ('ffn', 2, 0), ('fnet', 2), ('ffn', 2, 1),
        ('ffn', 3, 0), ('hgrn', 3, 1, True), ('ffn', 3, 1, True),
        'final']


def kernel(**inputs):
    inp = {k: np.asarray(v) for k, v in inputs.items()}
    maps = prep_inputs(inp)
    nc, g = build(PLAN)
    need = [a.memorylocations[0].name for a in nc.allocations
            if getattr(a, "kind", None) == "ExternalInput"]
    in_maps = [{k: m[k] for k in need if k in m} for m in maps]
    res = run_bass_kernel_spmd(nc, in_maps, core_ids=[0, 1])
    out = np.stack([np.ascontiguousarray(res.results[b]["out"].T) for b in range(2)])
    return out.astype(np.float32)
```

```python
import contextlib
import types
import numpy as np
import concourse.bass as bass
import concourse.mybir as mybir
from concourse.bass_utils import run_bass_kernel_spmd

F32 = mybir.dt.float32
BF16 = mybir.dt.bfloat16
AF = mybir.ActivationFunctionType
ALU = mybir.AluOpType

ENGS = ("pe", "act", "dve", "pool", "sp")
ST = "pool"
NDMASEM = 6

D = 2048
DC = 16
DFF = 5632
FC = 44
EPS = 1e-6
NCTX = 256
NLAT = 4096
L = NCTX + NLAT
TB = 1024
BLOCKS = [(0, NCTX, 1)] + [(NCTX + TB * i, TB, 0) for i in range(NLAT // TB)]
NMODC = 144
ARENA = (DC + FC) * TB + 3 * 8192 + 2 * 4096


def freeze(fn):
    if fn.__closure__ is None:
        return fn
    cells = []
    for c in fn.__closure__:
        try:
            cells.append(types.CellType(c.cell_contents))
        except ValueError:
            cells.append(c)
    return types.FunctionType(fn.__code__, fn.__globals__, fn.__name__, fn.__defaults__, tuple(cells))


class Prog:
    def __init__(self, nc):
        self.nc = nc
        self.ops = []
        self.last_w = {}
        self.readers = {}
        self.last_eng = {}
        self.dmas_since = []

    def add(self, eng, fn, reads=(), writes=(), dma=False, extra_deps=()):
        idx = len(self.ops)
        deps = set(extra_deps)
        for k in reads:
            w = self.last_w.get(k)
            if w is not None:
                deps.add(w)
        for k in writes:
            w = self.last_w.get(k)
            if w is not None:
                deps.add(w)
            for r in self.readers.get(k, ()):
                deps.add(r)
        for k in reads:
            lst = self.readers.setdefault(k, [])
            if not dma:
                lst[:] = [r for r in lst if self.ops[r]["dma"] or self.ops[r]["eng"] != eng]
            lst.append(idx)
        for k in writes:
            self.last_w[k] = idx
            self.readers[k] = []
        deps.discard(idx)
        self.ops.append(dict(eng=eng, fn=freeze(fn), deps=deps, dma=dma, sig=False))
        self.last_eng[eng] = idx
        if dma:
            self.dmas_since.append(idx)
        return idx

    def barrier(self):
        deps = set(self.last_eng.values()) | set(self.dmas_since)
        self.dmas_since = []
        for e in ENGS:
            self.add(e, lambda eng: eng.nop(), extra_deps=deps)
        self.last_w = {}
        self.readers = {}

    def emit(self):
        nc = self.nc
        ops = self.ops
        for i, o in enumerate(ops):
            nd = set()
            for d in o["deps"]:
                p = ops[d]
                if p["eng"] == o["eng"] and o["eng"] == "pe" and not p["dma"]:
                    continue
                nd.add(d)
                p["sig"] = True
            o["deps"] = nd
        with contextlib.ExitStack() as st:
            esem = {e: st.enter_context(nc.semaphore("s_" + e)) for e in ENGS}
            dsem = {e: [st.enter_context(nc.semaphore("d_%s%d" % (e, j))) for j in range(NDMASEM)]
                    for e in ("sp", "act", "pool")}
            ecount = {e: 0 for e in ENGS}
            dcount = {e: 0 for e in dsem}
            dtarget = {e: [0] * NDMASEM for e in dsem}
            per_eng = {e: [] for e in ENGS}
            for i, o in enumerate(ops):
                e = o["eng"]
                if o["dma"]:
                    j = dcount[e] % NDMASEM
                    dcount[e] += 1
                    o["prev_target"] = dtarget[e][j]
                    dtarget[e][j] += 16
                    o["sem"] = ("d", e, j)
                    o["target"] = dtarget[e][j]
                elif o["sig"]:
                    ecount[e] += 1
                    o["sem"] = ("e", e, 0)
                    o["target"] = ecount[e]
                per_eng[e].append(i)
            self.stats = dict(ecount=dict(ecount), dcount=dict(dcount), nops={e: len(per_eng[e]) for e in ENGS})

            def semof(s):
                return esem[s[1]] if s[0] == "e" else dsem[s[1]][s[2]]

            block = st.enter_context(nc.Block())

            def body(e, eng):
                seen = {}
                for i in per_eng[e]:
                    o = ops[i]
                    waits = {}
                    for d in o["deps"]:
                        p = ops[d]
                        s = p["sem"]
                        waits[s] = max(waits.get(s, 0), p["target"])
                    if o["dma"] and o["prev_target"] > 0:
                        s = o["sem"]
                        waits[s] = max(waits.get(s, 0), o["prev_target"])
                    for s, v in waits.items():
                        if seen.get(s, 0) >= v:
                            continue
                        seen[s] = v
                        eng.wait_ge(semof(s), v)
                    ins = o["fn"](eng)
                    if o["dma"]:
                        ins.then_inc(semof(o["sem"]), 16)
                    elif o["sig"]:
                        ins.then_inc(semof(o["sem"]), 1)
                if e in dsem:
                    for j in range(NDMASEM):
                        if dtarget[e][j] > 0 and seen.get(("d", e, j), 0) < dtarget[e][j]:
                            eng.wait_ge(dsem[e][j], dtarget[e][j])

            if per_eng["sp"]:
                block.sync(lambda eng: body("sp", eng))
            if per_eng["act"]:
                block.scalar(lambda eng: body("act", eng))
            if per_eng["dve"]:
                block.vector(lambda eng: body("dve", eng))
            if per_eng["pool"]:
                block.gpsimd(lambda eng: body("pool", eng))
            if per_eng["pe"]:
                block.tensor(lambda eng: body("pe", eng))


class Ring:
    def __init__(self, name, aps):
        self.name, self.aps, self.i = name, aps, 0

    def next(self):
        j = self.i % len(self.aps)
        self.i += 1
        return self.aps[j], (self.name, j)


def tgroups(n, step):
    return [(t, min(step, n - t)) for t in range(0, n, step)]


def xkeys(pref, dc, t0, n):
    return [(pref, dc, t) for t in range((t0 // 256) * 256, t0 + n, 256)]


class G:
    pass


def cast_op(P, g, i, dst, src, reads, writes):
    if i % 2 == 0:
        P.add("act", lambda e: e.copy(dst, src), reads=reads, writes=writes)
    else:
        P.add("dve", lambda e: e.tensor_copy(dst, src), reads=reads, writes=writes)


def emit_rstd(P, g, ssp, ssk, n, dim):
    rs, rsk = g.rstd.next()
    P.add("dve", lambda e: e.tensor_scalar(rs[:, 0:n], ssp[:, 0:n], 1.0 / dim, EPS, ALU.mult, ALU.add),
          reads=[ssk], writes=[rsk])
    P.add("act", lambda e: e.sqrt(rs[:, 0:n], rs[:, 0:n]), reads=[rsk], writes=[rsk])
    P.add("dve", lambda e: e.reciprocal(rs[:, 0:n], rs[:, 0:n]), reads=[rsk], writes=[rsk])
    return rs, rsk


def emit_adanorm(P, g, src, c0, nb, gcol, shcol, colkeys, hkey="h"):
    xTv = src.rearrange("(c p) t -> p c t", p=128)
    for (t0, n) in tgroups(nb, 256):
        xs, xk = g.stg.next()
        P.add("sp", lambda e, xs=xs, t0=t0, n=n: e.dma_start(
            out=xs[:, 0:DC * n].rearrange("p (c t) -> p c t", t=n), in_=xTv[:, :, c0 + t0:c0 + t0 + n]),
            reads=[(hkey, c0 + t0)], writes=[xk], dma=True)
        ssp, ssk = g.psr.next()
        for dc in range(DC):
            sq, sqk = g.sq.next()
            P.add("act", lambda e, sq=sq, xs=xs, dc=dc, n=n: e.activation(
                out=sq[:, 0:n], in_=xs[:, dc * n:(dc + 1) * n], func=AF.Square), reads=[xk], writes=[sqk])
            P.add("pe", lambda e, sq=sq, ssp=ssp, dc=dc, n=n: e.matmul(
                ssp[:, 0:n], lhsT=g.ones[:, :], rhs=sq[:, 0:n], start=(dc == 0), stop=(dc == DC - 1)),
                reads=[sqk, "ones"], writes=[ssk])
        rs, rsk = emit_rstd(P, g, ssp, ssk, n, D)
        for dc in range(DC):
            tm, tmk = g.tmp.next()
            P.add("dve", lambda e, tm=tm, xs=xs, rs=rs, dc=dc, n=n: e.scalar_tensor_tensor(
                out=tm[:, 0:n], in0=xs[:, dc * n:(dc + 1) * n], scalar=gcol(dc), in1=rs[:, 0:n],
                op0=ALU.mult, op1=ALU.mult), reads=[xk, rsk] + colkeys, writes=[tmk])
            P.add("act", lambda e, tm=tm, dc=dc, n=n, t0=t0: e.activation(
                out=g.xn[:, dc * nb + t0: dc * nb + t0 + n], in_=tm[:, 0:n], func=AF.Identity,
                bias=shcol(dc), scale=1.0), reads=[tmk] + colkeys, writes=[("xn", dc, t0)])


def emit_linear(P, g, xn, KC, nb, wsrc, NOC, epi, xpref="xn", step=512):
    for oc in range(NOC):
        s, sk = g.stg.next()
        P.add("sp", lambda e, s=s, oc=oc: e.dma_start(out=s[:, 0:KC * 128], in_=wsrc[oc]), writes=[sk], dma=True)
        w, wk = g.wbr.next()
        cast_op(P, g, oc, w[:, 0:KC * 128], s[:, 0:KC * 128], [sk], [wk])
        for (t0, n) in tgroups(nb, step):
            pp, ppk = g.psr.next()
            for kc in range(KC):
                P.add("pe", lambda e, pp=pp, w=w, kc=kc, t0=t0, n=n: e.matmul(
                    pp[:, 0:n], lhsT=w[:, kc * 128:(kc + 1) * 128],
                    rhs=xn[:, kc * nb + t0: kc * nb + t0 + n], start=(kc == 0), stop=(kc == KC - 1)),
                    reads=[wk] + xkeys(xpref, kc, t0, n), writes=[ppk])
            epi(oc, t0, n, pp, ppk)


def epi_store(P, g, dst, c0, okey, scale=None):
    def epi(oc, t0, n, pp, ppk):
        o, ok = g.ost.next()
        if (oc + t0 // 512) % 2 == 0:
            P.add("act", lambda e: e.copy(o[:, 0:n], pp[:, 0:n]), reads=[ppk], writes=[ok])
        else:
            P.add("dve", lambda e: e.tensor_copy(o[:, 0:n], pp[:, 0:n]), reads=[ppk], writes=[ok])
        P.add(ST, lambda e: e.dma_start(out=dst[oc * 128:(oc + 1) * 128, c0 + t0:c0 + t0 + n], in_=o[:, 0:n]),
              reads=[ok], writes=[(okey, oc, c0 + t0)], dma=True)
    return epi


def epi_residual(P, g, h, c0, gtcol, colkeys):
    def epi(oc, t0, n, pp, ppk):
        o, ok = g.ost.next()
        P.add("sp", lambda e: e.dma_start(out=o[:, 0:n], in_=h[oc * 128:(oc + 1) * 128, c0 + t0:c0 + t0 + n]),
              reads=[("hc", oc, c0 + t0)], writes=[ok], dma=True)
        P.add("dve", lambda e: e.scalar_tensor_tensor(
            out=o[:, 0:n], in0=pp[:, 0:n], scalar=gtcol(oc), in1=o[:, 0:n], op0=ALU.mult, op1=ALU.add),
            reads=[ppk, ok] + colkeys, writes=[ok])
        P.add(ST, lambda e: e.dma_start(out=h[oc * 128:(oc + 1) * 128, c0 + t0:c0 + t0 + n], in_=o[:, 0:n]),
              reads=[ok], writes=[("hc", oc, c0 + t0)] + [("h", c0 + t) for t in range((t0 // 256) * 256, t0 + n, 256)],
              dma=True)
    return epi


def modcol(g, l, kind, s, t, dc):
    j = ((l * NMODC + s * 48 + t * 16 + dc) * 2) + kind
    return g.modc[:, j:j + 1]


def emit_cols(P, g, l, s, kind, gsrc, half_gate):
    sc = g.modc[:, :].rearrange("p (l c r) -> p l c r", l=4, r=2)[:, l, s * 48 + 16: s * 48 + 32, kind]
    gt = g.modc[:, :].rearrange("p (l c r) -> p l c r", l=4, r=2)[:, l, s * 48 + 32: s * 48 + 48, kind]
    P.add("dve", lambda e: e.scalar_tensor_tensor(out=g.dcol[:, 0:16], in0=sc, scalar=1.0, in1=gsrc,
                                                   op0=ALU.add, op1=ALU.mult), reads=["modc", "gcols"], writes=["dcol"])
    P.add("dve", lambda e: e.tensor_scalar_mul(g.dcol[:, 16:32], gt, 0.5 if half_gate else 1.0),
          reads=["modc", "dcol"], writes=["dcol"])


def stage_mod(P, g):
    P.add("sp", lambda e: e.dma_start(out=g.cs[:, :], in_=g.d_cT), writes=["cs"], dma=True)
    P.add("act", lambda e: e.activation(out=g.cs[:, :], in_=g.cs[:, :], func=AF.Silu), reads=["cs"], writes=["cs"])
    g.modb = carve(g, 0, 4 * NMODC, F32)
    P.add("sp", lambda e: e.dma_start(out=g.modb[:, :], in_=g.d_modb), writes=["modb"], dma=True)
    for l in range(4):
        wv = g.d_modw[l].rearrange("(c p) n -> p c n", p=128)
        for nb_ in range(36):
            for hf in range(2):
                s, sk = g.stg.next()
                P.add("sp", lambda e, s=s, nb_=nb_, hf=hf, wv=wv: e.dma_start(
                    out=s[:, :].rearrange("p (c n) -> p c n", n=512), in_=wv[:, hf * 8:(hf + 1) * 8, nb_ * 512:(nb_ + 1) * 512]),
                    writes=[sk], dma=True)
                for j in range(4):
                    pp = g.ps[(nb_ * 4 + j) % 8]
                    ppk = ("ps", (nb_ * 4 + j) % 8)
                    for k8 in range(8):
                        kc = hf * 8 + k8
                        P.add("pe", lambda e, s=s, pp=pp, j=j, k8=k8, kc=kc: e.matmul(
                            pp[:, 0:2], lhsT=s[:, k8 * 512 + j * 128: k8 * 512 + (j + 1) * 128],
                            rhs=g.cs[:, kc * 2:(kc + 1) * 2], start=(kc == 0), stop=(kc == 15)),
                            reads=[sk, "cs"], writes=[ppk])
                    if hf == 1:
                        ch = l * NMODC + nb_ * 4 + j
                        P.add("dve", lambda e, pp=pp, ch=ch: e.tensor_tensor(
                            g.modc[:, ch * 2:ch * 2 + 2], pp[:, 0:2],
                            g.modb[:, ch:ch + 1].to_broadcast([128, 2]), ALU.add),
                            reads=[ppk, "modb"], writes=["modc"])


def stage_ffn(P, g, l, which, skip_ctx=False):
    s = 0 if which == 0 else 2
    wgu, wdn = (g.d_wgu1[l], g.d_wdn1[l]) if which == 0 else (g.d_wgu2[l], g.d_wdn2[l])
    gsrc = g.gcols[:, (l * 3 + s) * 16:(l * 3 + s + 1) * 16]
    for (c0, nb, kind) in BLOCKS:
        if kind == 1 and skip_ctx:
            continue
        ffn_block(P, g, l, s, wgu, wdn, gsrc, c0, nb, kind)
    P.barrier()


def ffn_block(P, g, l, s, wgu, wdn, gsrc, c0, nb, kind):
    if True:
        emit_cols(P, g, l, s, kind, gsrc, True)
        gcol = lambda dc: g.dcol[:, dc:dc + 1]
        hgcol = lambda dc: g.dcol[:, 16 + dc:17 + dc]
        shcol = lambda dc, kind=kind: modcol(g, l, kind, s, 0, dc)
        emit_adanorm(P, g, g.d_h, c0, nb, gcol, shcol, ["dcol", "modc"])
        TG = tgroups(nb, 512)
        for fc in range(FC):
            st_, sk = g.stg.next()
            P.add("sp", lambda e, st_=st_, fc=fc: e.dma_start(out=st_[:, :], in_=wgu[fc]), writes=[sk], dma=True)
            w, wk = g.wbr.next()
            P.add("act", lambda e, w=w, st_=st_: e.copy(w[:, 0:2048], st_[:, 0:2048]), reads=[sk], writes=[(wk, 0)])
            P.add("dve", lambda e, w=w, st_=st_: e.tensor_copy(w[:, 2048:4096], st_[:, 2048:4096]), reads=[sk], writes=[(wk, 1)])
            for (t0, n) in TG:
                pg, pgk = g.psr.next()
                pu, puk = g.psr.next()
                for hh, (pp, ppk) in enumerate(((pg, pgk), (pu, puk))):
                    for dc in range(DC):
                        P.add("pe", lambda e, pp=pp, w=w, hh=hh, dc=dc, t0=t0, n=n: e.matmul(
                            pp[:, 0:n], lhsT=w[:, (hh * DC + dc) * 128:(hh * DC + dc + 1) * 128],
                            rhs=g.xn[:, dc * nb + t0: dc * nb + t0 + n], start=(dc == 0), stop=(dc == DC - 1)),
                            reads=[(wk, hh)] + xkeys("xn", dc, t0, n), writes=[ppk])
                g_, gk = g.sg.next()
                P.add("act", lambda e, g_=g_, pg=pg, n=n: e.activation(out=g_[:, 0:n], in_=pg[:, 0:n], func=AF.Silu),
                      reads=[pgk], writes=[gk])
                P.add("dve", lambda e, g_=g_, pu=pu, n=n, fc=fc, t0=t0: e.tensor_tensor(
                    g.act[:, fc * nb + t0: fc * nb + t0 + n], pu[:, 0:n], g_[:, 0:n], ALU.mult),
                    reads=[puk, gk], writes=[("act", fc, t0)])
        for dc in range(DC):
            ws, wks = [], []
            for hf in range(2):
                st_, sk = g.stg.next()
                P.add("sp", lambda e, st_=st_, dc=dc, hf=hf: e.dma_start(
                    out=st_[:, 0:2816], in_=wdn[dc][:, hf * 2816:(hf + 1) * 2816]), writes=[sk], dma=True)
                w, wk = g.wbr.next()
                cast_op(P, g, hf, w[:, 0:2816], st_[:, 0:2816], [sk], [wk])
                ws.append(w)
                wks.append(wk)
            for (t0, n) in TG:
                xr, xrk = g.ost.next()
                P.add("sp", lambda e, xr=xr, dc=dc, t0=t0, n=n: e.dma_start(
                    out=xr[:, 0:n], in_=g.d_h[dc * 128:(dc + 1) * 128, c0 + t0:c0 + t0 + n]),
                    reads=[("hc", dc, c0 + t0)], writes=[xrk], dma=True)
                pp, ppk = g.psr.next()
                for fc in range(FC):
                    w, wk = ws[fc // 22], wks[fc // 22]
                    P.add("pe", lambda e, pp=pp, w=w, fc=fc, t0=t0, n=n: e.matmul(
                        pp[:, 0:n], lhsT=w[:, (fc % 22) * 128:(fc % 22 + 1) * 128],
                        rhs=g.act[:, fc * nb + t0: fc * nb + t0 + n], start=(fc == 0), stop=(fc == FC - 1)),
                        reads=[wk, ("act", fc, t0)], writes=[ppk])
                P.add("dve", lambda e, xr=xr, pp=pp, n=n, dc=dc: e.scalar_tensor_tensor(
                    out=xr[:, 0:n], in0=pp[:, 0:n], scalar=hgcol(dc), in1=xr[:, 0:n],
                    op0=ALU.mult, op1=ALU.add), reads=[ppk, xrk, "dcol"], writes=[xrk])
                P.add(ST, lambda e, xr=xr, dc=dc, t0=t0, n=n: e.dma_start(
                    out=g.d_h[dc * 128:(dc + 1) * 128, c0 + t0:c0 + t0 + n], in_=xr[:, 0:n]), reads=[xrk],
                    writes=[("hc", dc, c0 + t0)] + [("h", c0 + t) for t in range((t0 // 256) * 256, t0 + n, 256)], dma=True)


def carve(g, off_bytes, ncols, dt):
    assert off_bytes % 4 == 0
    e0 = off_bytes // 2
    if dt == BF16:
        assert e0 + ncols <= ARENA
        return g.arena[:, e0:e0 + ncols]
    assert e0 + 2 * ncols <= ARENA
    return g.arena[:, e0:e0 + 2 * ncols].bitcast(F32)


def lin_views(g):
    g.xn = g.arena[:, 0:DC * TB]
    o_ = DC * TB * 2
    stage = [carve(g, o_ + i * 16384, 4096, F32) for i in range(6)]
    o_ += 6 * 16384
    wbf = [carve(g, o_ + i * 8192, 4096, BF16) for i in range(3)]
    g.stg, g.wbr = Ring("stage", stage[0:3]), Ring("wbf", wbf)
    g.stg2 = Ring("stage2", stage[4:6])
    g.ost = Ring("ost_l", [carve(g, DC * TB * 2 + 3 * 16384 + i * 2048, 512, F32) for i in range(8)])
    g.psr = Ring("ps", g.ps)


def ffn_views(g):
    g.xn = g.arena[:, 0:DC * TB]
    g.act = g.arena[:, DC * TB:(DC + FC) * TB]
    o_ = (DC + FC) * TB
    stage = [g.arena[:, o_ + i * 8192: o_ + (i + 1) * 8192].bitcast(F32) for i in range(3)]
    o_ += 3 * 8192
    wbf = [g.arena[:, o_ + i * 4096: o_ + (i + 1) * 4096] for i in range(2)]
    g.stg, g.wbr = Ring("stage", stage), Ring("wbf", wbf)
    g.psr = Ring("ps", g.ps)
    g.ost = Ring("ost", g.ost_global)


def stage_lbcols(P, g):
    P.add("sp", lambda e: e.dma_start(out=g.lbl[:, :], in_=g.d_lbl), writes=["lbl"], dma=True)
    P.add("pool", lambda e: e.memset(g.lbc[:, :], 0.0), writes=["lbc"])
    for d_ in range(2):
        b0 = ((1 * 2 + d_) * 3) * 16
        l0 = g.lbl[:, (d_ * 2 + 0) * 16:(d_ * 2 + 1) * 16]
        l1 = g.lbl[:, (d_ * 2 + 1) * 16:(d_ * 2 + 2) * 16]
        P.add("dve", lambda e, b0=b0, l0=l0, l1=l1: e.tensor_tensor(g.lbc[:, b0:b0 + 16], l1, l0, ALU.subtract),
              reads=["lbl", "lbc"], writes=["lbc"])
        P.add("act", lambda e, b0=b0: e.activation(out=g.lbc[:, b0:b0 + 16], in_=g.lbc[:, b0:b0 + 16], func=AF.Sigmoid),
              reads=["lbc"], writes=["lbc"])
    for j in range(2):
        for d_ in range(2):
            b0 = ((j * 2 + d_) * 3) * 16
            P.add("dve", lambda e, b0=b0: e.tensor_scalar(g.lbc[:, b0 + 16:b0 + 32], g.lbc[:, b0:b0 + 16], -1.0, 1.0, ALU.mult, ALU.add),
                  reads=["lbc"], writes=["lbc"])
            P.add("dve", lambda e, b0=b0: e.tensor_scalar(g.lbc[:, b0 + 32:b0 + 48], g.lbc[:, b0:b0 + 16], 1.0, -1.0, ALU.mult, ALU.add),
                  reads=["lbc"], writes=["lbc"])
    P.add("sp", lambda e: e.dma_start(out=g.cst[:, :], in_=g.d_cst), writes=["cst"], dma=True)
    P.add("act", lambda e: e.copy(g.identb[:, :], g.cst[:, 0:128]), reads=["cst"], writes=["identb"])


def stage_hgrn_inproj(P, g, l, j):
    lin_views(g)
    gsrc = g.gcols[:, (l * 3 + 1) * 16:(l * 3 + 2) * 16]
    for (c0, nb, kind) in BLOCKS:
        emit_cols(P, g, l, 1, kind, gsrc, False)
        gcol = lambda dc: g.dcol[:, dc:dc + 1]
        shcol = lambda dc, kind=kind: modcol(g, l, kind, 1, 0, dc)
        emit_adanorm(P, g, g.d_h, c0, nb, gcol, shcol, ["dcol", "modc"])
        emit_linear(P, g, g.xn, DC, nb, g.d_hgw_in[j], 80, epi_store(P, g, g.d_pj, c0, "pj"))
    P.barrier()


def stage_hgrn_scan(P, g, j, nheads=16):
    NTI = L // 128
    NCH = L // 32
    SZ = L * 4
    Q = carve(g, 0 * SZ, L, F32)
    Z = carve(g, 1 * SZ, L, F32)
    A = carve(g, 2 * SZ, L, F32)
    Bc = carve(g, 3 * SZ, L, F32)
    E = carve(g, 4 * SZ, L, F32)
    I = carve(g, 5 * SZ, L, F32)
    o_ = 6 * SZ
    qd = carve(g, o_, L, BF16); o_ += L * 2
    ki = carve(g, o_, L, BF16); o_ += L * 2
    vtok = carve(g, o_, L, BF16); o_ += L * 2
    ibf = carve(g, o_, L, BF16); o_ += L * 2
    dec = carve(g, o_, NCH, F32); o_ += NCH * 4
    S32 = [carve(g, o_ + i * 512, 128, F32) for i in range(4)]; o_ += 4 * 512
    T32 = [carve(g, o_ + i * 512, 128, F32) for i in range(4)]; o_ += 4 * 512
    Sbf = [carve(g, o_ + i * 256, 128, BF16) for i in range(4)]; o_ += 4 * 256
    ATm = [carve(g, o_ + i * 256, 128, BF16) for i in range(3)]; o_ += 3 * 256
    KT = [carve(g, o_ + i * 1024, 512, BF16) for i in range(3)]; o_ += 3 * 1024
    OS = [carve(g, o_ + i * 512, 128, F32) for i in range(4)]; o_ += 4 * 512
    assert o_ <= ARENA * 2
    s32r, t32r, sbfr, atmr, ktr, osr = Ring("S32", S32), Ring("T32", T32), Ring("Sbf", Sbf), Ring("ATm", ATm), Ring("KT", KT), Ring("OS", OS)
    par, ptr, por, pur = Ring("ps", g.ps[0:2]), Ring("ps2", g.ps[2:3]), Ring("ps3", g.ps[3:5]), Ring("ps5", g.ps[5:7])
    pv = g.ps[7]
    identb = g.identb
    maskf = [g.cst[:, 128:256], g.cst[:, 256:384]]
    m01 = g.cst[:, 384:512]
    HALF = L // 2
    for h in range(nheads):
        hs = slice(h * 128, (h + 1) * 128)
        for hf in range(2):
            cs_ = slice(hf * HALF, (hf + 1) * HALF)
            P.add("sp", lambda e, cs_=cs_, h=h: e.dma_start(out=Q[:, cs_], in_=g.d_pj[h * 128:(h + 1) * 128, cs_]),
                  reads=["pj_all"], writes=[("Q", hf)], dma=True)
            P.add("sp", lambda e, cs_=cs_, h=h: e.dma_start(out=I[:, cs_], in_=g.d_pj[6144 + h * 128:6144 + (h + 1) * 128, cs_]),
                  reads=["pj_all"], writes=[("I", hf)], dma=True)
            P.add("act", lambda e, cs_=cs_: e.activation(out=Q[:, cs_], in_=Q[:, cs_], func=AF.Silu), reads=[("Q", hf)], writes=[("Q", hf)])
            P.add("pool", lambda e, cs_=cs_: e.tensor_copy(ibf[:, cs_], I[:, cs_]), reads=[("I", hf)], writes=[("ibf", hf)])
        pvb = pv[:, :].bitcast(BF16)
        for t4 in range(0, NTI, 4):
            nt_ = min(4, NTI - t4)
            for q in range(nt_):
                ti = t4 + q
                P.add("pe", lambda e, q=q, ti=ti: e.transpose(pvb[:, q * 128:(q + 1) * 128], ibf[:, ti * 128:(ti + 1) * 128], identb[:, :]),
                      reads=[("ibf", 0), ("ibf", 1), "identb"], writes=["pv"])
            P.add("act", lambda e, t4=t4, nt_=nt_: e.copy(vtok[:, t4 * 128:(t4 + nt_) * 128], pvb[:, 0:nt_ * 128]),
                  reads=["pv"], writes=["vtok"])
        for d_ in range(2):
            lb0 = ((j * 2 + d_) * 3) * 16
            lbcol = g.lbc[:, lb0 + h: lb0 + h + 1]
            omlcol = g.lbc[:, lb0 + 16 + h: lb0 + 16 + h + 1]
            nomlcol = g.lbc[:, lb0 + 32 + h: lb0 + 32 + h + 1]
            zrow = 2048 * (1 + d_) + h * 128
            P.add("sp", lambda e, zrow=zrow: e.dma_start(out=Z[:, :], in_=g.d_pj[zrow:zrow + 128, :]), reads=["pj_all"], writes=["Z"], dma=True)
            P.add("act", lambda e: e.activation(out=Z[:, :], in_=Z[:, :], func=AF.Sigmoid), reads=["Z"], writes=["Z"])
            P.add("dve", lambda e, omlcol=omlcol, lbcol=lbcol: e.tensor_scalar(A[:, :], Z[:, :], omlcol, lbcol, ALU.mult, ALU.add),
                  reads=["Z", "lbc"], writes=["A"])
            P.add("act", lambda e: e.activation(out=A[:, :], in_=A[:, :], func=AF.Ln), reads=["A"], writes=["A"])
            P.add("dve", lambda e, omlcol=omlcol, nomlcol=nomlcol: e.tensor_scalar(Z[:, :], Z[:, :], nomlcol, omlcol, ALU.mult, ALU.add),
                  reads=["Z", "lbc"], writes=["Z"])
            for (t0, n) in tgroups(L, 128):
                P.add("dve", lambda e, t0=t0, n=n: e.tensor_tensor_scan(Bc[:, t0:t0 + n], m01[:, 0:n], A[:, t0:t0 + n], 0.0, ALU.mult, ALU.add),
                      reads=["A", "cst"], writes=["Bc"])
            P.add("act", lambda e: e.activation(out=dec[:, :], in_=Bc[:, :].rearrange("p (n c) -> p n c", c=32)[:, :, 31], func=AF.Exp),
                  reads=["Bc"], writes=["dec"])
            if d_ == 1:
                P.add("dve", lambda e: e.tensor_tensor(Bc[:, :], Bc[:, :], A[:, :], ALU.subtract), reads=["Bc", "A", "dec"], writes=["Bc"])
            sq_, sk_ = (1.0, -1.0) if d_ == 0 else (-1.0, 1.0)
            P.add("act", lambda e, sq_=sq_: e.activation(out=E[:, :], in_=Bc[:, :], func=AF.Exp, scale=sq_), reads=["Bc"], writes=["E"])
            P.add("dve", lambda e: e.tensor_tensor(qd[:, :], Q[:, :], E[:, :], ALU.mult), reads=["E", ("Q", 0), ("Q", 1)], writes=["qd"])
            P.add("act", lambda e, sk_=sk_: e.activation(out=E[:, :], in_=Bc[:, :], func=AF.Exp, scale=sk_), reads=["Bc", "qd"], writes=["E"])
            P.add("dve", lambda e: e.tensor_tensor(ki[:, :], Z[:, :], E[:, :], ALU.mult), reads=["E", "Z"], writes=["ki"])
            s_cur, s_cur_k = s32r.next()
            P.add("pool", lambda e, s_cur=s_cur: e.memset(s_cur[:, :], 0.0), writes=[s_cur_k])
            sb_cur, sb_cur_k = sbfr.next()
            P.add("pool", lambda e, sb_cur=sb_cur: e.memset(sb_cur[:, :], 0.0), writes=[sb_cur_k])
            if d_ == 0:
                order = list(range(NTI))
            else:
                order = [1, 0] + list(range(NTI - 1, 1, -1))
            dst = g.d_of if d_ == 0 else g.d_ob
            fr = {}

            def front(ti):
                ts_ = slice(ti * 128, (ti + 1) * 128)
                pa, pak = par.next()
                P.add("pe", lambda e: e.matmul(pa[:, 0:128], lhsT=ki[:, ts_], rhs=qd[:, ts_], start=True, stop=True),
                      reads=["ki", "qd"], writes=[pak])
                am, amk = atmr.next()
                P.add("dve", lambda e: e.tensor_tensor(am[:, :], pa[:, 0:128], maskf[d_], ALU.mult), reads=[pak, "cst"], writes=[amk])
                pt, ptk = ptr.next()
                ptb = pt[:, :].bitcast(BF16)
                P.add("pe", lambda e: e.transpose(ptb[:, 0:128], ki[:, ts_], identb[:, :]), reads=["ki", "identb"], writes=[ptk])
                kt, ktk = ktr.next()
                for c in range(4):
                    P.add("act", lambda e, c=c: e.activation(out=kt[:, c * 128:(c + 1) * 128], in_=ptb[:, 0:128], func=AF.Identity,
                                                            scale=g.cst[:, 512 + c:513 + c]), reads=[ptk, "cst"], writes=[(ktk, c)])
                po, pok = por.next()
                P.add("pe", lambda e: e.matmul(po[:, 0:128], lhsT=vtok[:, ts_], rhs=am[:, :], start=True, stop=False),
                      reads=["vtok", amk], writes=[pok])
                pu, puk = pur.next()
                for c in range(4):
                    P.add("pe", lambda e, c=c: e.matmul(pu[:, c * 128:(c + 1) * 128], lhsT=kt[:, c * 128:(c + 1) * 128], rhs=vtok[:, ts_],
                                                        start=True, stop=True), reads=[(ktk, c), "vtok"], writes=[(puk, c)])
                fr[ti] = (po, pok, pu, puk)

            def chain(ti, s_cur, s_cur_k, sb_cur, sb_cur_k):
                po, pok, pu, puk = fr.pop(ti)
                corder = range(4) if d_ == 0 else range(3, -1, -1)
                for ci, c in enumerate(corder):
                    ch = ti * 4 + c
                    cs_ = slice(ti * 128 + c * 32, ti * 128 + c * 32 + 32)
                    last = (ci == 3)
                    if d_ == 0:
                        P.add("pe", lambda e, c=c, sb_cur=sb_cur, cs_=cs_, last=last: e.matmul(
                            po[:, c * 32:(c + 1) * 32], lhsT=sb_cur[:, :], rhs=qd[:, cs_], start=False, stop=last),
                            reads=[sb_cur_k, "qd"], writes=[pok])
                        tt, ttk = t32r.next()
                        P.add("dve", lambda e, tt=tt, s_cur=s_cur, c=c: e.tensor_tensor(tt[:, :], pu[:, c * 128:(c + 1) * 128], s_cur[:, :], ALU.add),
                              reads=[(puk, c), s_cur_k], writes=[ttk])
                        s_new, s_new_k = s32r.next()
                        P.add("dve", lambda e, tt=tt, s_new=s_new, ch=ch: e.tensor_scalar_mul(s_new[:, :], tt[:, :], dec[:, ch:ch + 1]),
                              reads=[ttk, "dec"], writes=[s_new_k])
                        sb_new, sb_new_k = sbfr.next()
                        P.add("act", lambda e, sb_new=sb_new, s_new=s_new: e.copy(sb_new[:, :], s_new[:, :]), reads=[s_new_k], writes=[sb_new_k])
                    else:
                        tt, ttk = t32r.next()
                        P.add("dve", lambda e, tt=tt, s_cur=s_cur, ch=ch: e.tensor_scalar_mul(tt[:, :], s_cur[:, :], dec[:, ch:ch + 1]),
                              reads=[s_cur_k, "dec"], writes=[ttk])
                        sb_new, sb_new_k = sbfr.next()
                        P.add("act", lambda e, sb_new=sb_new, tt=tt: e.copy(sb_new[:, :], tt[:, :]), reads=[ttk], writes=[sb_new_k])
                        P.add("pe", lambda e, c=c, sb_new=sb_new, cs_=cs_, last=last: e.matmul(
                            po[:, c * 32:(c + 1) * 32], lhsT=sb_new[:, :], rhs=qd[:, cs_], start=False, stop=last),
                            reads=[sb_new_k, "qd"], writes=[pok])
                        s_new, s_new_k = s32r.next()
                        P.add("dve", lambda e, tt=tt, s_new=s_new, c=c: e.tensor_tensor(s_new[:, :], pu[:, c * 128:(c + 1) * 128], tt[:, :], ALU.add),
                              reads=[(puk, c), ttk], writes=[s_new_k])
                    s_cur, s_cur_k, sb_cur, sb_cur_k = s_new, s_new_k, sb_new, sb_new_k
                os_, osk = osr.next()
                P.add("act", lambda e: e.copy(os_[:, :], po[:, 0:128]), reads=[pok], writes=[osk])
                P.add(ST, lambda e: e.dma_start(out=dst[h * 128:(h + 1) * 128, ti * 128:(ti + 1) * 128], in_=os_[:, :]),
                      reads=[osk], writes=[("o", d_, h, ti)], dma=True)
                return s_cur, s_cur_k, sb_cur, sb_cur_k

            front(order[0])
            for i_, ti in enumerate(order):
                if i_ + 1 < len(order):
                    front(order[i_ + 1])
                s_cur, s_cur_k, sb_cur, sb_cur_k = chain(ti, s_cur, s_cur_k, sb_cur, sb_cur_k)
    P.barrier()


def stage_hgrn_readout(P, g, l, j, skip_ctx):
    lin_views(g)
    gtcol = None
    for (c0, nb, kind) in BLOCKS:
        if kind == 1 and skip_ctx:
            continue
        hgrn_readout_block(P, g, l, j, c0, nb, kind)
    P.barrier()


def hgrn_readout_block(P, g, l, j, c0, nb, kind):
    gtcol = lambda oc: modcol(g, l, kind, 1, 2, oc)
    ofv = g.d_of.rearrange("(c p) t -> p c t", p=128)
    obv = g.d_ob.rearrange("(c p) t -> p c t", p=128)
    gtv = g.d_pj[8192:10240, :].rearrange("(c p) t -> p c t", p=128)
    for (t0, n) in tgroups(nb, 256):
        s1, k1 = g.stg.next()
        s2, k2 = g.stg.next()
        s3, k3 = g.stg.next()
        for (sx, kx, src) in ((s1, k1, ofv), (s2, k2, obv), (s3, k3, gtv)):
            P.add("sp", lambda e, sx=sx, src=src: e.dma_start(
                out=sx[:, 0:DC * n].rearrange("p (c t) -> p c t", t=n), in_=src[:, :, c0 + t0:c0 + t0 + n]), writes=[kx], dma=True)
        P.add("pool", lambda e: e.tensor_tensor(s1[:, 0:DC * n], s1[:, 0:DC * n], s2[:, 0:DC * n], ALU.add), reads=[k1, k2], writes=[k1])
        P.add("act", lambda e: e.activation(out=s3[:, 0:DC * n], in_=s3[:, 0:DC * n], func=AF.Silu), reads=[k3], writes=[k3])
        for dc in range(DC):
            sq_, sqk = g.sq.next()
            ssp, ssk = g.psr.next()
            P.add("act", lambda e, sq_=sq_, dc=dc: e.activation(out=sq_[:, 0:n], in_=s1[:, dc * n:(dc + 1) * n], func=AF.Square),
                  reads=[k1], writes=[sqk])
            P.add("pe", lambda e, sq_=sq_, ssp=ssp: e.matmul(ssp[:, 0:n], lhsT=g.ones[:, :], rhs=sq_[:, 0:n], start=True, stop=True),
                  reads=[sqk, "ones"], writes=[ssk])
            rs, rsk = emit_rstd(P, g, ssp, ssk, n, 128)
            tm, tmk = g.tmp.next()
            P.add("dve", lambda e, tm=tm, rs=rs, dc=dc: e.scalar_tensor_tensor(
                out=tm[:, 0:n], in0=s1[:, dc * n:(dc + 1) * n], scalar=g.hgn[:, j * 16 + dc: j * 16 + dc + 1], in1=rs[:, 0:n],
                op0=ALU.mult, op1=ALU.mult), reads=[k1, rsk, "hgn"], writes=[tmk])
            P.add("pool", lambda e, tm=tm, dc=dc: e.tensor_tensor(
                g.xn[:, dc * nb + t0: dc * nb + t0 + n], tm[:, 0:n], s3[:, dc * n:(dc + 1) * n], ALU.mult),
                reads=[tmk, k3], writes=[("xn", dc, t0)])
    emit_linear(P, g, g.xn, DC, nb, g.d_hgw_out[j], DC, epi_residual(P, g, g.d_h, c0, gtcol, ["modc"]))


MLA_SCALE = (128 + 64) ** -0.5


def stage_mla_proj(P, g, l):
    gsrc = g.gcols[:, (l * 3 + 1) * 16:(l * 3 + 2) * 16]
    for (c0, nb, kind) in BLOCKS:
        mla_proj_block(P, g, l, gsrc, c0, nb, kind)
    P.barrier()


def mla_proj_block(P, g, l, gsrc, c0, nb, kind):
    g.xn = g.arena[:, 0:DC * TB]
    cbuf = carve(g, 32768, 8 * TB, F32)
    cn = carve(g, 65536, 8 * TB, BF16)
    rope = carve(g, 81920, 2 * TB, F32)
    vbf = [carve(g, 90112 + i * 2048, TB, BF16) for i in range(2)]
    obf = [carve(g, 94208 + i * 1024, 512, BF16) for i in range(4)]
    stage = [carve(g, 98304 + i * 16384, 4096, F32) for i in range(3)]
    wbf = [carve(g, 147456 + i * 8192, 4096, BF16) for i in range(3)]
    rt = [carve(g, 172032 + i * 2048, 512, F32) for i in range(4)]
    g.stg, g.wbr = Ring("stage", stage), Ring("wbf", wbf)
    g.psr = Ring("ps", g.ps[0:7])
    vbr, obr, rtr = Ring("vbf", vbf), Ring("obf", obf), Ring("rt", rt)
    emit_cols(P, g, l, 1, kind, gsrc, False)
    gcol = lambda dc: g.dcol[:, dc:dc + 1]
    shcol = lambda dc: modcol(g, l, kind, 1, 0, dc)
    emit_adanorm(P, g, g.d_h, c0, nb, gcol, shcol, ["dcol", "modc"])
    P.add("sp", lambda e: e.dma_start(out=rope[:, 0:2 * nb].rearrange("p (a t) -> p a t", a=2),
                                      in_=g.d_rope.rearrange("p (a t) -> p a t", a=2)[:, :, c0:c0 + nb]), writes=["rope"], dma=True)
    pvb = g.ps[7][:, :].bitcast(BF16)

    def rope_epi(dst, first):
        st_ = {}

        def epi(oc, t0, n, pp, ppk):
            if first(oc):
                r1, r1k = rtr.next()
                P.add("dve", lambda e: e.tensor_tensor(r1[:, 0:n], pp[:, 0:n], rope[:, t0:t0 + n], ALU.mult), reads=[ppk, "rope"], writes=[r1k])
                st_[t0] = (r1, r1k)
            else:
                r1, r1k = st_.pop(t0)
                r2, r2k = rtr.next()
                P.add("dve", lambda e: e.tensor_tensor(r2[:, 0:n], pp[:, 0:n], rope[:, nb + t0:nb + t0 + n], ALU.mult), reads=[ppk, "rope"], writes=[r2k])
                o, ok = obr.next()
                P.add("pool", lambda e: e.tensor_tensor(o[:, 0:n], r1[:, 0:n], r2[:, 0:n], ALU.add), reads=[r1k, r2k], writes=[ok])
                P.add(ST, lambda e: e.dma_start(out=dst(oc)[:, c0 + t0:c0 + t0 + n], in_=o[:, 0:n]), reads=[ok], writes=[("mla_o", oc, c0 + t0)], dma=True)
        return epi

    def store_bf(dst):
        def epi(oc, t0, n, pp, ppk):
            o, ok = obr.next()
            if (oc + t0 // 512) % 2 == 0:
                P.add("act", lambda e: e.copy(o[:, 0:n], pp[:, 0:n]), reads=[ppk], writes=[ok])
            else:
                P.add("dve", lambda e: e.tensor_copy(o[:, 0:n], pp[:, 0:n]), reads=[ppk], writes=[ok])
            P.add(ST, lambda e: e.dma_start(out=dst(oc)[:, c0 + t0:c0 + t0 + n], in_=o[:, 0:n]), reads=[ok], writes=[("mla_o2", oc, c0 + t0)], dma=True)
        return epi

    krope = rope_epi(lambda oc: g.d_kr, lambda oc: oc == 8)

    def epi1(oc, t0, n, pp, ppk):
        if oc < 8:
            if oc % 2 == 0:
                P.add("act", lambda e: e.copy(cbuf[:, oc * nb + t0: oc * nb + t0 + n], pp[:, 0:n]), reads=[ppk], writes=[("cb", oc, t0)])
            else:
                P.add("dve", lambda e: e.tensor_copy(cbuf[:, oc * nb + t0: oc * nb + t0 + n], pp[:, 0:n]), reads=[ppk], writes=[("cb", oc, t0)])
        else:
            krope(oc, t0, n, pp, ppk)
    emit_linear(P, g, g.xn, DC, nb, g.d_wdqkv, 10, epi1)
    for grp in range(2):
        for (t0, n) in tgroups(nb, 256):
            ssp, ssk = g.psr.next()
            for q in range(4):
                oc = grp * 4 + q
                sq, sqk = g.sq.next()
                P.add("act", lambda e, sq=sq, oc=oc: e.activation(out=sq[:, 0:n], in_=cbuf[:, oc * nb + t0: oc * nb + t0 + n], func=AF.Square),
                      reads=[("cb", oc, (t0 // 512) * 512)], writes=[sqk])
                P.add("pe", lambda e, sq=sq, q=q: e.matmul(ssp[:, 0:n], lhsT=g.ones[:, :], rhs=sq[:, 0:n], start=(q == 0), stop=(q == 3)),
                      reads=[sqk, "ones"], writes=[ssk])
            rs, rsk = emit_rstd(P, g, ssp, ssk, n, 512)
            for q in range(4):
                oc = grp * 4 + q
                P.add("dve", lambda e, rs=rs, oc=oc: e.scalar_tensor_tensor(
                    out=cn[:, oc * nb + t0: oc * nb + t0 + n], in0=cbuf[:, oc * nb + t0: oc * nb + t0 + n],
                    scalar=g.mlan[:, oc:oc + 1], in1=rs[:, 0:n], op0=ALU.mult, op1=ALU.mult),
                    reads=[("cb", oc, (t0 // 512) * 512), rsk, "mlan"], writes=[("cn" if oc < 4 else "cn4", oc % 4, t0)])
    qrope = rope_epi(lambda oc: g.d_qr[oc // 3], lambda oc: oc % 3 == 1)
    qn_store = store_bf(lambda oc: g.d_qn[oc // 3])

    def epi3(oc, t0, n, pp, ppk):
        if oc % 3 == 0:
            qn_store(oc, t0, n, pp, ppk)
        else:
            qrope(oc, t0, n, pp, ppk)
    emit_linear(P, g, cn, 4, nb, g.d_wuq, 48, epi3, xpref="cn")
    kn_store = store_bf(lambda oc: g.d_kn[oc // 2])

    def epi4(oc, t0, n, pp, ppk):
        if oc % 2 == 0:
            kn_store(oc, t0, n, pp, ppk)
        else:
            hh = oc // 2
            v, vk = vbr.next()
            P.add("act", lambda e: e.copy(v[:, 0:n], pp[:, 0:n]), reads=[ppk], writes=[vk])
            for q in range(n // 128):
                P.add("pe", lambda e, q=q: e.transpose(pvb[:, q * 128:(q + 1) * 128], v[:, q * 128:(q + 1) * 128], g.identb[:, :]),
                      reads=[vk, "identb"], writes=["pv"])
            o, ok = obr.next()
            P.add("dve", lambda e: e.tensor_copy(o[:, 0:n], pvb[:, 0:n]), reads=["pv"], writes=[ok])
            tb0 = (c0 + t0) // 128
            P.add(ST, lambda e: e.dma_start(
                out=g.d_vt[hh].rearrange("(n p) v -> p n v", p=128)[:, tb0:tb0 + n // 128, :],
                in_=o[:, 0:n].rearrange("p (n v) -> p n v", v=128)), reads=[ok], writes=[("mla_v", oc, c0 + t0)], dma=True)
    emit_linear(P, g, cn[:, 4 * nb:8 * nb], 4, nb, g.d_wukv, 32, epi4, xpref="cn4")


def stage_mla_attn(P, g, last):
    NTI = L // 128
    Kr = carve(g, 0, L, BF16)
    o_ = L * 2
    Kn = [carve(g, o_ + i * L * 2, L, BF16) for i in range(2)]; o_ += 2 * L * 2
    Vt = [carve(g, o_ + i * L * 2, L, BF16) for i in range(2)]; o_ += 2 * L * 2
    Qn = [carve(g, o_ + i * L * 2, L, BF16) for i in range(2)]; o_ += 2 * L * 2
    Qr = [carve(g, o_ + i * L * 2, L, BF16) for i in range(2)]; o_ += 2 * L * 2
    pT = [carve(g, o_ + i * 1024, 512, BF16) for i in range(4)]; o_ += 4 * 1024
    rl = [carve(g, o_ + i * 2048, 512, F32) for i in range(2)]; o_ += 2 * 2048
    oo = [carve(g, o_ + i * 2048, 512, F32) for i in range(3)]; o_ += 3 * 2048
    onesb = carve(g, o_, 128, BF16); o_ += 256
    assert o_ <= ARENA * 2
    knr, vtr, qnr, qrr, ptr_, rlr, oor = Ring("Kn", Kn), Ring("Vt", Vt), Ring("Qn", Qn), Ring("Qr", Qr), Ring("pT", pT), Ring("rl", rl), Ring("oo", oo)
    psr_s, psr_o, psr_l = Ring("ps", g.ps[0:3]), Ring("ps3", g.ps[3:5]), Ring("ps5", g.ps[5:7])
    P.add("pool", lambda e: e.memset(onesb[:, :], 1.0), writes=["onesb"])
    P.add("sp", lambda e: e.dma_start(out=Kr[:, :], in_=g.d_kr), writes=["Kr"], dma=True)
    qblocks = [(NCTX + 512 * i, 512, 0, NTI) for i in range(NLAT // 512)]
    if not last:
        qblocks.append((0, NCTX, 0, NCTX // 128))
    for h in range(16):
        kn, knk = knr.next()
        vt, vtk = vtr.next()
        qn, qnk = qnr.next()
        qr, qrk = qrr.next()
        P.add("sp", lambda e, kn=kn, h=h: e.dma_start(out=kn[:, :], in_=g.d_kn[h]), writes=[knk], dma=True)
        P.add("sp", lambda e, vt=vt, h=h: e.dma_start(out=vt[:, :].rearrange("p (n v) -> p n v", v=128),
                                                     in_=g.d_vt[h].rearrange("(n p) v -> p n v", p=128)), writes=[vtk], dma=True)
        P.add("sp", lambda e, qn=qn, h=h: e.dma_start(out=qn[:, :], in_=g.d_qn[h]), writes=[qnk], dma=True)
        P.add("sp", lambda e, qr=qr, h=h: e.dma_start(out=qr[:, :], in_=g.d_qr[h]), writes=[qrk], dma=True)
        for (q0, nq, k0, k1) in qblocks:
            po, pok = psr_o.next()
            pl, plk = psr_l.next()
            def scores(kt):
                ks = slice(kt * 128, (kt + 1) * 128)
                ps_, psk = psr_s.next()
                P.add("pe", lambda e: e.matmul(
                    ps_[:, 0:nq], lhsT=kn[:, ks], rhs=qn[:, q0:q0 + nq], start=True, stop=False), reads=[knk, qnk], writes=[psk])
                P.add("pe", lambda e: e.matmul(
                    ps_[:, 0:nq], lhsT=Kr[:, ks], rhs=qr[:, q0:q0 + nq], start=False, stop=True), reads=["Kr", qrk], writes=[psk])
                return ps_, psk

            nxt = scores(k0)
            for kt in range(k0, k1):
                ks = slice(kt * 128, (kt + 1) * 128)
                ps_, psk = nxt
                if kt + 1 < k1:
                    nxt = scores(kt + 1)
                p_, pk = ptr_.next()
                P.add("act", lambda e, p_=p_, ps_=ps_, nq=nq: e.activation(out=p_[:, 0:nq], in_=ps_[:, 0:nq], func=AF.Exp, scale=MLA_SCALE),
                      reads=[psk], writes=[pk])
                P.add("pe", lambda e, p_=p_, po=po, vt=vt, ks=ks, nq=nq, kt=kt, k0=k0, k1=k1: e.matmul(
                    po[:, 0:nq], lhsT=vt[:, ks], rhs=p_[:, 0:nq], start=(kt == k0), stop=(kt == k1 - 1)), reads=[pk, vtk], writes=[pok])
                P.add("pe", lambda e, p_=p_, pl=pl, nq=nq, kt=kt, k0=k0, k1=k1: e.matmul(
                    pl[:, 0:nq], lhsT=onesb[:, :], rhs=p_[:, 0:nq], start=(kt == k0), stop=(kt == k1 - 1)), reads=[pk, "onesb"], writes=[plk])
            r_, rk = rlr.next()
            P.add("dve", lambda e, r_=r_, pl=pl, nq=nq: e.reciprocal(r_[:, 0:nq], pl[:, 0:nq]), reads=[plk], writes=[rk])
            o, ok = oor.next()
            P.add("dve", lambda e, o=o, po=po, r_=r_, nq=nq: e.tensor_tensor(o[:, 0:nq], po[:, 0:nq], r_[:, 0:nq], ALU.mult),
                  reads=[pok, rk], writes=[ok])
            P.add(ST, lambda e, o=o, h=h, q0=q0, nq=nq: e.dma_start(out=g.d_of[h * 128:(h + 1) * 128, q0:q0 + nq], in_=o[:, 0:nq]),
                  reads=[ok], writes=[("ao", h, q0)], dma=True)
    P.barrier()


def stage_outproj(P, g, l, src, wsrc, skip_ctx):
    lin_views(g)
    for (c0, nb, kind) in BLOCKS:
        if kind == 1 and skip_ctx:
            continue
        outproj_block(P, g, l, src, wsrc, c0, nb, kind)
    P.barrier()


def outproj_block(P, g, l, src, wsrc, c0, nb, kind):
    gtcol = lambda oc: modcol(g, l, kind, 1, 2, oc)
    sv = src.rearrange("(c p) t -> p c t", p=128)
    for (t0, n) in tgroups(nb, 256):
        s_, k_ = g.stg.next()
        P.add("sp", lambda e, s_=s_, t0=t0, n=n: e.dma_start(
            out=s_[:, 0:DC * n].rearrange("p (c t) -> p c t", t=n), in_=sv[:, :, c0 + t0:c0 + t0 + n]), writes=[k_], dma=True)
        for dc in range(DC):
            cast_op(P, g, dc, g.xn[:, dc * nb + t0: dc * nb + t0 + n], s_[:, dc * n:(dc + 1) * n], [k_], [("xn", dc, t0)])
    emit_linear(P, g, g.xn, DC, nb, wsrc, DC, epi_residual(P, g, g.d_h, c0, gtcol, ["modc"]))


def stage_fnet_a(P, g, l):
    gsrc = g.gcols[:, (l * 3 + 1) * 16:(l * 3 + 2) * 16]
    dst32 = carve(g, 32768, 1024, F32)
    dftc = carve(g, 32768 + 4096, 1024, BF16)
    P.add("sp", lambda e: e.dma_start(out=dst32[:, :], in_=g.d_dft256), writes=["dft32"], dma=True)
    P.add("act", lambda e: e.copy(dftc[:, :], dst32[:, :]), reads=["dft32"], writes=["dftc"])
    for (c0, nb, kind) in BLOCKS:
        fnet_a_block(P, g, l, gsrc, dftc, c0, nb, kind)
    P.barrier()


def fnet_a_block(P, g, l, gsrc, dftc, c0, nb, kind):
    g.xn = g.arena[:, 0:DC * TB]
    stage = [carve(g, 40960 + i * 16384, 4096, F32) for i in range(3)]
    xo = [carve(g, 90112 + i * 1024, 512, BF16) for i in range(4)]
    g.stg = Ring("stage", stage)
    g.psr = Ring("ps", g.ps)
    xor_ = Ring("xo", xo)
    emit_cols(P, g, l, 1, kind, gsrc, False)
    gcol = lambda dc: g.dcol[:, dc:dc + 1]
    shcol = lambda dc: modcol(g, l, kind, 1, 0, dc)
    emit_adanorm(P, g, g.d_h, c0, nb, gcol, shcol, ["dcol", "modc"])
    for tt in range(nb // 128):
        for gq in range(8):
            pp, ppk = g.psr.next()
            for cs_ in range(2):
                for kk in range(2):
                    dc = gq * 2 + kk
                    P.add("pe", lambda e: e.matmul(
                        pp[:, cs_ * 256:(cs_ + 1) * 256], lhsT=g.xn[:, dc * nb + tt * 128: dc * nb + (tt + 1) * 128],
                        rhs=dftc[:, kk * 512 + cs_ * 256: kk * 512 + (cs_ + 1) * 256], start=(kk == 0), stop=(kk == 1)),
                        reads=xkeys("xn", dc, tt * 128, 128) + ["dftc"], writes=[ppk])
            o, ok = xor_.next()
            if gq % 2 == 0:
                P.add("act", lambda e: e.copy(o[:, :], pp[:, :]), reads=[ppk], writes=[ok])
            else:
                P.add("dve", lambda e: e.tensor_copy(o[:, :], pp[:, :]), reads=[ppk], writes=[ok])
            r0 = c0 + tt * 128
            P.add(ST, lambda e: e.dma_start(out=g.d_xc[r0:r0 + 128, gq * 512:(gq + 1) * 512], in_=o[:, :]),
                  reads=[ok], writes=[("xc", r0, gq)], dma=True)


def stage_fnet_b(P, g):
    TC = NLAT // 128
    tabs = [carve(g, i * 32768, TC * 512, BF16) for i in range(2)]
    xt = [[carve(g, 65536 + (i * 2 + a) * 8192, TC * 128, BF16) for a in range(2)] for i in range(2)]
    oo = [carve(g, 98304 + i * 2048, 512, F32) for i in range(3)]
    ctab = carve(g, 104448, 2 * 2 * 256, BF16)
    xtr, oor = Ring("xt", xt), Ring("oo", oo)
    g.psr = Ring("ps", g.ps)
    xcl = g.d_xc[NCTX:L, :].rearrange("(n p) c -> p n c", p=128)
    xcc = g.d_xc[0:NCTX, :].rearrange("(n p) c -> p n c", p=128)
    for tb in range(NLAT // 512):
        for a in range(2):
            P.add("sp", lambda e: e.dma_start(
                out=tabs[a][:, :].rearrange("p (n t) -> p n t", t=512),
                in_=g.d_dftT[a].rearrange("(n p) t -> p n t", p=128)[:, :, tb * 512:(tb + 1) * 512]), writes=[("tab", a)], dma=True)
        for cc in range(16):
            x2, xk = xtr.next()
            for a in range(2):
                col = (cc // 2) * 512 + a * 256 + (cc % 2) * 128
                P.add("sp", lambda e: e.dma_start(out=x2[a][:, :].rearrange("p (n c) -> p n c", c=128), in_=xcl[:, :, col:col + 128]),
                      writes=[(xk, a)], dma=True)
            pp, ppk = g.psr.next()
            for a in range(2):
                for tc in range(TC):
                    P.add("pe", lambda e: e.matmul(pp[:, :], lhsT=x2[a][:, tc * 128:(tc + 1) * 128], rhs=tabs[a][:, tc * 512:(tc + 1) * 512],
                                                   start=(a == 0 and tc == 0), stop=(a == 1 and tc == TC - 1)),
                          reads=[(xk, a), ("tab", a)], writes=[ppk])
            o, ok = oor.next()
            if cc % 2 == 0:
                P.add("act", lambda e: e.copy(o[:, :], pp[:, :]), reads=[ppk], writes=[ok])
            else:
                P.add("dve", lambda e: e.tensor_copy(o[:, :], pp[:, :]), reads=[ppk], writes=[ok])
            P.add(ST, lambda e: e.dma_start(out=g.d_of[cc * 128:(cc + 1) * 128, NCTX + tb * 512: NCTX + (tb + 1) * 512], in_=o[:, :]),
                  reads=[ok], writes=[("fo", cc, tb)], dma=True)
    P.add("sp", lambda e: e.dma_start(out=ctab[:, :].rearrange("p (a n t) -> p a n t", a=2, t=256),
                                      in_=g.d_dftC.rearrange("a (n p) t -> p a n t", p=128)), writes=["ctab"], dma=True)
    for cc in range(16):
        x2, xk = xtr.next()
        for a in range(2):
            col = (cc // 2) * 512 + a * 256 + (cc % 2) * 128
            P.add("sp", lambda e: e.dma_start(out=x2[a][:, 0:256].rearrange("p (n c) -> p n c", c=128), in_=xcc[:, :, col:col + 128]),
                  writes=[(xk, a)], dma=True)
        pp, ppk = g.psr.next()
        for a in range(2):
            for tc in range(2):
                P.add("pe", lambda e: e.matmul(pp[:, 0:256], lhsT=x2[a][:, tc * 128:(tc + 1) * 128],
                                               rhs=ctab[:, (a * 2 + tc) * 256:(a * 2 + tc + 1) * 256],
                                               start=(a == 0 and tc == 0), stop=(a == 1 and tc == 1)),
                      reads=[(xk, a), "ctab"], writes=[ppk])
        o, ok = oor.next()
        P.add("act", lambda e: e.copy(o[:, 0:256], pp[:, 0:256]), reads=[ppk], writes=[ok])
        P.add(ST, lambda e: e.dma_start(out=g.d_of[cc * 128:(cc + 1) * 128, 0:NCTX], in_=o[:, 0:256]),
              reads=[ok], writes=[("fo", cc, -1)], dma=True)
    P.barrier()


def stage_final(P, g):
    for (c0, nb, kind) in BLOCKS:
        if kind == 1:
            continue
        final_block(P, g, c0, nb)


def final_block(P, g, c0, nb):
    if True:
        xTv = g.d_h.rearrange("(c p) t -> p c t", p=128)
        for (t0, n) in tgroups(nb, 256):
            xs, xk = g.stg.next()
            P.add("sp", lambda e, xs=xs, t0=t0, n=n: e.dma_start(
                out=xs[:, 0:DC * n].rearrange("p (c t) -> p c t", t=n), in_=xTv[:, :, c0 + t0:c0 + t0 + n]),
                reads=[("h", c0 + t0)], writes=[xk], dma=True)
            ssp, ssk = g.psr.next()
            for dc in range(DC):
                sq, sqk = g.sq.next()
                P.add("act", lambda e, sq=sq, xs=xs, dc=dc, n=n: e.activation(
                    out=sq[:, 0:n], in_=xs[:, dc * n:(dc + 1) * n], func=AF.Square), reads=[xk], writes=[sqk])
                P.add("pe", lambda e, sq=sq, ssp=ssp, dc=dc, n=n: e.matmul(
                    ssp[:, 0:n], lhsT=g.ones[:, :], rhs=sq[:, 0:n], start=(dc == 0), stop=(dc == DC - 1)),
                    reads=[sqk, "ones"], writes=[ssk])
            rs, rsk = emit_rstd(P, g, ssp, ssk, n, D)
            for dc in range(DC):
                P.add("dve", lambda e, xs=xs, rs=rs, dc=dc, n=n: e.scalar_tensor_tensor(
                    out=xs[:, dc * n:(dc + 1) * n], in0=xs[:, dc * n:(dc + 1) * n], scalar=g.fgcol[:, dc:dc + 1],
                    in1=rs[:, 0:n], op0=ALU.mult, op1=ALU.mult), reads=[xk, rsk, "gcols"], writes=[xk])
            ov = g.d_out.rearrange("(c p) t -> p c t", p=128)
            P.add(ST, lambda e, xs=xs, t0=t0, n=n, ov=ov: e.dma_start(
                out=ov[:, :, c0 - NCTX + t0: c0 - NCTX + t0 + n], in_=xs[:, 0:DC * n].rearrange("p (c t) -> p c t", t=n)),
                reads=[xk], dma=True)


def build(plan):
    nc = bass.Bass("TRN2", target_bir_lowering=False)
    g = G()
    kinds = set(p if isinstance(p, str) else p[0] for p in plan)
    fam = {"wgu1": "ffn", "wdn1": "ffn", "wgu2": "ffn", "wdn2": "ffn", "hgw_in": "hgrn", "hgw_out": "hgrn",
           "wdqkv": "mla", "wuq": "mla", "wukv": "mla", "wo": "mla", "rope": "mla", "fw": "fnet", "dft256": "fnet",
           "dftT": "fnet", "dftC": "fnet", "mod_w": "mod"}

    def di(name, shape, dt=F32):
        f = fam.get(name)
        if f is not None and not any(k.startswith(f) for k in kinds):
            return None
        return nc.dram_tensor(name, shape, dt, kind="ExternalInput").ap()
    g.d_x = di("x", [D, L])
    g.d_cT = di("cT", [128, 32])
    g.d_modw = di("mod_w", [4, D, 18432])
    g.d_modb = di("mod_b", [128, 4 * NMODC])
    g.d_gcols = di("gcols", [128, 13 * 16])
    g.d_wgu1 = di("wgu1", [4, FC, 128, 4096])
    g.d_wdn1 = di("wdn1", [4, DC, 128, DFF])
    g.d_wgu2 = di("wgu2", [4, FC, 128, 4096])
    g.d_wdn2 = di("wdn2", [4, DC, 128, DFF])
    g.d_hgw_in = di("hgw_in", [2, 80, 128, D])
    g.d_hgw_out = di("hgw_out", [2, DC, 128, D])
    g.d_hgn = di("hgn", [128, 32])
    g.d_lbl = di("lbl", [128, 64])
    g.d_cst = di("cst", [128, 516])
    g.d_wdqkv = di("wdqkv", [10, 128, D])
    g.d_wuq = di("wuq", [48, 128, 512])
    g.d_wukv = di("wukv", [32, 128, 512])
    g.d_wo = di("wo", [DC, 128, D])
    g.d_mlan = di("mlan", [128, 8])
    g.d_rope = di("rope", [128, 2 * L])
    g.d_fw = di("fw", [DC, 128, D])
    g.d_dft256 = di("dft256", [128, 1024])
    g.d_dftT = di("dftT", [2, NLAT, NLAT], BF16)
    g.d_dftC = di("dftC", [2, NCTX, NCTX], BF16)
    g.d_xc = nc.dram_tensor("xc_scr", [L, 4096], BF16).ap()
    g.d_kr = nc.dram_tensor("kr_scr", [128, L], BF16).ap()
    g.d_qn = nc.dram_tensor("qn_scr", [16, 128, L], BF16).ap()
    g.d_qr = nc.dram_tensor("qr_scr", [16, 128, L], BF16).ap()
    g.d_kn = nc.dram_tensor("kn_scr", [16, 128, L], BF16).ap()
    g.d_vt = nc.dram_tensor("vt_scr", [16, L, 128], BF16).ap()
    g.d_out = nc.dram_tensor("out", [D, NLAT], F32, kind="ExternalOutput").ap()
    g.d_h = nc.dram_tensor("h_scr", [D, L], F32).ap()
    g.d_pj = nc.dram_tensor("pj_scr", [10240, L], F32).ap()
    g.d_of = nc.dram_tensor("of_scr", [D, L], F32).ap()
    g.d_ob = nc.dram_tensor("ob_scr", [D, L], F32).ap()
    with contextlib.ExitStack() as st:
        sb = lambda name, shape, dt: st.enter_context(nc.sbuf_tensor(name, shape, dt))
        g.arena = sb("arena", [128, ARENA], BF16)
        g.xn = g.arena[:, 0:DC * TB]
        g.act = g.arena[:, DC * TB:(DC + FC) * TB]
        o_ = (DC + FC) * TB
        stage = [g.arena[:, o_ + i * 8192: o_ + (i + 1) * 8192].bitcast(F32) for i in range(3)]
        o_ += 3 * 8192
        wbf = [g.arena[:, o_ + i * 4096: o_ + (i + 1) * 4096] for i in range(2)]
        g.modc = sb("modc", [128, 4 * NMODC * 2], F32)
        g.gcols = sb("gcols_sb", [128, 13 * 16], F32)
        g.fgcol = g.gcols[:, 12 * 16:13 * 16]
        g.dcol = sb("dcol", [128, 32], F32)
        g.hgn = sb("hgn_sb", [128, 32], F32)
        g.mlan = sb("mlan_sb", [128, 8], F32)
        g.lbl = sb("lbl_sb", [128, 64], F32)
        g.lbc = sb("lbc", [128, 192], F32)
        g.cst = sb("cst_sb", [128, 516], F32)
        g.identb = sb("identb", [128, 128], BF16)
        g.cs = sb("cs", [128, 32], F32)
        g.ones = sb("ones", [128, 128], F32)
        sq = [sb("sq%d" % i, [128, 256], F32) for i in range(2)]
        rstd = [sb("rstd%d" % i, [128, 256], F32) for i in range(2)]
        tmp = [sb("tmp%d" % i, [128, 256], F32) for i in range(2)]
        sg = [sb("sg%d" % i, [128, 512], F32) for i in range(2)]
        ost = [sb("ost%d" % i, [128, 512], F32) for i in range(2)]
        g.ps = [st.enter_context(nc.psum_tensor("ps%d" % i, [128, 512], F32)) for i in range(8)]
        g.psr = Ring("ps", g.ps)
        g.stg, g.wbr = Ring("stage", stage), Ring("wbf", wbf)
        g.sq, g.rstd, g.tmp, g.sg, g.ost = Ring("sq", sq), Ring("rstd", rstd), Ring("tmp", tmp), Ring("sg", sg), Ring("ost", ost)
        g.ost_global = ost
        P = Prog(nc)
        P.add("pool", lambda e: e.memset(g.ones[:, :], 1.0), writes=["ones"])
        P.add("sp", lambda e: e.dma_start(out=g.gcols[:, :], in_=g.d_gcols), writes=["gcols"], dma=True)
        P.add("sp", lambda e: e.dma_start(out=g.hgn[:, :], in_=g.d_hgn), writes=["hgn"], dma=True)
        P.add("sp", lambda e: e.dma_start(out=g.mlan[:, :], in_=g.d_mlan), writes=["mlan"], dma=True)
        stage_lbcols(P, g)
        for i in range(DC):
            P.add("sp", lambda e, i=i: e.dma_start(out=g.d_h[i * 128:(i + 1) * 128, :], in_=g.d_x[i * 128:(i + 1) * 128, :]),
                  writes=[("hinit", i)], dma=True)
        P.barrier()
        for stg_ in plan:
            if stg_ == "mod":
                stage_mod(P, g)
                P.barrier()
            elif stg_[0] == "ffn":
                ffn_views(g)
                stage_ffn(P, g, stg_[1], stg_[2], skip_ctx=(len(stg_) > 3 and stg_[3]))
            elif stg_[0] == "fnet":
                l_ = stg_[1]
                stage_fnet_a(P, g, l_)
                stage_fnet_b(P, g)
                stage_outproj(P, g, l_, g.d_of, g.d_fw, False)
            elif stg_[0] == "mla":
                l_, last_ = stg_[1], stg_[2]
                stage_mla_proj(P, g, l_)
                stage_mla_attn(P, g, last_)
                stage_outproj(P, g, l_, g.d_of, g.d_wo, last_)
            elif stg_[0] == "hgrn_in":
                stage_hgrn_inproj(P, g, stg_[1], stg_[2])
            elif stg_[0] == "hgrn_scan":
                stage_hgrn_scan(P, g, stg_[1], stg_[2])
            elif stg_[0] == "hgrn_out":
                stage_hgrn_readout(P, g, stg_[1], stg_[2], stg_[3])
            elif stg_[0] == "dumpcols":
                P.barrier()
                for i in range(DC):
                    P.add("sp", lambda e, i=i, a=stg_[1], n=stg_[2], o=stg_[3]: e.dma_start(
                        out=g.d_out[i * 128:(i + 1) * 128, o:o + n], in_=g.d_h[i * 128:(i + 1) * 128, a:a + n]), dma=True)
                P.barrier()
                g.dumped = True
            elif stg_[0] == "dump":
                src_ = getattr(g, stg_[1])
                P.barrier()
                for i in range(stg_[3] // 128):
                    P.add("sp", lambda e, i=i, src_=src_, r0=stg_[2], o0=stg_[4]: e.dma_start(
                        out=g.d_out[o0 + i * 128: o0 + (i + 1) * 128, :], in_=src_[r0 + i * 128: r0 + (i + 1) * 128, NCTX:L]), dma=True)
                g.dumped = True
            elif stg_[0] == "hgrn":
                l_, j_, last_ = stg_[1], stg_[2], stg_[3]
                stage_hgrn_inproj(P, g, l_, j_)
                stage_hgrn_scan(P, g, j_)
                stage_hgrn_readout(P, g, l_, j_, last_)
            elif stg_ == "final":
                stage_final(P, g)
            elif stg_ == "dbg_modc":
                P.barrier()
                P.add(ST, lambda e: e.dma_start(out=g.d_out[0:128, 0:4 * NMODC * 2], in_=g.modc[:, :]), dma=True)
                P.emit()
                g.stats = P.stats
                return nc, g
        if "final" not in plan and not getattr(g, "dumped", False):
            P.barrier()
            for i in range(DC):
                P.add("sp", lambda e, i=i: e.dma_start(out=g.d_out[i * 128:(i + 1) * 128, :], in_=g.d_h[i * 128:(i + 1) * 128, NCTX:L]),
                      dma=True)
        P.emit()
        g.stats = P.stats
    return nc, g


def tile_w(W):
    K, N = W.shape
    return np.ascontiguousarray(W.reshape(K // 128, 128, N // 128, 128).transpose(2, 1, 0, 3))


def col16(v):
    return np.ascontiguousarray(v.reshape(-1, 128).T)


def prep_ffn_w(w_gu, w_dn):
    tg = tile_w(w_gu)
    wgu = np.ascontiguousarray(np.stack([tg[:FC], tg[FC:]], axis=2).reshape(FC, 128, 4096))
    wdn = np.ascontiguousarray(tile_w(w_dn).reshape(DC, 128, DFF))
    return wgu, wdn


def prep_inputs(inp, nlayers=4):
    shared = {}
    shared["mod_w"] = np.ascontiguousarray(inp["mod_w"])
    shared["mod_b"] = np.ascontiguousarray(np.concatenate([col16(inp["mod_b"][l]) for l in range(4)], axis=1))
    shared["gcols"] = np.ascontiguousarray(np.concatenate(
        [col16(inp["norm_g"][l, s]) for l in range(4) for s in range(3)] + [col16(inp["final_g"])], axis=1))
    for nm, gu, dn in (("1", "ffn1_w_gu", "ffn1_w_down"), ("2", "ffn2_w_gu", "ffn2_w_down")):
        a, b = zip(*[prep_ffn_w(inp[gu][l], inp[dn][l]) for l in range(4)])
        shared["wgu" + nm] = np.stack(a)
        shared["wdn" + nm] = np.stack(b)
    shared["hgw_in"] = np.stack([tile_w(inp["hgrn_w_in"][j]).reshape(80, 128, D) for j in range(2)])
    shared["hgw_out"] = np.stack([tile_w(inp["hgrn_w_out"][j]).reshape(DC, 128, D) for j in range(2)])
    shared["hgn"] = np.ascontiguousarray(np.concatenate([col16(inp["hgrn_g_norm"][j]) for j in range(2)], axis=1))
    shared["lbl"] = np.ascontiguousarray(np.concatenate(
        [col16(inp["hgrn_lb_logits"][d_, j]) for d_ in range(2) for j in range(2)], axis=1))
    shared["cst"] = make_consts()
    pidx = np.array([a * 32 + (1 - hf) * 16 + f for a in range(2) for hf in range(2) for f in range(16)])
    zpad = lambda w: np.concatenate([w, np.zeros((w.shape[0], 64), np.float32)], axis=1)
    wd = inp["mla_w_dqkv"][0]
    wd_ext = np.concatenate([wd[:, :1024], zpad(wd[:, 1024:1088]), zpad(wd[:, 1024:1088][:, pidx])], axis=1)
    shared["wdqkv"] = tile_w(wd_ext).reshape(10, 128, D)
    wq = inp["mla_w_uq"][0].reshape(512, 16, 192)
    wq_ext = np.concatenate([np.concatenate([wq[:, hh, :128], zpad(wq[:, hh, 128:]), zpad(wq[:, hh, 128:][:, pidx])], axis=1)
                             for hh in range(16)], axis=1)
    shared["wuq"] = tile_w(wq_ext).reshape(48, 128, 512)
    shared["wukv"] = tile_w(inp["mla_w_ukv"][0]).reshape(32, 128, 512)
    shared["wo"] = tile_w(inp["mla_w_o"][0]).reshape(DC, 128, D)
    shared["mlan"] = np.ascontiguousarray(np.concatenate([col16(inp["mla_q_norm"][0]), col16(inp["mla_kv_norm"][0])], axis=1))
    shared["rope"] = make_rope()
    shared["fw"] = tile_w(inp["fnet_w_out"][0]).reshape(DC, 128, D)
    shared.update(make_dft())
    maps = []
    for b in range(2):
        m = dict(shared)
        m["x"] = np.ascontiguousarray(np.concatenate([inp["ctx"][b], inp["x"][b]], axis=0).T)
        cvec = np.stack([inp["c"][b], inp["c_ctx"]])
        m["cT"] = np.ascontiguousarray(cvec.reshape(2, 16, 128).transpose(2, 1, 0).reshape(128, 32))
        maps.append(m)
    return maps


def make_consts():
    c = np.zeros((128, 516), np.float32)
    idx = np.arange(128)
    c[:, 0:128] = np.eye(128, dtype=np.float32)
    same = (idx[:, None] // 32) == (idx[None, :] // 32)
    c[:, 128:256] = (same & (idx[:, None] <= idx[None, :])).astype(np.float32)
    c[:, 256:384] = (same & (idx[:, None] >= idx[None, :])).astype(np.float32)
    c[:, 384:512] = (np.arange(128) % 32 != 0).astype(np.float32)[None, :]
    for k in range(4):
        c[:, 512 + k] = (idx // 32 == k).astype(np.float32)
    return c


def make_rope():
    t = np.arange(NLAT)
    pos = np.stack([(t // 64).astype(np.float32), (t % 64).astype(np.float32)], axis=-1)
    inv_freq = (np.float32(10000.0) ** (-np.arange(16, dtype=np.float32) / np.float32(16))).astype(np.float32)
    ang = (pos[:, :, None] * inv_freq[None, None, :]).astype(np.float32)
    cos, sin = np.cos(ang).astype(np.float32), np.sin(ang).astype(np.float32)
    tab = np.zeros((128, 2, L), np.float32)
    tab[:, 0, :] = 1.0
    for a in range(2):
        for hf in range(2):
            rows = slice(a * 32 + hf * 16, a * 32 + hf * 16 + 16)
            tab[rows, 0, NCTX:] = cos[:, a, :].T
            tab[rows, 1, NCTX:] = (-sin[:, a, :].T) if hf == 0 else sin[:, a, :].T
    return np.ascontiguousarray(tab.reshape(128, 2 * L))


def make_dft():
    import ml_dtypes
    c = np.arange(256)
    ang = 2.0 * np.pi * ((c[:, None] * c[None, :]) % 256) / 256.0
    t256 = np.zeros((128, 2, 2, 256), np.float32)
    for kk in range(2):
        t256[:, kk, 0, :] = np.cos(ang[kk * 128:(kk + 1) * 128])
        t256[:, kk, 1, :] = np.sin(ang[kk * 128:(kk + 1) * 128])
    t = np.arange(NLAT)
    angT = 2.0 * np.pi * ((t[:, None] * t[None, :]) % NLAT) / float(NLAT)
    dftT = np.stack([np.cos(angT) / 1024.0, -np.sin(angT) / 1024.0]).astype(np.float32).astype(ml_dtypes.bfloat16)
    tc_ = np.arange(NCTX)
    angC = 2.0 * np.pi * ((tc_[:, None] * tc_[None, :]) % NCTX) / float(NCTX)
    dftC = np.stack([np.cos(angC) / 256.0, -np.sin(angC) / 256.0]).astype(np.float32).astype(ml_dtypes.bfloat16)
    return {"dft256": np.ascontiguousarray(t256.reshape(128, 1024)), "dftT": np.ascontiguousarray(dftT), "dftC": np.ascontiguousarray(dftC)}


PLAN = ['mod',
        ('ffn', 0, 0), ('hgrn', 0, 0, False), ('ffn', 0, 1),
        ('ffn', 1, 0), ('mla', 1, False), ('ffn', 1, 1),
        ('ffn', 2, 0), ('fnet', 2), ('ffn', 2, 1),
        ('ffn', 3, 0), ('hgrn', 3, 1, True), ('ffn', 3, 1, True),
        'final']


def kernel(**inputs):
    inp = {k: np.asarray(v) for k, v in inputs.items()}
    maps = prep_inputs(inp)
    nc, g = build(PLAN)
    need = [a.memorylocations[0].name for a in nc.allocations
            if getattr(a, "kind", None) == "ExternalInput"]
    in_maps = [{k: m[k] for k in need if k in m} for m in maps]
    res = run_bass_kernel_spmd(nc, in_maps, core_ids=[0, 1])
    out = np.stack([np.ascontiguousarray(res.results[b]["out"].T) for b in range(2)])
    return out.astype(np.float32)
```

```python
import contextlib
import types
import numpy as np
import concourse.bass as bass
import concourse.mybir as mybir
from concourse.bass_utils import run_bass_kernel_spmd

F32 = mybir.dt.float32
BF16 = mybir.dt.bfloat16
AF = mybir.ActivationFunctionType
ALU = mybir.AluOpType

ENGS = ("pe", "act", "dve", "pool", "sp")
ST = "pool"
NDMASEM = 8

D = 2048
DC = 16
DFF = 5632
FC = 44
EPS = 1e-6
NCTX = 256
NLAT = 4096
L = NCTX + NLAT
TB = 1024
BLOCKS = [(0, NCTX, 1)] + [(NCTX + TB * i, TB, 0) for i in range(NLAT // TB)]
NMODC = 144
ARENA = (DC + FC) * TB + 3 * 8192 + 2 * 4096


def freeze(fn):
    if fn.__closure__ is None:
        return fn
    cells = []
    for c in fn.__closure__:
        try:
            cells.append(types.CellType(c.cell_contents))
        except ValueError:
            cells.append(c)
    return types.FunctionType(fn.__code__, fn.__globals__, fn.__name__, fn.__defaults__, tuple(cells))


class Prog:
    def __init__(self, nc):
        self.nc = nc
        self.ops = []
        self.last_w = {}
        self.readers = {}
        self.last_eng = {}
        self.dmas_since = []

    def add(self, eng, fn, reads=(), writes=(), dma=False, extra_deps=()):
        idx = len(self.ops)
        deps = set(extra_deps)
        for k in reads:
            w = self.last_w.get(k)
            if w is not None:
                deps.add(w)
        for k in writes:
            w = self.last_w.get(k)
            if w is not None:
                deps.add(w)
            for r in self.readers.get(k, ()):
                deps.add(r)
        for k in reads:
            lst = self.readers.setdefault(k, [])
            if not dma:
                lst[:] = [r for r in lst if self.ops[r]["dma"] or self.ops[r]["eng"] != eng]
            lst.append(idx)
        for k in writes:
            self.last_w[k] = idx
            self.readers[k] = []
        deps.discard(idx)
        self.ops.append(dict(eng=eng, fn=freeze(fn), deps=deps, dma=dma, sig=False))
        self.last_eng[eng] = idx
        if dma:
            self.dmas_since.append(idx)
        return idx

    def barrier(self):
        deps = set(self.last_eng.values()) | set(self.dmas_since)
        self.dmas_since = []
        for e in ENGS:
            self.add(e, lambda eng: eng.nop(), extra_deps=deps)
        self.last_w = {}
        self.readers = {}

    def emit(self):
        nc = self.nc
        ops = self.ops
        for i, o in enumerate(ops):
            nd = set()
            for d in o["deps"]:
                p = ops[d]
                if p["eng"] == o["eng"] and o["eng"] == "pe" and not p["dma"]:
                    continue
                nd.add(d)
                p["sig"] = True
            o["deps"] = nd
        with contextlib.ExitStack() as st:
            esem = {e: st.enter_context(nc.semaphore("s_" + e)) for e in ENGS}
            dsem = {e: [st.enter_context(nc.semaphore("d_%s%d" % (e, j))) for j in range(NDMASEM)]
                    for e in ("sp", "act", "pool")}
            ecount = {e: 0 for e in ENGS}
            dcount = {e: 0 for e in dsem}
            dtarget = {e: [0] * NDMASEM for e in dsem}
            per_eng = {e: [] for e in ENGS}
            for i, o in enumerate(ops):
                e = o["eng"]
                if o["dma"]:
                    j = dcount[e] % NDMASEM
                    dcount[e] += 1
                    o["prev_target"] = dtarget[e][j]
                    dtarget[e][j] += 16
                    o["sem"] = ("d", e, j)
                    o["target"] = dtarget[e][j]
                elif o["sig"]:
                    ecount[e] += 1
                    o["sem"] = ("e", e, 0)
                    o["target"] = ecount[e]
                per_eng[e].append(i)
            self.stats = dict(ecount=dict(ecount), dcount=dict(dcount), nops={e: len(per_eng[e]) for e in ENGS})

            def semof(s):
                return esem[s[1]] if s[0] == "e" else dsem[s[1]][s[2]]

            block = st.enter_context(nc.Block())

            def body(e, eng):
                seen = {}
                for i in per_eng[e]:
                    o = ops[i]
                    waits = {}
                    for d in o["deps"]:
                        p = ops[d]
                        s = p["sem"]
                        waits[s] = max(waits.get(s, 0), p["target"])
                    if o["dma"] and o["prev_target"] > 0:
                        s = o["sem"]
                        waits[s] = max(waits.get(s, 0), o["prev_target"])
                    for s, v in waits.items():
                        if seen.get(s, 0) >= v:
                            continue
                        seen[s] = v
                        eng.wait_ge(semof(s), v)
                    ins = o["fn"](eng)
                    if o["dma"]:
                        ins.then_inc(semof(o["sem"]), 16)
                    elif o["sig"]:
                        ins.then_inc(semof(o["sem"]), 1)
                if e in dsem:
                    for j in range(NDMASEM):
                        if dtarget[e][j] > 0 and seen.get(("d", e, j), 0) < dtarget[e][j]:
                            eng.wait_ge(dsem[e][j], dtarget[e][j])

            if per_eng["sp"]:
                block.sync(lambda eng: body("sp", eng))
            if per_eng["act"]:
                block.scalar(lambda eng: body("act", eng))
            if per_eng["dve"]:
                block.vector(lambda eng: body("dve", eng))
            if per_eng["pool"]:
                block.gpsimd(lambda eng: body("pool", eng))
            if per_eng["pe"]:
                block.tensor(lambda eng: body("pe", eng))


class Ring:
    def __init__(self, name, aps):
        self.name, self.aps, self.i = name, aps, 0

    def next(self):
        j = self.i % len(self.aps)
        self.i += 1
        return self.aps[j], (self.name, j)


def tgroups(n, step):
    return [(t, min(step, n - t)) for t in range(0, n, step)]


def xkeys(pref, dc, t0, n):
    return [(pref, dc, t) for t in range((t0 // 256) * 256, t0 + n, 256)]


class G:
    pass


def cast_op(P, g, i, dst, src, reads, writes):
    if i % 2 == 0:
        P.add("act", lambda e: e.copy(dst, src), reads=reads, writes=writes)
    else:
        P.add("dve", lambda e: e.tensor_copy(dst, src), reads=reads, writes=writes)


def emit_rstd(P, g, ssp, ssk, n, dim):
    rs, rsk = g.rstd.next()
    P.add("dve", lambda e: e.tensor_scalar(rs[:, 0:n], ssp[:, 0:n], 1.0 / dim, EPS, ALU.mult, ALU.add),
          reads=[ssk], writes=[rsk])
    P.add("act", lambda e: e.sqrt(rs[:, 0:n], rs[:, 0:n]), reads=[rsk], writes=[rsk])
    P.add("dve", lambda e: e.reciprocal(rs[:, 0:n], rs[:, 0:n]), reads=[rsk], writes=[rsk])
    return rs, rsk


def emit_adanorm(P, g, src, c0, nb, gcol, shcol, colkeys, hkey="h"):
    xTv = src.rearrange("(c p) t -> p c t", p=128)
    for (t0, n) in tgroups(nb, 256):
        xs, xk = g.stg.next()
        P.add("sp", lambda e, xs=xs, t0=t0, n=n: e.dma_start(
            out=xs[:, 0:DC * n].rearrange("p (c t) -> p c t", t=n), in_=xTv[:, :, c0 + t0:c0 + t0 + n]),
            reads=[(hkey, c0 + t0)], writes=[xk], dma=True)
        ssp, ssk = g.psr.next()
        for dc in range(DC):
            sq, sqk = g.sq.next()
            P.add("act", lambda e, sq=sq, xs=xs, dc=dc, n=n: e.activation(
                out=sq[:, 0:n], in_=xs[:, dc * n:(dc + 1) * n], func=AF.Square), reads=[xk], writes=[sqk])
            P.add("pe", lambda e, sq=sq, ssp=ssp, dc=dc, n=n: e.matmul(
                ssp[:, 0:n], lhsT=g.ones[:, :], rhs=sq[:, 0:n], start=(dc == 0), stop=(dc == DC - 1)),
                reads=[sqk, "ones"], writes=[ssk])
        rs, rsk = emit_rstd(P, g, ssp, ssk, n, D)
        for dc in range(DC):
            tm, tmk = g.tmp.next()
            P.add("dve", lambda e, tm=tm, xs=xs, rs=rs, dc=dc, n=n: e.scalar_tensor_tensor(
                out=tm[:, 0:n], in0=xs[:, dc * n:(dc + 1) * n], scalar=gcol(dc), in1=rs[:, 0:n],
                op0=ALU.mult, op1=ALU.mult), reads=[xk, rsk] + colkeys, writes=[tmk])
            P.add("act", lambda e, tm=tm, dc=dc, n=n, t0=t0: e.activation(
                out=g.xn[:, dc * nb + t0: dc * nb + t0 + n], in_=tm[:, 0:n], func=AF.Identity,
                bias=shcol(dc), scale=1.0), reads=[tmk] + colkeys, writes=[("xn", dc, t0)])


def emit_linear(P, g, xn, KC, nb, wsrc, NOC, epi, xpref="xn", step=512):
    for oc in range(NOC):
        s, sk = g.stg.next()
        P.add("sp", lambda e, s=s, oc=oc: e.dma_start(out=s[:, 0:KC * 128], in_=wsrc[oc]), writes=[sk], dma=True)
        w, wk = g.wbr.next()
        cast_op(P, g, oc, w[:, 0:KC * 128], s[:, 0:KC * 128], [sk], [wk])
        for (t0, n) in tgroups(nb, step):
            pp, ppk = g.psr.next()
            for kc in range(KC):
                P.add("pe", lambda e, pp=pp, w=w, kc=kc, t0=t0, n=n: e.matmul(
                    pp[:, 0:n], lhsT=w[:, kc * 128:(kc + 1) * 128],
                    rhs=xn[:, kc * nb + t0: kc * nb + t0 + n], start=(kc == 0), stop=(kc == KC - 1)),
                    reads=[wk] + xkeys(xpref, kc, t0, n), writes=[ppk])
            epi(oc, t0, n, pp, ppk)


def epi_store(P, g, dst, c0, okey, scale=None):
    def epi(oc, t0, n, pp, ppk):
        o, ok = g.ost.next()
        if (oc + t0 // 512) % 2 == 0:
            P.add("act", lambda e: e.copy(o[:, 0:n], pp[:, 0:n]), reads=[ppk], writes=[ok])
        else:
            P.add("dve", lambda e: e.tensor_copy(o[:, 0:n], pp[:, 0:n]), reads=[ppk], writes=[ok])
        P.add(ST, lambda e: e.dma_start(out=dst[oc * 128:(oc + 1) * 128, c0 + t0:c0 + t0 + n], in_=o[:, 0:n]),
              reads=[ok], writes=[(okey, oc, c0 + t0)], dma=True)
    return epi


def epi_residual(P, g, h, c0, gtcol, colkeys):
    def epi(oc, t0, n, pp, ppk):
        o, ok = g.ost.next()
        P.add("sp", lambda e: e.dma_start(out=o[:, 0:n], in_=h[oc * 128:(oc + 1) * 128, c0 + t0:c0 + t0 + n]),
              reads=[("hc", oc, c0 + t0)], writes=[ok], dma=True)
        P.add("dve", lambda e: e.scalar_tensor_tensor(
            out=o[:, 0:n], in0=pp[:, 0:n], scalar=gtcol(oc), in1=o[:, 0:n], op0=ALU.mult, op1=ALU.add),
            reads=[ppk, ok] + colkeys, writes=[ok])
        P.add(ST, lambda e: e.dma_start(out=h[oc * 128:(oc + 1) * 128, c0 + t0:c0 + t0 + n], in_=o[:, 0:n]),
              reads=[ok], writes=[("hc", oc, c0 + t0)] + [("h", c0 + t) for t in range((t0 // 256) * 256, t0 + n, 256)],
              dma=True)
    return epi


def modcol(g, l, kind, s, t, dc):
    j = ((l * NMODC + s * 48 + t * 16 + dc) * 2) + kind
    return g.modc[:, j:j + 1]


def emit_cols(P, g, l, s, kind, gsrc, half_gate):
    sc = g.modc[:, :].rearrange("p (l c r) -> p l c r", l=4, r=2)[:, l, s * 48 + 16: s * 48 + 32, kind]
    gt = g.modc[:, :].rearrange("p (l c r) -> p l c r", l=4, r=2)[:, l, s * 48 + 32: s * 48 + 48, kind]
    P.add("dve", lambda e: e.scalar_tensor_tensor(out=g.dcol[:, 0:16], in0=sc, scalar=1.0, in1=gsrc,
                                                   op0=ALU.add, op1=ALU.mult), reads=["modc", "gcols"], writes=["dcol"])
    P.add("dve", lambda e: e.tensor_scalar_mul(g.dcol[:, 16:32], gt, 0.5 if half_gate else 1.0),
          reads=["modc", "dcol"], writes=["dcol"])


def stage_mod(P, g):
    P.add("sp", lambda e: e.dma_start(out=g.cs[:, :], in_=g.d_cT), writes=["cs"], dma=True)
    P.add("act", lambda e: e.activation(out=g.cs[:, :], in_=g.cs[:, :], func=AF.Silu), reads=["cs"], writes=["cs"])
    g.modb = carve(g, 0, 4 * NMODC, F32)
    P.add("sp", lambda e: e.dma_start(out=g.modb[:, :], in_=g.d_modb), writes=["modb"], dma=True)
    for l in range(4):
        wv = g.d_modw[l].rearrange("(c p) n -> p c n", p=128)
        for nb_ in range(36):
            for hf in range(2):
                s, sk = g.stg.next()
                P.add("sp", lambda e, s=s, nb_=nb_, hf=hf, wv=wv: e.dma_start(
                    out=s[:, :].rearrange("p (c n) -> p c n", n=512), in_=wv[:, hf * 8:(hf + 1) * 8, nb_ * 512:(nb_ + 1) * 512]),
                    writes=[sk], dma=True)
                for j in range(4):
                    pp = g.ps[(nb_ * 4 + j) % 8]
                    ppk = ("ps", (nb_ * 4 + j) % 8)
                    for k8 in range(8):
                        kc = hf * 8 + k8
                        P.add("pe", lambda e, s=s, pp=pp, j=j, k8=k8, kc=kc: e.matmul(
                            pp[:, 0:2], lhsT=s[:, k8 * 512 + j * 128: k8 * 512 + (j + 1) * 128],
                            rhs=g.cs[:, kc * 2:(kc + 1) * 2], start=(kc == 0), stop=(kc == 15)),
                            reads=[sk, "cs"], writes=[ppk])
                    if hf == 1:
                        ch = l * NMODC + nb_ * 4 + j
                        P.add("dve", lambda e, pp=pp, ch=ch: e.tensor_tensor(
                            g.modc[:, ch * 2:ch * 2 + 2], pp[:, 0:2],
                            g.modb[:, ch:ch + 1].to_broadcast([128, 2]), ALU.add),
                            reads=[ppk, "modb"], writes=["modc"])


def stage_ffn(P, g, l, which, skip_ctx=False):
    s = 0 if which == 0 else 2
    wgu, wdn = (g.d_wgu1[l], g.d_wdn1[l]) if which == 0 else (g.d_wgu2[l], g.d_wdn2[l])
    gsrc = g.gcols[:, (l * 3 + s) * 16:(l * 3 + s + 1) * 16]
    for (c0, nb, kind) in BLOCKS:
        if kind == 1 and skip_ctx:
            continue
        ffn_block(P, g, l, s, wgu, wdn, gsrc, c0, nb, kind)
    P.barrier()


def ffn_block(P, g, l, s, wgu, wdn, gsrc, c0, nb, kind):
    deep = (nb <= 256)
    if deep:
        base = (DC + FC) * TB * 2
        slots = [carve(g, base + i * 16384, 4096, F32) for i in range(3)]
        slots += [carve(g, 57344 + i * 16384, 4096, F32) for i in range(4)]
        g.stg = Ring("stage", slots)
    if True:
        emit_cols(P, g, l, s, kind, gsrc, True)
        gcol = lambda dc: g.dcol[:, dc:dc + 1]
        hgcol = lambda dc: g.dcol[:, 16 + dc:17 + dc]
        shcol = lambda dc, kind=kind: modcol(g, l, kind, s, 0, dc)
        emit_adanorm(P, g, g.d_h, c0, nb, gcol, shcol, ["dcol", "modc"])
        TG = tgroups(nb, 512)
        for fc in range(FC):
            st_, sk = g.stg.next()
            P.add("sp", lambda e, st_=st_, fc=fc: e.dma_start(out=st_[:, :], in_=wgu[fc]), writes=[sk], dma=True)
            w, wk = g.wbr.next()
            P.add("act", lambda e, w=w, st_=st_: e.copy(w[:, 0:2048], st_[:, 0:2048]), reads=[sk], writes=[(wk, 0)])
            P.add("dve", lambda e, w=w, st_=st_: e.tensor_copy(w[:, 2048:4096], st_[:, 2048:4096]), reads=[sk], writes=[(wk, 1)])
            for (t0, n) in TG:
                pg, pgk = g.psr.next()
                pu, puk = g.psr.next()
                for hh, (pp, ppk) in enumerate(((pg, pgk), (pu, puk))):
                    for dc in range(DC):
                        P.add("pe", lambda e, pp=pp, w=w, hh=hh, dc=dc, t0=t0, n=n: e.matmul(
                            pp[:, 0:n], lhsT=w[:, (hh * DC + dc) * 128:(hh * DC + dc + 1) * 128],
                            rhs=g.xn[:, dc * nb + t0: dc * nb + t0 + n], start=(dc == 0), stop=(dc == DC - 1)),
                            reads=[(wk, hh)] + xkeys("xn", dc, t0, n), writes=[ppk])
                g_, gk = g.sg.next()
                P.add("act", lambda e, g_=g_, pg=pg, n=n: e.activation(out=g_[:, 0:n], in_=pg[:, 0:n], func=AF.Silu),
                      reads=[pgk], writes=[gk])
                P.add("dve", lambda e, g_=g_, pu=pu, n=n, fc=fc, t0=t0: e.tensor_tensor(
                    g.act[:, fc * nb + t0: fc * nb + t0 + n], pu[:, 0:n], g_[:, 0:n], ALU.mult),
                    reads=[puk, gk], writes=[("act", fc, t0)])
        for dc in range(DC):
            ws, wks = [], []
            for hf in range(2):
                st_, sk = g.stg.next()
                P.add("sp", lambda e, st_=st_, dc=dc, hf=hf: e.dma_start(
                    out=st_[:, 0:2816], in_=wdn[dc][:, hf * 2816:(hf + 1) * 2816]), writes=[sk], dma=True)
                w, wk = g.wbr.next()
                cast_op(P, g, hf, w[:, 0:2816], st_[:, 0:2816], [sk], [wk])
                ws.append(w)
                wks.append(wk)
            for (t0, n) in TG:
                xr, xrk = g.ost.next()
                P.add("sp", lambda e, xr=xr, dc=dc, t0=t0, n=n: e.dma_start(
                    out=xr[:, 0:n], in_=g.d_h[dc * 128:(dc + 1) * 128, c0 + t0:c0 + t0 + n]),
                    reads=[("hc", dc, c0 + t0)], writes=[xrk], dma=True)
                pp, ppk = g.psr.next()
                for fc in range(FC):
                    w, wk = ws[fc // 22], wks[fc // 22]
                    P.add("pe", lambda e, pp=pp, w=w, fc=fc, t0=t0, n=n: e.matmul(
                        pp[:, 0:n], lhsT=w[:, (fc % 22) * 128:(fc % 22 + 1) * 128],
                        rhs=g.act[:, fc * nb + t0: fc * nb + t0 + n], start=(fc == 0), stop=(fc == FC - 1)),
                        reads=[wk, ("act", fc, t0)], writes=[ppk])
                P.add("dve", lambda e, xr=xr, pp=pp, n=n, dc=dc: e.scalar_tensor_tensor(
                    out=xr[:, 0:n], in0=pp[:, 0:n], scalar=hgcol(dc), in1=xr[:, 0:n],
                    op0=ALU.mult, op1=ALU.add), reads=[ppk, xrk, "dcol"], writes=[xrk])
                P.add(ST, lambda e, xr=xr, dc=dc, t0=t0, n=n: e.dma_start(
                    out=g.d_h[dc * 128:(dc + 1) * 128, c0 + t0:c0 + t0 + n], in_=xr[:, 0:n]), reads=[xrk],
                    writes=[("hc", dc, c0 + t0)] + [("h", c0 + t) for t in range((t0 // 256) * 256, t0 + n, 256)], dma=True)
    if deep:
        P.barrier()
        ffn_views(g)


def carve(g, off_bytes, ncols, dt):
    assert off_bytes % 4 == 0
    e0 = off_bytes // 2
    if dt == BF16:
        assert e0 + ncols <= ARENA
        return g.arena[:, e0:e0 + ncols]
    assert e0 + 2 * ncols <= ARENA
    return g.arena[:, e0:e0 + 2 * ncols].bitcast(F32)


def lin_views(g):
    g.xn = g.arena[:, 0:DC * TB]
    o_ = DC * TB * 2
    stage = [carve(g, o_ + i * 16384, 4096, F32) for i in range(6)]
    o_ += 6 * 16384
    wbf = [carve(g, o_ + i * 8192, 4096, BF16) for i in range(3)]
    g.stg, g.wbr = Ring("stage", stage[0:3] + stage[4:6]), Ring("wbf", wbf)
    g.ost = Ring("ost_l", [carve(g, DC * TB * 2 + 3 * 16384 + i * 2048, 512, F32) for i in range(8)])
    g.psr = Ring("ps", g.ps)


def ffn_views(g):
    g.xn = g.arena[:, 0:DC * TB]
    g.act = g.arena[:, DC * TB:(DC + FC) * TB]
    o_ = (DC + FC) * TB
    stage = [g.arena[:, o_ + i * 8192: o_ + (i + 1) * 8192].bitcast(F32) for i in range(3)]
    o_ += 3 * 8192
    wbf = [g.arena[:, o_ + i * 4096: o_ + (i + 1) * 4096] for i in range(2)]
    g.stg, g.wbr = Ring("stage", stage), Ring("wbf", wbf)
    g.psr = Ring("ps", g.ps)
    g.ost = Ring("ost", g.ost_global)


def stage_lbcols(P, g):
    P.add("sp", lambda e: e.dma_start(out=g.lbl[:, :], in_=g.d_lbl), writes=["lbl"], dma=True)
    P.add("pool", lambda e: e.memset(g.lbc[:, :], 0.0), writes=["lbc"])
    for d_ in range(2):
        b0 = ((1 * 2 + d_) * 3) * 16
        l0 = g.lbl[:, (d_ * 2 + 0) * 16:(d_ * 2 + 1) * 16]
        l1 = g.lbl[:, (d_ * 2 + 1) * 16:(d_ * 2 + 2) * 16]
        P.add("dve", lambda e, b0=b0, l0=l0, l1=l1: e.tensor_tensor(g.lbc[:, b0:b0 + 16], l1, l0, ALU.subtract),
              reads=["lbl", "lbc"], writes=["lbc"])
        P.add("act", lambda e, b0=b0: e.activation(out=g.lbc[:, b0:b0 + 16], in_=g.lbc[:, b0:b0 + 16], func=AF.Sigmoid),
              reads=["lbc"], writes=["lbc"])
    for j in range(2):
        for d_ in range(2):
            b0 = ((j * 2 + d_) * 3) * 16
            P.add("dve", lambda e, b0=b0: e.tensor_scalar(g.lbc[:, b0 + 16:b0 + 32], g.lbc[:, b0:b0 + 16], -1.0, 1.0, ALU.mult, ALU.add),
                  reads=["lbc"], writes=["lbc"])
            P.add("dve", lambda e, b0=b0: e.tensor_scalar(g.lbc[:, b0 + 32:b0 + 48], g.lbc[:, b0:b0 + 16], 1.0, -1.0, ALU.mult, ALU.add),
                  reads=["lbc"], writes=["lbc"])
    P.add("sp", lambda e: e.dma_start(out=g.cst[:, :], in_=g.d_cst), writes=["cst"], dma=True)
    P.add("act", lambda e: e.copy(g.identb[:, :], g.cst[:, 0:128]), reads=["cst"], writes=["identb"])


def stage_hgrn_inproj(P, g, l, j):
    lin_views(g)
    gsrc = g.gcols[:, (l * 3 + 1) * 16:(l * 3 + 2) * 16]
    for (c0, nb, kind) in BLOCKS:
        emit_cols(P, g, l, 1, kind, gsrc, False)
        gcol = lambda dc: g.dcol[:, dc:dc + 1]
        shcol = lambda dc, kind=kind: modcol(g, l, kind, 1, 0, dc)
        emit_adanorm(P, g, g.d_h, c0, nb, gcol, shcol, ["dcol", "modc"])
        emit_linear(P, g, g.xn, DC, nb, g.d_hgw_in[j], 80, epi_store(P, g, g.d_pj, c0, "pj"))
    P.barrier()


def stage_hgrn_scan(P, g, j, nheads=16):
    NTI = L // 128
    NCH = L // 32
    SZ = L * 4
    Q = carve(g, 0 * SZ, L, F32)
    Z = carve(g, 1 * SZ, L, F32)
    A = carve(g, 2 * SZ, L, F32)
    Bc = carve(g, 3 * SZ, L, F32)
    E = carve(g, 4 * SZ, L, F32)
    I = carve(g, 5 * SZ, L, F32)
    o_ = 6 * SZ
    qd = carve(g, o_, L, BF16); o_ += L * 2
    ki = carve(g, o_, L, BF16); o_ += L * 2
    vtok = carve(g, o_, L, BF16); o_ += L * 2
    ibf = carve(g, o_, L, BF16); o_ += L * 2
    dec = carve(g, o_, NCH, F32); o_ += NCH * 4
    S32 = [carve(g, o_ + i * 512, 128, F32) for i in range(4)]; o_ += 4 * 512
    T32 = [carve(g, o_ + i * 512, 128, F32) for i in range(4)]; o_ += 4 * 512
    Sbf = [carve(g, o_ + i * 256, 128, BF16) for i in range(4)]; o_ += 4 * 256
    ATm = [carve(g, o_ + i * 256, 128, BF16) for i in range(3)]; o_ += 3 * 256
    KT = [carve(g, o_ + i * 1024, 512, BF16) for i in range(3)]; o_ += 3 * 1024
    OS = [carve(g, o_ + i * 512, 128, F32) for i in range(4)]; o_ += 4 * 512
    assert o_ <= ARENA * 2
    s32r, t32r, sbfr, atmr, ktr, osr = Ring("S32", S32), Ring("T32", T32), Ring("Sbf", Sbf), Ring("ATm", ATm), Ring("KT", KT), Ring("OS", OS)
    par, ptr, por, pur = Ring("ps", g.ps[0:2]), Ring("ps2", g.ps[2:3]), Ring("ps3", g.ps[3:5]), Ring("ps5", g.ps[5:7])
    pv = g.ps[7]
    identb = g.identb
    maskf = [g.cst[:, 128:256], g.cst[:, 256:384]]
    m01 = g.cst[:, 384:512]
    HALF = L // 2
    for h in range(nheads):
        hs = slice(h * 128, (h + 1) * 128)
        for hf in range(2):
            cs_ = slice(hf * HALF, (hf + 1) * HALF)
            P.add("sp", lambda e, cs_=cs_, h=h: e.dma_start(out=Q[:, cs_], in_=g.d_pj[h * 128:(h + 1) * 128, cs_]),
                  reads=["pj_all"], writes=[("Q", hf)], dma=True)
            P.add("sp", lambda e, cs_=cs_, h=h: e.dma_start(out=I[:, cs_], in_=g.d_pj[6144 + h * 128:6144 + (h + 1) * 128, cs_]),
                  reads=["pj_all"], writes=[("I", hf)], dma=True)
            P.add("act", lambda e, cs_=cs_: e.activation(out=Q[:, cs_], in_=Q[:, cs_], func=AF.Silu), reads=[("Q", hf)], writes=[("Q", hf)])
            P.add("pool", lambda e, cs_=cs_: e.tensor_copy(ibf[:, cs_], I[:, cs_]), reads=[("I", hf)], writes=[("ibf", hf)])
        pvb = pv[:, :].bitcast(BF16)
        for t4 in range(0, NTI, 4):
            nt_ = min(4, NTI - t4)
            for q in range(nt_):
                ti = t4 + q
                P.add("pe", lambda e, q=q, ti=ti: e.transpose(pvb[:, q * 128:(q + 1) * 128], ibf[:, ti * 128:(ti + 1) * 128], identb[:, :]),
                      reads=[("ibf", 0), ("ibf", 1), "identb"], writes=["pv"])
            P.add("act", lambda e, t4=t4, nt_=nt_: e.copy(vtok[:, t4 * 128:(t4 + nt_) * 128], pvb[:, 0:nt_ * 128]),
                  reads=["pv"], writes=["vtok"])
        for d_ in range(2):
            lb0 = ((j * 2 + d_) * 3) * 16
            lbcol = g.lbc[:, lb0 + h: lb0 + h + 1]
            omlcol = g.lbc[:, lb0 + 16 + h: lb0 + 16 + h + 1]
            nomlcol = g.lbc[:, lb0 + 32 + h: lb0 + 32 + h + 1]
            zrow = 2048 * (1 + d_) + h * 128
            P.add("sp", lambda e, zrow=zrow: e.dma_start(out=Z[:, :], in_=g.d_pj[zrow:zrow + 128, :]), reads=["pj_all"], writes=["Z"], dma=True)
            P.add("act", lambda e: e.activation(out=Z[:, :], in_=Z[:, :], func=AF.Sigmoid), reads=["Z"], writes=["Z"])
            P.add("dve", lambda e, omlcol=omlcol, lbcol=lbcol: e.tensor_scalar(A[:, :], Z[:, :], omlcol, lbcol, ALU.mult, ALU.add),
                  reads=["Z", "lbc"], writes=["A"])
            P.add("act", lambda e: e.activation(out=A[:, :], in_=A[:, :], func=AF.Ln), reads=["A"], writes=["A"])
            P.add("dve", lambda e, omlcol=omlcol, nomlcol=nomlcol: e.tensor_scalar(Z[:, :], Z[:, :], nomlcol, omlcol, ALU.mult, ALU.add),
                  reads=["Z", "lbc"], writes=["Z"])
            for (t0, n) in tgroups(L, 128):
                P.add("dve", lambda e, t0=t0, n=n: e.tensor_tensor_scan(Bc[:, t0:t0 + n], m01[:, 0:n], A[:, t0:t0 + n], 0.0, ALU.mult, ALU.add),
                      reads=["A", "cst"], writes=["Bc"])
            P.add("act", lambda e: e.activation(out=dec[:, :], in_=Bc[:, :].rearrange("p (n c) -> p n c", c=32)[:, :, 31], func=AF.Exp),
                  reads=["Bc"], writes=["dec"])
            if d_ == 1:
                P.add("dve", lambda e: e.tensor_tensor(Bc[:, :], Bc[:, :], A[:, :], ALU.subtract), reads=["Bc", "A", "dec"], writes=["Bc"])
            sq_, sk_ = (1.0, -1.0) if d_ == 0 else (-1.0, 1.0)
            P.add("act", lambda e, sq_=sq_: e.activation(out=E[:, :], in_=Bc[:, :], func=AF.Exp, scale=sq_), reads=["Bc"], writes=["E"])
            P.add("dve", lambda e: e.tensor_tensor(qd[:, :], Q[:, :], E[:, :], ALU.mult), reads=["E", ("Q", 0), ("Q", 1)], writes=["qd"])
            P.add("act", lambda e, sk_=sk_: e.activation(out=E[:, :], in_=Bc[:, :], func=AF.Exp, scale=sk_), reads=["Bc", "qd"], writes=["E"])
            P.add("dve", lambda e: e.tensor_tensor(ki[:, :], Z[:, :], E[:, :], ALU.mult), reads=["E", "Z"], writes=["ki"])
            s_cur, s_cur_k = s32r.next()
            P.add("pool", lambda e, s_cur=s_cur: e.memset(s_cur[:, :], 0.0), writes=[s_cur_k])
            sb_cur, sb_cur_k = sbfr.next()
            P.add("pool", lambda e, sb_cur=sb_cur: e.memset(sb_cur[:, :], 0.0), writes=[sb_cur_k])
            if d_ == 0:
                order = list(range(NTI))
            else:
                order = [1, 0] + list(range(NTI - 1, 1, -1))
            dst = g.d_of if d_ == 0 else g.d_ob
            fr = {}

            def front(ti):
                ts_ = slice(ti * 128, (ti + 1) * 128)
                pa, pak = par.next()
                P.add("pe", lambda e: e.matmul(pa[:, 0:128], lhsT=ki[:, ts_], rhs=qd[:, ts_], start=True, stop=True),
                      reads=["ki", "qd"], writes=[pak])
                am, amk = atmr.next()
                P.add("dve", lambda e: e.tensor_tensor(am[:, :], pa[:, 0:128], maskf[d_], ALU.mult), reads=[pak, "cst"], writes=[amk])
                pt, ptk = ptr.next()
                ptb = pt[:, :].bitcast(BF16)
                P.add("pe", lambda e: e.transpose(ptb[:, 0:128], ki[:, ts_], identb[:, :]), reads=["ki", "identb"], writes=[ptk])
                kt, ktk = ktr.next()
                for c in range(4):
                    P.add("act", lambda e, c=c: e.activation(out=kt[:, c * 128:(c + 1) * 128], in_=ptb[:, 0:128], func=AF.Identity,
                                                            scale=g.cst[:, 512 + c:513 + c]), reads=[ptk, "cst"], writes=[(ktk, c)])
                po, pok = por.next()
                P.add("pe", lambda e: e.matmul(po[:, 0:128], lhsT=vtok[:, ts_], rhs=am[:, :], start=True, stop=False),
                      reads=["vtok", amk], writes=[pok])
                pu, puk = pur.next()
                for c in range(4):
                    P.add("pe", lambda e, c=c: e.matmul(pu[:, c * 128:(c + 1) * 128], lhsT=kt[:, c * 128:(c + 1) * 128], rhs=vtok[:, ts_],
                                                        start=True, stop=True), reads=[(ktk, c), "vtok"], writes=[(puk, c)])
                fr[ti] = (po, pok, pu, puk)

            def chain(ti, s_cur, s_cur_k, sb_cur, sb_cur_k):
                po, pok, pu, puk = fr.pop(ti)
                corder = range(4) if d_ == 0 else range(3, -1, -1)
                for ci, c in enumerate(corder):
                    ch = ti * 4 + c
                    cs_ = slice(ti * 128 + c * 32, ti * 128 + c * 32 + 32)
                    last = (ci == 3)
                    if d_ == 0:
                        P.add("pe", lambda e, c=c, sb_cur=sb_cur, cs_=cs_, last=last: e.matmul(
                            po[:, c * 32:(c + 1) * 32], lhsT=sb_cur[:, :], rhs=qd[:, cs_], start=False, stop=last),
                            reads=[sb_cur_k, "qd"], writes=[pok])
                        tt, ttk = t32r.next()
                        P.add("dve", lambda e, tt=tt, s_cur=s_cur, c=c: e.tensor_tensor(tt[:, :], pu[:, c * 128:(c + 1) * 128], s_cur[:, :], ALU.add),
                              reads=[(puk, c), s_cur_k], writes=[ttk])
                        s_new, s_new_k = s32r.next()
                        P.add("dve", lambda e, tt=tt, s_new=s_new, ch=ch: e.tensor_scalar_mul(s_new[:, :], tt[:, :], dec[:, ch:ch + 1]),
                              reads=[ttk, "dec"], writes=[s_new_k])
                        sb_new, sb_new_k = sbfr.next()
                        P.add("act", lambda e, sb_new=sb_new, s_new=s_new: e.copy(sb_new[:, :], s_new[:, :]), reads=[s_new_k], writes=[sb_new_k])
                    else:
                        tt, ttk = t32r.next()
                        P.add("dve", lambda e, tt=tt, s_cur=s_cur, ch=ch: e.tensor_scalar_mul(tt[:, :], s_cur[:, :], dec[:, ch:ch + 1]),
                              reads=[s_cur_k, "dec"], writes=[ttk])
                        sb_new, sb_new_k = sbfr.next()
                        P.add("act", lambda e, sb_new=sb_new, tt=tt: e.copy(sb_new[:, :], tt[:, :]), reads=[ttk], writes=[sb_new_k])
                        P.add("pe", lambda e, c=c, sb_new=sb_new, cs_=cs_, last=last: e.matmul(
                            po[:, c * 32:(c + 1) * 32], lhsT=sb_new[:, :], rhs=qd[:, cs_], start=False, stop=last),
                            reads=[sb_new_k, "qd"], writes=[pok])
                        s_new, s_new_k = s32r.next()
                        P.add("dve", lambda e, tt=tt, s_new=s_new, c=c: e.tensor_tensor(s_new[:, :], pu[:, c * 128:(c + 1) * 128], tt[:, :], ALU.add),
                              reads=[(puk, c), ttk], writes=[s_new_k])
                    s_cur, s_cur_k, sb_cur, sb_cur_k = s_new, s_new_k, sb_new, sb_new_k
                os_, osk = osr.next()
                P.add("act", lambda e: e.copy(os_[:, :], po[:, 0:128]), reads=[pok], writes=[osk])
                P.add(ST, lambda e: e.dma_start(out=dst[h * 128:(h + 1) * 128, ti * 128:(ti + 1) * 128], in_=os_[:, :]),
                      reads=[osk], writes=[("o", d_, h, ti)], dma=True)
                return s_cur, s_cur_k, sb_cur, sb_cur_k

            front(order[0])
            for i_, ti in enumerate(order):
                if i_ + 1 < len(order):
                    front(order[i_ + 1])
                s_cur, s_cur_k, sb_cur, sb_cur_k = chain(ti, s_cur, s_cur_k, sb_cur, sb_cur_k)
    P.barrier()


def stage_hgrn_readout(P, g, l, j, skip_ctx):
    lin_views(g)
    gtcol = None
    for (c0, nb, kind) in BLOCKS:
        if kind == 1 and skip_ctx:
            continue
        hgrn_readout_block(P, g, l, j, c0, nb, kind)
    P.barrier()


def hgrn_readout_block(P, g, l, j, c0, nb, kind):
    gtcol = lambda oc: modcol(g, l, kind, 1, 2, oc)
    ofv = g.d_of.rearrange("(c p) t -> p c t", p=128)
    obv = g.d_ob.rearrange("(c p) t -> p c t", p=128)
    gtv = g.d_pj[8192:10240, :].rearrange("(c p) t -> p c t", p=128)
    for (t0, n) in tgroups(nb, 256):
        s1, k1 = g.stg.next()
        s2, k2 = g.stg.next()
        s3, k3 = g.stg.next()
        for (sx, kx, src) in ((s1, k1, ofv), (s2, k2, obv), (s3, k3, gtv)):
            P.add("sp", lambda e, sx=sx, src=src: e.dma_start(
                out=sx[:, 0:DC * n].rearrange("p (c t) -> p c t", t=n), in_=src[:, :, c0 + t0:c0 + t0 + n]), writes=[kx], dma=True)
        P.add("pool", lambda e: e.tensor_tensor(s1[:, 0:DC * n], s1[:, 0:DC * n], s2[:, 0:DC * n], ALU.add), reads=[k1, k2], writes=[k1])
        P.add("act", lambda e: e.activation(out=s3[:, 0:DC * n], in_=s3[:, 0:DC * n], func=AF.Silu), reads=[k3], writes=[k3])
        for dc in range(DC):
            sq_, sqk = g.sq.next()
            ssp, ssk = g.psr.next()
            P.add("act", lambda e, sq_=sq_, dc=dc: e.activation(out=sq_[:, 0:n], in_=s1[:, dc * n:(dc + 1) * n], func=AF.Square),
                  reads=[k1], writes=[sqk])
            P.add("pe", lambda e, sq_=sq_, ssp=ssp: e.matmul(ssp[:, 0:n], lhsT=g.ones[:, :], rhs=sq_[:, 0:n], start=True, stop=True),
                  reads=[sqk, "ones"], writes=[ssk])
            rs, rsk = emit_rstd(P, g, ssp, ssk, n, 128)
            tm, tmk = g.tmp.next()
            P.add("dve", lambda e, tm=tm, rs=rs, dc=dc: e.scalar_tensor_tensor(
                out=tm[:, 0:n], in0=s1[:, dc * n:(dc + 1) * n], scalar=g.hgn[:, j * 16 + dc: j * 16 + dc + 1], in1=rs[:, 0:n],
                op0=ALU.mult, op1=ALU.mult), reads=[k1, rsk, "hgn"], writes=[tmk])
            P.add("pool", lambda e, tm=tm, dc=dc: e.tensor_tensor(
                g.xn[:, dc * nb + t0: dc * nb + t0 + n], tm[:, 0:n], s3[:, dc * n:(dc + 1) * n], ALU.mult),
                reads=[tmk, k3], writes=[("xn", dc, t0)])
    emit_linear(P, g, g.xn, DC, nb, g.d_hgw_out[j], DC, epi_residual(P, g, g.d_h, c0, gtcol, ["modc"]))


MLA_SCALE = (128 + 64) ** -0.5


def stage_mla_proj(P, g, l):
    gsrc = g.gcols[:, (l * 3 + 1) * 16:(l * 3 + 2) * 16]
    for (c0, nb, kind) in BLOCKS:
        mla_proj_block(P, g, l, gsrc, c0, nb, kind)
    P.barrier()


def mla_proj_block(P, g, l, gsrc, c0, nb, kind):
    g.xn = g.arena[:, 0:DC * TB]
    cbuf = carve(g, 32768, 8 * TB, F32)
    cn = carve(g, 65536, 8 * TB, BF16)
    rope = carve(g, 81920, 2 * TB, F32)
    vbf = [carve(g, 90112 + i * 2048, TB, BF16) for i in range(2)]
    obf = [carve(g, 94208 + i * 1024, 512, BF16) for i in range(4)]
    stage = [carve(g, 98304 + i * 16384, 4096, F32) for i in range(3)]
    wbf = [carve(g, 147456 + i * 8192, 4096, BF16) for i in range(3)]
    rt = [carve(g, 172032 + i * 2048, 512, F32) for i in range(4)]
    g.stg, g.wbr = Ring("stage", stage), Ring("wbf", wbf)
    g.psr = Ring("ps", g.ps[0:7])
    vbr, obr, rtr = Ring("vbf", vbf), Ring("obf", obf), Ring("rt", rt)
    emit_cols(P, g, l, 1, kind, gsrc, False)
    gcol = lambda dc: g.dcol[:, dc:dc + 1]
    shcol = lambda dc: modcol(g, l, kind, 1, 0, dc)
    emit_adanorm(P, g, g.d_h, c0, nb, gcol, shcol, ["dcol", "modc"])
    P.add("sp", lambda e: e.dma_start(out=rope[:, 0:2 * nb].rearrange("p (a t) -> p a t", a=2),
                                      in_=g.d_rope.rearrange("p (a t) -> p a t", a=2)[:, :, c0:c0 + nb]), writes=["rope"], dma=True)
    pvb = g.ps[7][:, :].bitcast(BF16)

    def rope_epi(dst, first):
        st_ = {}

        def epi(oc, t0, n, pp, ppk):
            if first(oc):
                r1, r1k = rtr.next()
                P.add("dve", lambda e: e.tensor_tensor(r1[:, 0:n], pp[:, 0:n], rope[:, t0:t0 + n], ALU.mult), reads=[ppk, "rope"], writes=[r1k])
                st_[t0] = (r1, r1k)
            else:
                r1, r1k = st_.pop(t0)
                r2, r2k = rtr.next()
                P.add("dve", lambda e: e.tensor_tensor(r2[:, 0:n], pp[:, 0:n], rope[:, nb + t0:nb + t0 + n], ALU.mult), reads=[ppk, "rope"], writes=[r2k])
                o, ok = obr.next()
                P.add("pool", lambda e: e.tensor_tensor(o[:, 0:n], r1[:, 0:n], r2[:, 0:n], ALU.add), reads=[r1k, r2k], writes=[ok])
                P.add(ST, lambda e: e.dma_start(out=dst(oc)[:, c0 + t0:c0 + t0 + n], in_=o[:, 0:n]), reads=[ok], writes=[("mla_o", oc, c0 + t0)], dma=True)
        return epi

    def store_bf(dst):
        def epi(oc, t0, n, pp, ppk):
            o, ok = obr.next()
            if (oc + t0 // 512) % 2 == 0:
                P.add("act", lambda e: e.copy(o[:, 0:n], pp[:, 0:n]), reads=[ppk], writes=[ok])
            else:
                P.add("dve", lambda e: e.tensor_copy(o[:, 0:n], pp[:, 0:n]), reads=[ppk], writes=[ok])
            P.add(ST, lambda e: e.dma_start(out=dst(oc)[:, c0 + t0:c0 + t0 + n], in_=o[:, 0:n]), reads=[ok], writes=[("mla_o2", oc, c0 + t0)], dma=True)
        return epi

    krope = rope_epi(lambda oc: g.d_kr, lambda oc: oc == 8)

    def epi1(oc, t0, n, pp, ppk):
        if oc < 8:
            if oc % 2 == 0:
                P.add("act", lambda e: e.copy(cbuf[:, oc * nb + t0: oc * nb + t0 + n], pp[:, 0:n]), reads=[ppk], writes=[("cb", oc, t0)])
            else:
                P.add("dve", lambda e: e.tensor_copy(cbuf[:, oc * nb + t0: oc * nb + t0 + n], pp[:, 0:n]), reads=[ppk], writes=[("cb", oc, t0)])
        else:
            krope(oc, t0, n, pp, ppk)
    emit_linear(P, g, g.xn, DC, nb, g.d_wdqkv, 10, epi1)
    for grp in range(2):
        for (t0, n) in tgroups(nb, 256):
            ssp, ssk = g.psr.next()
            for q in range(4):
                oc = grp * 4 + q
                sq, sqk = g.sq.next()
                P.add("act", lambda e, sq=sq, oc=oc: e.activation(out=sq[:, 0:n], in_=cbuf[:, oc * nb + t0: oc * nb + t0 + n], func=AF.Square),
                      reads=[("cb", oc, (t0 // 512) * 512)], writes=[sqk])
                P.add("pe", lambda e, sq=sq, q=q: e.matmul(ssp[:, 0:n], lhsT=g.ones[:, :], rhs=sq[:, 0:n], start=(q == 0), stop=(q == 3)),
                      reads=[sqk, "ones"], writes=[ssk])
            rs, rsk = emit_rstd(P, g, ssp, ssk, n, 512)
            for q in range(4):
                oc = grp * 4 + q
                P.add("dve", lambda e, rs=rs, oc=oc: e.scalar_tensor_tensor(
                    out=cn[:, oc * nb + t0: oc * nb + t0 + n], in0=cbuf[:, oc * nb + t0: oc * nb + t0 + n],
                    scalar=g.mlan[:, oc:oc + 1], in1=rs[:, 0:n], op0=ALU.mult, op1=ALU.mult),
                    reads=[("cb", oc, (t0 // 512) * 512), rsk, "mlan"], writes=[("cn" if oc < 4 else "cn4", oc % 4, t0)])
    qrope = rope_epi(lambda oc: g.d_qr[oc // 3], lambda oc: oc % 3 == 1)
    qn_store = store_bf(lambda oc: g.d_qn[oc // 3])

    def epi3(oc, t0, n, pp, ppk):
        if oc % 3 == 0:
            qn_store(oc, t0, n, pp, ppk)
        else:
            qrope(oc, t0, n, pp, ppk)
    emit_linear(P, g, cn, 4, nb, g.d_wuq, 48, epi3, xpref="cn")
    kn_store = store_bf(lambda oc: g.d_kn[oc // 2])

    def epi4(oc, t0, n, pp, ppk):
        if oc % 2 == 0:
            kn_store(oc, t0, n, pp, ppk)
        else:
            hh = oc // 2
            v, vk = vbr.next()
            P.add("act", lambda e: e.copy(v[:, 0:n], pp[:, 0:n]), reads=[ppk], writes=[vk])
            for q in range(n // 128):
                P.add("pe", lambda e, q=q: e.transpose(pvb[:, q * 128:(q + 1) * 128], v[:, q * 128:(q + 1) * 128], g.identb[:, :]),
                      reads=[vk, "identb"], writes=["pv"])
            o, ok = obr.next()
            P.add("dve", lambda e: e.tensor_copy(o[:, 0:n], pvb[:, 0:n]), reads=["pv"], writes=[ok])
            tb0 = (c0 + t0) // 128
            P.add(ST, lambda e: e.dma_start(
                out=g.d_vt[hh].rearrange("(n p) v -> p n v", p=128)[:, tb0:tb0 + n // 128, :],
                in_=o[:, 0:n].rearrange("p (n v) -> p n v", v=128)), reads=[ok], writes=[("mla_v", oc, c0 + t0)], dma=True)
    emit_linear(P, g, cn[:, 4 * nb:8 * nb], 4, nb, g.d_wukv, 32, epi4, xpref="cn4")


def stage_mla_attn(P, g, last):
    NTI = L // 128
    Kr = carve(g, 0, L, BF16)
    o_ = L * 2
    Kn = [carve(g, o_ + i * L * 2, L, BF16) for i in range(2)]; o_ += 2 * L * 2
    Vt = [carve(g, o_ + i * L * 2, L, BF16) for i in range(2)]; o_ += 2 * L * 2
    Qn = [carve(g, o_ + i * L * 2, L, BF16) for i in range(2)]; o_ += 2 * L * 2
    Qr = [carve(g, o_ + i * L * 2, L, BF16) for i in range(2)]; o_ += 2 * L * 2
    pT = [carve(g, o_ + i * 1024, 512, BF16) for i in range(4)]; o_ += 4 * 1024
    rl = [carve(g, o_ + i * 2048, 512, F32) for i in range(2)]; o_ += 2 * 2048
    oo = [carve(g, o_ + i * 2048, 512, F32) for i in range(3)]; o_ += 3 * 2048
    onesb = carve(g, o_, 128, BF16); o_ += 256
    assert o_ <= ARENA * 2
    knr, vtr, qnr, qrr, ptr_, rlr, oor = Ring("Kn", Kn), Ring("Vt", Vt), Ring("Qn", Qn), Ring("Qr", Qr), Ring("pT", pT), Ring("rl", rl), Ring("oo", oo)
    psr_s, psr_o, psr_l = Ring("ps", g.ps[0:3]), Ring("ps3", g.ps[3:5]), Ring("ps5", g.ps[5:7])
    P.add("pool", lambda e: e.memset(onesb[:, :], 1.0), writes=["onesb"])
    P.add("sp", lambda e: e.dma_start(out=Kr[:, :], in_=g.d_kr), writes=["Kr"], dma=True)
    qblocks = [(NCTX + 512 * i, 512, 0, NTI) for i in range(NLAT // 512)]
    if not last:
        qblocks.append((0, NCTX, 0, NCTX // 128))
    for h in range(16):
        kn, knk = knr.next()
        vt, vtk = vtr.next()
        qn, qnk = qnr.next()
        qr, qrk = qrr.next()
        P.add("sp", lambda e, kn=kn, h=h: e.dma_start(out=kn[:, :], in_=g.d_kn[h]), writes=[knk], dma=True)
        P.add("sp", lambda e, vt=vt, h=h: e.dma_start(out=vt[:, :].rearrange("p (n v) -> p n v", v=128),
                                                     in_=g.d_vt[h].rearrange("(n p) v -> p n v", p=128)), writes=[vtk], dma=True)
        P.add("sp", lambda e, qn=qn, h=h: e.dma_start(out=qn[:, :], in_=g.d_qn[h]), writes=[qnk], dma=True)
        P.add("sp", lambda e, qr=qr, h=h: e.dma_start(out=qr[:, :], in_=g.d_qr[h]), writes=[qrk], dma=True)
        for (q0, nq, k0, k1) in qblocks:
            po, pok = psr_o.next()
            pl, plk = psr_l.next()
            def scores(kt):
                ks = slice(kt * 128, (kt + 1) * 128)
                ps_, psk = psr_s.next()
                P.add("pe", lambda e: e.matmul(
                    ps_[:, 0:nq], lhsT=kn[:, ks], rhs=qn[:, q0:q0 + nq], start=True, stop=False), reads=[knk, qnk], writes=[psk])
                P.add("pe", lambda e: e.matmul(
                    ps_[:, 0:nq], lhsT=Kr[:, ks], rhs=qr[:, q0:q0 + nq], start=False, stop=True), reads=["Kr", qrk], writes=[psk])
                return ps_, psk

            nxt = scores(k0)
            for kt in range(k0, k1):
                ks = slice(kt * 128, (kt + 1) * 128)
                ps_, psk = nxt
                if kt + 1 < k1:
                    nxt = scores(kt + 1)
                p_, pk = ptr_.next()
                P.add("act", lambda e, p_=p_, ps_=ps_, nq=nq: e.activation(out=p_[:, 0:nq], in_=ps_[:, 0:nq], func=AF.Exp, scale=MLA_SCALE),
                      reads=[psk], writes=[pk])
                P.add("pe", lambda e, p_=p_, po=po, vt=vt, ks=ks, nq=nq, kt=kt, k0=k0, k1=k1: e.matmul(
                    po[:, 0:nq], lhsT=vt[:, ks], rhs=p_[:, 0:nq], start=(kt == k0), stop=(kt == k1 - 1)), reads=[pk, vtk], writes=[pok])
                P.add("pe", lambda e, p_=p_, pl=pl, nq=nq, kt=kt, k0=k0, k1=k1: e.matmul(
                    pl[:, 0:nq], lhsT=onesb[:, :], rhs=p_[:, 0:nq], start=(kt == k0), stop=(kt == k1 - 1)), reads=[pk, "onesb"], writes=[plk])
            r_, rk = rlr.next()
            P.add("dve", lambda e, r_=r_, pl=pl, nq=nq: e.reciprocal(r_[:, 0:nq], pl[:, 0:nq]), reads=[plk], writes=[rk])
            o, ok = oor.next()
            P.add("dve", lambda e, o=o, po=po, r_=r_, nq=nq: e.tensor_tensor(o[:, 0:nq], po[:, 0:nq], r_[:, 0:nq], ALU.mult),
                  reads=[pok, rk], writes=[ok])
            P.add(ST, lambda e, o=o, h=h, q0=q0, nq=nq: e.dma_start(out=g.d_of[h * 128:(h + 1) * 128, q0:q0 + nq], in_=o[:, 0:nq]),
                  reads=[ok], writes=[("ao", h, q0)], dma=True)
    P.barrier()


def stage_outproj(P, g, l, src, wsrc, skip_ctx):
    lin_views(g)
    for (c0, nb, kind) in BLOCKS:
        if kind == 1 and skip_ctx:
            continue
        outproj_block(P, g, l, src, wsrc, c0, nb, kind)
    P.barrier()


def outproj_block(P, g, l, src, wsrc, c0, nb, kind):
    gtcol = lambda oc: modcol(g, l, kind, 1, 2, oc)
    sv = src.rearrange("(c p) t -> p c t", p=128)
    for (t0, n) in tgroups(nb, 256):
        s_, k_ = g.stg.next()
        P.add("sp", lambda e, s_=s_, t0=t0, n=n: e.dma_start(
            out=s_[:, 0:DC * n].rearrange("p (c t) -> p c t", t=n), in_=sv[:, :, c0 + t0:c0 + t0 + n]), writes=[k_], dma=True)
        for dc in range(DC):
            cast_op(P, g, dc, g.xn[:, dc * nb + t0: dc * nb + t0 + n], s_[:, dc * n:(dc + 1) * n], [k_], [("xn", dc, t0)])
    emit_linear(P, g, g.xn, DC, nb, wsrc, DC, epi_residual(P, g, g.d_h, c0, gtcol, ["modc"]))


def stage_fnet_a(P, g, l):
    gsrc = g.gcols[:, (l * 3 + 1) * 16:(l * 3 + 2) * 16]
    dst32 = carve(g, 32768, 1024, F32)
    dftc = carve(g, 32768 + 4096, 1024, BF16)
    P.add("sp", lambda e: e.dma_start(out=dst32[:, :], in_=g.d_dft256), writes=["dft32"], dma=True)
    P.add("act", lambda e: e.copy(dftc[:, :], dst32[:, :]), reads=["dft32"], writes=["dftc"])
    for (c0, nb, kind) in BLOCKS:
        fnet_a_block(P, g, l, gsrc, dftc, c0, nb, kind)
    P.barrier()


def fnet_a_block(P, g, l, gsrc, dftc, c0, nb, kind):
    g.xn = g.arena[:, 0:DC * TB]
    stage = [carve(g, 40960 + i * 16384, 4096, F32) for i in range(3)]
    xo = [carve(g, 90112 + i * 1024, 512, BF16) for i in range(4)]
    g.stg = Ring("stage", stage)
    g.psr = Ring("ps", g.ps)
    xor_ = Ring("xo", xo)
    emit_cols(P, g, l, 1, kind, gsrc, False)
    gcol = lambda dc: g.dcol[:, dc:dc + 1]
    shcol = lambda dc: modcol(g, l, kind, 1, 0, dc)
    emit_adanorm(P, g, g.d_h, c0, nb, gcol, shcol, ["dcol", "modc"])
    for tt in range(nb // 128):
        for gq in range(8):
            pp, ppk = g.psr.next()
            for cs_ in range(2):
                for kk in range(2):
                    dc = gq * 2 + kk
                    P.add("pe", lambda e: e.matmul(
                        pp[:, cs_ * 256:(cs_ + 1) * 256], lhsT=g.xn[:, dc * nb + tt * 128: dc * nb + (tt + 1) * 128],
                        rhs=dftc[:, kk * 512 + cs_ * 256: kk * 512 + (cs_ + 1) * 256], start=(kk == 0), stop=(kk == 1)),
                        reads=xkeys("xn", dc, tt * 128, 128) + ["dftc"], writes=[ppk])
            o, ok = xor_.next()
            if gq % 2 == 0:
                P.add("act", lambda e: e.copy(o[:, :], pp[:, :]), reads=[ppk], writes=[ok])
            else:
                P.add("dve", lambda e: e.tensor_copy(o[:, :], pp[:, :]), reads=[ppk], writes=[ok])
            r0 = c0 + tt * 128
            P.add(ST, lambda e: e.dma_start(out=g.d_xc[r0:r0 + 128, gq * 512:(gq + 1) * 512], in_=o[:, :]),
                  reads=[ok], writes=[("xc", r0, gq)], dma=True)


def stage_fnet_b(P, g):
    TC = NLAT // 128
    tabs = [carve(g, i * 32768, TC * 512, BF16) for i in range(2)]
    xt = [[carve(g, 65536 + (i * 2 + a) * 8192, TC * 128, BF16) for a in range(2)] for i in range(2)]
    oo = [carve(g, 98304 + i * 2048, 512, F32) for i in range(3)]
    ctab = carve(g, 104448, 2 * 2 * 256, BF16)
    xtr, oor = Ring("xt", xt), Ring("oo", oo)
    g.psr = Ring("ps", g.ps)
    xcl = g.d_xc[NCTX:L, :].rearrange("(n p) c -> p n c", p=128)
    xcc = g.d_xc[0:NCTX, :].rearrange("(n p) c -> p n c", p=128)
    for tb in range(NLAT // 512):
        for a in range(2):
            P.add("sp", lambda e: e.dma_start(
                out=tabs[a][:, :].rearrange("p (n t) -> p n t", t=512),
                in_=g.d_dftT[a].rearrange("(n p) t -> p n t", p=128)[:, :, tb * 512:(tb + 1) * 512]), writes=[("tab", a)], dma=True)
        for cc in range(16):
            x2, xk = xtr.next()
            for a in range(2):
                col = (cc // 2) * 512 + a * 256 + (cc % 2) * 128
                P.add("sp", lambda e: e.dma_start(out=x2[a][:, :].rearrange("p (n c) -> p n c", c=128), in_=xcl[:, :, col:col + 128]),
                      writes=[(xk, a)], dma=True)
            pp, ppk = g.psr.next()
            for a in range(2):
                for tc in range(TC):
                    P.add("pe", lambda e: e.matmul(pp[:, :], lhsT=x2[a][:, tc * 128:(tc + 1) * 128], rhs=tabs[a][:, tc * 512:(tc + 1) * 512],
                                                   start=(a == 0 and tc == 0), stop=(a == 1 and tc == TC - 1)),
                          reads=[(xk, a), ("tab", a)], writes=[ppk])
            o, ok = oor.next()
            if cc % 2 == 0:
                P.add("act", lambda e: e.copy(o[:, :], pp[:, :]), reads=[ppk], writes=[ok])
            else:
                P.add("dve", lambda e: e.tensor_copy(o[:, :], pp[:, :]), reads=[ppk], writes=[ok])
            P.add(ST, lambda e: e.dma_start(out=g.d_of[cc * 128:(cc + 1) * 128, NCTX + tb * 512: NCTX + (tb + 1) * 512], in_=o[:, :]),
                  reads=[ok], writes=[("fo", cc, tb)], dma=True)
    P.add("sp", lambda e: e.dma_start(out=ctab[:, :].rearrange("p (a n t) -> p a n t", a=2, t=256),
                                      in_=g.d_dftC.rearrange("a (n p) t -> p a n t", p=128)), writes=["ctab"], dma=True)
    for cc in range(16):
        x2, xk = xtr.next()
        for a in range(2):
            col = (cc // 2) * 512 + a * 256 + (cc % 2) * 128
            P.add("sp", lambda e: e.dma_start(out=x2[a][:, 0:256].rearrange("p (n c) -> p n c", c=128), in_=xcc[:, :, col:col + 128]),
                  writes=[(xk, a)], dma=True)
        pp, ppk = g.psr.next()
        for a in range(2):
            for tc in range(2):
                P.add("pe", lambda e: e.matmul(pp[:, 0:256], lhsT=x2[a][:, tc * 128:(tc + 1) * 128],
                                               rhs=ctab[:, (a * 2 + tc) * 256:(a * 2 + tc + 1) * 256],
                                               start=(a == 0 and tc == 0), stop=(a == 1 and tc == 1)),
                      reads=[(xk, a), "ctab"], writes=[ppk])
        o, ok = oor.next()
        P.add("act", lambda e: e.copy(o[:, 0:256], pp[:, 0:256]), reads=[ppk], writes=[ok])
        P.add(ST, lambda e: e.dma_start(out=g.d_of[cc * 128:(cc + 1) * 128, 0:NCTX], in_=o[:, 0:256]),
              reads=[ok], writes=[("fo", cc, -1)], dma=True)
    P.barrier()


def stage_final(P, g):
    for (c0, nb, kind) in BLOCKS:
        if kind == 1:
            continue
        final_block(P, g, c0, nb)


def final_block(P, g, c0, nb):
    if True:
        xTv = g.d_h.rearrange("(c p) t -> p c t", p=128)
        for (t0, n) in tgroups(nb, 256):
            xs, xk = g.stg.next()
            P.add("sp", lambda e, xs=xs, t0=t0, n=n: e.dma_start(
                out=xs[:, 0:DC * n].rearrange("p (c t) -> p c t", t=n), in_=xTv[:, :, c0 + t0:c0 + t0 + n]),
                reads=[("h", c0 + t0)], writes=[xk], dma=True)
            ssp, ssk = g.psr.next()
            for dc in range(DC):
                sq, sqk = g.sq.next()
                P.add("act", lambda e, sq=sq, xs=xs, dc=dc, n=n: e.activation(
                    out=sq[:, 0:n], in_=xs[:, dc * n:(dc + 1) * n], func=AF.Square), reads=[xk], writes=[sqk])
                P.add("pe", lambda e, sq=sq, ssp=ssp, dc=dc, n=n: e.matmul(
                    ssp[:, 0:n], lhsT=g.ones[:, :], rhs=sq[:, 0:n], start=(dc == 0), stop=(dc == DC - 1)),
                    reads=[sqk, "ones"], writes=[ssk])
            rs, rsk = emit_rstd(P, g, ssp, ssk, n, D)
            for dc in range(DC):
                P.add("dve", lambda e, xs=xs, rs=rs, dc=dc, n=n: e.scalar_tensor_tensor(
                    out=xs[:, dc * n:(dc + 1) * n], in0=xs[:, dc * n:(dc + 1) * n], scalar=g.fgcol[:, dc:dc + 1],
                    in1=rs[:, 0:n], op0=ALU.mult, op1=ALU.mult), reads=[xk, rsk, "gcols"], writes=[xk])
            ov = g.d_out.rearrange("(c p) t -> p c t", p=128)
            P.add(ST, lambda e, xs=xs, t0=t0, n=n, ov=ov: e.dma_start(
                out=ov[:, :, c0 - NCTX + t0: c0 - NCTX + t0 + n], in_=xs[:, 0:DC * n].rearrange("p (c t) -> p c t", t=n)),
                reads=[xk], dma=True)


def build(plan):
    nc = bass.Bass("TRN2", target_bir_lowering=False)
    g = G()
    kinds = set(p if isinstance(p, str) else p[0] for p in plan)
    fam = {"wgu1": "ffn", "wdn1": "ffn", "wgu2": "ffn", "wdn2": "ffn", "hgw_in": "hgrn", "hgw_out": "hgrn",
           "wdqkv": "mla", "wuq": "mla", "wukv": "mla", "wo": "mla", "rope": "mla", "fw": "fnet", "dft256": "fnet",
           "dftT": "fnet", "dftC": "fnet", "mod_w": "mod"}

    def di(name, shape, dt=F32):
        f = fam.get(name)
        if f is not None and not any(k.startswith(f) for k in kinds):
            return None
        return nc.dram_tensor(name, shape, dt, kind="ExternalInput").ap()
    g.d_x = di("x", [D, L])
    g.d_cT = di("cT", [128, 32])
    g.d_modw = di("mod_w", [4, D, 18432])
    g.d_modb = di("mod_b", [128, 4 * NMODC])
    g.d_gcols = di("gcols", [128, 13 * 16])
    g.d_wgu1 = di("wgu1", [4, FC, 128, 4096])
    g.d_wdn1 = di("wdn1", [4, DC, 128, DFF])
    g.d_wgu2 = di("wgu2", [4, FC, 128, 4096])
    g.d_wdn2 = di("wdn2", [4, DC, 128, DFF])
    g.d_hgw_in = di("hgw_in", [2, 80, 128, D])
    g.d_hgw_out = di("hgw_out", [2, DC, 128, D])
    g.d_hgn = di("hgn", [128, 32])
    g.d_lbl = di("lbl", [128, 64])
    g.d_cst = di("cst", [128, 516])
    g.d_wdqkv = di("wdqkv", [10, 128, D])
    g.d_wuq = di("wuq", [48, 128, 512])
    g.d_wukv = di("wukv", [32, 128, 512])
    g.d_wo = di("wo", [DC, 128, D])
    g.d_mlan = di("mlan", [128, 8])
    g.d_rope = di("rope", [128, 2 * L])
    g.d_fw = di("fw", [DC, 128, D])
    g.d_dft256 = di("dft256", [128, 1024])
    g.d_dftT = di("dftT", [2, NLAT, NLAT], BF16)
    g.d_dftC = di("dftC", [2, NCTX, NCTX], BF16)
    g.d_xc = nc.dram_tensor("xc_scr", [L, 4096], BF16).ap()
    g.d_kr = nc.dram_tensor("kr_scr", [128, L], BF16).ap()
    g.d_qn = nc.dram_tensor("qn_scr", [16, 128, L], BF16).ap()
    g.d_qr = nc.dram_tensor("qr_scr", [16, 128, L], BF16).ap()
    g.d_kn = nc.dram_tensor("kn_scr", [16, 128, L], BF16).ap()
    g.d_vt = nc.dram_tensor("vt_scr", [16, L, 128], BF16).ap()
    g.d_out = nc.dram_tensor("out", [D, NLAT], F32, kind="ExternalOutput").ap()
    g.d_h = nc.dram_tensor("h_scr", [D, L], F32).ap()
    g.d_pj = nc.dram_tensor("pj_scr", [10240, L], F32).ap()
    g.d_of = nc.dram_tensor("of_scr", [D, L], F32).ap()
    g.d_ob = nc.dram_tensor("ob_scr", [D, L], F32).ap()
    with contextlib.ExitStack() as st:
        sb = lambda name, shape, dt: st.enter_context(nc.sbuf_tensor(name, shape, dt))
        g.arena = sb("arena", [128, ARENA], BF16)
        g.xn = g.arena[:, 0:DC * TB]
        g.act = g.arena[:, DC * TB:(DC + FC) * TB]
        o_ = (DC + FC) * TB
        stage = [g.arena[:, o_ + i * 8192: o_ + (i + 1) * 8192].bitcast(F32) for i in range(3)]
        o_ += 3 * 8192
        wbf = [g.arena[:, o_ + i * 4096: o_ + (i + 1) * 4096] for i in range(2)]
        g.modc = sb("modc", [128, 4 * NMODC * 2], F32)
        g.gcols = sb("gcols_sb", [128, 13 * 16], F32)
        g.fgcol = g.gcols[:, 12 * 16:13 * 16]
        g.dcol = sb("dcol", [128, 32], F32)
        g.hgn = sb("hgn_sb", [128, 32], F32)
        g.mlan = sb("mlan_sb", [128, 8], F32)
        g.lbl = sb("lbl_sb", [128, 64], F32)
        g.lbc = sb("lbc", [128, 192], F32)
        g.cst = sb("cst_sb", [128, 516], F32)
        g.identb = sb("identb", [128, 128], BF16)
        g.cs = sb("cs", [128, 32], F32)
        g.ones = sb("ones", [128, 128], F32)
        sq = [sb("sq%d" % i, [128, 256], F32) for i in range(2)]
        rstd = [sb("rstd%d" % i, [128, 256], F32) for i in range(2)]
        tmp = [sb("tmp%d" % i, [128, 256], F32) for i in range(2)]
        sg = [sb("sg%d" % i, [128, 512], F32) for i in range(2)]
        ost = [sb("ost%d" % i, [128, 512], F32) for i in range(2)]
        g.ps = [st.enter_context(nc.psum_tensor("ps%d" % i, [128, 512], F32)) for i in range(8)]
        g.psr = Ring("ps", g.ps)
        g.stg, g.wbr = Ring("stage", stage), Ring("wbf", wbf)
        g.sq, g.rstd, g.tmp, g.sg, g.ost = Ring("sq", sq), Ring("rstd", rstd), Ring("tmp", tmp), Ring("sg", sg), Ring("ost", ost)
        g.ost_global = ost
        P = Prog(nc)
        P.add("pool", lambda e: e.memset(g.ones[:, :], 1.0), writes=["ones"])
        P.add("sp", lambda e: e.dma_start(out=g.gcols[:, :], in_=g.d_gcols), writes=["gcols"], dma=True)
        P.add("sp", lambda e: e.dma_start(out=g.hgn[:, :], in_=g.d_hgn), writes=["hgn"], dma=True)
        P.add("sp", lambda e: e.dma_start(out=g.mlan[:, :], in_=g.d_mlan), writes=["mlan"], dma=True)
        stage_lbcols(P, g)
        for i in range(DC):
            P.add("sp", lambda e, i=i: e.dma_start(out=g.d_h[i * 128:(i + 1) * 128, :], in_=g.d_x[i * 128:(i + 1) * 128, :]),
                  writes=[("hinit", i)], dma=True)
        P.barrier()
        for stg_ in plan:
            if stg_ == "mod":
                stage_mod(P, g)
                P.barrier()
            elif stg_[0] == "ffn":
                ffn_views(g)
                stage_ffn(P, g, stg_[1], stg_[2], skip_ctx=(len(stg_) > 3 and stg_[3]))
            elif stg_[0] == "fnet":
                l_ = stg_[1]
                stage_fnet_a(P, g, l_)
                stage_fnet_b(P, g)
                stage_outproj(P, g, l_, g.d_of, g.d_fw, False)
            elif stg_[0] == "mla":
                l_, last_ = stg_[1], stg_[2]
                stage_mla_proj(P, g, l_)
                stage_mla_attn(P, g, last_)
                stage_outproj(P, g, l_, g.d_of, g.d_wo, last_)
            elif stg_[0] == "hgrn_in":
                stage_hgrn_inproj(P, g, stg_[1], stg_[2])
            elif stg_[0] == "hgrn_scan":
                stage_hgrn_scan(P, g, stg_[1], stg_[2])
            elif stg_[0] == "hgrn_out":
                stage_hgrn_readout(P, g, stg_[1], stg_[2], stg_[3])
            elif stg_[0] == "dumpcols":
                P.barrier()
                for i in range(DC):
                    P.add("sp", lambda e, i=i, a=stg_[1], n=stg_[2], o=stg_[3]: e.dma_start(
                        out=g.d_out[i * 128:(i + 1) * 128, o:o + n], in_=g.d_h[i * 128:(i + 1) * 128, a:a + n]), dma=True)
                P.barrier()
                g.dumped = True
            elif stg_[0] == "dump":
                src_ = getattr(g, stg_[1])
                P.barrier()
                for i in range(stg_[3] // 128):
                    P.add("sp", lambda e, i=i, src_=src_, r0=stg_[2], o0=stg_[4]: e.dma_start(
                        out=g.d_out[o0 + i * 128: o0 + (i + 1) * 128, :], in_=src_[r0 + i * 128: r0 + (i + 1) * 128, NCTX:L]), dma=True)
                g.dumped = True
            elif stg_[0] == "hgrn":
                l_, j_, last_ = stg_[1], stg_[2], stg_[3]
                stage_hgrn_inproj(P, g, l_, j_)
                stage_hgrn_scan(P, g, j_)
                stage_hgrn_readout(P, g, l_, j_, last_)
            elif stg_ == "final":
                stage_final(P, g)
            elif stg_ == "dbg_modc":
                P.barrier()
                P.add(ST, lambda e: e.dma_start(out=g.d_out[0:128, 0:4 * NMODC * 2], in_=g.modc[:, :]), dma=True)
                P.emit()
                g.stats = P.stats
                return nc, g
        if "final" not in plan and not getattr(g, "dumped", False):
            P.barrier()
            for i in range(DC):
                P.add("sp", lambda e, i=i: e.dma_start(out=g.d_out[i * 128:(i + 1) * 128, :], in_=g.d_h[i * 128:(i + 1) * 128, NCTX:L]),
                      dma=True)
        P.emit()
        g.stats = P.stats
    return nc, g


def tile_w(W):
    K, N = W.shape
    return np.ascontiguousarray(W.reshape(K // 128, 128, N // 128, 128).transpose(2, 1, 0, 3))


def col16(v):
    return np.ascontiguousarray(v.reshape(-1, 128).T)


def prep_ffn_w(w_gu, w_dn):
    tg = tile_w(w_gu)
    wgu = np.ascontiguousarray(np.stack([tg[:FC], tg[FC:]], axis=2).reshape(FC, 128, 4096))
    wdn = np.ascontiguousarray(tile_w(w_dn).reshape(DC, 128, DFF))
    return wgu, wdn


def prep_inputs(inp, nlayers=4):
    shared = {}
    shared["mod_w"] = np.ascontiguousarray(inp["mod_w"])
    shared["mod_b"] = np.ascontiguousarray(np.concatenate([col16(inp["mod_b"][l]) for l in range(4)], axis=1))
    shared["gcols"] = np.ascontiguousarray(np.concatenate(
        [col16(inp["norm_g"][l, s]) for l in range(4) for s in range(3)] + [col16(inp["final_g"])], axis=1))
    for nm, gu, dn in (("1", "ffn1_w_gu", "ffn1_w_down"), ("2", "ffn2_w_gu", "ffn2_w_down")):
        a, b = zip(*[prep_ffn_w(inp[gu][l], inp[dn][l]) for l in range(4)])
        shared["wgu" + nm] = np.stack(a)
        shared["wdn" + nm] = np.stack(b)
    shared["hgw_in"] = np.stack([tile_w(inp["hgrn_w_in"][j]).reshape(80, 128, D) for j in range(2)])
    shared["hgw_out"] = np.stack([tile_w(inp["hgrn_w_out"][j]).reshape(DC, 128, D) for j in range(2)])
    shared["hgn"] = np.ascontiguousarray(np.concatenate([col16(inp["hgrn_g_norm"][j]) for j in range(2)], axis=1))
    shared["lbl"] = np.ascontiguousarray(np.concatenate(
        [col16(inp["hgrn_lb_logits"][d_, j]) for d_ in range(2) for j in range(2)], axis=1))
    shared["cst"] = make_consts()
    pidx = np.array([a * 32 + (1 - hf) * 16 + f for a in range(2) for hf in range(2) for f in range(16)])
    zpad = lambda w: np.concatenate([w, np.zeros((w.shape[0], 64), np.float32)], axis=1)
    wd = inp["mla_w_dqkv"][0]
    wd_ext = np.concatenate([wd[:, :1024], zpad(wd[:, 1024:1088]), zpad(wd[:, 1024:1088][:, pidx])], axis=1)
    shared["wdqkv"] = tile_w(wd_ext).reshape(10, 128, D)
    wq = inp["mla_w_uq"][0].reshape(512, 16, 192)
    wq_ext = np.concatenate([np.concatenate([wq[:, hh, :128], zpad(wq[:, hh, 128:]), zpad(wq[:, hh, 128:][:, pidx])], axis=1)
                             for hh in range(16)], axis=1)
    shared["wuq"] = tile_w(wq_ext).reshape(48, 128, 512)
    shared["wukv"] = tile_w(inp["mla_w_ukv"][0]).reshape(32, 128, 512)
    shared["wo"] = tile_w(inp["mla_w_o"][0]).reshape(DC, 128, D)
    shared["mlan"] = np.ascontiguousarray(np.concatenate([col16(inp["mla_q_norm"][0]), col16(inp["mla_kv_norm"][0])], axis=1))
    shared["rope"] = make_rope()
    shared["fw"] = tile_w(inp["fnet_w_out"][0]).reshape(DC, 128, D)
    shared.update(make_dft())
    maps = []
    for b in range(2):
        m = dict(shared)
        m["x"] = np.ascontiguousarray(np.concatenate([inp["ctx"][b], inp["x"][b]], axis=0).T)
        cvec = np.stack([inp["c"][b], inp["c_ctx"]])
        m["cT"] = np.ascontiguousarray(cvec.reshape(2, 16, 128).transpose(2, 1, 0).reshape(128, 32))
        maps.append(m)
    return maps


def make_consts():
    c = np.zeros((128, 516), np.float32)
    idx = np.arange(128)
    c[:, 0:128] = np.eye(128, dtype=np.float32)
    same = (idx[:, None] // 32) == (idx[None, :] // 32)
    c[:, 128:256] = (same & (idx[:, None] <= idx[None, :])).astype(np.float32)
    c[:, 256:384] = (same & (idx[:, None] >= idx[None, :])).astype(np.float32)
    c[:, 384:512] = (np.arange(128) % 32 != 0).astype(np.float32)[None, :]
    for k in range(4):
        c[:, 512 + k] = (idx // 32 == k).astype(np.float32)
    return c


def make_rope():
    t = np.arange(NLAT)
    pos = np.stack([(t // 64).astype(np.float32), (t % 64).astype(np.float32)], axis=-1)
    inv_freq = (np.float32(10000.0) ** (-np.arange(16, dtype=np.float32) / np.float32(16))).astype(np.float32)
    ang = (pos[:, :, None] * inv_freq[None, None, :]).astype(np.float32)
    cos, sin = np.cos(ang).astype(np.float32), np.sin(ang).astype(np.float32)
    tab = np.zeros((128, 2, L), np.float32)
    tab[:, 0, :] = 1.0
    for a in range(2):
        for hf in range(2):
            rows = slice(a * 32 + hf * 16, a * 32 + hf * 16 + 16)
            tab[rows, 0, NCTX:] = cos[:, a, :].T
            tab[rows, 1, NCTX:] = (-sin[:, a, :].T) if hf == 0 else sin[:, a, :].T
    return np.ascontiguousarray(tab.reshape(128, 2 * L))


def make_dft():
    import ml_dtypes
    c = np.arange(256)
    ang = 2.0 * np.pi * ((c[:, None] * c[None, :]) % 256) / 256.0
    t256 = np.zeros((128, 2, 2, 256), np.float32)
    for kk in range(2):
        t256[:, kk, 0, :] = np.cos(ang[kk * 128:(kk + 1) * 128])
        t256[:, kk, 1, :] = np.sin(ang[kk * 128:(kk + 1) * 128])
    t = np.arange(NLAT)
    angT = 2.0 * np.pi * ((t[:, None] * t[None, :]) % NLAT) / float(NLAT)
    dftT = np.stack([np.cos(angT) / 1024.0, -np.sin(angT) / 1024.0]).astype(np.float32).astype(ml_dtypes.bfloat16)
    tc_ = np.arange(NCTX)
    angC = 2.0 * np.pi * ((tc_[:, None] * tc_[None, :]) % NCTX) / float(NCTX)
    dftC = np.stack([np.cos(angC) / 256.0, -np.sin(angC) / 256.0]).astype(np.float32).astype(ml_dtypes.bfloat16)
    return {"dft256": np.ascontiguousarray(t256.reshape(128, 1024)), "dftT": np.ascontiguousarray(dftT), "dftC": np.ascontiguousarray(dftC)}


PLAN = ['mod',
        ('ffn', 0, 0), ('hgrn', 0, 0, False), ('ffn', 0, 1),
        ('ffn', 1, 0), ('mla', 1, False), ('ffn', 1, 1),
        ('ffn', 2, 0), ('fnet', 2), ('ffn', 2, 1),
        ('ffn', 3, 0), ('hgrn', 3, 1, True), ('ffn', 3, 1, True),
        'final']


def kernel(**inputs):
    inp = {k: np.asarray(v) for k, v in inputs.items()}
    maps = prep_inputs(inp)
    nc, g = build(PLAN)
    need = [a.memorylocations[0].name for a in nc.allocations
            if getattr(a, "kind", None) == "ExternalInput"]
    in_maps = [{k: m[k] for k in need if k in m} for m in maps]
    res = run_bass_kernel_spmd(nc, in_maps, core_ids=[0, 1])
    out = np.stack([np.ascontiguousarray(res.results[b]["out"].T) for b in range(2)])
    return out.astype(np.float32)
```

```python
import contextlib
import types
import numpy as np
import concourse.bass as bass
import concourse.mybir as mybir
from concourse.bass_utils import run_bass_kernel_spmd

F32 = mybir.dt.float32
BF16 = mybir.dt.bfloat16
AF = mybir.ActivationFunctionType
ALU = mybir.AluOpType

ENGS = ("pe", "act", "dve", "pool", "sp")
ST = "pool"
NDMASEM = 8

D = 2048
DC = 16
DFF = 5632
FC = 44
EPS = 1e-6
NCTX = 256
NLAT = 4096
L = NCTX + NLAT
TB = 1024
BLOCKS = [(0, NCTX, 1)] + [(NCTX + TB * i, TB, 0) for i in range(NLAT // TB)]
NMODC = 144
ARENA = (DC + FC) * TB + 3 * 8192 + 2 * 4096


def freeze(fn):
    if fn.__closure__ is None:
        return fn
    cells = []
    for c in fn.__closure__:
        try:
            cells.append(types.CellType(c.cell_contents))
        except ValueError:
            cells.append(c)
    return types.FunctionType(fn.__code__, fn.__globals__, fn.__name__, fn.__defaults__, tuple(cells))


class Prog:
    def __init__(self, nc):
        self.nc = nc
        self.ops = []
        self.last_w = {}
        self.readers = {}
        self.last_eng = {}
        self.dmas_since = []

    def add(self, eng, fn, reads=(), writes=(), dma=False, extra_deps=()):
        idx = len(self.ops)
        deps = set(extra_deps)
        for k in reads:
            w = self.last_w.get(k)
            if w is not None:
                deps.add(w)
        for k in writes:
            w = self.last_w.get(k)
            if w is not None:
                deps.add(w)
            for r in self.readers.get(k, ()):
                deps.add(r)
        for k in reads:
            lst = self.readers.setdefault(k, [])
            if not dma:
                lst[:] = [r for r in lst if self.ops[r]["dma"] or self.ops[r]["eng"] != eng]
            lst.append(idx)
        for k in writes:
            self.last_w[k] = idx
            self.readers[k] = []
        deps.discard(idx)
        self.ops.append(dict(eng=eng, fn=freeze(fn), deps=deps, dma=dma, sig=False))
        self.last_eng[eng] = idx
        if dma:
            self.dmas_since.append(idx)
        return idx

    def barrier(self):
        deps = set(self.last_eng.values()) | set(self.dmas_since)
        self.dmas_since = []
        for e in ENGS:
            self.add(e, lambda eng: eng.nop(), extra_deps=deps)
        self.last_w = {}
        self.readers = {}

    def emit(self):
        nc = self.nc
        ops = self.ops
        for i, o in enumerate(ops):
            nd = set()
            for d in o["deps"]:
                p = ops[d]
                if p["eng"] == o["eng"] and o["eng"] == "pe" and not p["dma"]:
                    continue
                nd.add(d)
                p["sig"] = True
            o["deps"] = nd
        with contextlib.ExitStack() as st:
            esem = {e: st.enter_context(nc.semaphore("s_" + e)) for e in ENGS}
            dsem = {e: [st.enter_context(nc.semaphore("d_%s%d" % (e, j))) for j in range(NDMASEM)]
                    for e in ("sp", "act", "pool")}
            ecount = {e: 0 for e in ENGS}
            dcount = {e: 0 for e in dsem}
            dtarget = {e: [0] * NDMASEM for e in dsem}
            per_eng = {e: [] for e in ENGS}
            for i, o in enumerate(ops):
                e = o["eng"]
                if o["dma"]:
                    j = dcount[e] % NDMASEM
                    dcount[e] += 1
                    o["prev_target"] = dtarget[e][j]
                    dtarget[e][j] += 16
                    o["sem"] = ("d", e, j)
                    o["target"] = dtarget[e][j]
                elif o["sig"]:
                    ecount[e] += 1
                    o["sem"] = ("e", e, 0)
                    o["target"] = ecount[e]
                per_eng[e].append(i)
            self.stats = dict(ecount=dict(ecount), dcount=dict(dcount), nops={e: len(per_eng[e]) for e in ENGS})

            def semof(s):
                return esem[s[1]] if s[0] == "e" else dsem[s[1]][s[2]]

            block = st.enter_context(nc.Block())

            def body(e, eng):
                seen = {}
                for i in per_eng[e]:
                    o = ops[i]
                    waits = {}
                    for d in o["deps"]:
                        p = ops[d]
                        s = p["sem"]
                        waits[s] = max(waits.get(s, 0), p["target"])
                    if o["dma"] and o["prev_target"] > 0:
                        s = o["sem"]
                        waits[s] = max(waits.get(s, 0), o["prev_target"])
                    for s, v in waits.items():
                        if seen.get(s, 0) >= v:
                            continue
                        seen[s] = v
                        eng.wait_ge(semof(s), v)
                    ins = o["fn"](eng)
                    if o["dma"]:
                        ins.then_inc(semof(o["sem"]), 16)
                    elif o["sig"]:
                        ins.then_inc(semof(o["sem"]), 1)
                if e in dsem:
                    for j in range(NDMASEM):
                        if dtarget[e][j] > 0 and seen.get(("d", e, j), 0) < dtarget[e][j]:
                            eng.wait_ge(dsem[e][j], dtarget[e][j])

            if per_eng["sp"]:
                block.sync(lambda eng: body("sp", eng))
            if per_eng["act"]:
                block.scalar(lambda eng: body("act", eng))
            if per_eng["dve"]:
                block.vector(lambda eng: body("dve", eng))
            if per_eng["pool"]:
                block.gpsimd(lambda eng: body("pool", eng))
            if per_eng["pe"]:
                block.tensor(lambda eng: body("pe", eng))


class Ring:
    def __init__(self, name, aps):
        self.name, self.aps, self.i = name, aps, 0

    def next(self):
        j = self.i % len(self.aps)
        self.i += 1
        return self.aps[j], (self.name, j)


def tgroups(n, step):
    return [(t, min(step, n - t)) for t in range(0, n, step)]


def xkeys(pref, dc, t0, n):
    return [(pref, dc, t) for t in range((t0 // 256) * 256, t0 + n, 256)]


class G:
    pass


def cast_op(P, g, i, dst, src, reads, writes):
    if i % 2 == 0:
        P.add("act", lambda e: e.copy(dst, src), reads=reads, writes=writes)
    else:
        P.add("dve", lambda e: e.tensor_copy(dst, src), reads=reads, writes=writes)


def emit_rstd(P, g, ssp, ssk, n, dim):
    rs, rsk = g.rstd.next()
    P.add("dve", lambda e: e.tensor_scalar(rs[:, 0:n], ssp[:, 0:n], 1.0 / dim, EPS, ALU.mult, ALU.add),
          reads=[ssk], writes=[rsk])
    P.add("act", lambda e: e.sqrt(rs[:, 0:n], rs[:, 0:n]), reads=[rsk], writes=[rsk])
    P.add("dve", lambda e: e.reciprocal(rs[:, 0:n], rs[:, 0:n]), reads=[rsk], writes=[rsk])
    return rs, rsk


def emit_adanorm(P, g, src, c0, nb, gcol, shcol, colkeys, hkey="h"):
    xTv = src.rearrange("(c p) t -> p c t", p=128)
    for (t0, n) in tgroups(nb, 256):
        xs, xk = g.stg.next()
        P.add("sp", lambda e, xs=xs, t0=t0, n=n: e.dma_start(
            out=xs[:, 0:DC * n].rearrange("p (c t) -> p c t", t=n), in_=xTv[:, :, c0 + t0:c0 + t0 + n]),
            reads=[(hkey, c0 + t0)], writes=[xk], dma=True)
        ssp, ssk = g.psr.next()
        for dc in range(DC):
            sq, sqk = g.sq.next()
            P.add("act", lambda e, sq=sq, xs=xs, dc=dc, n=n: e.activation(
                out=sq[:, 0:n], in_=xs[:, dc * n:(dc + 1) * n], func=AF.Square), reads=[xk], writes=[sqk])
            P.add("pe", lambda e, sq=sq, ssp=ssp, dc=dc, n=n: e.matmul(
                ssp[:, 0:n], lhsT=g.ones[:, :], rhs=sq[:, 0:n], start=(dc == 0), stop=(dc == DC - 1)),
                reads=[sqk, "ones"], writes=[ssk])
        rs, rsk = emit_rstd(P, g, ssp, ssk, n, D)
        for dc in range(DC):
            tm, tmk = g.tmp.next()
            P.add("dve", lambda e, tm=tm, xs=xs, rs=rs, dc=dc, n=n: e.scalar_tensor_tensor(
                out=tm[:, 0:n], in0=xs[:, dc * n:(dc + 1) * n], scalar=gcol(dc), in1=rs[:, 0:n],
                op0=ALU.mult, op1=ALU.mult), reads=[xk, rsk] + colkeys, writes=[tmk])
            P.add("act", lambda e, tm=tm, dc=dc, n=n, t0=t0: e.activation(
                out=g.xn[:, dc * nb + t0: dc * nb + t0 + n], in_=tm[:, 0:n], func=AF.Identity,
                bias=shcol(dc), scale=1.0), reads=[tmk] + colkeys, writes=[("xn", dc, t0)])


def emit_linear(P, g, xn, KC, nb, wsrc, NOC, epi, xpref="xn", step=512):
    for oc in range(NOC):
        s, sk = g.stg.next()
        P.add("sp", lambda e, s=s, oc=oc: e.dma_start(out=s[:, 0:KC * 128], in_=wsrc[oc]), writes=[sk], dma=True)
        w, wk = g.wbr.next()
        cast_op(P, g, oc, w[:, 0:KC * 128], s[:, 0:KC * 128], [sk], [wk])
        for (t0, n) in tgroups(nb, step):
            pp, ppk = g.psr.next()
            for kc in range(KC):
                P.add("pe", lambda e, pp=pp, w=w, kc=kc, t0=t0, n=n: e.matmul(
                    pp[:, 0:n], lhsT=w[:, kc * 128:(kc + 1) * 128],
                    rhs=xn[:, kc * nb + t0: kc * nb + t0 + n], start=(kc == 0), stop=(kc == KC - 1)),
                    reads=[wk] + xkeys(xpref, kc, t0, n), writes=[ppk])
            epi(oc, t0, n, pp, ppk)


def epi_store(P, g, dst, c0, okey, scale=None):
    def epi(oc, t0, n, pp, ppk):
        o, ok = g.ost.next()
        if (oc + t0 // 512) % 2 == 0:
            P.add("act", lambda e: e.copy(o[:, 0:n], pp[:, 0:n]), reads=[ppk], writes=[ok])
        else:
            P.add("dve", lambda e: e.tensor_copy(o[:, 0:n], pp[:, 0:n]), reads=[ppk], writes=[ok])
        P.add(ST, lambda e: e.dma_start(out=dst[oc * 128:(oc + 1) * 128, c0 + t0:c0 + t0 + n], in_=o[:, 0:n]),
              reads=[ok], writes=[(okey, oc, c0 + t0)], dma=True)
    return epi


def epi_residual(P, g, h, c0, gtcol, colkeys):
    def epi(oc, t0, n, pp, ppk):
        o, ok = g.ost.next()
        P.add("sp", lambda e: e.dma_start(out=o[:, 0:n], in_=h[oc * 128:(oc + 1) * 128, c0 + t0:c0 + t0 + n]),
              reads=[("hc", oc, c0 + t0)], writes=[ok], dma=True)
        P.add("dve", lambda e: e.scalar_tensor_tensor(
            out=o[:, 0:n], in0=pp[:, 0:n], scalar=gtcol(oc), in1=o[:, 0:n], op0=ALU.mult, op1=ALU.add),
            reads=[ppk, ok] + colkeys, writes=[ok])
        P.add(ST, lambda e: e.dma_start(out=h[oc * 128:(oc + 1) * 128, c0 + t0:c0 + t0 + n], in_=o[:, 0:n]),
              reads=[ok], writes=[("hc", oc, c0 + t0)] + [("h", c0 + t) for t in range((t0 // 256) * 256, t0 + n, 256)],
              dma=True)
    return epi


def modcol(g, l, kind, s, t, dc):
    j = ((l * NMODC + s * 48 + t * 16 + dc) * 2) + kind
    return g.modc[:, j:j + 1]


def emit_cols(P, g, l, s, kind, gsrc, half_gate):
    sc = g.modc[:, :].rearrange("p (l c r) -> p l c r", l=4, r=2)[:, l, s * 48 + 16: s * 48 + 32, kind]
    gt = g.modc[:, :].rearrange("p (l c r) -> p l c r", l=4, r=2)[:, l, s * 48 + 32: s * 48 + 48, kind]
    P.add("dve", lambda e: e.scalar_tensor_tensor(out=g.dcol[:, 0:16], in0=sc, scalar=1.0, in1=gsrc,
                                                   op0=ALU.add, op1=ALU.mult), reads=["modc", "gcols"], writes=["dcol"])
    P.add("dve", lambda e: e.tensor_scalar_mul(g.dcol[:, 16:32], gt, 0.5 if half_gate else 1.0),
          reads=["modc", "dcol"], writes=["dcol"])


def stage_mod(P, g):
    P.add("sp", lambda e: e.dma_start(out=g.cs[:, :], in_=g.d_cT), writes=["cs"], dma=True)
    P.add("act", lambda e: e.activation(out=g.cs[:, :], in_=g.cs[:, :], func=AF.Silu), reads=["cs"], writes=["cs"])
    g.modb = carve(g, 0, 4 * NMODC, F32)
    P.add("sp", lambda e: e.dma_start(out=g.modb[:, :], in_=g.d_modb), writes=["modb"], dma=True)
    for l in range(4):
        wv = g.d_modw[l].rearrange("(c p) n -> p c n", p=128)
        for nb_ in range(36):
            for hf in range(2):
                s, sk = g.stg.next()
                P.add("sp", lambda e, s=s, nb_=nb_, hf=hf, wv=wv: e.dma_start(
                    out=s[:, :].rearrange("p (c n) -> p c n", n=512), in_=wv[:, hf * 8:(hf + 1) * 8, nb_ * 512:(nb_ + 1) * 512]),
                    writes=[sk], dma=True)
                for j in range(4):
                    pp = g.ps[(nb_ * 4 + j) % 8]
                    ppk = ("ps", (nb_ * 4 + j) % 8)
                    for k8 in range(8):
                        kc = hf * 8 + k8
                        P.add("pe", lambda e, s=s, pp=pp, j=j, k8=k8, kc=kc: e.matmul(
                            pp[:, 0:2], lhsT=s[:, k8 * 512 + j * 128: k8 * 512 + (j + 1) * 128],
                            rhs=g.cs[:, kc * 2:(kc + 1) * 2], start=(kc == 0), stop=(kc == 15)),
                            reads=[sk, "cs"], writes=[ppk])
                    if hf == 1:
                        ch = l * NMODC + nb_ * 4 + j
                        P.add("dve", lambda e, pp=pp, ch=ch: e.tensor_tensor(
                            g.modc[:, ch * 2:ch * 2 + 2], pp[:, 0:2],
                            g.modb[:, ch:ch + 1].to_broadcast([128, 2]), ALU.add),
                            reads=[ppk, "modb"], writes=["modc"])


def stage_ffn(P, g, l, which, skip_ctx=False):
    s = 0 if which == 0 else 2
    wgu, wdn = (g.d_wgu1[l], g.d_wdn1[l]) if which == 0 else (g.d_wgu2[l], g.d_wdn2[l])
    gsrc = g.gcols[:, (l * 3 + s) * 16:(l * 3 + s + 1) * 16]
    for (c0, nb, kind) in BLOCKS:
        if kind == 1 and skip_ctx:
            continue
        ffn_block(P, g, l, s, wgu, wdn, gsrc, c0, nb, kind)
    P.barrier()


def ffn_block(P, g, l, s, wgu, wdn, gsrc, c0, nb, kind):
    deep = (nb <= 256)
    if deep:
        base = (DC + FC) * TB * 2
        slots = [carve(g, base + i * 16384, 4096, F32) for i in range(3)]
        slots += [carve(g, 57344 + i * 16384, 4096, F32) for i in range(4)]
        g.stg = Ring("stage", slots)
    if True:
        emit_cols(P, g, l, s, kind, gsrc, True)
        gcol = lambda dc: g.dcol[:, dc:dc + 1]
        hgcol = lambda dc: g.dcol[:, 16 + dc:17 + dc]
        shcol = lambda dc, kind=kind: modcol(g, l, kind, s, 0, dc)
        emit_adanorm(P, g, g.d_h, c0, nb, gcol, shcol, ["dcol", "modc"])
        TG = tgroups(nb, 512)
        for fc in range(FC):
            st_, sk = g.stg.next()
            P.add("sp", lambda e, st_=st_, fc=fc: e.dma_start(out=st_[:, :], in_=wgu[fc]), writes=[sk], dma=True)
            w, wk = g.wbr.next()
            P.add("act", lambda e, w=w, st_=st_: e.copy(w[:, 0:2048], st_[:, 0:2048]), reads=[sk], writes=[(wk, 0)])
            P.add("dve", lambda e, w=w, st_=st_: e.tensor_copy(w[:, 2048:4096], st_[:, 2048:4096]), reads=[sk], writes=[(wk, 1)])
            for (t0, n) in TG:
                pg, pgk = g.psr.next()
                pu, puk = g.psr.next()
                for hh, (pp, ppk) in enumerate(((pg, pgk), (pu, puk))):
                    for dc in range(DC):
                        P.add("pe", lambda e, pp=pp, w=w, hh=hh, dc=dc, t0=t0, n=n: e.matmul(
                            pp[:, 0:n], lhsT=w[:, (hh * DC + dc) * 128:(hh * DC + dc + 1) * 128],
                            rhs=g.xn[:, dc * nb + t0: dc * nb + t0 + n], start=(dc == 0), stop=(dc == DC - 1)),
                            reads=[(wk, hh)] + xkeys("xn", dc, t0, n), writes=[ppk])
                g_, gk = g.sg.next()
                P.add("act", lambda e, g_=g_, pg=pg, n=n: e.activation(out=g_[:, 0:n], in_=pg[:, 0:n], func=AF.Silu),
                      reads=[pgk], writes=[gk])
                P.add("dve", lambda e, g_=g_, pu=pu, n=n, fc=fc, t0=t0: e.tensor_tensor(
                    g.act[:, fc * nb + t0: fc * nb + t0 + n], pu[:, 0:n], g_[:, 0:n], ALU.mult),
                    reads=[puk, gk], writes=[("act", fc, t0)])
        for dc in range(DC):
            ws, wks = [], []
            for hf in range(2):
                st_, sk = g.stg.next()
                P.add("sp", lambda e, st_=st_, dc=dc, hf=hf: e.dma_start(
                    out=st_[:, 0:2816], in_=wdn[dc][:, hf * 2816:(hf + 1) * 2816]), writes=[sk], dma=True)
                w, wk = g.wbr.next()
                cast_op(P, g, hf, w[:, 0:2816], st_[:, 0:2816], [sk], [wk])
                ws.append(w)
                wks.append(wk)
            for (t0, n) in TG:
                xr, xrk = g.ost.next()
                P.add("sp", lambda e, xr=xr, dc=dc, t0=t0, n=n: e.dma_start(
                    out=xr[:, 0:n], in_=g.d_h[dc * 128:(dc + 1) * 128, c0 + t0:c0 + t0 + n]),
                    reads=[("hc", dc, c0 + t0)], writes=[xrk], dma=True)
                pp, ppk = g.psr.next()
                for fc in range(FC):
                    w, wk = ws[fc // 22], wks[fc // 22]
                    P.add("pe", lambda e, pp=pp, w=w, fc=fc, t0=t0, n=n: e.matmul(
                        pp[:, 0:n], lhsT=w[:, (fc % 22) * 128:(fc % 22 + 1) * 128],
                        rhs=g.act[:, fc * nb + t0: fc * nb + t0 + n], start=(fc == 0), stop=(fc == FC - 1)),
                        reads=[wk, ("act", fc, t0)], writes=[ppk])
                P.add("dve", lambda e, xr=xr, pp=pp, n=n, dc=dc: e.scalar_tensor_tensor(
                    out=xr[:, 0:n], in0=pp[:, 0:n], scalar=hgcol(dc), in1=xr[:, 0:n],
                    op0=ALU.mult, op1=ALU.add), reads=[ppk, xrk, "dcol"], writes=[xrk])
                P.add(ST, lambda e, xr=xr, dc=dc, t0=t0, n=n: e.dma_start(
                    out=g.d_h[dc * 128:(dc + 1) * 128, c0 + t0:c0 + t0 + n], in_=xr[:, 0:n]), reads=[xrk],
                    writes=[("hc", dc, c0 + t0)] + [("h", c0 + t) for t in range((t0 // 256) * 256, t0 + n, 256)], dma=True)
    if deep:
        P.barrier()
        ffn_views(g)


def carve(g, off_bytes, ncols, dt):
    assert off_bytes % 4 == 0
    e0 = off_bytes // 2
    if dt == BF16:
        assert e0 + ncols <= ARENA
        return g.arena[:, e0:e0 + ncols]
    assert e0 + 2 * ncols <= ARENA
    return g.arena[:, e0:e0 + 2 * ncols].bitcast(F32)


def lin_views(g):
    g.xn = g.arena[:, 0:DC * TB]
    o_ = DC * TB * 2
    stage = [carve(g, o_ + i * 16384, 4096, F32) for i in range(6)]
    o_ += 6 * 16384
    wbf = [carve(g, o_ + i * 8192, 4096, BF16) for i in range(3)]
    g.stg, g.wbr = Ring("stage", stage[0:3] + stage[4:6]), Ring("wbf", wbf)
    g.ost = Ring("ost_l", [carve(g, DC * TB * 2 + 3 * 16384 + i * 2048, 512, F32) for i in range(8)])
    g.psr = Ring("ps", g.ps)


def ffn_views(g):
    g.xn = g.arena[:, 0:DC * TB]
    g.act = g.arena[:, DC * TB:(DC + FC) * TB]
    o_ = (DC + FC) * TB
    stage = [g.arena[:, o_ + i * 8192: o_ + (i + 1) * 8192].bitcast(F32) for i in range(3)]
    o_ += 3 * 8192
    wbf = [g.arena[:, o_ + i * 4096: o_ + (i + 1) * 4096] for i in range(2)]
    g.stg, g.wbr = Ring("stage", stage), Ring("wbf", wbf)
    g.psr = Ring("ps", g.ps)
    g.ost = Ring("ost", g.ost_global)


def stage_lbcols(P, g):
    P.add("sp", lambda e: e.dma_start(out=g.lbl[:, :], in_=g.d_lbl), writes=["lbl"], dma=True)
    P.add("pool", lambda e: e.memset(g.lbc[:, :], 0.0), writes=["lbc"])
    for d_ in range(2):
        b0 = ((1 * 2 + d_) * 3) * 16
        l0 = g.lbl[:, (d_ * 2 + 0) * 16:(d_ * 2 + 1) * 16]
        l1 = g.lbl[:, (d_ * 2 + 1) * 16:(d_ * 2 + 2) * 16]
        P.add("dve", lambda e, b0=b0, l0=l0, l1=l1: e.tensor_tensor(g.lbc[:, b0:b0 + 16], l1, l0, ALU.subtract),
              reads=["lbl", "lbc"], writes=["lbc"])
        P.add("act", lambda e, b0=b0: e.activation(out=g.lbc[:, b0:b0 + 16], in_=g.lbc[:, b0:b0 + 16], func=AF.Sigmoid),
              reads=["lbc"], writes=["lbc"])
    for j in range(2):
        for d_ in range(2):
            b0 = ((j * 2 + d_) * 3) * 16
            P.add("dve", lambda e, b0=b0: e.tensor_scalar(g.lbc[:, b0 + 16:b0 + 32], g.lbc[:, b0:b0 + 16], -1.0, 1.0, ALU.mult, ALU.add),
                  reads=["lbc"], writes=["lbc"])
            P.add("dve", lambda e, b0=b0: e.tensor_scalar(g.lbc[:, b0 + 32:b0 + 48], g.lbc[:, b0:b0 + 16], 1.0, -1.0, ALU.mult, ALU.add),
                  reads=["lbc"], writes=["lbc"])
    P.add("sp", lambda e: e.dma_start(out=g.cst[:, :], in_=g.d_cst), writes=["cst"], dma=True)
    P.add("act", lambda e: e.copy(g.identb[:, :], g.cst[:, 0:128]), reads=["cst"], writes=["identb"])


def stage_hgrn_inproj(P, g, l, j):
    lin_views(g)
    gsrc = g.gcols[:, (l * 3 + 1) * 16:(l * 3 + 2) * 16]
    for (c0, nb, kind) in BLOCKS:
        emit_cols(P, g, l, 1, kind, gsrc, False)
        gcol = lambda dc: g.dcol[:, dc:dc + 1]
        shcol = lambda dc, kind=kind: modcol(g, l, kind, 1, 0, dc)
        emit_adanorm(P, g, g.d_h, c0, nb, gcol, shcol, ["dcol", "modc"])
        emit_linear(P, g, g.xn, DC, nb, g.d_hgw_in[j], 80, epi_store(P, g, g.d_pj, c0, "pj"))
    P.barrier()


def stage_hgrn_scan(P, g, j, nheads=16):
    NTI = L // 128
    NCH = L // 32
    SZ = L * 4
    Q = carve(g, 0 * SZ, L, F32)
    Z = carve(g, 1 * SZ, L, F32)
    A = carve(g, 2 * SZ, L, F32)
    Bc = carve(g, 3 * SZ, L, F32)
    E = carve(g, 4 * SZ, L, F32)
    I = carve(g, 5 * SZ, L, F32)
    o_ = 6 * SZ
    qd = carve(g, o_, L, BF16); o_ += L * 2
    ki = carve(g, o_, L, BF16); o_ += L * 2
    vtok = carve(g, o_, L, BF16); o_ += L * 2
    ibf = carve(g, o_, L, BF16); o_ += L * 2
    dec = carve(g, o_, NCH, F32); o_ += NCH * 4
    S32 = [carve(g, o_ + i * 512, 128, F32) for i in range(4)]; o_ += 4 * 512
    T32 = [carve(g, o_ + i * 512, 128, F32) for i in range(4)]; o_ += 4 * 512
    Sbf = [carve(g, o_ + i * 256, 128, BF16) for i in range(4)]; o_ += 4 * 256
    ATm = [carve(g, o_ + i * 256, 128, BF16) for i in range(3)]; o_ += 3 * 256
    KT = [carve(g, o_ + i * 1024, 512, BF16) for i in range(3)]; o_ += 3 * 1024
    OS = [carve(g, o_ + i * 512, 128, F32) for i in range(4)]; o_ += 4 * 512
    assert o_ <= ARENA * 2
    s32r, t32r, sbfr, atmr, ktr, osr = Ring("S32", S32), Ring("T32", T32), Ring("Sbf", Sbf), Ring("ATm", ATm), Ring("KT", KT), Ring("OS", OS)
    par, ptr, por, pur = Ring("ps", g.ps[0:2]), Ring("ps2", g.ps[2:3]), Ring("ps3", g.ps[3:5]), Ring("ps5", g.ps[5:7])
    pv = g.ps[7]
    identb = g.identb
    maskf = [g.cst[:, 128:256], g.cst[:, 256:384]]
    m01 = g.cst[:, 384:512]
    HALF = L // 2
    for h in range(nheads):
        hs = slice(h * 128, (h + 1) * 128)
        for hf in range(2):
            cs_ = slice(hf * HALF, (hf + 1) * HALF)
            P.add("sp", lambda e, cs_=cs_, h=h: e.dma_start(out=Q[:, cs_], in_=g.d_pj[h * 128:(h + 1) * 128, cs_]),
                  reads=["pj_all"], writes=[("Q", hf)], dma=True)
            P.add("sp", lambda e, cs_=cs_, h=h: e.dma_start(out=I[:, cs_], in_=g.d_pj[6144 + h * 128:6144 + (h + 1) * 128, cs_]),
                  reads=["pj_all"], writes=[("I", hf)], dma=True)
            P.add("act", lambda e, cs_=cs_: e.activation(out=Q[:, cs_], in_=Q[:, cs_], func=AF.Silu), reads=[("Q", hf)], writes=[("Q", hf)])
            P.add("pool", lambda e, cs_=cs_: e.tensor_copy(ibf[:, cs_], I[:, cs_]), reads=[("I", hf)], writes=[("ibf", hf)])
        pvb = pv[:, :].bitcast(BF16)
        for t4 in range(0, NTI, 4):
            nt_ = min(4, NTI - t4)
            for q in range(nt_):
                ti = t4 + q
                P.add("pe", lambda e, q=q, ti=ti: e.transpose(pvb[:, q * 128:(q + 1) * 128], ibf[:, ti * 128:(ti + 1) * 128], identb[:, :]),
                      reads=[("ibf", 0), ("ibf", 1), "identb"], writes=["pv"])
            P.add("act", lambda e, t4=t4, nt_=nt_: e.copy(vtok[:, t4 * 128:(t4 + nt_) * 128], pvb[:, 0:nt_ * 128]),
                  reads=["pv"], writes=["vtok"])
        for d_ in range(2):
            lb0 = ((j * 2 + d_) * 3) * 16
            lbcol = g.lbc[:, lb0 + h: lb0 + h + 1]
            omlcol = g.lbc[:, lb0 + 16 + h: lb0 + 16 + h + 1]
            nomlcol = g.lbc[:, lb0 + 32 + h: lb0 + 32 + h + 1]
            zrow = 2048 * (1 + d_) + h * 128
            P.add("sp", lambda e, zrow=zrow: e.dma_start(out=Z[:, :], in_=g.d_pj[zrow:zrow + 128, :]), reads=["pj_all"], writes=["Z"], dma=True)
            P.add("act", lambda e: e.activation(out=Z[:, :], in_=Z[:, :], func=AF.Sigmoid), reads=["Z"], writes=["Z"])
            P.add("dve", lambda e, omlcol=omlcol, lbcol=lbcol: e.tensor_scalar(A[:, :], Z[:, :], omlcol, lbcol, ALU.mult, ALU.add),
                  reads=["Z", "lbc"], writes=["A"])
            P.add("act", lambda e: e.activation(out=A[:, :], in_=A[:, :], func=AF.Ln), reads=["A"], writes=["A"])
            P.add("dve", lambda e, omlcol=omlcol, nomlcol=nomlcol: e.tensor_scalar(Z[:, :], Z[:, :], nomlcol, omlcol, ALU.mult, ALU.add),
                  reads=["Z", "lbc"], writes=["Z"])
            for (t0, n) in tgroups(L, 128):
                P.add("dve", lambda e, t0=t0, n=n: e.tensor_tensor_scan(Bc[:, t0:t0 + n], m01[:, 0:n], A[:, t0:t0 + n], 0.0, ALU.mult, ALU.add),
                      reads=["A", "cst"], writes=["Bc"])
            P.add("act", lambda e: e.activation(out=dec[:, :], in_=Bc[:, :].rearrange("p (n c) -> p n c", c=32)[:, :, 31], func=AF.Exp),
                  reads=["Bc"], writes=["dec"])
            if d_ == 1:
                P.add("dve", lambda e: e.tensor_tensor(Bc[:, :], Bc[:, :], A[:, :], ALU.subtract), reads=["Bc", "A", "dec"], writes=["Bc"])
            sq_, sk_ = (1.0, -1.0) if d_ == 0 else (-1.0, 1.0)
            P.add("act", lambda e, sq_=sq_: e.activation(out=E[:, :], in_=Bc[:, :], func=AF.Exp, scale=sq_), reads=["Bc"], writes=["E"])
            P.add("dve", lambda e: e.tensor_tensor(qd[:, :], Q[:, :], E[:, :], ALU.mult), reads=["E", ("Q", 0), ("Q", 1)], writes=["qd"])
            P.add("act", lambda e, sk_=sk_: e.activation(out=E[:, :], in_=Bc[:, :], func=AF.Exp, scale=sk_), reads=["Bc", "qd"], writes=["E"])
            P.add("dve", lambda e: e.tensor_tensor(ki[:, :], Z[:, :], E[:, :], ALU.mult), reads=["E", "Z"], writes=["ki"])
            s_cur, s_cur_k = s32r.next()
            P.add("pool", lambda e, s_cur=s_cur: e.memset(s_cur[:, :], 0.0), writes=[s_cur_k])
            sb_cur, sb_cur_k = sbfr.next()
            P.add("pool", lambda e, sb_cur=sb_cur: e.memset(sb_cur[:, :], 0.0), writes=[sb_cur_k])
            if d_ == 0:
                order = list(range(NTI))
            else:
                order = [1, 0] + list(range(NTI - 1, 1, -1))
            dst = g.d_of if d_ == 0 else g.d_ob
            fr = {}

            def front(ti):
                ts_ = slice(ti * 128, (ti + 1) * 128)
                pa, pak = par.next()
                P.add("pe", lambda e: e.matmul(pa[:, 0:128], lhsT=ki[:, ts_], rhs=qd[:, ts_], start=True, stop=True),
                      reads=["ki", "qd"], writes=[pak])
                am, amk = atmr.next()
                P.add("dve", lambda e: e.tensor_tensor(am[:, :], pa[:, 0:128], maskf[d_], ALU.mult), reads=[pak, "cst"], writes=[amk])
                pt, ptk = ptr.next()
                ptb = pt[:, :].bitcast(BF16)
                P.add("pe", lambda e: e.transpose(ptb[:, 0:128], ki[:, ts_], identb[:, :]), reads=["ki", "identb"], writes=[ptk])
                kt, ktk = ktr.next()
                for c in range(4):
                    P.add("act", lambda e, c=c: e.activation(out=kt[:, c * 128:(c + 1) * 128], in_=ptb[:, 0:128], func=AF.Identity,
                                                            scale=g.cst[:, 512 + c:513 + c]), reads=[ptk, "cst"], writes=[(ktk, c)])
                po, pok = por.next()
                P.add("pe", lambda e: e.matmul(po[:, 0:128], lhsT=vtok[:, ts_], rhs=am[:, :], start=True, stop=False),
                      reads=["vtok", amk], writes=[pok])
                pu, puk = pur.next()
                for c in range(4):
                    P.add("pe", lambda e, c=c: e.matmul(pu[:, c * 128:(c + 1) * 128], lhsT=kt[:, c * 128:(c + 1) * 128], rhs=vtok[:, ts_],
                                                        start=True, stop=True), reads=[(ktk, c), "vtok"], writes=[(puk, c)])
                fr[ti] = (po, pok, pu, puk)

            def chain(ti, s_cur, s_cur_k, sb_cur, sb_cur_k):
                po, pok, pu, puk = fr.pop(ti)
                corder = range(4) if d_ == 0 else range(3, -1, -1)
                for ci, c in enumerate(corder):
                    ch = ti * 4 + c
                    cs_ = slice(ti * 128 + c * 32, ti * 128 + c * 32 + 32)
                    last = (ci == 3)
                    if d_ == 0:
                        P.add("pe", lambda e, c=c, sb_cur=sb_cur, cs_=cs_, last=last: e.matmul(
                            po[:, c * 32:(c + 1) * 32], lhsT=sb_cur[:, :], rhs=qd[:, cs_], start=False, stop=last),
                            reads=[sb_cur_k, "qd"], writes=[pok])
                        tt, ttk = t32r.next()
                        P.add("dve", lambda e, tt=tt, s_cur=s_cur, c=c: e.tensor_tensor(tt[:, :], pu[:, c * 128:(c + 1) * 128], s_cur[:, :], ALU.add),
                              reads=[(puk, c), s_cur_k], writes=[ttk])
                        s_new, s_new_k = s32r.next()
                        P.add("dve", lambda e, tt=tt, s_new=s_new, ch=ch: e.tensor_scalar_mul(s_new[:, :], tt[:, :], dec[:, ch:ch + 1]),
                              reads=[ttk, "dec"], writes=[s_new_k])
                        sb_new, sb_new_k = sbfr.next()
                        P.add("act", lambda e, sb_new=sb_new, tt=tt, ch=ch: e.activation(
                            out=sb_new[:, :], in_=tt[:, :], func=AF.Identity, scale=dec[:, ch:ch + 1]), reads=[ttk, "dec"], writes=[sb_new_k])
                    else:
                        tt, ttk = t32r.next()
                        P.add("dve", lambda e, tt=tt, s_cur=s_cur, ch=ch: e.tensor_scalar_mul(tt[:, :], s_cur[:, :], dec[:, ch:ch + 1]),
                              reads=[s_cur_k, "dec"], writes=[ttk])
                        sb_new, sb_new_k = sbfr.next()
                        P.add("act", lambda e, sb_new=sb_new, tt=tt: e.copy(sb_new[:, :], tt[:, :]), reads=[ttk], writes=[sb_new_k])
                        P.add("pe", lambda e, c=c, sb_new=sb_new, cs_=cs_, last=last: e.matmul(
                            po[:, c * 32:(c + 1) * 32], lhsT=sb_new[:, :], rhs=qd[:, cs_], start=False, stop=last),
                            reads=[sb_new_k, "qd"], writes=[pok])
                        s_new, s_new_k = s32r.next()
                        P.add("dve", lambda e, tt=tt, s_new=s_new, c=c: e.tensor_tensor(s_new[:, :], pu[:, c * 128:(c + 1) * 128], tt[:, :], ALU.add),
                              reads=[(puk, c), ttk], writes=[s_new_k])
                    s_cur, s_cur_k, sb_cur, sb_cur_k = s_new, s_new_k, sb_new, sb_new_k
                os_, osk = osr.next()
                P.add("act", lambda e: e.copy(os_[:, :], po[:, 0:128]), reads=[pok], writes=[osk])
                P.add(ST, lambda e: e.dma_start(out=dst[h * 128:(h + 1) * 128, ti * 128:(ti + 1) * 128], in_=os_[:, :]),
                      reads=[osk], writes=[("o", d_, h, ti)], dma=True)
                return s_cur, s_cur_k, sb_cur, sb_cur_k

            front(order[0])
            for i_, ti in enumerate(order):
                if i_ + 1 < len(order):
                    front(order[i_ + 1])
                s_cur, s_cur_k, sb_cur, sb_cur_k = chain(ti, s_cur, s_cur_k, sb_cur, sb_cur_k)
    P.barrier()


def stage_hgrn_readout(P, g, l, j, skip_ctx):
    lin_views(g)
    gtcol = None
    for (c0, nb, kind) in BLOCKS:
        if kind == 1 and skip_ctx:
            continue
        hgrn_readout_block(P, g, l, j, c0, nb, kind)
    P.barrier()


def hgrn_readout_block(P, g, l, j, c0, nb, kind):
    gtcol = lambda oc: modcol(g, l, kind, 1, 2, oc)
    ofv = g.d_of.rearrange("(c p) t -> p c t", p=128)
    obv = g.d_ob.rearrange("(c p) t -> p c t", p=128)
    gtv = g.d_pj[8192:10240, :].rearrange("(c p) t -> p c t", p=128)
    for (t0, n) in tgroups(nb, 256):
        s1, k1 = g.stg.next()
        s2, k2 = g.stg.next()
        s3, k3 = g.stg.next()
        for (sx, kx, src) in ((s1, k1, ofv), (s2, k2, obv), (s3, k3, gtv)):
            P.add("sp", lambda e, sx=sx, src=src: e.dma_start(
                out=sx[:, 0:DC * n].rearrange("p (c t) -> p c t", t=n), in_=src[:, :, c0 + t0:c0 + t0 + n]), writes=[kx], dma=True)
        P.add("pool", lambda e: e.tensor_tensor(s1[:, 0:DC * n], s1[:, 0:DC * n], s2[:, 0:DC * n], ALU.add), reads=[k1, k2], writes=[k1])
        P.add("act", lambda e: e.activation(out=s3[:, 0:DC * n], in_=s3[:, 0:DC * n], func=AF.Silu), reads=[k3], writes=[k3])
        for dc in range(DC):
            sq_, sqk = g.sq.next()
            ssp, ssk = g.psr.next()
            P.add("act", lambda e, sq_=sq_, dc=dc: e.activation(out=sq_[:, 0:n], in_=s1[:, dc * n:(dc + 1) * n], func=AF.Square),
                  reads=[k1], writes=[sqk])
            P.add("pe", lambda e, sq_=sq_, ssp=ssp: e.matmul(ssp[:, 0:n], lhsT=g.ones[:, :], rhs=sq_[:, 0:n], start=True, stop=True),
                  reads=[sqk, "ones"], writes=[ssk])
            rs, rsk = emit_rstd(P, g, ssp, ssk, n, 128)
            tm, tmk = g.tmp.next()
            P.add("dve", lambda e, tm=tm, rs=rs, dc=dc: e.scalar_tensor_tensor(
                out=tm[:, 0:n], in0=s1[:, dc * n:(dc + 1) * n], scalar=g.hgn[:, j * 16 + dc: j * 16 + dc + 1], in1=rs[:, 0:n],
                op0=ALU.mult, op1=ALU.mult), reads=[k1, rsk, "hgn"], writes=[tmk])
            P.add("pool", lambda e, tm=tm, dc=dc: e.tensor_tensor(
                g.xn[:, dc * nb + t0: dc * nb + t0 + n], tm[:, 0:n], s3[:, dc * n:(dc + 1) * n], ALU.mult),
                reads=[tmk, k3], writes=[("xn", dc, t0)])
    emit_linear(P, g, g.xn, DC, nb, g.d_hgw_out[j], DC, epi_residual(P, g, g.d_h, c0, gtcol, ["modc"]))


MLA_SCALE = (128 + 64) ** -0.5


def stage_mla_proj(P, g, l):
    gsrc = g.gcols[:, (l * 3 + 1) * 16:(l * 3 + 2) * 16]
    for (c0, nb, kind) in BLOCKS:
        mla_proj_block(P, g, l, gsrc, c0, nb, kind)
    P.barrier()


def mla_proj_block(P, g, l, gsrc, c0, nb, kind):
    g.xn = g.arena[:, 0:DC * TB]
    cbuf = carve(g, 32768, 8 * TB, F32)
    cn = carve(g, 65536, 8 * TB, BF16)
    rope = carve(g, 81920, 2 * TB, F32)
    vbf = [carve(g, 90112 + i * 2048, TB, BF16) for i in range(2)]
    obf = [carve(g, 94208 + i * 1024, 512, BF16) for i in range(4)]
    stage = [carve(g, 98304 + i * 16384, 4096, F32) for i in range(3)]
    wbf = [carve(g, 147456 + i * 8192, 4096, BF16) for i in range(3)]
    rt = [carve(g, 172032 + i * 2048, 512, F32) for i in range(4)]
    g.stg, g.wbr = Ring("stage", stage), Ring("wbf", wbf)
    g.psr = Ring("ps", g.ps[0:7])
    vbr, obr, rtr = Ring("vbf", vbf), Ring("obf", obf), Ring("rt", rt)
    emit_cols(P, g, l, 1, kind, gsrc, False)
    gcol = lambda dc: g.dcol[:, dc:dc + 1]
    shcol = lambda dc: modcol(g, l, kind, 1, 0, dc)
    emit_adanorm(P, g, g.d_h, c0, nb, gcol, shcol, ["dcol", "modc"])
    P.add("sp", lambda e: e.dma_start(out=rope[:, 0:2 * nb].rearrange("p (a t) -> p a t", a=2),
                                      in_=g.d_rope.rearrange("p (a t) -> p a t", a=2)[:, :, c0:c0 + nb]), writes=["rope"], dma=True)
    pvb = g.ps[7][:, :].bitcast(BF16)

    def rope_epi(dst, first):
        st_ = {}

        def epi(oc, t0, n, pp, ppk):
            if first(oc):
                r1, r1k = rtr.next()
                P.add("dve", lambda e: e.tensor_tensor(r1[:, 0:n], pp[:, 0:n], rope[:, t0:t0 + n], ALU.mult), reads=[ppk, "rope"], writes=[r1k])
                st_[t0] = (r1, r1k)
            else:
                r1, r1k = st_.pop(t0)
                r2, r2k = rtr.next()
                P.add("dve", lambda e: e.tensor_tensor(r2[:, 0:n], pp[:, 0:n], rope[:, nb + t0:nb + t0 + n], ALU.mult), reads=[ppk, "rope"], writes=[r2k])
                o, ok = obr.next()
                P.add("pool", lambda e: e.tensor_tensor(o[:, 0:n], r1[:, 0:n], r2[:, 0:n], ALU.add), reads=[r1k, r2k], writes=[ok])
                P.add(ST, lambda e: e.dma_start(out=dst(oc)[:, c0 + t0:c0 + t0 + n], in_=o[:, 0:n]), reads=[ok], writes=[("mla_o", oc, c0 + t0)], dma=True)
        return epi

    def store_bf(dst):
        def epi(oc, t0, n, pp, ppk):
            o, ok = obr.next()
            if (oc + t0 // 512) % 2 == 0:
                P.add("act", lambda e: e.copy(o[:, 0:n], pp[:, 0:n]), reads=[ppk], writes=[ok])
            else:
                P.add("dve", lambda e: e.tensor_copy(o[:, 0:n], pp[:, 0:n]), reads=[ppk], writes=[ok])
            P.add(ST, lambda e: e.dma_start(out=dst(oc)[:, c0 + t0:c0 + t0 + n], in_=o[:, 0:n]), reads=[ok], writes=[("mla_o2", oc, c0 + t0)], dma=True)
        return epi

    krope = rope_epi(lambda oc: g.d_kr, lambda oc: oc == 8)

    def epi1(oc, t0, n, pp, ppk):
        if oc < 8:
            if oc % 2 == 0:
                P.add("act", lambda e: e.copy(cbuf[:, oc * nb + t0: oc * nb + t0 + n], pp[:, 0:n]), reads=[ppk], writes=[("cb", oc, t0)])
            else:
                P.add("dve", lambda e: e.tensor_copy(cbuf[:, oc * nb + t0: oc * nb + t0 + n], pp[:, 0:n]), reads=[ppk], writes=[("cb", oc, t0)])
        else:
            krope(oc, t0, n, pp, ppk)
    emit_linear(P, g, g.xn, DC, nb, g.d_wdqkv, 10, epi1)
    for grp in range(2):
        for (t0, n) in tgroups(nb, 256):
            ssp, ssk = g.psr.next()
            for q in range(4):
                oc = grp * 4 + q
                sq, sqk = g.sq.next()
                P.add("act", lambda e, sq=sq, oc=oc: e.activation(out=sq[:, 0:n], in_=cbuf[:, oc * nb + t0: oc * nb + t0 + n], func=AF.Square),
                      reads=[("cb", oc, (t0 // 512) * 512)], writes=[sqk])
                P.add("pe", lambda e, sq=sq, q=q: e.matmul(ssp[:, 0:n], lhsT=g.ones[:, :], rhs=sq[:, 0:n], start=(q == 0), stop=(q == 3)),
                      reads=[sqk, "ones"], writes=[ssk])
            rs, rsk = emit_rstd(P, g, ssp, ssk, n, 512)
            for q in range(4):
                oc = grp * 4 + q
                P.add("dve", lambda e, rs=rs, oc=oc: e.scalar_tensor_tensor(
                    out=cn[:, oc * nb + t0: oc * nb + t0 + n], in0=cbuf[:, oc * nb + t0: oc * nb + t0 + n],
                    scalar=g.mlan[:, oc:oc + 1], in1=rs[:, 0:n], op0=ALU.mult, op1=ALU.mult),
                    reads=[("cb", oc, (t0 // 512) * 512), rsk, "mlan"], writes=[("cn" if oc < 4 else "cn4", oc % 4, t0)])
    qrope = rope_epi(lambda oc: g.d_qr[oc // 3], lambda oc: oc % 3 == 1)
    qn_store = store_bf(lambda oc: g.d_qn[oc // 3])

    def epi3(oc, t0, n, pp, ppk):
        if oc % 3 == 0:
            qn_store(oc, t0, n, pp, ppk)
        else:
            qrope(oc, t0, n, pp, ppk)
    emit_linear(P, g, cn, 4, nb, g.d_wuq, 48, epi3, xpref="cn")
    kn_store = store_bf(lambda oc: g.d_kn[oc // 2])

    def epi4(oc, t0, n, pp, ppk):
        if oc % 2 == 0:
            kn_store(oc, t0, n, pp, ppk)
        else:
            hh = oc // 2
            v, vk = vbr.next()
            P.add("act", lambda e: e.copy(v[:, 0:n], pp[:, 0:n]), reads=[ppk], writes=[vk])
            for q in range(n // 128):
                P.add("pe", lambda e, q=q: e.transpose(pvb[:, q * 128:(q + 1) * 128], v[:, q * 128:(q + 1) * 128], g.identb[:, :]),
                      reads=[vk, "identb"], writes=["pv"])
            o, ok = obr.next()
            P.add("dve", lambda e: e.tensor_copy(o[:, 0:n], pvb[:, 0:n]), reads=["pv"], writes=[ok])
            tb0 = (c0 + t0) // 128
            P.add(ST, lambda e: e.dma_start(
                out=g.d_vt[hh].rearrange("(n p) v -> p n v", p=128)[:, tb0:tb0 + n // 128, :],
                in_=o[:, 0:n].rearrange("p (n v) -> p n v", v=128)), reads=[ok], writes=[("mla_v", oc, c0 + t0)], dma=True)
    emit_linear(P, g, cn[:, 4 * nb:8 * nb], 4, nb, g.d_wukv, 32, epi4, xpref="cn4")


def stage_mla_attn(P, g, last):
    NTI = L // 128
    Kr = carve(g, 0, L, BF16)
    o_ = L * 2
    Kn = [carve(g, o_ + i * L * 2, L, BF16) for i in range(2)]; o_ += 2 * L * 2
    Vt = [carve(g, o_ + i * L * 2, L, BF16) for i in range(2)]; o_ += 2 * L * 2
    Qn = [carve(g, o_ + i * L * 2, L, BF16) for i in range(2)]; o_ += 2 * L * 2
    Qr = [carve(g, o_ + i * L * 2, L, BF16) for i in range(2)]; o_ += 2 * L * 2
    pT = [carve(g, o_ + i * 1024, 512, BF16) for i in range(4)]; o_ += 4 * 1024
    rl = [carve(g, o_ + i * 2048, 512, F32) for i in range(2)]; o_ += 2 * 2048
    oo = [carve(g, o_ + i * 2048, 512, F32) for i in range(3)]; o_ += 3 * 2048
    onesb = carve(g, o_, 128, BF16); o_ += 256
    assert o_ <= ARENA * 2
    knr, vtr, qnr, qrr, ptr_, rlr, oor = Ring("Kn", Kn), Ring("Vt", Vt), Ring("Qn", Qn), Ring("Qr", Qr), Ring("pT", pT), Ring("rl", rl), Ring("oo", oo)
    psr_s, psr_o, psr_l = Ring("ps", g.ps[0:3]), Ring("ps3", g.ps[3:5]), Ring("ps5", g.ps[5:7])
    P.add("pool", lambda e: e.memset(onesb[:, :], 1.0), writes=["onesb"])
    P.add("sp", lambda e: e.dma_start(out=Kr[:, :], in_=g.d_kr), writes=["Kr"], dma=True)
    qblocks = [(NCTX + 512 * i, 512, 0, NTI) for i in range(NLAT // 512)]
    if not last:
        qblocks.append((0, NCTX, 0, NCTX // 128))
    for h in range(16):
        kn, knk = knr.next()
        vt, vtk = vtr.next()
        qn, qnk = qnr.next()
        qr, qrk = qrr.next()
        P.add("sp", lambda e, kn=kn, h=h: e.dma_start(out=kn[:, :], in_=g.d_kn[h]), writes=[knk], dma=True)
        P.add("sp", lambda e, vt=vt, h=h: e.dma_start(out=vt[:, :].rearrange("p (n v) -> p n v", v=128),
                                                     in_=g.d_vt[h].rearrange("(n p) v -> p n v", p=128)), writes=[vtk], dma=True)
        P.add("sp", lambda e, qn=qn, h=h: e.dma_start(out=qn[:, :], in_=g.d_qn[h]), writes=[qnk], dma=True)
        P.add("sp", lambda e, qr=qr, h=h: e.dma_start(out=qr[:, :], in_=g.d_qr[h]), writes=[qrk], dma=True)
        for (q0, nq, k0, k1) in qblocks:
            po, pok = psr_o.next()
            pl, plk = psr_l.next()
            def scores(kt):
                ks = slice(kt * 128, (kt + 1) * 128)
                ps_, psk = psr_s.next()
                P.add("pe", lambda e: e.matmul(
                    ps_[:, 0:nq], lhsT=kn[:, ks], rhs=qn[:, q0:q0 + nq], start=True, stop=False), reads=[knk, qnk], writes=[psk])
                P.add("pe", lambda e: e.matmul(
                    ps_[:, 0:nq], lhsT=Kr[:, ks], rhs=qr[:, q0:q0 + nq], start=False, stop=True), reads=["Kr", qrk], writes=[psk])
                return ps_, psk

            nxt = scores(k0)
            for kt in range(k0, k1):
                ks = slice(kt * 128, (kt + 1) * 128)
                ps_, psk = nxt
                if kt + 1 < k1:
                    nxt = scores(kt + 1)
                p_, pk = ptr_.next()
                P.add("act", lambda e, p_=p_, ps_=ps_, nq=nq: e.activation(out=p_[:, 0:nq], in_=ps_[:, 0:nq], func=AF.Exp, scale=MLA_SCALE),
                      reads=[psk], writes=[pk])
                P.add("pe", lambda e, p_=p_, po=po, vt=vt, ks=ks, nq=nq, kt=kt, k0=k0, k1=k1: e.matmul(
                    po[:, 0:nq], lhsT=vt[:, ks], rhs=p_[:, 0:nq], start=(kt == k0), stop=(kt == k1 - 1)), reads=[pk, vtk], writes=[pok])
                P.add("pe", lambda e, p_=p_, pl=pl, nq=nq, kt=kt, k0=k0, k1=k1: e.matmul(
                    pl[:, 0:nq], lhsT=onesb[:, :], rhs=p_[:, 0:nq], start=(kt == k0), stop=(kt == k1 - 1)), reads=[pk, "onesb"], writes=[plk])
            r_, rk = rlr.next()
            P.add("dve", lambda e, r_=r_, pl=pl, nq=nq: e.reciprocal(r_[:, 0:nq], pl[:, 0:nq]), reads=[plk], writes=[rk])
            o, ok = oor.next()
            P.add("dve", lambda e, o=o, po=po, r_=r_, nq=nq: e.tensor_tensor(o[:, 0:nq], po[:, 0:nq], r_[:, 0:nq], ALU.mult),
                  reads=[pok, rk], writes=[ok])
            P.add(ST, lambda e, o=o, h=h, q0=q0, nq=nq: e.dma_start(out=g.d_of[h * 128:(h + 1) * 128, q0:q0 + nq], in_=o[:, 0:nq]),
                  reads=[ok], writes=[("ao", h, q0)], dma=True)
    P.barrier()


def stage_outproj(P, g, l, src, wsrc, skip_ctx):
    lin_views(g)
    for (c0, nb, kind) in BLOCKS:
        if kind == 1 and skip_ctx:
            continue
        outproj_block(P, g, l, src, wsrc, c0, nb, kind)
    P.barrier()


def outproj_block(P, g, l, src, wsrc, c0, nb, kind):
    gtcol = lambda oc: modcol(g, l, kind, 1, 2, oc)
    sv = src.rearrange("(c p) t -> p c t", p=128)
    for (t0, n) in tgroups(nb, 256):
        s_, k_ = g.stg.next()
        P.add("sp", lambda e, s_=s_, t0=t0, n=n: e.dma_start(
            out=s_[:, 0:DC * n].rearrange("p (c t) -> p c t", t=n), in_=sv[:, :, c0 + t0:c0 + t0 + n]), writes=[k_], dma=True)
        for dc in range(DC):
            cast_op(P, g, dc, g.xn[:, dc * nb + t0: dc * nb + t0 + n], s_[:, dc * n:(dc + 1) * n], [k_], [("xn", dc, t0)])
    emit_linear(P, g, g.xn, DC, nb, wsrc, DC, epi_residual(P, g, g.d_h, c0, gtcol, ["modc"]))


def stage_fnet_a(P, g, l):
    gsrc = g.gcols[:, (l * 3 + 1) * 16:(l * 3 + 2) * 16]
    dst32 = carve(g, 32768, 1024, F32)
    dftc = carve(g, 32768 + 4096, 1024, BF16)
    P.add("sp", lambda e: e.dma_start(out=dst32[:, :], in_=g.d_dft256), writes=["dft32"], dma=True)
    P.add("act", lambda e: e.copy(dftc[:, :], dst32[:, :]), reads=["dft32"], writes=["dftc"])
    for (c0, nb, kind) in BLOCKS:
        fnet_a_block(P, g, l, gsrc, dftc, c0, nb, kind)
    P.barrier()


def fnet_a_block(P, g, l, gsrc, dftc, c0, nb, kind):
    g.xn = g.arena[:, 0:DC * TB]
    stage = [carve(g, 40960 + i * 16384, 4096, F32) for i in range(3)]
    xo = [carve(g, 90112 + i * 1024, 512, BF16) for i in range(4)]
    g.stg = Ring("stage", stage)
    g.psr = Ring("ps", g.ps)
    xor_ = Ring("xo", xo)
    emit_cols(P, g, l, 1, kind, gsrc, False)
    gcol = lambda dc: g.dcol[:, dc:dc + 1]
    shcol = lambda dc: modcol(g, l, kind, 1, 0, dc)
    emit_adanorm(P, g, g.d_h, c0, nb, gcol, shcol, ["dcol", "modc"])
    for tt in range(nb // 128):
        for gq in range(8):
            pp, ppk = g.psr.next()
            for cs_ in range(2):
                for kk in range(2):
                    dc = gq * 2 + kk
                    P.add("pe", lambda e: e.matmul(
                        pp[:, cs_ * 256:(cs_ + 1) * 256], lhsT=g.xn[:, dc * nb + tt * 128: dc * nb + (tt + 1) * 128],
                        rhs=dftc[:, kk * 512 + cs_ * 256: kk * 512 + (cs_ + 1) * 256], start=(kk == 0), stop=(kk == 1)),
                        reads=xkeys("xn", dc, tt * 128, 128) + ["dftc"], writes=[ppk])
            o, ok = xor_.next()
            if gq % 2 == 0:
                P.add("act", lambda e: e.copy(o[:, :], pp[:, :]), reads=[ppk], writes=[ok])
            else:
                P.add("dve", lambda e: e.tensor_copy(o[:, :], pp[:, :]), reads=[ppk], writes=[ok])
            r0 = c0 + tt * 128
            P.add(ST, lambda e: e.dma_start(out=g.d_xc[r0:r0 + 128, gq * 512:(gq + 1) * 512], in_=o[:, :]),
                  reads=[ok], writes=[("xc", r0, gq)], dma=True)


def stage_fnet_b(P, g):
    TC = NLAT // 128
    tabs = [carve(g, i * 32768, TC * 512, BF16) for i in range(2)]
    xt = [[carve(g, 65536 + (i * 2 + a) * 8192, TC * 128, BF16) for a in range(2)] for i in range(2)]
    oo = [carve(g, 98304 + i * 2048, 512, F32) for i in range(3)]
    ctab = carve(g, 104448, 2 * 2 * 256, BF16)
    xtr, oor = Ring("xt", xt), Ring("oo", oo)
    g.psr = Ring("ps", g.ps)
    xcl = g.d_xc[NCTX:L, :].rearrange("(n p) c -> p n c", p=128)
    xcc = g.d_xc[0:NCTX, :].rearrange("(n p) c -> p n c", p=128)
    for tb in range(NLAT // 512):
        for a in range(2):
            P.add("sp", lambda e: e.dma_start(
                out=tabs[a][:, :].rearrange("p (n t) -> p n t", t=512),
                in_=g.d_dftT[a].rearrange("(n p) t -> p n t", p=128)[:, :, tb * 512:(tb + 1) * 512]), writes=[("tab", a)], dma=True)
        for cc in range(16):
            x2, xk = xtr.next()
            for a in range(2):
                col = (cc // 2) * 512 + a * 256 + (cc % 2) * 128
                P.add("sp", lambda e: e.dma_start(out=x2[a][:, :].rearrange("p (n c) -> p n c", c=128), in_=xcl[:, :, col:col + 128]),
                      writes=[(xk, a)], dma=True)
            pp, ppk = g.psr.next()
            for a in range(2):
                for tc in range(TC):
                    P.add("pe", lambda e: e.matmul(pp[:, :], lhsT=x2[a][:, tc * 128:(tc + 1) * 128], rhs=tabs[a][:, tc * 512:(tc + 1) * 512],
                                                   start=(a == 0 and tc == 0), stop=(a == 1 and tc == TC - 1)),
                          reads=[(xk, a), ("tab", a)], writes=[ppk])
            o, ok = oor.next()
            if cc % 2 == 0:
                P.add("act", lambda e: e.copy(o[:, :], pp[:, :]), reads=[ppk], writes=[ok])
            else:
                P.add("dve", lambda e: e.tensor_copy(o[:, :], pp[:, :]), reads=[ppk], writes=[ok])
            P.add(ST, lambda e: e.dma_start(out=g.d_of[cc * 128:(cc + 1) * 128, NCTX + tb * 512: NCTX + (tb + 1) * 512], in_=o[:, :]),
                  reads=[ok], writes=[("fo", cc, tb)], dma=True)
    P.add("sp", lambda e: e.dma_start(out=ctab[:, :].rearrange("p (a n t) -> p a n t", a=2, t=256),
                                      in_=g.d_dftC.rearrange("a (n p) t -> p a n t", p=128)), writes=["ctab"], dma=True)
    for cc in range(16):
        x2, xk = xtr.next()
        for a in range(2):
            col = (cc // 2) * 512 + a * 256 + (cc % 2) * 128
            P.add("sp", lambda e: e.dma_start(out=x2[a][:, 0:256].rearrange("p (n c) -> p n c", c=128), in_=xcc[:, :, col:col + 128]),
                  writes=[(xk, a)], dma=True)
        pp, ppk = g.psr.next()
        for a in range(2):
            for tc in range(2):
                P.add("pe", lambda e: e.matmul(pp[:, 0:256], lhsT=x2[a][:, tc * 128:(tc + 1) * 128],
                                               rhs=ctab[:, (a * 2 + tc) * 256:(a * 2 + tc + 1) * 256],
                                               start=(a == 0 and tc == 0), stop=(a == 1 and tc == 1)),
                      reads=[(xk, a), "ctab"], writes=[ppk])
        o, ok = oor.next()
        P.add("act", lambda e: e.copy(o[:, 0:256], pp[:, 0:256]), reads=[ppk], writes=[ok])
        P.add(ST, lambda e: e.dma_start(out=g.d_of[cc * 128:(cc + 1) * 128, 0:NCTX], in_=o[:, 0:256]),
              reads=[ok], writes=[("fo", cc, -1)], dma=True)
    P.barrier()


def stage_final(P, g):
    for (c0, nb, kind) in BLOCKS:
        if kind == 1:
            continue
        final_block(P, g, c0, nb)


def final_block(P, g, c0, nb):
    if True:
        xTv = g.d_h.rearrange("(c p) t -> p c t", p=128)
        for (t0, n) in tgroups(nb, 256):
            xs, xk = g.stg.next()
            P.add("sp", lambda e, xs=xs, t0=t0, n=n: e.dma_start(
                out=xs[:, 0:DC * n].rearrange("p (c t) -> p c t", t=n), in_=xTv[:, :, c0 + t0:c0 + t0 + n]),
                reads=[("h", c0 + t0)], writes=[xk], dma=True)
            ssp, ssk = g.psr.next()
            for dc in range(DC):
                sq, sqk = g.sq.next()
                P.add("act", lambda e, sq=sq, xs=xs, dc=dc, n=n: e.activation(
                    out=sq[:, 0:n], in_=xs[:, dc * n:(dc + 1) * n], func=AF.Square), reads=[xk], writes=[sqk])
                P.add("pe", lambda e, sq=sq, ssp=ssp, dc=dc, n=n: e.matmul(
                    ssp[:, 0:n], lhsT=g.ones[:, :], rhs=sq[:, 0:n], start=(dc == 0), stop=(dc == DC - 1)),
                    reads=[sqk, "ones"], writes=[ssk])
            rs, rsk = emit_rstd(P, g, ssp, ssk, n, D)
            for dc in range(DC):
                P.add("dve", lambda e, xs=xs, rs=rs, dc=dc, n=n: e.scalar_tensor_tensor(
                    out=xs[:, dc * n:(dc + 1) * n], in0=xs[:, dc * n:(dc + 1) * n], scalar=g.fgcol[:, dc:dc + 1],
                    in1=rs[:, 0:n], op0=ALU.mult, op1=ALU.mult), reads=[xk, rsk, "gcols"], writes=[xk])
            ov = g.d_out.rearrange("(c p) t -> p c t", p=128)
            P.add(ST, lambda e, xs=xs, t0=t0, n=n, ov=ov: e.dma_start(
                out=ov[:, :, c0 - NCTX + t0: c0 - NCTX + t0 + n], in_=xs[:, 0:DC * n].rearrange("p (c t) -> p c t", t=n)),
                reads=[xk], dma=True)


def build(plan):
    nc = bass.Bass("TRN2", target_bir_lowering=False)
    g = G()
    kinds = set(p if isinstance(p, str) else p[0] for p in plan)
    fam = {"wgu1": "ffn", "wdn1": "ffn", "wgu2": "ffn", "wdn2": "ffn", "hgw_in": "hgrn", "hgw_out": "hgrn",
           "wdqkv": "mla", "wuq": "mla", "wukv": "mla", "wo": "mla", "rope": "mla", "fw": "fnet", "dft256": "fnet",
           "dftT": "fnet", "dftC": "fnet", "mod_w": "mod"}

    def di(name, shape, dt=F32):
        f = fam.get(name)
        if f is not None and not any(k.startswith(f) for k in kinds):
            return None
        return nc.dram_tensor(name, shape, dt, kind="ExternalInput").ap()
    g.d_x = di("x", [D, L])
    g.d_cT = di("cT", [128, 32])
    g.d_modw = di("mod_w", [4, D, 18432])
    g.d_modb = di("mod_b", [128, 4 * NMODC])
    g.d_gcols = di("gcols", [128, 13 * 16])
    g.d_wgu1 = di("wgu1", [4, FC, 128, 4096])
    g.d_wdn1 = di("wdn1", [4, DC, 128, DFF])
    g.d_wgu2 = di("wgu2", [4, FC, 128, 4096])
    g.d_wdn2 = di("wdn2", [4, DC, 128, DFF])
    g.d_hgw_in = di("hgw_in", [2, 80, 128, D])
    g.d_hgw_out = di("hgw_out", [2, DC, 128, D])
    g.d_hgn = di("hgn", [128, 32])
    g.d_lbl = di("lbl", [128, 64])
    g.d_cst = di("cst", [128, 516])
    g.d_wdqkv = di("wdqkv", [10, 128, D])
    g.d_wuq = di("wuq", [48, 128, 512])
    g.d_wukv = di("wukv", [32, 128, 512])
    g.d_wo = di("wo", [DC, 128, D])
    g.d_mlan = di("mlan", [128, 8])
    g.d_rope = di("rope", [128, 2 * L])
    g.d_fw = di("fw", [DC, 128, D])
    g.d_dft256 = di("dft256", [128, 1024])
    g.d_dftT = di("dftT", [2, NLAT, NLAT], BF16)
    g.d_dftC = di("dftC", [2, NCTX, NCTX], BF16)
    g.d_xc = nc.dram_tensor("xc_scr", [L, 4096], BF16).ap()
    g.d_kr = nc.dram_tensor("kr_scr", [128, L], BF16).ap()
    g.d_qn = nc.dram_tensor("qn_scr", [16, 128, L], BF16).ap()
    g.d_qr = nc.dram_tensor("qr_scr", [16, 128, L], BF16).ap()
    g.d_kn = nc.dram_tensor("kn_scr", [16, 128, L], BF16).ap()
    g.d_vt = nc.dram_tensor("vt_scr", [16, L, 128], BF16).ap()
    g.d_out = nc.dram_tensor("out", [D, NLAT], F32, kind="ExternalOutput").ap()
    g.d_h = nc.dram_tensor("h_scr", [D, L], F32).ap()
    g.d_pj = nc.dram_tensor("pj_scr", [10240, L], F32).ap()
    g.d_of = nc.dram_tensor("of_scr", [D, L], F32).ap()
    g.d_ob = nc.dram_tensor("ob_scr", [D, L], F32).ap()
    with contextlib.ExitStack() as st:
        sb = lambda name, shape, dt: st.enter_context(nc.sbuf_tensor(name, shape, dt))
        g.arena = sb("arena", [128, ARENA], BF16)
        g.xn = g.arena[:, 0:DC * TB]
        g.act = g.arena[:, DC * TB:(DC + FC) * TB]
        o_ = (DC + FC) * TB
        stage = [g.arena[:, o_ + i * 8192: o_ + (i + 1) * 8192].bitcast(F32) for i in range(3)]
        o_ += 3 * 8192
        wbf = [g.arena[:, o_ + i * 4096: o_ + (i + 1) * 4096] for i in range(2)]
        g.modc = sb("modc", [128, 4 * NMODC * 2], F32)
        g.gcols = sb("gcols_sb", [128, 13 * 16], F32)
        g.fgcol = g.gcols[:, 12 * 16:13 * 16]
        g.dcol = sb("dcol", [128, 32], F32)
        g.hgn = sb("hgn_sb", [128, 32], F32)
        g.mlan = sb("mlan_sb", [128, 8], F32)
        g.lbl = sb("lbl_sb", [128, 64], F32)
        g.lbc = sb("lbc", [128, 192], F32)
        g.cst = sb("cst_sb", [128, 516], F32)
        g.identb = sb("identb", [128, 128], BF16)
        g.cs = sb("cs", [128, 32], F32)
        g.ones = sb("ones", [128, 128], F32)
        sq = [sb("sq%d" % i, [128, 256], F32) for i in range(2)]
        rstd = [sb("rstd%d" % i, [128, 256], F32) for i in range(2)]
        tmp = [sb("tmp%d" % i, [128, 256], F32) for i in range(2)]
        sg = [sb("sg%d" % i, [128, 512], F32) for i in range(2)]
        ost = [sb("ost%d" % i, [128, 512], F32) for i in range(2)]
        g.ps = [st.enter_context(nc.psum_tensor("ps%d" % i, [128, 512], F32)) for i in range(8)]
        g.psr = Ring("ps", g.ps)
        g.stg, g.wbr = Ring("stage", stage), Ring("wbf", wbf)
        g.sq, g.rstd, g.tmp, g.sg, g.ost = Ring("sq", sq), Ring("rstd", rstd), Ring("tmp", tmp), Ring("sg", sg), Ring("ost", ost)
        g.ost_global = ost
        P = Prog(nc)
        P.add("pool", lambda e: e.memset(g.ones[:, :], 1.0), writes=["ones"])
        P.add("sp", lambda e: e.dma_start(out=g.gcols[:, :], in_=g.d_gcols), writes=["gcols"], dma=True)
        P.add("sp", lambda e: e.dma_start(out=g.hgn[:, :], in_=g.d_hgn), writes=["hgn"], dma=True)
        P.add("sp", lambda e: e.dma_start(out=g.mlan[:, :], in_=g.d_mlan), writes=["mlan"], dma=True)
        stage_lbcols(P, g)
        for i in range(DC):
            P.add("sp", lambda e, i=i: e.dma_start(out=g.d_h[i * 128:(i + 1) * 128, :], in_=g.d_x[i * 128:(i + 1) * 128, :]),
                  writes=[("hinit", i)], dma=True)
        P.barrier()
        for stg_ in plan:
            if stg_ == "mod":
                stage_mod(P, g)
                P.barrier()
            elif stg_[0] == "ffn":
                ffn_views(g)
                stage_ffn(P, g, stg_[1], stg_[2], skip_ctx=(len(stg_) > 3 and stg_[3]))
            elif stg_[0] == "fnet":
                l_ = stg_[1]
                stage_fnet_a(P, g, l_)
                stage_fnet_b(P, g)
                stage_outproj(P, g, l_, g.d_of, g.d_fw, False)
            elif stg_[0] == "mla":
                l_, last_ = stg_[1], stg_[2]
                stage_mla_proj(P, g, l_)
                stage_mla_attn(P, g, last_)
                stage_outproj(P, g, l_, g.d_of, g.d_wo, last_)
            elif stg_[0] == "hgrn_in":
                stage_hgrn_inproj(P, g, stg_[1], stg_[2])
            elif stg_[0] == "hgrn_scan":
                stage_hgrn_scan(P, g, stg_[1], stg_[2])
            elif stg_[0] == "hgrn_out":
                stage_hgrn_readout(P, g, stg_[1], stg_[2], stg_[3])
            elif stg_[0] == "dumpcols":
                P.barrier()
                for i in range(DC):
                    P.add("sp", lambda e, i=i, a=stg_[1], n=stg_[2], o=stg_[3]: e.dma_start(
                        out=g.d_out[i * 128:(i + 1) * 128, o:o + n], in_=g.d_h[i * 128:(i + 1) * 128, a:a + n]), dma=True)
                P.barrier()
                g.dumped = True
            elif stg_[0] == "dump":
                src_ = getattr(g, stg_[1])
                P.barrier()
                for i in range(stg_[3] // 128):
                    P.add("sp", lambda e, i=i, src_=src_, r0=stg_[2], o0=stg_[4]: e.dma_start(
                        out=g.d_out[o0 + i * 128: o0 + (i + 1) * 128, :], in_=src_[r0 + i * 128: r0 + (i + 1) * 128, NCTX:L]), dma=True)
                g.dumped = True
            elif stg_[0] == "hgrn":
                l_, j_, last_ = stg_[1], stg_[2], stg_[3]
                stage_hgrn_inproj(P, g, l_, j_)
                stage_hgrn_scan(P, g, j_)
                stage_hgrn_readout(P, g, l_, j_, last_)
            elif stg_ == "final":
                stage_final(P, g)
            elif stg_ == "dbg_modc":
                P.barrier()
                P.add(ST, lambda e: e.dma_start(out=g.d_out[0:128, 0:4 * NMODC * 2], in_=g.modc[:, :]), dma=True)
                P.emit()
                g.stats = P.stats
                return nc, g
        if "final" not in plan and not getattr(g, "dumped", False):
            P.barrier()
            for i in range(DC):
                P.add("sp", lambda e, i=i: e.dma_start(out=g.d_out[i * 128:(i + 1) * 128, :], in_=g.d_h[i * 128:(i + 1) * 128, NCTX:L]),
                      dma=True)
        P.emit()
        g.stats = P.stats
    return nc, g


def tile_w(W):
    K, N = W.shape
    return np.ascontiguousarray(W.reshape(K // 128, 128, N // 128, 128).transpose(2, 1, 0, 3))


def col16(v):
    return np.ascontiguousarray(v.reshape(-1, 128).T)


def prep_ffn_w(w_gu, w_dn):
    tg = tile_w(w_gu)
    wgu = np.ascontiguousarray(np.stack([tg[:FC], tg[FC:]], axis=2).reshape(FC, 128, 4096))
    wdn = np.ascontiguousarray(tile_w(w_dn).reshape(DC, 128, DFF))
    return wgu, wdn


def prep_inputs(inp, nlayers=4):
    shared = {}
    shared["mod_w"] = np.ascontiguousarray(inp["mod_w"])
    shared["mod_b"] = np.ascontiguousarray(np.concatenate([col16(inp["mod_b"][l]) for l in range(4)], axis=1))
    shared["gcols"] = np.ascontiguousarray(np.concatenate(
        [col16(inp["norm_g"][l, s]) for l in range(4) for s in range(3)] + [col16(inp["final_g"])], axis=1))
    for nm, gu, dn in (("1", "ffn1_w_gu", "ffn1_w_down"), ("2", "ffn2_w_gu", "ffn2_w_down")):
        a, b = zip(*[prep_ffn_w(inp[gu][l], inp[dn][l]) for l in range(4)])
        shared["wgu" + nm] = np.stack(a)
        shared["wdn" + nm] = np.stack(b)
    shared["hgw_in"] = np.stack([tile_w(inp["hgrn_w_in"][j]).reshape(80, 128, D) for j in range(2)])
    shared["hgw_out"] = np.stack([tile_w(inp["hgrn_w_out"][j]).reshape(DC, 128, D) for j in range(2)])
    shared["hgn"] = np.ascontiguousarray(np.concatenate([col16(inp["hgrn_g_norm"][j]) for j in range(2)], axis=1))
    shared["lbl"] = np.ascontiguousarray(np.concatenate(
        [col16(inp["hgrn_lb_logits"][d_, j]) for d_ in range(2) for j in range(2)], axis=1))
    shared["cst"] = make_consts()
    pidx = np.array([a * 32 + (1 - hf) * 16 + f for a in range(2) for hf in range(2) for f in range(16)])
    zpad = lambda w: np.concatenate([w, np.zeros((w.shape[0], 64), np.float32)], axis=1)
    wd = inp["mla_w_dqkv"][0]
    wd_ext = np.concatenate([wd[:, :1024], zpad(wd[:, 1024:1088]), zpad(wd[:, 1024:1088][:, pidx])], axis=1)
    shared["wdqkv"] = tile_w(wd_ext).reshape(10, 128, D)
    wq = inp["mla_w_uq"][0].reshape(512, 16, 192)
    wq_ext = np.concatenate([np.concatenate([wq[:, hh, :128], zpad(wq[:, hh, 128:]), zpad(wq[:, hh, 128:][:, pidx])], axis=1)
                             for hh in range(16)], axis=1)
    shared["wuq"] = tile_w(wq_ext).reshape(48, 128, 512)
    shared["wukv"] = tile_w(inp["mla_w_ukv"][0]).reshape(32, 128, 512)
    shared["wo"] = tile_w(inp["mla_w_o"][0]).reshape(DC, 128, D)
    shared["mlan"] = np.ascontiguousarray(np.concatenate([col16(inp["mla_q_norm"][0]), col16(inp["mla_kv_norm"][0])], axis=1))
    shared["rope"] = make_rope()
    shared["fw"] = tile_w(inp["fnet_w_out"][0]).reshape(DC, 128, D)
    shared.update(make_dft())
    maps = []
    for b in range(2):
        m = dict(shared)
        m["x"] = np.ascontiguousarray(np.concatenate([inp["ctx"][b], inp["x"][b]], axis=0).T)
        cvec = np.stack([inp["c"][b], inp["c_ctx"]])
        m["cT"] = np.ascontiguousarray(cvec.reshape(2, 16, 128).transpose(2, 1, 0).reshape(128, 32))
        maps.append(m)
    return maps


def make_consts():
    c = np.zeros((128, 516), np.float32)
    idx = np.arange(128)
    c[:, 0:128] = np.eye(128, dtype=np.float32)
    same = (idx[:, None] // 32) == (idx[None, :] // 32)
    c[:, 128:256] = (same & (idx[:, None] <= idx[None, :])).astype(np.float32)
    c[:, 256:384] = (same & (idx[:, None] >= idx[None, :])).astype(np.float32)
    c[:, 384:512] = (np.arange(128) % 32 != 0).astype(np.float32)[None, :]
    for k in range(4):
        c[:, 512 + k] = (idx // 32 == k).astype(np.float32)
    return c


def make_rope():
    t = np.arange(NLAT)
    pos = np.stack([(t // 64).astype(np.float32), (t % 64).astype(np.float32)], axis=-1)
    inv_freq = (np.float32(10000.0) ** (-np.arange(16, dtype=np.float32) / np.float32(16))).astype(np.float32)
    ang = (pos[:, :, None] * inv_freq[None, None, :]).astype(np.float32)
    cos, sin = np.cos(ang).astype(np.float32), np.sin(ang).astype(np.float32)
    tab = np.zeros((128, 2, L), np.float32)
    tab[:, 0, :] = 1.0
    for a in range(2):
        for hf in range(2):
            rows = slice(a * 32 + hf * 16, a * 32 + hf * 16 + 16)
            tab[rows, 0, NCTX:] = cos[:, a, :].T
            tab[rows, 1, NCTX:] = (-sin[:, a, :].T) if hf == 0 else sin[:, a, :].T
    return np.ascontiguousarray(tab.reshape(128, 2 * L))


def make_dft():
    import ml_dtypes
    c = np.arange(256)
    ang = 2.0 * np.pi * ((c[:, None] * c[None, :]) % 256) / 256.0
    t256 = np.zeros((128, 2, 2, 256), np.float32)
    for kk in range(2):
        t256[:, kk, 0, :] = np.cos(ang[kk * 128:(kk + 1) * 128])
        t256[:, kk, 1, :] = np.sin(ang[kk * 128:(kk + 1) * 128])
    t = np.arange(NLAT)
    angT = 2.0 * np.pi * ((t[:, None] * t[None, :]) % NLAT) / float(NLAT)
    dftT = np.stack([np.cos(angT) / 1024.0, -np.sin(angT) / 1024.0]).astype(np.float32).astype(ml_dtypes.bfloat16)
    tc_ = np.arange(NCTX)
    angC = 2.0 * np.pi * ((tc_[:, None] * tc_[None, :]) % NCTX) / float(NCTX)
    dftC = np.stack([np.cos(angC) / 256.0, -np.sin(angC) / 256.0]).astype(np.float32).astype(ml_dtypes.bfloat16)
    return {"dft256": np.ascontiguousarray(t256.reshape(128, 1024)), "dftT": np.ascontiguousarray(dftT), "dftC": np.ascontiguousarray(dftC)}


PLAN = ['mod',
        ('ffn', 0, 0), ('hgrn', 0, 0, False), ('ffn', 0, 1),
        ('ffn', 1, 0), ('mla', 1, False), ('ffn', 1, 1),
        ('ffn', 2, 0), ('fnet', 2), ('ffn', 2, 1),
        ('ffn', 3, 0), ('hgrn', 3, 1, True), ('ffn', 3, 1, True),
        'final']


def kernel(**inputs):
    inp = {k: np.asarray(v) for k, v in inputs.items()}
    maps = prep_inputs(inp)
    nc, g = build(PLAN)
    need = [a.memorylocations[0].name for a in nc.allocations
            if getattr(a, "kind", None) == "ExternalInput"]
    in_maps = [{k: m[k] for k in need if k in m} for m in maps]
    res = run_bass_kernel_spmd(nc, in_maps, core_ids=[0, 1])
    out = np.stack([np.ascontiguousarray(res.results[b]["out"].T) for b in range(2)])
    return out.astype(np.float32)
```
